# Optimizing a Trainium2 kernel written in Bass

```python
import math
import jax, jax.numpy as jnp
from jax import lax
import numpy as np

D_MODEL = 2048
BATCH = 4
SEQ = 2048
DEPTH = 1

ATTN_WIDTH = D_MODEL // 2
RWKV_WIDTH = D_MODEL - ATTN_WIDTH
ATTN_HEAD_DIM = 64
ATTN_HEADS = ATTN_WIDTH // (2 * ATTN_HEAD_DIM)
RWKV_HEAD_SIZE = 64
RWKV_HEADS = RWKV_WIDTH // RWKV_HEAD_SIZE
DECAY_LORA = 64
ICLR_LORA = 64
GATE_LORA = 160
D_FF = 5632
Q_BLOCK = 128
LN_EPS = 1e-5
ATTN_NORM_EPS = 1e-5
GN_EPS = 64e-5
ATTN_COLS = 3 * ATTN_WIDTH
RWKV_COLS = 3 * RWKV_WIDTH + DECAY_LORA + ICLR_LORA + GATE_LORA
IN_COLS = ATTN_COLS + RWKV_COLS

kernel_name = "hymba_diffattn_rwkv7_macaron_deepnorm"


def layer_norm(x, g, b):
    xf = x.astype(jnp.float32)
    mu = xf.mean(-1, keepdims=True)
    var = jnp.square(xf - mu).mean(-1, keepdims=True)
    return ((xf - mu) * lax.rsqrt(var + LN_EPS) * g + b).astype(x.dtype)


def swiglu(x, w_gate, w_up, w_down):
    return (jax.nn.silu(x @ w_gate) * (x @ w_up)) @ w_down


def alibi_slopes(n_heads):
    return jnp.exp2(-8.0 * (jnp.arange(n_heads, dtype=jnp.float32) + 1.0) / n_heads)


def diff_attention(q1, q2, k1, k2, v, lam):
    B, H, S, d = q1.shape
    nb = S // Q_BLOCK
    scale = d ** -0.5
    slopes = alibi_slopes(H)
    kpos = jnp.arange(S)

    def block(args):
        qa, qb, i = args
        qpos = i * Q_BLOCK + jnp.arange(Q_BLOCK)
        dist = qpos[:, None] - kpos[None, :]
        bias = jnp.where(dist[None] >= 0,
                         -slopes[:, None, None] * dist[None].astype(jnp.float32),
                         -jnp.inf)
        s1 = jnp.einsum('bhqd,bhkd->bhqk', qa, k1).astype(jnp.float32) * scale + bias
        s2 = jnp.einsum('bhqd,bhkd->bhqk', qb, k2).astype(jnp.float32) * scale + bias
        p = jax.nn.softmax(s1, axis=-1) - lam * jax.nn.softmax(s2, axis=-1)
        return jnp.einsum('bhqk,bhke->bhqe', p.astype(v.dtype), v)

    to_blocks = lambda q: q.reshape(B, H, nb, Q_BLOCK, d).transpose(2, 0, 1, 3, 4)
    o = lax.map(block, (to_blocks(q1), to_blocks(q2), jnp.arange(nb)))
    return o.transpose(1, 2, 0, 3, 4).reshape(B, H, S, 2 * d)


def diff_attn_group(pa, lq1, lk1, lq2, lk2, norm_g, lambda_init):
    B, S, _ = pa.shape
    H, d = ATTN_HEADS, ATTN_HEAD_DIM
    q = pa[..., :ATTN_WIDTH].reshape(B, S, H, 2, d).transpose(0, 2, 1, 3, 4)
    k = pa[..., ATTN_WIDTH:2 * ATTN_WIDTH].reshape(B, S, H, 2, d).transpose(0, 2, 1, 3, 4)
    v = pa[..., 2 * ATTN_WIDTH:].reshape(B, S, H, 2 * d).transpose(0, 2, 1, 3)
    lam = (jnp.exp(jnp.sum(lq1.astype(jnp.float32) * lk1.astype(jnp.float32)))
           - jnp.exp(jnp.sum(lq2.astype(jnp.float32) * lk2.astype(jnp.float32)))
           + lambda_init)
    o = diff_attention(q[..., 0, :], q[..., 1, :], k[..., 0, :], k[..., 1, :], v, lam)
    of = o.astype(jnp.float32)
    of = of * lax.rsqrt(jnp.mean(of * of, -1, keepdims=True) + ATTN_NORM_EPS) * norm_g
    of = of * (1.0 - lambda_init)
    return of.transpose(0, 2, 1, 3).reshape(B, S, ATTN_WIDTH).astype(pa.dtype)


def rwkv7_group(pr, mu, w0, w2, a0, a2, g2, k_k, k_a, r_k, gn_g, gn_b):
    B, S, _ = pr.shape
    H, N, C = RWKV_HEADS, RWKV_HEAD_SIZE, RWKV_WIDTH
    f32 = jnp.float32
    prev = jnp.pad(pr, ((0, 0), (1, 0), (0, 0)))[:, :-1]
    pr = pr + mu * (prev - pr)
    c1, c2, c3 = C, 2 * C, 3 * C
    c4, c5 = c3 + DECAY_LORA, c3 + DECAY_LORA + ICLR_LORA
    r, k, v = pr[..., :c1], pr[..., c1:c2], pr[..., c2:c3]
    wd, ad, gd = pr[..., c3:c4], pr[..., c4:c5], pr[..., c5:]
    w = -jax.nn.softplus(-(w0 + jnp.tanh(wd) @ w2)) - 0.5
    decay = jnp.exp(-jnp.exp(w.astype(f32)))
    a = jax.nn.sigmoid(a0 + ad @ a2)
    g = jax.nn.sigmoid(gd) @ g2
    kk = (k * k_k).astype(f32).reshape(B, S, H, N)
    kk = kk / jnp.maximum(jnp.linalg.norm(kk, axis=-1, keepdims=True), 1e-12)
    k = k * (1.0 + (a - 1.0) * k_a)
    heads = lambda t: t.astype(f32).reshape(B, S, H, N)
    r_h, k_h, v_h, a_h, w_h = heads(r), heads(k), heads(v), heads(a), heads(decay)

    def step(state, xs):
        r_t, w_t, k_t, v_t, kk_t, a_t = xs
        sa = jnp.einsum('bhvk,bhk->bhv', state, -kk_t)
        state = (state * w_t[:, :, None, :]
                 + sa[..., None] * (kk_t * a_t)[:, :, None, :]
                 + v_t[..., None] * k_t[:, :, None, :])
        return state, jnp.einsum('bhvk,bhk->bhv', state, r_t)

    tmaj = lambda t: t.transpose(1, 0, 2, 3)
    state0 = jnp.zeros((B, H, N, N), f32)
    _, y = lax.scan(step, state0, (tmaj(r_h), tmaj(w_h), tmaj(k_h), tmaj(v_h), tmaj(kk), tmaj(a_h)))
    y = y.transpose(1, 0, 2, 3)
    ym = y.mean(-1, keepdims=True)
    yv = jnp.square(y - ym).mean(-1, keepdims=True)
    yn = (y - ym) * lax.rsqrt(yv + GN_EPS) * gn_g.reshape(H, N) + gn_b.reshape(H, N)
    bonus = jnp.sum(r_h * k_h * r_k.astype(f32), -1, keepdims=True) * v_h
    return (yn + bonus).reshape(B, S, C).astype(pr.dtype) * g


def setup_inputs(seed: int = 0) -> dict:
    key = jax.random.key(seed)
    ks = jax.random.split(key, 40)
    f32 = jnp.float32
    L = DEPTH
    beta = (8.0 * DEPTH) ** -0.25
    nrm = lambda k, shape, s: jax.random.normal(k, shape, f32) * s
    gain = lambda k, shape: 1.0 + 0.02 * jax.random.normal(k, shape, f32)
    return {
        "x": nrm(ks[0], (BATCH, SEQ, D_MODEL), 1.0),
        "ffn1_w_gate": nrm(ks[1], (L, D_MODEL, D_FF), D_MODEL ** -0.5),
        "ffn1_w_up": nrm(ks[2], (L, D_MODEL, D_FF), D_MODEL ** -0.5),
        "ffn1_w_down": nrm(ks[3], (L, D_FF, D_MODEL), beta * D_FF ** -0.5),
        "ln1_g": gain(ks[4], (L, D_MODEL)),
        "ln1_b": nrm(ks[5], (L, D_MODEL), 0.02),
        "w_in": nrm(ks[6], (L, D_MODEL, IN_COLS), D_MODEL ** -0.5),
        "lambda_q1": nrm(ks[7], (L, ATTN_HEAD_DIM), 0.1),
        "lambda_k1": nrm(ks[8], (L, ATTN_HEAD_DIM), 0.1),
        "lambda_q2": nrm(ks[9], (L, ATTN_HEAD_DIM), 0.1),
        "lambda_k2": nrm(ks[10], (L, ATTN_HEAD_DIM), 0.1),
        "attn_norm_g": gain(ks[11], (L, 2 * ATTN_HEAD_DIM)),
        "rwkv_mu": jax.random.uniform(ks[12], (L, RWKV_COLS), f32),
        "rwkv_w0": nrm(ks[13], (L, RWKV_WIDTH), 1.0),
        "rwkv_w2": nrm(ks[14], (L, DECAY_LORA, RWKV_WIDTH), 0.3 * DECAY_LORA ** -0.5),
        "rwkv_a0": nrm(ks[15], (L, RWKV_WIDTH), 0.1),
        "rwkv_a2": nrm(ks[16], (L, ICLR_LORA, RWKV_WIDTH), 0.3 * ICLR_LORA ** -0.5),
        "rwkv_g2": nrm(ks[17], (L, GATE_LORA, RWKV_WIDTH), GATE_LORA ** -0.5),
        "rwkv_k_k": 0.85 + nrm(ks[18], (L, RWKV_WIDTH), 0.05),
        "rwkv_k_a": 1.0 + nrm(ks[19], (L, RWKV_WIDTH), 0.05),
        "rwkv_r_k": nrm(ks[20], (L, RWKV_HEADS, RWKV_HEAD_SIZE), 0.1),
        "rwkv_gn_g": gain(ks[21], (L, RWKV_WIDTH)),
        "rwkv_gn_b": nrm(ks[22], (L, RWKV_WIDTH), 0.02),
        "w_out": nrm(ks[23], (L, ATTN_WIDTH + RWKV_WIDTH, D_MODEL), beta * (ATTN_WIDTH + RWKV_WIDTH) ** -0.5),
        "ln2_g": gain(ks[24], (L, D_MODEL)),
        "ln2_b": nrm(ks[25], (L, D_MODEL), 0.02),
        "ffn2_w_gate": nrm(ks[26], (L, D_MODEL, D_FF), D_MODEL ** -0.5),
        "ffn2_w_up": nrm(ks[27], (L, D_MODEL, D_FF), D_MODEL ** -0.5),
        "ffn2_w_down": nrm(ks[28], (L, D_FF, D_MODEL), beta * D_FF ** -0.5),
        "ln3_g": gain(ks[29], (L, D_MODEL)),
        "ln3_b": nrm(ks[30], (L, D_MODEL), 0.02),
    }


def reference(x, ffn1_w_gate, ffn1_w_up, ffn1_w_down, ln1_g, ln1_b, w_in,
              lambda_q1, lambda_k1, lambda_q2, lambda_k2, attn_norm_g,
              rwkv_mu, rwkv_w0, rwkv_w2, rwkv_a0, rwkv_a2, rwkv_g2,
              rwkv_k_k, rwkv_k_a, rwkv_r_k, rwkv_gn_g, rwkv_gn_b,
              w_out, ln2_g, ln2_b, ffn2_w_gate, ffn2_w_up, ffn2_w_down, ln3_g, ln3_b):
    alpha = (2.0 * DEPTH) ** 0.25
    for l in range(DEPTH):
        lambda_init = 0.8 - 0.6 * math.exp(-0.3 * l)
        x = layer_norm(alpha * x + 0.5 * swiglu(x, ffn1_w_gate[l], ffn1_w_up[l], ffn1_w_down[l]),
                       ln1_g[l], ln1_b[l])
        p = x @ w_in[l]
        o_attn = diff_attn_group(p[..., :ATTN_COLS], lambda_q1[l], lambda_k1[l],
                                 lambda_q2[l], lambda_k2[l], attn_norm_g[l], lambda_init)
        o_rwkv = rwkv7_group(p[..., ATTN_COLS:], rwkv_mu[l], rwkv_w0[l], rwkv_w2[l],
                             rwkv_a0[l], rwkv_a2[l], rwkv_g2[l], rwkv_k_k[l], rwkv_k_a[l],
                             rwkv_r_k[l], rwkv_gn_g[l], rwkv_gn_b[l])
        mix = jnp.concatenate([o_attn, o_rwkv], axis=-1) @ w_out[l]
        x = layer_norm(alpha * x + mix, ln2_g[l], ln2_b[l])
        x = layer_norm(alpha * x + 0.5 * swiglu(x, ffn2_w_gate[l], ffn2_w_up[l], ffn2_w_down[l]),
                       ln3_g[l], ln3_b[l])
    return x
```

```python
import numpy as np
import concourse.bass as bass
import concourse.mybir as mybir
from concourse.bass_utils import run_bass_kernel_spmd

F32 = mybir.dt.float32
BF16 = mybir.dt.bfloat16
AF = mybir.ActivationFunctionType
ALU = mybir.AluOpType
AX = mybir.AxisListType

SAME_ENGINE_SYNC = True
SEM_CHUNK = 4000
N_DMA_SEMS = 12


class Buf:
    __slots__ = ("name", "last_w", "readers", "excl")

    def __init__(self, name, excl=False):
        self.name = name
        self.last_w = None
        self.readers = []
        self.excl = excl


class Op:
    __slots__ = ("eng", "fn", "deps", "dma", "needs_inc", "inc_no", "dsem", "dval", "pos", "tag", "prev_dval")

    def __init__(self, eng, fn, dma, tag):
        self.eng = eng
        self.fn = fn
        self.deps = []
        self.dma = dma
        self.needs_inc = False
        self.inc_no = None
        self.dsem = None
        self.dval = None
        self.tag = tag


class Alloc:
    def __init__(self, name, off, nbytes, handle, bufs):
        self.name, self.off, self.nbytes, self.handle, self.bufs = name, off, nbytes, handle, bufs


ENGS = ("pe", "act", "dve", "pool", "sp")


class Prog:
    def __init__(self, nc):
        self.nc = nc
        self.ops = {e: [] for e in ENGS}
        self.all_ops = []
        self.live = []
        self.ghosts = []
        self.uid = 0
        self.sbuf_top = 0

    def sbuf(self, name, shape, dtype, off, nbufs=1):
        esz = 4 if dtype == F32 else 2
        free = 1
        for s in shape[1:]:
            free *= s
        nbytes = free * esz
        assert off % 32 == 0, (name, off)
        assert 16384 <= off and off + nbytes <= 224 * 1024 - 160, (name, off, nbytes)
        self.uid += 1
        h = self.nc.alloc_sbuf_tensor_at(f"{name}_{self.uid}", list(shape), dtype, offset=off)
        bufs = [Buf(f"{name}[{i}]") for i in range(nbufs)]
        for a in self.live:
            assert a.off + a.nbytes <= off or off + nbytes <= a.off, ("overlap", name, a.name)
        keep = []
        for g in self.ghosts:
            if g.off + g.nbytes <= off or off + nbytes <= g.off:
                keep.append(g)
                continue
            hz = []
            for b in g.bufs:
                if b.last_w is not None:
                    hz.append(b.last_w)
                hz.extend(b.readers)
            for b in bufs:
                b.readers.extend(hz)
            keep.append(g)
        self.ghosts = keep
        a = Alloc(name, off, nbytes, h, bufs)
        self.live.append(a)
        return a

    def free(self, a):
        self.live.remove(a)
        self.ghosts.append(a)

    def op(self, eng, fn, reads=(), writes=(), dma=False, tag="", pe_sync=False):
        o = Op(eng, fn, dma, tag)
        deps = []
        for b in reads:
            if b.excl:
                continue
            if b.last_w is not None:
                deps.append(b.last_w)
        for b in list(writes) + [b for b in reads if b.excl]:
            if b.last_w is not None:
                deps.append(b.last_w)
            deps.extend(b.readers)
        for b in reads:
            if not b.excl:
                b.readers.append(o)
        for b in list(writes) + [b for b in reads if b.excl]:
            b.last_w = o
            b.readers = []
        seen = set()
        for d in deps:
            if id(d) in seen or d is o:
                continue
            seen.add(id(d))
            if (not d.dma) and d.eng == eng and not SAME_ENGINE_SYNC:
                continue
            if (not d.dma) and d.eng == eng and eng == "pe" and not pe_sync:
                continue
            o.deps.append(d)
        o.pos = len(self.ops[eng])
        self.ops[eng].append(o)
        self.all_ops.append(o)
        return o

    def emit(self):
        nc = self.nc
        for o in self.all_ops:
            for d in o.deps:
                d.needs_inc = True
        cnt = {e: 0 for e in ENGS}
        dma_cnt = {e: 0 for e in ENGS}
        n_sems = {}
        for e in ENGS:
            n = sum(1 for o in self.ops[e] if o.needs_inc and not o.dma)
            n_sems[e] = max(1, (n + SEM_CHUNK - 1) // SEM_CHUNK)
        import contextlib
        with contextlib.ExitStack() as st:
            esems = {e: [st.enter_context(nc.semaphore(f"s_{e}_{i}")) for i in range(n_sems[e])] for e in ENGS}
            dsems = {e: [st.enter_context(nc.semaphore(f"d_{e}_{i}")) for i in range(N_DMA_SEMS)]
                     for e in ("sp", "act", "pool")}
            dsem_val = {e: [0] * N_DMA_SEMS for e in dsems}
            ccsems = []
            dsems["cc"] = ccsems
            for e in ENGS:
                for o in self.ops[e]:
                    if o.dma == "cc":
                        o.dsem = ("cc", len(ccsems))
                        ccsems.append(st.enter_context(nc.semaphore(f"cc_{len(ccsems)}")))
                        o.prev_dval = 0
                        o.dval = 1
                    elif o.dma:
                        k = dma_cnt[e] % N_DMA_SEMS
                        dma_cnt[e] += 1
                        o.dsem = (e, k)
                        o.prev_dval = dsem_val[e][k]
                        dsem_val[e][k] += 16
                        o.dval = dsem_val[e][k]
                    elif o.needs_inc:
                        o.inc_no = cnt[e]
                        cnt[e] += 1
            items = {e: [] for e in ENGS}
            for e in ENGS:
                waited = {}
                for o in self.ops[e]:
                    ws = []
                    for d in o.deps:
                        if d.dma:
                            key = ("d",) + d.dsem
                            val = d.dval
                            sem = dsems[d.dsem[0]][d.dsem[1]]
                        else:
                            ch = d.inc_no // SEM_CHUNK
                            key = ("e", d.eng, ch)
                            val = d.inc_no % SEM_CHUNK + 1
                            sem = esems[d.eng][ch]
                            later = any(k[0] == "e" and k[1] == d.eng and k[2] > ch for k in waited)
                            if later:
                                continue
                        if waited.get(key, 0) >= val:
                            continue
                        waited[key] = val
                        ws.append((sem, val))
                    if o.dma and o.prev_dval > 0:
                        key = ("d",) + o.dsem
                        if waited.get(key, 0) < o.prev_dval:
                            waited[key] = o.prev_dval
                            ws.append((dsems[o.dsem[0]][o.dsem[1]], o.prev_dval))
                    items[e].append((ws, o))
            self.stats = {e: (len(self.ops[e]), sum(len(w) for w, _ in items[e])) for e in ENGS}

            def run(e, eng):
                for ws, o in items[e]:
                    for sem, val in ws:
                        eng.wait_ge(sem, val)
                    ins = o.fn(eng)
                    if o.dma == "cc":
                        ins.then_inc(dsems["cc"][o.dsem[1]], 1)
                    elif o.dma:
                        ins.then_inc(dsems[o.dsem[0]][o.dsem[1]], 16)
                    elif o.needs_inc:
                        ins.then_inc(esems[e][o.inc_no // SEM_CHUNK], 1)

            with nc.Block() as block:
                @block.tensor
                def _(eng):
                    run("pe", eng)

                @block.scalar
                def _(eng):
                    run("act", eng)

                @block.vector
                def _(eng):
                    run("dve", eng)

                @block.gpsimd
                def _(eng):
                    run("pool", eng)

                @block.sync
                def _(eng):
                    run("sp", eng)


D = 2048
DFF = 5632
NTOK = 1024
SEQ = 2048
ALPHA = 2.0 ** 0.25
LN_EPS = 1e-5
FG = 256
NG = DFF // FG
KC = D // 128


class Ctx:
    pass


def mk_ctx(nc):
    K = Ctx()
    K.nc = nc
    K.P = Prog(nc)
    K.ps = []
    K.psb = []
    for i in range(8):
        h = nc.alloc_psum_tensor(f"psum{i}", [128, 512], F32)
        K.ps.append(h)
        K.psb.append(Buf(f"psum{i}", excl=True))
    K.dram = {}
    return K


def dram_in(K, name, shape, dtype=F32):
    t = K.nc.dram_tensor(name, list(shape), dtype, kind="ExternalInput")
    K.dram[name] = (t, Buf("dram_" + name))
    return t


def dram_tmp(K, name, shape, dtype, kind="Internal"):
    t = K.nc.dram_tensor(name, list(shape), dtype, kind=kind)
    K.dram[name] = (t, Buf("dram_" + name))
    return t


def load_const(K, name, alloc, dst_ap, src_ap, q="sp"):
    K.P.op(q, lambda e, o=dst_ap, i=src_ap: e.dma_start(out=o, in_=i), reads=[], writes=alloc.bufs, dma=True,
           tag="const " + name)


def load_xT(K, src, src_buf, ntok, XT, xb_off, banks, ident, src_f32):
    P = K.P
    nt = ntok // 128
    xb = P.sbuf("xb", [128, 2, D], BF16, xb_off, nbufs=2)
    xbh = xb.handle
    XTh = XT.handle
    for t in range(nt):
        s = t % 2
        q = "pool" if src_f32 else "sp"
        if callable(src):
            sap, sbuf_ = src(t)
        else:
            sap, sbuf_ = src[t * 128:(t + 1) * 128, :], src_buf
        P.op(q, lambda e, s=s, sap=sap: e.dma_start(out=xbh[:, s, :], in_=sap),
             reads=[sbuf_], writes=[xb.bufs[s]], dma=True, tag="xb load")
        for half in range(2):
            bk = banks[(2 * t + half) % len(banks)]
            pb = K.ps[bk].bitcast(BF16)

            def tr(e, s=s, half=half, pb=pb):
                ins = None
                for j in range(8):
                    kc = half * 8 + j
                    ins = e.transpose(pb[:, j * 128:(j + 1) * 128], xbh[:, s, kc * 128:(kc + 1) * 128], ident[:])
                return ins
            P.op("pe", tr, reads=[xb.bufs[s], K.ident_buf], writes=[K.psb[bk]], tag="xT transposes")
            eng = "act" if half == 0 else "dve"

            def ev(e, t=t, half=half, pb=pb, eng=eng):
                o = XTh[:, half * 8:(half + 1) * 8, t * 128:(t + 1) * 128]
                i = pb.rearrange("p (j c) -> p j c", j=8)
                if eng == "act":
                    return e.activation(out=o, in_=i, func=AF.Copy)
                return e.tensor_copy(out=o, in_=i)
            P.op(eng, ev, reads=[K.psb[bk]], writes=[XT.bufs[t]], tag="xT evac")
    P.free(xb)


def ffn_stage(K, x_src, x_buf, wg, wu, wd, lng, lnb, outs, base=0):
    P = K.P
    nc = K.nc
    NT = NTOK // 128
    off = base
    acc = P.sbuf("acc", [128, NT, D], F32, off, nbufs=NT * 4); off += NT * D * 4
    XT = P.sbuf("XT", [128, KC, NTOK], BF16, off, nbufs=NT); off += KC * NTOK * 2
    wgs = P.sbuf("wgs", [128, 2, KC, FG], BF16, off, nbufs=2); off += 2 * KC * FG * 2
    wus = P.sbuf("wus", [128, 2, KC, FG], BF16, off, nbufs=2); off += 2 * KC * FG * 2
    wds = P.sbuf("wds", [128, 2, 2, D], BF16, off, nbufs=2); off += 2 * 2 * D * 2
    hT = P.sbuf("hT", [128, 2, 2, NTOK], BF16, off, nbufs=2); off += 2 * 2 * NTOK * 2
    stmp = P.sbuf("stmp", [128, 2, 512], F32, off, nbufs=2); off += 2 * 512 * 4
    gb = P.sbuf("lngb", [128, 2, D], F32, off, nbufs=1); off += 2 * D * 4
    st = P.sbuf("lnst", [128, 2, 4 * 6 + 8], F32, off, nbufs=2); off += 2 * 32 * 4
    xb_off = off
    acch, XTh, wgh, wuh, wdh, hTh, sth, gbh, lsth = (acc.handle, XT.handle, wgs.handle, wus.handle, wds.handle,
                                                      hT.handle, stmp.handle, gb.handle, st.handle)

    load_const(K, "lng", gb, gbh[:, 0, :], lng[:, :])
    load_const(K, "lnb", gb, gbh[:, 1, :], lnb[:, :])

    for t in range(NT):
        P.op("sp", lambda e, t=t: e.dma_start(out=acch[:, t, :], in_=x_src[t * 128:(t + 1) * 128, :]),
             reads=[x_buf], writes=acc.bufs[t * 4:(t + 1) * 4], dma=True, tag="acc load")
        P.op("act", lambda e, t=t: e.activation(out=acch[:, t, :], in_=acch[:, t, :], func=AF.Copy, scale=ALPHA),
             reads=[], writes=acc.bufs[t * 4:(t + 1) * 4], tag="acc scale")

    load_xT(K, x_src, x_buf, NTOK, XT, xb_off, [4, 5, 6, 7], K.ident, True)

    def load_wgu(g):
        s = g % 2
        for (wsrc, wh, wa, nm) in ((wg, wgh, wgs, "wg"), (wu, wuh, wus, "wu")):
            for piece in range(2):
                P.op("pool", lambda e, g=g, s=s, wsrc=wsrc, wh=wh, piece=piece: e.dma_start(
                    out=wh[:, s, piece * 8:(piece + 1) * 8, :], in_=wsrc[g, :, piece * 8:(piece + 1) * 8, :]),
                    reads=[], writes=[wa.bufs[s]], dma=True, tag=nm + " load")

    def load_wd(g):
        s = g % 2
        wdv = wd.rearrange("(g fc p) d -> g p fc d", fc=2, p=128)
        P.op("pool", lambda e, g=g, s=s: e.dma_start(out=wdh[:, s, :, :], in_=wdv[g]),
             reads=[], writes=[wds.bufs[s]], dma=True, tag="wd load")

    def upgate(g):
        s = g % 2
        for c in range(2):
            for th in range(2):
                i = (c * 2 + th) % 2
                bg, bu = i, 2 + i
                toks = slice(th * 512, (th + 1) * 512)
                xbufs = XT.bufs[th * 4:(th + 1) * 4]
                for (wh, wa, bk) in ((wgh, wgs, bg), (wuh, wus, bu)):
                    def mm(e, wh=wh, bk=bk, c=c, toks=toks, s=s):
                        ins = None
                        for kc in range(KC):
                            ins = e.matmul(K.ps[bk][:, :], wh[:, s, kc, c * 128:(c + 1) * 128], XTh[:, kc, toks],
                                           start=(kc == 0), stop=(kc == KC - 1))
                        return ins
                    P.op("pe", mm, reads=xbufs + [wa.bufs[s]], writes=[K.psb[bk]], tag="upgate mm")
                P.op("act", lambda e, i=i, bg=bg: e.activation(out=sth[:, i, :], in_=K.ps[bg][:, :], func=AF.Silu),
                     reads=[K.psb[bg]], writes=[stmp.bufs[i]], tag="silu")
                P.op("dve", lambda e, i=i, bu=bu, c=c, toks=toks, s=s: e.tensor_tensor(
                    out=hTh[:, s, c, toks], in0=K.ps[bu][:, :], in1=sth[:, i, :], op=ALU.mult),
                    reads=[K.psb[bu], stmp.bufs[i]], writes=[hT.bufs[s]], tag="hmul")

    dcnt = [0]

    def down(g):
        s = g % 2
        for t in range(NT):
            for db in range(4):
                bk = 4 + dcnt[0] % 4
                dcnt[0] += 1

                def mm(e, bk=bk, t=t, db=db, s=s):
                    ins = None
                    for fc in range(2):
                        ins = e.matmul(K.ps[bk][:, :], hTh[:, s, fc, t * 128:(t + 1) * 128],
                                       wdh[:, s, fc, db * 512:(db + 1) * 512], start=(fc == 0), stop=(fc == 1))
                    return ins
                P.op("pe", mm, reads=[hT.bufs[s], wds.bufs[s]], writes=[K.psb[bk]], tag="down mm")
                P.op("dve", lambda e, bk=bk, t=t, db=db: e.scalar_tensor_tensor(
                    out=acch[:, t, db * 512:(db + 1) * 512], in0=K.ps[bk][:, :], scalar=0.5,
                    in1=acch[:, t, db * 512:(db + 1) * 512], op0=ALU.mult, op1=ALU.add),
                    reads=[K.psb[bk]], writes=[acc.bufs[t * 4 + db]], tag="acc add")

    ngr = K.ng_override if hasattr(K, "ng_override") else NG
    load_wgu(0)
    if ngr > 1:
        load_wgu(1)
    load_wd(0)
    for g in range(ngr):
        upgate(g)
        if g > 0:
            down(g - 1)
        if g + 2 < ngr:
            load_wgu(g + 2)
        if g + 1 < ngr:
            load_wd(g + 1)
    down(ngr - 1)

    P.lnbf = P.sbuf("lnbf", [128, 2, D], BF16, xb_off, nbufs=2)
    for t in range(NT):
        ln_tile(P, acch, acc.bufs[t * 4:(t + 1) * 4], t, gbh, gb, lsth, st, t % 2, outs)
    P.free(P.lnbf)
    for a in (acc, XT, wgs, wus, wds, hT, stmp, gb, st):
        P.free(a)


def setup_consts(K, identd):
    P = K.P
    a = P.sbuf("ident", [128, 128], BF16, 16384)
    K.ident = a.handle
    K.ident_buf = a.bufs[0]
    P.op("pool", lambda e: e.dma_start(out=K.ident[:, :], in_=identd.ap()[:, :]), reads=[], writes=a.bufs, dma=True,
         tag="ident")
    K.const_top = 16384 + 256


def finish(K, out_bufs):
    K.P.op("sp", lambda e: e.nop(), reads=out_bufs, writes=[], tag="final wait")


NCH_ATT = 12
NCH_RW = 15
NEG = -30000.0
LAMBDA_INIT = 0.2
ATTN_EPS = 1e-5
GN_EPS = 64e-5


def project_fm(K, X1T, wslot, wslot_buf, bank_rot, evac):
    P = K.P
    X1Th = X1T.handle
    for tg in range(4):
        bk = bank_rot()

        def mm(e, bk=bk, tg=tg):
            ins = None
            for kc in range(KC):
                ins = e.matmul(K.ps[bk][:, :], wslot[:, kc, :], X1Th[:, kc, tg * 512:(tg + 1) * 512],
                               start=(kc == 0), stop=(kc == KC - 1))
            return ins
        P.op("pe", mm, reads=X1T.bufs[tg * 4:(tg + 1) * 4] + [wslot_buf], writes=[K.psb[bk]], tag="proj fm")
        evac(tg, bk)


def attention_stage(K, X1T, win_t, bm, ctab, lamv, ng_t, o_dst, o_buf, base):
    P = K.P
    off = base
    QT = P.sbuf("QT", [128, 4, SEQ], BF16, off, nbufs=4); off += 4 * SEQ * 2
    KT = P.sbuf("KT", [128, 4, SEQ], BF16, off, nbufs=4); off += 4 * SEQ * 2
    VA = P.sbuf("VA", [128, 16, 4, 130], BF16, off, nbufs=16); off += 16 * 4 * 130 * 2
    wr = P.sbuf("wring", [128, 4, KC, 128], BF16, off, nbufs=4); off += 4 * KC * 128 * 2
    bms = P.sbuf("bms", [128, 2, 5, 512], F32, off, nbufs=2); off += 2 * 5 * 512 * 4
    ct = P.sbuf("ctab", [128, 64], F32, off); off += 64 * 4
    lm = P.sbuf("lam", [128, 4 * 64 + 16], F32, off); off += (4 * 64 + 16) * 4
    ngs = P.sbuf("ngs", [128, 512], F32, off); off += 512 * 4
    stm = P.sbuf("stmp", [128, 2, 512], F32, off, nbufs=2); off += 2 * 512 * 4
    pT = P.sbuf("pT", [128, 2, 512], BF16, off, nbufs=2); off += 2 * 512 * 2
    Osb = P.sbuf("Osb", [128, 2, 4, 130], F32, off, nbufs=2); off += 2 * 4 * 130 * 4
    ow = P.sbuf("ow", [128, 4, 4, 128], F32, off, nbufs=1); off += 16 * 128 * 4
    osq = P.sbuf("osq", [128, 4, 128], F32, off, nbufs=1); off += 4 * 128 * 4
    sm = P.sbuf("sm", [128, 32], F32, off, nbufs=1); off += 32 * 4
    owb = P.sbuf("owb", [128, 4, 4, 128], BF16, off, nbufs=1); off += 16 * 128 * 2
    owbh = owb.handle
    QTh, KTh, VAh, wrh, bmh, cth, lmh, ngh, stmh, pTh, Osh, owh, osqh, smh = (
        QT.handle, KT.handle, VA.handle, wr.handle, bms.handle, ct.handle, lm.handle, ngs.handle, stm.handle,
        pT.handle, Osb.handle, ow.handle, osq.handle, sm.handle)
    X1Th = X1T.handle

    load_const(K, "ctab", ct, cth[:, :], ctab[:, :])
    load_const(K, "lamv", lm, lmh[:, 0:256], lamv.rearrange("p a d -> p (a d)"))
    load_const(K, "ngt", ngs, ngh[:, :], ng_t[:, :])
    P.op("pool", lambda e: e.memset(VAh[:, :, :, 128:130], 1.0), reads=[], writes=VA.bufs, tag="va ones")

    P.op("dve", lambda e: e.tensor_tensor(out=lmh[:, 0:64], in0=lmh[:, 0:64], in1=lmh[:, 64:128], op=ALU.mult),
         reads=[], writes=lm.bufs, tag="lam1")
    P.op("dve", lambda e: e.tensor_tensor(out=lmh[:, 128:192], in0=lmh[:, 128:192], in1=lmh[:, 192:256], op=ALU.mult),
         reads=[], writes=lm.bufs, tag="lam2")
    P.op("dve", lambda e: e.tensor_reduce(out=lmh[:, 256:257], in_=lmh[:, 0:64], axis=AX.X, op=ALU.add),
         reads=[], writes=lm.bufs, tag="lam3")
    P.op("dve", lambda e: e.tensor_reduce(out=lmh[:, 257:258], in_=lmh[:, 128:192], axis=AX.X, op=ALU.add),
         reads=[], writes=lm.bufs, tag="lam4")
    P.op("act", lambda e: e.activation(out=lmh[:, 258:260], in_=lmh[:, 256:258], func=AF.Exp),
         reads=[], writes=lm.bufs, tag="lam5")
    P.op("dve", lambda e: e.tensor_tensor(out=lmh[:, 260:261], in0=lmh[:, 258:259], in1=lmh[:, 259:260],
                                          op=ALU.subtract), reads=[], writes=lm.bufs, tag="lam6")
    P.op("dve", lambda e: e.tensor_scalar(out=lmh[:, 261:262], in0=lmh[:, 260:261], scalar1=LAMBDA_INIT, scalar2=None,
                                          op0=ALU.add), reads=[], writes=lm.bufs, tag="lam7")
    LAM = lmh[:, 261:262]

    rot = [0]

    def bank_rot():
        rot[0] += 1
        return rot[0] % 4

    def load_w(ci):
        s = ci % 4
        P.op("pool", lambda e, ci=ci, s=s: e.dma_start(out=wrh[:, s, :, :], in_=win_t[ci]),
             reads=[], writes=[wr.bufs[s]], dma=True, tag="win load")
        return s
    for ci in range(min(3, NCH_ATT)):
        load_w(ci)
    for ci in range(NCH_ATT):
        s = ci % 4
        if ci + 3 < NCH_ATT:
            load_w(ci + 3)
        if ci < 8:
            dstT, dbuf = (QTh, QT) if ci < 4 else (KTh, KT)
            a = ci % 4

            def evac(tg, bk, dstT=dstT, dbuf=dbuf, a=a):
                eng = "act" if tg % 2 == 0 else "dve"

                def ev(e, tg=tg, bk=bk):
                    o = dstT[:, a, tg * 512:(tg + 1) * 512]
                    if eng == "act":
                        return e.activation(out=o, in_=K.ps[bk][:, :], func=AF.Copy)
                    return e.tensor_copy(out=o, in_=K.ps[bk][:, :])
                P.op(eng, ev, reads=[K.psb[bk]], writes=[dbuf.bufs[a]], tag="qk evac")
            project_fm(K, X1T, wrh[:, s], wr.bufs[s], bank_rot, evac)
        else:
            a = ci - 8
            for tq in range(4):
                bk = bank_rot()

                def mm(e, bk=bk, tq=tq, s=s):
                    ins = None
                    for tt in range(4):
                        t = tq * 4 + tt
                        for kc in range(KC):
                            ins = e.matmul(K.ps[bk][:, tt * 128:(tt + 1) * 128], X1Th[:, kc, t * 128:(t + 1) * 128],
                                           wrh[:, s, kc, :], start=(kc == 0 and tt == 0), stop=(kc == KC - 1),
                                           skip_group_check=True)
                    return ins
                P.op("pe", mm, reads=X1T.bufs[tq * 4:(tq + 1) * 4] + [wr.bufs[s]], writes=[K.psb[bk]], tag="v proj")
                P.op("act", lambda e, bk=bk, tq=tq, a=a: e.activation(
                    out=VAh[:, tq * 4:(tq + 1) * 4, a, 0:128], in_=K.ps[bk].rearrange("p (t c) -> p t c", t=4),
                    func=AF.Copy), reads=[K.psb[bk]], writes=VA.bufs[tq * 4:(tq + 1) * 4], tag="v evac")

    SCALE = 64 ** -0.5
    tcnt = [0]
    for a in range(4):
        bs = a % 2
        P.op("sp", lambda e, a=a, bs=bs: e.dma_start(out=bmh[:, bs, :, :], in_=bm[a].rearrange("r p q -> p r q")),
             reads=[], writes=[bms.bufs[bs]], dma=True, tag="bm load")
        for qg in range(4):
            nkb = 4 * qg + 4
            for m in range(2):
                pr = slice(m * 64, (m + 1) * 64)
                ob = (4, 5) if m == 0 else (6, 7)
                for kb in range(nkb):
                    i = tcnt[0] % 2
                    tcnt[0] += 1
                    sb = i
                    P.op("pe", lambda e, sb=sb, a=a, pr=pr, kb=kb, qg=qg: e.matmul(
                        K.ps[sb][:, :], KTh[pr, a, kb * 128:(kb + 1) * 128], QTh[pr, a, qg * 512:(qg + 1) * 512],
                        start=True, stop=True), reads=[KT.bufs[a], QT.bufs[a]], writes=[K.psb[sb]], tag="qk mm")
                    r = kb - 4 * qg
                    var = 4 if r < 0 else r
                    P.op("dve", lambda e, sb=sb, i=i, bs=bs, var=var: e.scalar_tensor_tensor(
                        out=stmh[:, i, :], in0=K.ps[sb][:, :], scalar=SCALE, in1=bmh[:, bs, var, :],
                        op0=ALU.mult, op1=ALU.add), reads=[K.psb[sb], bms.bufs[bs]], writes=[stm.bufs[i]],
                        tag="score bias")
                    cidx = a * 16 + ((qg * 512 - kb * 128 + 384) // 128 if r < 0 else 3)
                    P.op("act", lambda e, i=i, cidx=cidx: e.activation(
                        out=pTh[:, i, :], in_=stmh[:, i, :], func=AF.Exp, bias=cth[:, cidx:cidx + 1], scale=1.0),
                        reads=[stm.bufs[i], ct.bufs[0]], writes=[pT.bufs[i]], tag="exp")

                    def pv(e, i=i, kb=kb, a=a, ob=ob, nkb=nkb):
                        ins = None
                        for qb in range(4):
                            bk = ob[qb // 2]
                            c0 = (qb % 2) * 130
                            ins = e.matmul(K.ps[bk][:, c0:c0 + 130], pTh[:, i, qb * 128:(qb + 1) * 128],
                                           VAh[:, kb, a, :], start=(kb == 0 and qb % 2 == 0), stop=(kb == nkb - 1),
                                           skip_group_check=True)
                        return ins
                    P.op("pe", pv, reads=[pT.bufs[i], VA.bufs[kb]], writes=[K.psb[ob[0]], K.psb[ob[1]]], tag="pv mm")
                for hb in range(2):
                    P.op("act", lambda e, m=m, hb=hb, ob=ob: e.activation(
                        out=Osh[:, m, hb * 2:(hb + 1) * 2, :],
                        in_=K.ps[ob[hb]][:, 0:260].rearrange("p (q c) -> p q c", q=2), func=AF.Copy),
                        reads=[K.psb[ob[hb]]], writes=[Osb.bufs[m]], tag="O evac")
            P.op("dve", lambda e: e.reciprocal(out=smh[:, 0:4], in_=Osh[:, 0, :, 128:129].rearrange("p q c -> p (q c)")),
                 reads=[Osb.bufs[0]], writes=sm.bufs, tag="r1")
            P.op("dve", lambda e: e.reciprocal(out=smh[:, 4:8], in_=Osh[:, 1, :, 128:129].rearrange("p q c -> p (q c)")),
                 reads=[Osb.bufs[1]], writes=sm.bufs, tag="r2")
            P.op("dve", lambda e: e.tensor_scalar(out=smh[:, 4:8], in0=smh[:, 4:8], scalar1=LAM, scalar2=None,
                                                  op0=ALU.mult), reads=[lm.bufs[0]], writes=sm.bufs, tag="r2lam")
            P.op("dve", lambda e, a=a: e.tensor_tensor(
                out=owh[:, :, a, :], in0=Osh[:, 0, :, 0:128], in1=smh[:, 0:4].unsqueeze(2).to_broadcast([128, 4, 128]),
                op=ALU.mult), reads=[Osb.bufs[0]], writes=ow.bufs, tag="o1")
            P.op("dve", lambda e: e.tensor_tensor(
                out=osqh[:, :, :], in0=Osh[:, 1, :, 0:128], in1=smh[:, 4:8].unsqueeze(2).to_broadcast([128, 4, 128]),
                op=ALU.mult), reads=[Osb.bufs[1]], writes=osq.bufs, tag="o2")
            P.op("dve", lambda e, a=a: e.tensor_tensor(out=owh[:, :, a, :], in0=owh[:, :, a, :], in1=osqh[:, :, :],
                                                       op=ALU.subtract), reads=[], writes=ow.bufs + osq.bufs, tag="o12")
            P.op("pool", lambda e, a=a: e.tensor_tensor(out=osqh[:, :, :], in0=owh[:, :, a, :], in1=owh[:, :, a, :],
                                                        op=ALU.mult), reads=[], writes=ow.bufs + osq.bufs, tag="osq")
            P.op("dve", lambda e: e.tensor_reduce(out=smh[:, 8:12], in_=osqh[:, :, :], axis=AX.X, op=ALU.add),
                 reads=[osq.bufs[0]], writes=sm.bufs, tag="ossq")
            P.op("dve", lambda e: e.tensor_scalar(out=smh[:, 8:12], in0=smh[:, 8:12], scalar1=1.0 / 128, scalar2=ATTN_EPS,
                                                  op0=ALU.mult, op1=ALU.add), reads=[], writes=sm.bufs, tag="oms")
            P.op("act", lambda e: e.activation(out=smh[:, 12:16], in_=smh[:, 8:12], func=AF.Sqrt),
                 reads=[], writes=sm.bufs, tag="orms")
            P.op("dve", lambda e: e.reciprocal(out=smh[:, 16:20], in_=smh[:, 12:16]), reads=[], writes=sm.bufs,
                 tag="orr")
            P.op("dve", lambda e, a=a: e.tensor_tensor(
                out=owh[:, :, a, :], in0=owh[:, :, a, :], in1=smh[:, 16:20].unsqueeze(2).to_broadcast([128, 4, 128]),
                op=ALU.mult), reads=[], writes=ow.bufs + sm.bufs, tag="onorm")
            P.op("dve", lambda e, a=a: e.scalar_tensor_tensor(
                out=owbh[:, :, a, :], in0=owh[:, :, a, :], scalar=1.0 - LAMBDA_INIT,
                in1=ngh[:, a * 128:(a + 1) * 128].unsqueeze(1).to_broadcast([128, 4, 128]),
                op0=ALU.mult, op1=ALU.mult), reads=[ngs.bufs[0]] + ow.bufs, writes=owb.bufs, tag="og")
            for qb in range(4):
                t0 = qg * 512 + qb * 128
                P.op("sp", lambda e, qb=qb, t0=t0, a=a: e.dma_start(
                    out=o_dst[t0:t0 + 128, a * 128:(a + 1) * 128], in_=owbh[:, qb, a, :]),
                    reads=owb.bufs, writes=[o_buf], dma=True, tag="o_attn out")
    for al in (QT, KT, VA, wr, bms, ct, lm, ngs, stm, pT, Osb, ow, osq, sm, owb):
        P.free(al)


def host_tile_ffn_w(w):
    return np.ascontiguousarray(w.reshape(KC, 128, NG, FG).transpose(2, 1, 0, 3))


def host_win_cols(half):
    cols = []
    for blk in range(3):
        for a in range(4):
            head = 4 * half + a
            cols.append(np.arange(blk * 1024 + head * 128, blk * 1024 + head * 128 + 128))
    for blk in range(3):
        for cc in range(4):
            c0 = 3072 + blk * 1024 + (8 * half + 2 * cc) * 64
            cols.append(np.arange(c0, c0 + 128))
    cols.append(np.arange(6144, 6144 + 128))
    cols.append(np.arange(6144 + 128, 6144 + 256))
    cols.append(np.arange(6144 + 256, 6144 + 288))
    return cols


def host_tile_win(w_in, half):
    cols = host_win_cols(half)
    out = np.zeros((len(cols), 128, KC, 128), np.float32)
    for ci, c in enumerate(cols):
        blk = w_in[:, c]
        out[ci, :, :, :len(c)] = blk.reshape(KC, 128, len(c)).transpose(1, 0, 2)
    return out


def host_attn_consts(half):
    bm = np.zeros((4, 5, 128, 512), np.float32)
    ctab = np.zeros((128, 64), np.float32)
    i = np.arange(128)[:, None].astype(np.float64)
    j = np.arange(512)[None, :].astype(np.float64)
    for a in range(4):
        slope = 2.0 ** (-(4 * half + a + 1))
        for r in range(4):
            d = j - i - 128 * r
            bm[a, r] = np.where(d >= 0, -slope * d, NEG)
        bm[a, 4] = -slope * (j - i)
        for idx in range(16):
            ctab[:, a * 16 + idx] = -slope * (idx * 128 - 384)
    return bm, ctab


C0 = float(np.exp(-0.5))
CH = 64
NCHUNK = SEQ // CH


def rwkv_alloc_persist(K, base):
    P = K.P
    R = Ctx()
    off = base
    R.AR = P.sbuf("AR", [128, 4, NCHUNK, 2, CH], BF16, off, nbufs=NCHUNK); off += 4 * NCHUNK * 2 * CH * 2
    R.BK = P.sbuf("BK", [128, 4, NCHUNK, 2, CH], BF16, off, nbufs=NCHUNK); off += 4 * NCHUNK * 2 * CH * 2
    R.VC = P.sbuf("VC", [128, 4, SEQ], BF16, off, nbufs=NCHUNK); off += 4 * SEQ * 2
    R.LW = P.sbuf("LW", [128, SEQ], BF16, off, nbufs=4); off += SEQ * 2
    R.SG = P.sbuf("SG", [128, 2, SEQ], BF16, off, nbufs=4); off += 2 * SEQ * 2
    R.GC = P.sbuf("GC", [128, 4, NCHUNK], F32, off, nbufs=4); off += 4 * NCHUNK * 4
    R.BON = P.sbuf("BON", [64, NCHUNK, 8], F32, off, nbufs=4); off += NCHUNK * 8 * 4
    R.pcol = P.sbuf("pcol", [128, 64], F32, off); off += 256
    R.w2a2 = P.sbuf("w2a2", [128, 512], BF16, off); off += 1024
    R.g2 = P.sbuf("g2", [128, 2, 512], BF16, off); off += 2048
    R.bones = P.sbuf("bones", [128, 128], BF16, off); off += 256
    R.hsel = P.sbuf("hsel", [128, 16], BF16, off); off += 32
    R.top = off
    return R


def rwkv_prep(K, R, X1T, win_t, pcol_d, w2a2_d, g2_d, bones_d, hsel_d, base):
    P = K.P
    off = base
    wr = P.sbuf("wring2", [128, 3, KC, 128], BF16, off, nbufs=3); off += 3 * KC * 128 * 2
    ones = P.sbuf("ones", [128, 512], F32, off); off += 2048
    names = ["rm", "km", "sg", "av", "kk", "nrm", "ka", "kp", "cum", "cx", "dd"]
    T = {}
    for n in names:
        T[n] = P.sbuf(n, [128, 512], F32, off); off += 2048
    pre = P.sbuf("pre", [128, 3, 520], F32, off, nbufs=3); off += 3 * 520 * 4
    sq = P.sbuf("sq", [128, 512], BF16, off); off += 1024
    rb = P.sbuf("rb", [128, 512], BF16, off); off += 1024
    cb = P.sbuf("cb", [128, 16], F32, off); off += 64
    h = {n: T[n].handle for n in names}
    b = {n: T[n].bufs[0] for n in names}
    wrh, preh, sqh, rbh, cbh, onesh = wr.handle, pre.handle, sq.handle, rb.handle, cb.handle, ones.handle
    ARh, BKh, VCh, LWh, SGh, GCh, BONh, pc, w2h, g2h, boh, hsh = (
        R.AR.handle, R.BK.handle, R.VC.handle, R.LW.handle, R.SG.handle, R.GC.handle, R.BON.handle, R.pcol.handle,
        R.w2a2.handle, R.g2.handle, R.bones.handle, R.hsel.handle)
    X1Th = X1T.handle

    load_const(K, "pcol", R.pcol, pc[:, :], pcol_d[:, :])
    load_const(K, "w2a2", R.w2a2, w2h[:, :], w2a2_d[:, :], q="pool")
    load_const(K, "g2", R.g2, g2h[:, :, :], g2_d[:, :, :], q="pool")
    load_const(K, "bones", R.bones, boh[:, :], bones_d[:, :], q="pool")
    load_const(K, "hsel", R.hsel, hsh[:, :], hsel_d[:, :], q="pool")
    P.op("pool", lambda e: e.memset(onesh[:, :], 1.0), reads=[], writes=ones.bufs, tag="ones")
    P.op("pool", lambda e: e.memset(SGh[:, 1, :], 0.0), reads=[], writes=R.SG.bufs, tag="sg2 zero")

    rot = [0]

    def bank_rot():
        rot[0] += 1
        return rot[0] % 8

    def load_w(ci, s):
        P.op("pool", lambda e, ci=ci, s=s: e.dma_start(out=wrh[:, s, :, :], in_=win_t[NCH_ATT + ci]),
             reads=[], writes=[wr.bufs[s]], dma=True, tag="win2 load")

    def proj_mix(ci, s, st, tq, out_fn):
        bk = bank_rot()

        def mm(e, bk=bk, tq=tq, s=s):
            ins = None
            for kc in range(KC):
                ins = e.matmul(K.ps[bk][:, :], wrh[:, s, kc, :], X1Th[:, kc, tq * 512:(tq + 1) * 512],
                               start=(kc == 0), stop=(kc == KC - 1))
            return ins
        P.op("pe", mm, reads=X1T.bufs[tq * 4:(tq + 1) * 4] + [wr.bufs[s]], writes=[K.psb[bk]], tag="proj rw")
        if tq == 0:
            P.op("pool", lambda e, st=st: e.memset(preh[:, st, 0:1], 0.0), reads=[], writes=[pre.bufs[st]], tag="carry0")
        P.op("act", lambda e, bk=bk, st=st: e.activation(out=preh[:, st, 1:513], in_=K.ps[bk][:, :], func=AF.Copy),
             reads=[K.psb[bk]], writes=[pre.bufs[st]], tag="pre evac")
        P.op("dve", lambda e, st=st: e.tensor_tensor(out=h["dd"][:, :], in0=preh[:, st, 0:512], in1=preh[:, st, 1:513],
                                                     op=ALU.subtract), reads=[pre.bufs[st]], writes=[b["dd"]], tag="mix d")
        out_fn(preh[:, st, 1:513], pre.bufs[st])
        if tq < 3:
            P.op("act", lambda e, st=st: e.activation(out=preh[:, st, 0:1], in_=preh[:, st, 512:513], func=AF.Copy),
                 reads=[], writes=[pre.bufs[st]], tag="carry")

    def mixed_to(out_ap, out_bufs, mucol, eng="dve"):
        def f(pre1, prebuf):
            P.op("dve", lambda e: e.scalar_tensor_tensor(out=out_ap, in0=h["dd"][:, :], scalar=pc[:, mucol:mucol + 1],
                                                         in1=pre1, op0=ALU.mult, op1=ALU.add),
                 reads=[b["dd"], prebuf, R.pcol.bufs[0]], writes=out_bufs, tag="mix out")
        return f

    for li, ci in enumerate((12, 13, 14)):
        load_w(ci, li)
    for tq in range(4):
        tsl = slice(tq * 512, (tq + 1) * 512)
        proj_mix(12, 0, 0, tq, mixed_to(h["rm"][:, :], [b["rm"]], 12))
        P.op("act", lambda e, tsl=tsl: e.activation(out=LWh[0:64, tsl], in_=h["rm"][0:64, :], func=AF.Tanh),
             reads=[b["rm"]], writes=[R.LW.bufs[tq]], tag="tanh wd")
        P.op("dve", lambda e, tsl=tsl: e.tensor_copy(out=LWh[64:128, tsl], in_=h["rm"][64:128, :]),
             reads=[b["rm"]], writes=[R.LW.bufs[tq]], tag="copy ad")
        proj_mix(13, 1, 1, tq, mixed_to(h["km"][:, :], [b["km"]], 13))
        P.op("act", lambda e, tsl=tsl: e.activation(out=SGh[:, 0, tsl], in_=h["km"][:, :], func=AF.Sigmoid),
             reads=[b["km"]], writes=[R.SG.bufs[tq]], tag="sig gd")
        proj_mix(14, 2, 2, tq, mixed_to(h["sg"][:, :], [b["sg"]], 14))
        P.op("act", lambda e, tsl=tsl: e.activation(out=SGh[0:32, 1, tsl], in_=h["sg"][0:32, :], func=AF.Sigmoid),
             reads=[b["sg"]], writes=[R.SG.bufs[tq]], tag="sig gd2")

    for cc in range(4):
        for st in range(3):
            load_w(st * 4 + cc, st)
        csl = slice(cc * 128, (cc + 1) * 128)
        for tq in range(4):
            tsl = slice(tq * 512, (tq + 1) * 512)
            jsl = slice(tq * 8, (tq + 1) * 8)
            cbufs = R.AR.bufs[tq * 8:(tq + 1) * 8]
            kbufs = R.BK.bufs[tq * 8:(tq + 1) * 8]
            proj_mix(cc, 0, 0, tq, mixed_to(h["rm"][:, :], [b["rm"]], cc))
            proj_mix(4 + cc, 1, 1, tq, mixed_to(h["km"][:, :], [b["km"]], 4 + cc))
            proj_mix(8 + cc, 2, 2, tq, mixed_to(VCh[:, cc, tsl], R.VC.bufs[tq * 8:(tq + 1) * 8], 8 + cc))
            bz, ba = bank_rot(), bank_rot()
            P.op("pe", lambda e, bz=bz, csl=csl, tsl=tsl: e.matmul(K.ps[bz][:, :], w2h[0:64, csl], LWh[0:64, tsl],
                                                                   start=True, stop=True),
                 reads=[R.w2a2.bufs[0], R.LW.bufs[tq]], writes=[K.psb[bz]], tag="w lora")
            P.op("pe", lambda e, ba=ba, csl=csl, tsl=tsl: e.matmul(K.ps[ba][:, :], w2h[64:128, csl], LWh[64:128, tsl],
                                                                   start=True, stop=True),
                 reads=[R.w2a2.bufs[0], R.LW.bufs[tq]], writes=[K.psb[ba]], tag="a lora")
            P.op("act", lambda e, bz=bz, cc=cc: e.activation(out=h["sg"][:, :], in_=K.ps[bz][:, :], func=AF.Sigmoid,
                                                            bias=pc[:, 15 + cc:16 + cc]),
                 reads=[K.psb[bz], R.pcol.bufs[0]], writes=[b["sg"]], tag="sig w")
            P.op("act", lambda e, ba=ba, cc=cc: e.activation(out=h["av"][:, :], in_=K.ps[ba][:, :], func=AF.Sigmoid,
                                                            bias=pc[:, 19 + cc:20 + cc]),
                 reads=[K.psb[ba], R.pcol.bufs[0]], writes=[b["av"]], tag="sig a")
            P.op("dve", lambda e, cc=cc: e.tensor_scalar(out=h["kk"][:, :], in0=h["km"][:, :],
                                                        scalar1=pc[:, 23 + cc:24 + cc], scalar2=None, op0=ALU.mult),
                 reads=[b["km"], R.pcol.bufs[0]], writes=[b["kk"]], tag="kkraw")
            P.op("act", lambda e: e.activation(out=sqh[:, :], in_=h["kk"][:, :], func=AF.Square),
                 reads=[b["kk"]], writes=sq.bufs, tag="kk sq")
            bn_ = bank_rot()
            P.op("pe", lambda e, bn_=bn_: e.matmul(K.ps[bn_][:, :], boh[:, :], sqh[:, :], start=True, stop=True),
                 reads=[R.bones.bufs[0], sq.bufs[0]], writes=[K.psb[bn_]], tag="ssq mm")
            P.op("act", lambda e, bn_=bn_: e.activation(out=h["nrm"][:, :], in_=K.ps[bn_][:, :], func=AF.Sqrt),
                 reads=[K.psb[bn_]], writes=[b["nrm"]], tag="nrm sqrt")
            P.op("dve", lambda e: e.tensor_scalar(out=h["nrm"][:, :], in0=h["nrm"][:, :], scalar1=1e-12, scalar2=None,
                                                  op0=ALU.max), reads=[], writes=[b["nrm"]], tag="nrm max")
            P.op("dve", lambda e: e.reciprocal(out=h["nrm"][:, :], in_=h["nrm"][:, :]), reads=[], writes=[b["nrm"]],
                 tag="nrm rcp")
            P.op("dve", lambda e: e.tensor_tensor(out=h["kk"][:, :], in0=h["kk"][:, :], in1=h["nrm"][:, :], op=ALU.mult),
                 reads=[b["nrm"]], writes=[b["kk"]], tag="kk")
            P.op("pool", lambda e: e.tensor_tensor(out=h["ka"][:, :], in0=h["kk"][:, :], in1=h["av"][:, :], op=ALU.mult),
                 reads=[b["kk"], b["av"]], writes=[b["ka"]], tag="ka")
            P.op("dve", lambda e, cc=cc: e.tensor_scalar(out=h["kp"][:, :], in0=h["av"][:, :], scalar1=-1.0,
                                                        scalar2=pc[:, 27 + cc:28 + cc], op0=ALU.add, op1=ALU.mult),
                 reads=[b["av"], R.pcol.bufs[0]], writes=[b["kp"]], tag="kp1")
            P.op("dve", lambda e: e.scalar_tensor_tensor(out=h["kp"][:, :], in0=h["kp"][:, :], scalar=1.0,
                                                         in1=h["km"][:, :], op0=ALU.add, op1=ALU.mult),
                 reads=[b["km"]], writes=[b["kp"]], tag="kp2")
            P.op("dve", lambda e, cc=cc: e.scalar_tensor_tensor(out=rbh[:, :], in0=h["rm"][:, :],
                                                               scalar=pc[:, 31 + cc:32 + cc], in1=h["kp"][:, :],
                                                               op0=ALU.mult, op1=ALU.mult),
                 reads=[b["rm"], b["kp"], R.pcol.bufs[0]], writes=rb.bufs, tag="rb")
            bb_ = bank_rot()

            def bon_mm(e, bb_=bb_):
                ins = None
                for jj in range(8):
                    ins = e.matmul(K.ps[bb_][0:64, jj * 2:jj * 2 + 2], rbh[:, jj * 64:(jj + 1) * 64], hsh[:, 0:2],
                                   start=(jj == 0), stop=True, skip_group_check=True)
                return ins
            P.op("pe", bon_mm, reads=[rb.bufs[0], R.hsel.bufs[0]], writes=[K.psb[bb_]], tag="bonus mm")
            P.op("dve", lambda e, bb_=bb_, jsl=jsl, cc=cc: e.tensor_copy(
                out=BONh[:, jsl, cc * 2:cc * 2 + 2], in_=K.ps[bb_][0:64, 0:16].rearrange("p (j c) -> p j c", c=2)),
                reads=[K.psb[bb_]], writes=[R.BON.bufs[tq]], tag="bonus evac")
            if tq == 0:
                P.op("dve", lambda e: e.tensor_tensor_scan(out=h["cum"][:, :], data0=onesh[:, :], data1=h["sg"][:, :],
                                                           initial=0.0, op0=ALU.mult, op1=ALU.add),
                     reads=[ones.bufs[0], b["sg"]], writes=[b["cum"]], tag="scan")
                P.op("pool", lambda e: e.memset(cbh[:, 0:1], 0.0), reads=[], writes=cb.bufs, tag="cb0")
            else:
                P.op("dve", lambda e: e.tensor_tensor_scan(out=h["cum"][:, :], data0=onesh[:, :], data1=h["sg"][:, :],
                                                           initial=cbh[:, 8:9], op0=ALU.mult, op1=ALU.add),
                     reads=[ones.bufs[0], b["sg"], cb.bufs[0]], writes=[b["cum"]], tag="scan")
                P.op("act", lambda e: e.activation(out=cbh[:, 0:1], in_=cbh[:, 8:9], func=AF.Copy),
                     reads=[], writes=cb.bufs, tag="cb carry")
            cum3 = h["cum"].rearrange("p (j t) -> p j t", t=CH)
            P.op("act", lambda e, cum3=cum3: e.activation(out=cbh[:, 1:9], in_=cum3[:, :, CH - 1], func=AF.Copy),
                 reads=[b["cum"]], writes=cb.bufs, tag="cb ends")
            P.op("dve", lambda e, cum3=cum3: e.tensor_tensor(out=cum3, in0=cum3,
                                                             in1=cbh[:, 0:8].unsqueeze(2).to_broadcast([128, 8, CH]),
                                                             op=ALU.subtract),
                 reads=[cb.bufs[0]], writes=[b["cum"]], tag="cumrel")
            P.op("pool", lambda e: e.tensor_tensor(out=h["cx"][:, :], in0=h["cum"][:, :], in1=h["sg"][:, :],
                                                   op=ALU.subtract), reads=[b["cum"], b["sg"]], writes=[b["cx"]],
                 tag="cumex")
            P.op("act", lambda e, cum3=cum3, cc=cc, jsl=jsl: e.activation(out=GCh[:, cc, jsl], in_=cum3[:, :, CH - 1],
                                                                         func=AF.Exp, scale=-C0),
                 reads=[b["cum"]], writes=[R.GC.bufs[cc]], tag="gammaC")
            P.op("act", lambda e: e.activation(out=h["av"][:, :], in_=h["cum"][:, :], func=AF.Exp, scale=-C0),
                 reads=[b["cum"]], writes=[b["av"]], tag="Eg")
            P.op("act", lambda e: e.activation(out=h["km"][:, :], in_=h["cx"][:, :], func=AF.Exp, scale=-C0),
                 reads=[b["cx"]], writes=[b["km"]], tag="Egx")
            P.op("act", lambda e: e.activation(out=h["sg"][:, :], in_=h["cum"][:, :], func=AF.Exp, scale=C0),
                 reads=[b["cum"]], writes=[b["sg"]], tag="Ei")

            def v3(x):
                return x.rearrange("p (j t) -> p j t", t=CH)
            P.op("dve", lambda e, cc=cc, jsl=jsl: e.scalar_tensor_tensor(
                out=ARh[:, cc, jsl, 0, :], in0=v3(h["kk"]), scalar=-1.0, in1=v3(h["km"]), op0=ALU.mult, op1=ALU.mult),
                reads=[b["kk"], b["km"]], writes=cbufs, tag="A~")
            P.op("pool", lambda e, cc=cc, jsl=jsl: e.tensor_tensor(
                out=ARh[:, cc, jsl, 1, :], in0=v3(h["rm"]), in1=v3(h["av"]), op=ALU.mult),
                reads=[b["rm"], b["av"]], writes=cbufs, tag="R~")
            P.op("dve", lambda e, cc=cc, jsl=jsl: e.tensor_tensor(
                out=BKh[:, cc, jsl, 0, :], in0=v3(h["ka"]), in1=v3(h["sg"]), op=ALU.mult),
                reads=[b["ka"], b["sg"]], writes=kbufs, tag="B~")
            P.op("pool", lambda e, cc=cc, jsl=jsl: e.tensor_tensor(
                out=BKh[:, cc, jsl, 1, :], in0=v3(h["kp"]), in1=v3(h["sg"]), op=ALU.mult),
                reads=[b["kp"], b["sg"]], writes=kbufs, tag="K~")
    for al in [wr, ones, pre, sq, rb, cb] + [T[n] for n in names]:
        P.free(al)


def rwkv_chunks(K, R, masks_d, gnb_d, o_dst, o_buf, base):
    P = K.P
    off = base
    mk = P.sbuf("masks", [64, 4, 128], F32, off); off += 4 * 128 * 4
    gnb = P.sbuf("gnb", [64, 2, 512], F32, off); off += 2 * 512 * 4
    M1 = P.sbuf("M1", [64, 2, 8, 128], BF16, off, nbufs=2); off += 2 * 8 * 128 * 2
    M2 = P.sbuf("M2", [64, 2, 8, 128], BF16, off, nbufs=2); off += 2 * 8 * 128 * 2
    M3 = P.sbuf("M3", [64, 2, 8, 64], BF16, off, nbufs=2); off += 2 * 8 * 64 * 2
    NL = P.sbuf("NL", [64, 2, 2, 8, 64], BF16, off, nbufs=4); off += 2 * 2 * 8 * 64 * 2
    PP = P.sbuf("PP", [64, 2, 8, 64], BF16, off, nbufs=2); off += 2 * 8 * 64 * 2
    BKh_ = P.sbuf("BKhat", [128, 2, 4, 2, CH], BF16, off, nbufs=2); off += 2 * 4 * 2 * CH * 2
    TOK = P.sbuf("TOK", [64, 2, 4, 512], BF16, off, nbufs=2); off += 2 * 4 * 512 * 2
    WTs = P.sbuf("WTs", [128, 2, 4, CH], BF16, off, nbufs=2); off += 2 * 4 * CH * 2
    MAK = P.sbuf("MAK", [64, 2, 8, 64], BF16, off, nbufs=2); off += 2 * 8 * 64 * 2
    Ub = P.sbuf("Ub", [64, 2, 8, 64], BF16, off, nbufs=2); off += 2 * 8 * 64 * 2
    Hf = P.sbuf("Hf", [128, 4, 64], F32, off); off += 4 * 64 * 4
    Hb = P.sbuf("Hb", [128, 4, 64], BF16, off); off += 4 * 64 * 2
    Ys = P.sbuf("Ys", [64, 2, 512], F32, off, nbufs=2); off += 2 * 512 * 4
    Yq = P.sbuf("Yq", [64, 512], F32, off); off += 512 * 4
    Gs = P.sbuf("Gs", [64, 512], F32, off); off += 512 * 4
    st = P.sbuf("gst", [64, 64], F32, off); off += 64 * 4
    Yo = P.sbuf("Yo", [64, 2, 512], BF16, off, nbufs=2); off += 2 * 512 * 2
    Yoh = Yo.handle
    mkh, gnh, M1h, M2h, M3h, NLh, PPh, BHh, TOKh, WTh, MAKh, Ubh, Hfh, Hbh, Ysh, Yqh, Gsh, sth = (
        mk.handle, gnb.handle, M1.handle, M2.handle, M3.handle, NL.handle, PP.handle, BKh_.handle, TOK.handle,
        WTs.handle, MAK.handle, Ub.handle, Hf.handle, Hb.handle, Ys.handle, Yq.handle, Gs.handle, st.handle)
    ARh, BKh, VCh, SGh, GCh, BONh, g2h = (R.AR.handle, R.BK.handle, R.VC.handle, R.SG.handle, R.GC.handle,
                                         R.BON.handle, R.g2.handle)
    ident = K.ident
    load_const(K, "masks", mk, mkh[:, :, :], masks_d[:, 0:4, :])
    load_const(K, "gnb", gnb, gnh[:, :, :], gnb_d[:, :, :])
    P.op("pool", lambda e: e.memset(Hfh[:, :, :], 0.0), reads=[], writes=Hf.bufs, tag="H0")
    P.op("pool", lambda e: e.memset(Hbh[:, :, :], 0.0), reads=[], writes=Hb.bufs, tag="H0b")

    rot = [0]

    def nb():
        rot[0] += 1
        return rot[0] % 8

    def pr(hh):
        return slice(64 * hh, 64 * hh + 64)

    def ps3(bk, n, w):
        return K.ps[bk][0:64, 0:n * w].rearrange("p (n w) -> p n w", w=w)

    for j in range(getattr(K, 'nchunk_override', NCHUNK)):
        s = j % 2
        ab, kb_, vb = R.AR.bufs[j], R.BK.bufs[j], R.VC.bufs[j]
        tq = j // 8
        for hh in range(2):
            b1, b2, b3 = nb(), nb(), nb()

            def mm1(e, b1=b1, hh=hh, j=j):
                ins = None
                for cc in range(4):
                    ins = e.matmul(K.ps[b1][0:64, cc * 128:(cc + 1) * 128], ARh[pr(hh), cc, j, 0, :],
                                   BKh[pr(hh), cc, j, :, :].rearrange("p a t -> p (a t)"), start=(cc == 0), stop=True,
                                   skip_group_check=True)
                return ins
            P.op("pe", mm1, reads=[ab, kb_], writes=[K.psb[b1]], tag="SA mm")
            P.op("dve", lambda e, b1=b1, hh=hh, s=s: e.tensor_tensor(
                out=M1h[:, s, hh * 4:(hh + 1) * 4, :], in0=ps3(b1, 4, 128),
                in1=mkh[:, 0:1, :].to_broadcast([64, 4, 128]), op=ALU.mult),
                reads=[K.psb[b1], mk.bufs[0]], writes=[M1.bufs[s]], tag="M1 evac")

            def mm2(e, b2=b2, hh=hh, j=j):
                ins = None
                for cc in range(4):
                    ins = e.matmul(K.ps[b2][0:64, cc * 128:(cc + 1) * 128], BKh[pr(hh), cc, j, 0, :],
                                   ARh[pr(hh), cc, j, :, :].rearrange("p a t -> p (a t)"), start=(cc == 0), stop=True,
                                   skip_group_check=True)
                return ins
            P.op("pe", mm2, reads=[ab, kb_], writes=[K.psb[b2]], tag="SB mm")
            P.op("dve", lambda e, b2=b2, hh=hh, s=s: e.tensor_tensor(
                out=M2h[:, s, hh * 4:(hh + 1) * 4, :], in0=ps3(b2, 4, 128),
                in1=mkh[:, 1:2, :].to_broadcast([64, 4, 128]), op=ALU.mult),
                reads=[K.psb[b2], mk.bufs[0]], writes=[M2.bufs[s]], tag="M2 evac")

            def mm3(e, b3=b3, hh=hh, j=j):
                ins = None
                for cc in range(4):
                    ins = e.matmul(K.ps[b3][0:64, cc * 64:(cc + 1) * 64], BKh[pr(hh), cc, j, 1, :],
                                   ARh[pr(hh), cc, j, 1, :], start=(cc == 0), stop=True, skip_group_check=True)
                return ins
            P.op("pe", mm3, reads=[ab, kb_], writes=[K.psb[b3]], tag="SK mm")
            P.op("dve", lambda e, b3=b3, hh=hh, s=s: e.tensor_tensor(
                out=M3h[:, s, hh * 4:(hh + 1) * 4, :], in0=ps3(b3, 4, 64),
                in1=mkh[:, 2:3, 0:64].to_broadcast([64, 4, 64]), op=ALU.mult),
                reads=[K.psb[b3], mk.bufs[0]], writes=[M3.bufs[s]], tag="M3 evac")

        if getattr(K, 'chunk_cut', 99) <= 1:
            continue
        P.op("pool", lambda e, s=s: e.tensor_tensor(out=PPh[:, 0, :, :], in0=M2h[:, s, :, 0:64],
                                                    in1=mkh[:, 3:4, 0:64].to_broadcast([64, 8, 64]), op=ALU.add),
             reads=[M2.bufs[s], mk.bufs[0]], writes=[PP.bufs[0]], tag="P0")
        Ncur = lambda hd, s=s: M2h[:, s, hd, 0:64]
        Lcur = lambda hd, s=s: M1h[:, s, hd, 0:64]
        ncur_buf, lcur_buf = M2.bufs[s], M1.bufs[s]
        pc_ = 0
        for lev in range(1, 6):
            sl = lev % 2
            bl = nb()
            bn2 = nb() if lev < 5 else None

            def mmL(e, bl=bl, Ncur=Ncur, Lcur=Lcur):
                ins = None
                for hd in range(8):
                    ins = e.matmul(K.ps[bl][0:64, hd * 64:(hd + 1) * 64], Ncur(hd), Lcur(hd), start=(hd == 0), stop=True,
                                   skip_group_check=True)
                return ins
            P.op("pe", mmL, reads=[ncur_buf, lcur_buf], writes=[K.psb[bl]], tag="L sq")
            P.op("act", lambda e, bl=bl, sl=sl: e.activation(out=NLh[:, sl, 1, :, :], in_=ps3(bl, 8, 64), func=AF.Copy),
                 reads=[K.psb[bl]], writes=[NL.bufs[sl * 2 + 1]], tag="L evac")
            if lev < 5:
                def mmN(e, bn2=bn2, Ncur=Ncur, Lcur=Lcur):
                    ins = None
                    for hd in range(8):
                        ins = e.matmul(K.ps[bn2][0:64, hd * 64:(hd + 1) * 64], Lcur(hd), Ncur(hd), start=(hd == 0),
                                       stop=True, skip_group_check=True)
                    return ins
                P.op("pe", mmN, reads=[ncur_buf, lcur_buf], writes=[K.psb[bn2]], tag="N sq")
                P.op("dve", lambda e, bn2=bn2, sl=sl: e.tensor_copy(out=NLh[:, sl, 0, :, :], in_=ps3(bn2, 8, 64)),
                     reads=[K.psb[bn2]], writes=[NL.bufs[sl * 2 + 0]], tag="N evac")
            bp = nb()

            def mmP(e, bp=bp, sl=sl, pc_=pc_):
                ins = None
                for hd in range(8):
                    ins = e.matmul(K.ps[bp][0:64, hd * 64:(hd + 1) * 64], NLh[:, sl, 1, hd, :], PPh[:, pc_, hd, :],
                                   start=(hd == 0), stop=True, skip_group_check=True)
                return ins
            P.op("pe", mmP, reads=[NL.bufs[sl * 2 + 1], PP.bufs[pc_]], writes=[K.psb[bp]], tag="P mm")
            pn = 1 - pc_
            P.op("dve", lambda e, bp=bp, pc_=pc_, pn=pn: e.tensor_tensor(out=PPh[:, pn, :, :], in0=ps3(bp, 8, 64),
                                                                        in1=PPh[:, pc_, :, :], op=ALU.add),
                 reads=[K.psb[bp], PP.bufs[pc_]], writes=[PP.bufs[pn]], tag="P add")
            pc_ = pn
            Ncur = lambda hd, sl=sl: NLh[:, sl, 0, hd, :]
            Lcur = lambda hd, sl=sl: NLh[:, sl, 1, hd, :]
            ncur_buf, lcur_buf = NL.bufs[sl * 2 + 0], NL.bufs[sl * 2 + 1]
        TT = lambda hd, pc_=pc_: PPh[:, pc_, hd, :]
        tt_buf = PP.bufs[pc_]

        if getattr(K, 'chunk_cut', 99) <= 2:
            continue
        P.op("pool", lambda e, s=s, j=j: e.tensor_tensor(
            out=BHh[:, s, :, :, :], in0=BKh[:, :, j, :, :],
            in1=GCh[:, :, j:j + 1].unsqueeze(3).to_broadcast([128, 4, 2, CH]), op=ALU.mult),
            reads=[kb_, R.GC.bufs[0], R.GC.bufs[1], R.GC.bufs[2], R.GC.bufs[3]], writes=[BKh_.bufs[s]], tag="BKhat")
        for g2_ in range(2):
            bt = nb()
            ptb = K.ps[bt].bitcast(BF16)

            def trs(e, ptb=ptb, g2_=g2_, s=s, j=j):
                ins = None
                for kk_ in range(2):
                    kind = g2_ * 2 + kk_
                    for cc in range(4):
                        if kind == 0:
                            src = ARh[:, cc, j, 0, :]
                        elif kind == 1:
                            src = BHh[:, s, cc, 0, :]
                        elif kind == 2:
                            src = BHh[:, s, cc, 1, :]
                        else:
                            src = VCh[:, cc, j * CH:(j + 1) * CH]
                        ins = e.transpose(ptb[0:64, kk_ * 512 + cc * 128: kk_ * 512 + (cc + 1) * 128], src, ident[:, :])
                return ins
            P.op("pe", trs, reads=[ab, BKh_.bufs[s], vb, K.ident_buf], writes=[K.psb[bt]], tag="tok transposes")
            eng = "act" if g2_ == 0 else "dve"

            def tev(e, ptb=ptb, g2_=g2_, s=s, eng=eng):
                o = TOKh[:, s, g2_ * 2:(g2_ + 1) * 2, :]
                i = ptb[0:64, :].rearrange("p (k c) -> p k c", k=2)
                if eng == "act":
                    return e.activation(out=o, in_=i, func=AF.Copy)
                return e.tensor_copy(out=o, in_=i)
            P.op(eng, tev, reads=[K.psb[bt]], writes=[TOK.bufs[s]], tag="tok evac")

        if getattr(K, 'chunk_cut', 99) <= 3:
            continue
        bw = nb()

        def mmW(e, bw=bw, s=s, TT=TT):
            ins = None
            for hp in range(8):
                cc = hp % 4
                ins = e.matmul(K.ps[bw][:, hp * 64:(hp + 1) * 64], TOKh[:, s, 0, cc * 128:(cc + 1) * 128], TT(hp),
                               start=(hp == 0), stop=True, skip_group_check=True)
            return ins
        P.op("pe", mmW, reads=[TOK.bufs[s], tt_buf], writes=[K.psb[bw]], tag="WT mm")
        for hh in range(2):
            eng = "act" if hh == 0 else "dve"

            def wev(e, bw=bw, hh=hh, s=s, eng=eng):
                o = WTh[pr(hh), s, :, :]
                i = K.ps[bw][pr(hh), hh * 256:(hh + 1) * 256].rearrange("p (c t) -> p c t", c=4)
                if eng == "act":
                    return e.activation(out=o, in_=i, func=AF.Copy)
                return e.tensor_copy(out=o, in_=i)
            P.op(eng, wev, reads=[K.psb[bw]], writes=[WTs.bufs[s]], tag="WT evac")
        bm_ = nb()

        def mmM(e, bm_=bm_, s=s, TT=TT):
            ins = None
            for hp in range(8):
                ins = e.matmul(K.ps[bm_][0:64, hp * 64:(hp + 1) * 64], M1h[:, s, hp, 64:128], TT(hp),
                               start=(hp == 0), stop=True, skip_group_check=True)
            return ins
        P.op("pe", mmM, reads=[M1.bufs[s], tt_buf], writes=[K.psb[bm_]], tag="MAK mm")
        P.op("act", lambda e, bm_=bm_, s=s: e.activation(out=MAKh[:, s, :, :], in_=ps3(bm_, 8, 64), func=AF.Copy),
             reads=[K.psb[bm_]], writes=[MAK.bufs[s]], tag="MAK evac")

        if getattr(K, 'chunk_cut', 99) <= 4:
            continue
        Vt = lambda hp, s=s: TOKh[:, s, 3, (hp % 4) * 128 + (hp // 4) * 64:(hp % 4) * 128 + (hp // 4) * 64 + 64]
        for hh in range(2):
            bu = nb()

            def mmU1(e, bu=bu, hh=hh, s=s, Vt=Vt):
                ins = None
                for cc in range(4):
                    hp = hh * 4 + cc
                    ins = e.matmul(K.ps[bu][0:64, cc * 64:(cc + 1) * 64], MAKh[:, s, hp, :], Vt(hp), start=(cc == 0),
                                   stop=False, skip_group_check=True)
                return ins
            P.op("pe", mmU1, reads=[MAK.bufs[s], TOK.bufs[s]], writes=[K.psb[bu]], tag="U mm1")

            def mmU2(e, bu=bu, hh=hh, s=s):
                ins = None
                for cc in range(4):
                    ins = e.matmul(K.ps[bu][0:64, cc * 64:(cc + 1) * 64], WTh[pr(hh), s, cc, :], Hbh[pr(hh), cc, :],
                                   start=False, stop=True, skip_group_check=True)
                return ins
            P.op("pe", mmU2, reads=[WTs.bufs[s], Hb.bufs[0]], writes=[K.psb[bu]], tag="U mm2", pe_sync=True)
            P.op("act", lambda e, bu=bu, hh=hh, s=s: e.activation(out=Ubh[:, s, hh * 4:(hh + 1) * 4, :],
                                                                 in_=ps3(bu, 4, 64), func=AF.Copy),
                 reads=[K.psb[bu]], writes=[Ub.bufs[s]], tag="U evac")
        for hh in range(2):
            by = nb()

            def mmY1(e, by=by, hh=hh, s=s, Vt=Vt):
                ins = None
                for cc in range(4):
                    hp = hh * 4 + cc
                    o = K.ps[by][0:64, cc * 64:(cc + 1) * 64]
                    e.matmul(o, M2h[:, s, hp, 64:128], Ubh[:, s, hp, :], start=(cc == 0), stop=False,
                             skip_group_check=True)
                    ins = e.matmul(o, M3h[:, s, hp, :], Vt(hp), start=False, stop=False, skip_group_check=True)
                return ins
            P.op("pe", mmY1, reads=[M2.bufs[s], Ub.bufs[s], M3.bufs[s], TOK.bufs[s]], writes=[K.psb[by]], tag="Y mm1")

            def mmY2(e, by=by, hh=hh, j=j):
                ins = None
                for cc in range(4):
                    ins = e.matmul(K.ps[by][0:64, cc * 64:(cc + 1) * 64], ARh[pr(hh), cc, j, 1, :], Hbh[pr(hh), cc, :],
                                   start=False, stop=True, skip_group_check=True)
                return ins
            P.op("pe", mmY2, reads=[ab, Hb.bufs[0]], writes=[K.psb[by]], tag="Y mm2", pe_sync=True)
            P.op("act", lambda e, by=by, hh=hh, s=s: e.activation(
                out=Ysh[:, s, :].rearrange("p (c h v) -> p c h v", c=4, h=2)[:, :, hh, :], in_=ps3(by, 4, 64),
                func=AF.Copy), reads=[K.psb[by]], writes=[Ys.bufs[s]], tag="Y evac")
        bh = nb()

        def mmH(e, bh=bh, s=s, Vt=Vt):
            ins = None
            for hp in range(8):
                cc = hp % 4
                o = K.ps[bh][:, hp * 64:(hp + 1) * 64]
                e.matmul(o, TOKh[:, s, 1, cc * 128:(cc + 1) * 128], Ubh[:, s, hp, :], start=(hp == 0), stop=False,
                         skip_group_check=True)
                ins = e.matmul(o, TOKh[:, s, 2, cc * 128:(cc + 1) * 128], Vt(hp), start=False, stop=True,
                               skip_group_check=True)
            return ins
        P.op("pe", mmH, reads=[TOK.bufs[s], Ub.bufs[s]], writes=[K.psb[bh]], tag="H mm")
        P.op("pool", lambda e, j=j: e.tensor_tensor(out=Hfh[:, :, :], in0=Hfh[:, :, :],
                                                    in1=GCh[:, :, j:j + 1].to_broadcast([128, 4, 64]), op=ALU.mult),
             reads=[R.GC.bufs[0], R.GC.bufs[1], R.GC.bufs[2], R.GC.bufs[3]], writes=Hf.bufs, tag="H decay")
        for hh in range(2):
            P.op("dve", lambda e, bh=bh, hh=hh: e.tensor_tensor(
                out=Hfh[pr(hh), :, :], in0=Hfh[pr(hh), :, :],
                in1=K.ps[bh][pr(hh), hh * 256:(hh + 1) * 256].rearrange("p (c v) -> p c v", c=4), op=ALU.add),
                reads=[K.psb[bh]], writes=Hf.bufs, tag="H add")
        P.op("act", lambda e: e.activation(out=Hbh[:, :, :], in_=Hfh[:, :, :], func=AF.Copy),
             reads=Hf.bufs, writes=Hb.bufs, tag="H bf16")

        if getattr(K, 'chunk_cut', 99) <= 5:
            continue
        bg = nb()

        def mmG(e, bg=bg, j=j):
            e.matmul(K.ps[bg][0:64, :], SGh[:, 0, j * CH:(j + 1) * CH], g2h[:, 0, :], start=True, stop=False)
            return e.matmul(K.ps[bg][0:64, :], SGh[0:32, 1, j * CH:(j + 1) * CH], g2h[0:32, 1, :], start=False, stop=True)
        P.op("pe", mmG, reads=[R.SG.bufs[tq], R.g2.bufs[0]], writes=[K.psb[bg]], tag="gate mm")
        P.op("act", lambda e, bg=bg: e.activation(out=Gsh[:, :], in_=K.ps[bg][0:64, :], func=AF.Copy),
             reads=[K.psb[bg]], writes=Gs.bufs, tag="gate evac")
        if getattr(K, 'chunk_cut', 99) <= 6:
            continue
        Y3 = Ysh[:, s, :].rearrange("p (h v) -> p h v", v=64)
        Q3 = Yqh.rearrange("p (h v) -> p h v", v=64)
        yb = Ys.bufs[s]
        P.op("dve", lambda e, Y3=Y3: e.tensor_reduce(out=sth[:, 0:8], in_=Y3, axis=AX.X, op=ALU.add),
             reads=[yb], writes=st.bufs, tag="gn sum")
        P.op("pool", lambda e, s=s: e.tensor_tensor(out=Yqh[:, :], in0=Ysh[:, s, :], in1=Ysh[:, s, :], op=ALU.mult),
             reads=[yb], writes=Yq.bufs, tag="gn sq")
        P.op("dve", lambda e, Q3=Q3: e.tensor_reduce(out=sth[:, 8:16], in_=Q3, axis=AX.X, op=ALU.add),
             reads=Yq.bufs, writes=st.bufs, tag="gn ssq")
        P.op("dve", lambda e: e.tensor_scalar(out=sth[:, 16:24], in0=sth[:, 0:8], scalar1=1.0 / 64, scalar2=None,
                                              op0=ALU.mult), reads=[], writes=st.bufs, tag="gn mean")
        P.op("dve", lambda e: e.tensor_tensor(out=sth[:, 24:32], in0=sth[:, 16:24], in1=sth[:, 16:24], op=ALU.mult),
             reads=[], writes=st.bufs, tag="gn m2")
        P.op("dve", lambda e: e.scalar_tensor_tensor(out=sth[:, 32:40], in0=sth[:, 8:16], scalar=1.0 / 64,
                                                     in1=sth[:, 24:32], op0=ALU.mult, op1=ALU.subtract),
             reads=[], writes=st.bufs, tag="gn var")
        P.op("dve", lambda e: e.tensor_scalar(out=sth[:, 32:40], in0=sth[:, 32:40], scalar1=GN_EPS, scalar2=None,
                                              op0=ALU.add), reads=[], writes=st.bufs, tag="gn var eps")
        P.op("act", lambda e: e.activation(out=sth[:, 40:48], in_=sth[:, 32:40], func=AF.Sqrt),
             reads=[], writes=st.bufs, tag="gn sqrt")
        P.op("dve", lambda e: e.reciprocal(out=sth[:, 48:56], in_=sth[:, 40:48]), reads=[], writes=st.bufs, tag="gn rstd")
        P.op("dve", lambda e, Y3=Y3: e.tensor_tensor(out=Y3, in0=Y3, in1=sth[:, 16:24].unsqueeze(2).to_broadcast([64, 8, 64]),
                                                     op=ALU.subtract), reads=[], writes=[yb] + st.bufs, tag="gn sub")
        P.op("pool", lambda e, Y3=Y3: e.tensor_tensor(out=Y3, in0=Y3, in1=sth[:, 48:56].unsqueeze(2).to_broadcast([64, 8, 64]),
                                                      op=ALU.mult), reads=st.bufs, writes=[yb], tag="gn mul")
        P.op("pool", lambda e, s=s: e.tensor_tensor(out=Ysh[:, s, :], in0=Ysh[:, s, :], in1=gnh[:, 0, :], op=ALU.mult),
             reads=gnb.bufs, writes=[yb], tag="gn g")
        P.op("pool", lambda e, s=s: e.tensor_tensor(out=Ysh[:, s, :], in0=Ysh[:, s, :], in1=gnh[:, 1, :], op=ALU.add),
             reads=gnb.bufs, writes=[yb], tag="gn b")
        if getattr(K, 'chunk_cut', 99) <= 7:
            continue
        P.op("dve", lambda e, Q3=Q3, s=s, j=j: e.tensor_tensor(
            out=Q3, in0=TOKh[:, s, 3, :].rearrange("p (h v) -> p h v", v=64),
            in1=BONh[:, j, :].unsqueeze(2).to_broadcast([64, 8, 64]), op=ALU.mult),
            reads=[TOK.bufs[s], R.BON.bufs[tq]], writes=Yq.bufs, tag="bonus v")
        P.op("pool", lambda e, s=s: e.tensor_tensor(out=Ysh[:, s, :], in0=Ysh[:, s, :], in1=Yqh[:, :], op=ALU.add),
             reads=Yq.bufs, writes=[yb], tag="y+bonus")
        P.op("dve", lambda e, s=s: e.tensor_tensor(out=Yoh[:, s, :], in0=Ysh[:, s, :], in1=Gsh[:, :], op=ALU.mult),
             reads=Gs.bufs + [yb], writes=[Yo.bufs[s]], tag="gate mul")
        if getattr(K, 'chunk_cut', 99) <= 8:
            continue
        P.op("sp", lambda e, s=s, j=j: e.dma_start(out=o_dst[j * CH:(j + 1) * CH, 512:1024], in_=Yoh[:, s, :]),
             reads=[Yo.bufs[s]], writes=[o_buf], dma=True, tag="o_rwkv out")
    for al in (mk, gnb, M1, M2, M3, NL, PP, BKh_, TOK, WTs, MAK, Ub, Hf, Hb, Ys, Yq, Gs, st, Yo):
        P.free(al)


def host_rwkv_params(inp, half):
    c0, c1 = half * 512, (half + 1) * 512
    mu = inp["rwkv_mu"]
    pcol = np.zeros((128, 64), np.float32)
    for blk in range(3):
        for cc in range(4):
            pcol[:, blk * 4 + cc] = mu[blk * 1024 + c0 + cc * 128: blk * 1024 + c0 + (cc + 1) * 128]
    pcol[:, 12] = mu[3072:3072 + 128]
    pcol[:, 13] = mu[3072 + 128:3072 + 256]
    pcol[0:32, 14] = mu[3072 + 256:3072 + 288]
    rk = inp["rwkv_r_k"].reshape(-1)
    for cc in range(4):
        sl = slice(c0 + cc * 128, c0 + (cc + 1) * 128)
        pcol[:, 15 + cc] = inp["rwkv_w0"][sl]
        pcol[:, 19 + cc] = inp["rwkv_a0"][sl]
        pcol[:, 23 + cc] = inp["rwkv_k_k"][sl]
        pcol[:, 27 + cc] = inp["rwkv_k_a"][sl]
        pcol[:, 31 + cc] = rk[sl]
    w2a2 = np.concatenate([inp["rwkv_w2"][:, c0:c1], inp["rwkv_a2"][:, c0:c1]], 0).astype(np.float32)
    g2 = np.zeros((128, 2, 512), np.float32)
    g2[:, 0, :] = inp["rwkv_g2"][0:128, c0:c1]
    g2[0:32, 1, :] = inp["rwkv_g2"][128:160, c0:c1]
    gnb = np.zeros((64, 2, 512), np.float32)
    gnb[:, 0, :] = inp["rwkv_gn_g"][c0:c1][None]
    gnb[:, 1, :] = inp["rwkv_gn_b"][c0:c1][None]
    return dict(pcol=pcol, w2a2=np.ascontiguousarray(w2a2), g2=g2, gnb=gnb)


def host_rwkv_consts():
    bones = np.zeros((128, 128), np.float32)
    bones[0:64, 0:64] = 1.0
    bones[64:128, 64:128] = 1.0
    hsel = np.zeros((128, 16), np.float32)
    hsel[0:64, 0] = 1.0
    hsel[64:128, 1] = 1.0
    t = np.arange(64)
    sl = (t[None, :] < t[:, None]).astype(np.float32)
    su = (t[:, None] < t[None, :]).astype(np.float32)
    ui = (t[:, None] <= t[None, :]).astype(np.float32)
    masks = np.zeros((64, 6, 128), np.float32)
    masks[:, 0, 0:64] = sl
    masks[:, 0, 64:128] = sl
    masks[:, 1, 0:64] = su
    masks[:, 1, 64:128] = ui
    masks[:, 2, 0:64] = ui
    masks[:, 3, 0:64] = np.eye(64, dtype=np.float32)
    return dict(bones=bones, hsel=hsel, masks=masks)


def ln_tile(P, acch, ab, t, gbh, gb, lsth, st, s, outs):
    for j in range(4):
        P.op("dve", lambda e, t=t, j=j, s=s: e.bn_stats(out=lsth[:, s, j * 6:(j + 1) * 6],
                                                      in_=acch[:, t, j * 512:(j + 1) * 512]),
             reads=[ab[j]], writes=[st.bufs[s]], tag="bn_stats")
    P.op("dve", lambda e, s=s: e.bn_aggr(out=lsth[:, s, 24:26], in_=lsth[:, s, 0:24].rearrange("p (a b) -> p a b", b=6)),
         reads=[], writes=[st.bufs[s]], tag="bn_aggr")
    P.op("dve", lambda e, s=s: e.tensor_scalar(out=lsth[:, s, 28:29], in0=lsth[:, s, 25:26], scalar1=LN_EPS,
                                              scalar2=None, op0=ALU.add), reads=[], writes=[st.bufs[s]], tag="var+eps")
    P.op("act", lambda e, s=s: e.activation(out=lsth[:, s, 29:30], in_=lsth[:, s, 28:29], func=AF.Sqrt),
         reads=[], writes=[st.bufs[s]], tag="sqrt")
    P.op("dve", lambda e, s=s: e.reciprocal(out=lsth[:, s, 26:27], in_=lsth[:, s, 29:30]),
         reads=[], writes=[st.bufs[s]], tag="rstd")
    P.op("dve", lambda e, s=s: e.scalar_tensor_tensor(out=lsth[:, s, 27:28], in0=lsth[:, s, 24:25], scalar=-1.0,
                                                     in1=lsth[:, s, 26:27], op0=ALU.mult, op1=ALU.mult),
         reads=[], writes=[st.bufs[s]], tag="nmr")
    P.op("act", lambda e, t=t, s=s: e.activation(out=acch[:, t, :], in_=acch[:, t, :], func=AF.Identity,
                                                bias=lsth[:, s, 27:28], scale=lsth[:, s, 26:27]),
         reads=[st.bufs[s]], writes=ab, tag="ln norm")
    P.op("pool", lambda e, t=t: e.tensor_tensor(out=acch[:, t, :], in0=acch[:, t, :], in1=gbh[:, 0, :], op=ALU.mult),
         reads=[gb.bufs[0]], writes=ab, tag="ln g")
    P.op("pool", lambda e, t=t: e.tensor_tensor(out=acch[:, t, :], in0=acch[:, t, :], in1=gbh[:, 1, :], op=ALU.add),
         reads=[gb.bufs[0]], writes=ab, tag="ln b")
    for (dst, dbuf, dt) in outs:
        if dt == F32:
            P.op("sp", lambda e, t=t, dst=dst: e.dma_start(out=dst[t * 128:(t + 1) * 128, :], in_=acch[:, t, :]),
                 reads=ab, writes=[dbuf], dma=True, tag="ln out")
        else:
            obf = P.lnbf
            P.op("act", lambda e, t=t, s=s, obf=obf: e.activation(out=obf.handle[:, s, :], in_=acch[:, t, :], func=AF.Copy),
                 reads=ab, writes=[obf.bufs[s]], tag="ln out cast")
            P.op("sp", lambda e, t=t, s=s, dst=dst, obf=obf: e.dma_start(out=dst[t * 128:(t + 1) * 128, :],
                                                                        in_=obf.handle[:, s, :]),
                 reads=[obf.bufs[s]], writes=[dbuf], dma=True, tag="ln out bf16")


def wout_stage(K, og, og_buf, x1f, x1f_buf, wout, hsc_d, lng, lnb, outs, base):
    P = K.P
    NT = NTOK // 128
    off = base
    acc = P.sbuf("acc3", [128, NT, D], F32, off, nbufs=NT * 4); off += NT * D * 4
    XT = P.sbuf("oT", [128, KC, NTOK], BF16, off, nbufs=NT); off += KC * NTOK * 2
    ws = P.sbuf("wouts", [128, KC, D], BF16, off, nbufs=4); off += KC * D * 2
    gb = P.sbuf("lngb3", [128, 2, D], F32, off); off += 2 * D * 4
    st = P.sbuf("lnst3", [128, 2, 32], F32, off, nbufs=2); off += 2 * 32 * 4
    hs = P.sbuf("hsc", [128, 8], F32, off); off += 32
    xa = P.sbuf("xa", [128, 2, 2, D], BF16, off, nbufs=2); off += 2 * 2 * D * 2
    xb = P.sbuf("xb3", [128, 2, D], BF16, off, nbufs=2); off += 2 * D * 2
    acch, XTh, wsh, gbh, lsth, hsh, xah, xbh = (acc.handle, XT.handle, ws.handle, gb.handle, st.handle, hs.handle,
                                                xa.handle, xb.handle)
    load_const(K, "lng3", gb, gbh[:, 0, :], lng[:, :])
    load_const(K, "lnb3", gb, gbh[:, 1, :], lnb[:, :])
    load_const(K, "hsc", hs, hsh[:, 0:2], hsc_d[:, :])
    wv = wout.rearrange("(kc p) d -> p kc d", p=128)
    for q in range(4):
        P.op("pool", lambda e, q=q: e.dma_start(out=wsh[:, q * 4:(q + 1) * 4, :], in_=wv[:, q * 4:(q + 1) * 4, :]),
             reads=[], writes=[ws.bufs[q]], dma=True, tag="wout load")
    for t in range(NT):
        P.op("sp", lambda e, t=t: e.dma_start(out=acch[:, t, :], in_=x1f[t * 128:(t + 1) * 128, :]),
             reads=[x1f_buf], writes=acc.bufs[t * 4:(t + 1) * 4], dma=True, tag="acc3 load")
        P.op("act", lambda e, t=t: e.activation(out=acch[:, t, :], in_=acch[:, t, :], func=AF.Copy, scale=ALPHA),
             reads=[], writes=acc.bufs[t * 4:(t + 1) * 4], tag="acc3 scale")
    banks = [4, 5, 6, 7]
    for t in range(NT):
        s = t % 2
        for cand in range(2):
            for r in range(2):
                if callable(og):
                    oap, obuf_ = og(cand, r, t)
                else:
                    row0 = r * SEQ + cand * NTOK + t * 128
                    oap, obuf_ = og[row0:row0 + 128, :], og_buf
                P.op("sp", lambda e, s=s, cand=cand, r=r, oap=oap: e.dma_start(
                    out=xah[:, s, cand, r * 1024:(r + 1) * 1024], in_=oap),
                    reads=[obuf_], writes=[xa.bufs[s]], dma=True, tag="og load")
        P.op("dve", lambda e, s=s: e.tensor_scalar(out=xbh[:, s, :], in0=xah[:, s, 0, :], scalar1=hsh[:, 0:1],
                                                  scalar2=None, op0=ALU.mult),
             reads=[xa.bufs[s], hs.bufs[0]], writes=[xb.bufs[s]], tag="blend0")
        P.op("dve", lambda e, s=s: e.scalar_tensor_tensor(out=xbh[:, s, :], in0=xah[:, s, 1, :], scalar=hsh[:, 1:2],
                                                         in1=xbh[:, s, :], op0=ALU.mult, op1=ALU.add),
             reads=[xa.bufs[s], hs.bufs[0]], writes=[xb.bufs[s]], tag="blend1")
        for half in range(2):
            bk = banks[(2 * t + half) % 4]
            pb = K.ps[bk].bitcast(BF16)

            def tr(e, s=s, half=half, pb=pb):
                ins = None
                for j in range(8):
                    kc = half * 8 + j
                    ins = e.transpose(pb[:, j * 128:(j + 1) * 128], xbh[:, s, kc * 128:(kc + 1) * 128], K.ident[:])
                return ins
            P.op("pe", tr, reads=[xb.bufs[s], K.ident_buf], writes=[K.psb[bk]], tag="oT transposes")
            eng = "act" if half == 0 else "dve"

            def ev(e, t=t, half=half, pb=pb, eng=eng):
                o = XTh[:, half * 8:(half + 1) * 8, t * 128:(t + 1) * 128]
                i = pb.rearrange("p (j c) -> p j c", j=8)
                if eng == "act":
                    return e.activation(out=o, in_=i, func=AF.Copy)
                return e.tensor_copy(out=o, in_=i)
            P.op(eng, ev, reads=[K.psb[bk]], writes=[XT.bufs[t]], tag="oT evac")
    cnt = 0
    for t in range(NT):
        for db in range(4):
            bk = cnt % 4
            cnt += 1

            def mm(e, bk=bk, t=t, db=db):
                ins = None
                for kc in range(KC):
                    ins = e.matmul(K.ps[bk][:, :], XTh[:, kc, t * 128:(t + 1) * 128], wsh[:, kc, db * 512:(db + 1) * 512],
                                   start=(kc == 0), stop=(kc == KC - 1))
                return ins
            P.op("pe", mm, reads=[XT.bufs[t]] + ws.bufs, writes=[K.psb[bk]], tag="wout mm")
            P.op("dve", lambda e, bk=bk, t=t, db=db: e.tensor_tensor(
                out=acch[:, t, db * 512:(db + 1) * 512], in0=K.ps[bk][:, :], in1=acch[:, t, db * 512:(db + 1) * 512],
                op=ALU.add), reads=[K.psb[bk]], writes=[acc.bufs[t * 4 + db]], tag="acc3 add")
        ln_tile(P, acch, acc.bufs[t * 4:(t + 1) * 4], t, gbh, gb, lsth, st, t % 2, outs)
    for a in (acc, XT, ws, gb, st, hs, xa, xb):
        P.free(a)


def build_program():
    nc = bass.Bass("TRN2", target_bir_lowering=False)
    K = mk_ctx(nc)
    P = K.P
    d = {}

    def din(name, shape):
        d[name] = dram_in(K, name, shape)
        return d[name]
    x = din("x", [NTOK, D])
    identd = din("ident", [128, 128])
    f1 = [din("f1_wg", [NG, 128, KC, FG]), din("f1_wu", [NG, 128, KC, FG]), din("f1_wd", [DFF, D]),
          din("ln1g", [128, D]), din("ln1b", [128, D])]
    f2 = [din("f2_wg", [NG, 128, KC, FG]), din("f2_wu", [NG, 128, KC, FG]), din("f2_wd", [DFF, D]),
          din("ln3g", [128, D]), din("ln3b", [128, D])]
    win = din("win", [27, 128, KC, 128])
    bm = din("bm", [4, 5, 128, 512]); ctab = din("ctab", [128, 64]); lamv = din("lamv", [128, 4, 64])
    ngt = din("ngt", [128, 512])
    pcol = din("pcol", [128, 64]); w2a2 = din("w2a2", [128, 512]); g2 = din("g2", [128, 2, 512])
    gnb = din("gnb", [64, 2, 512]); bones = din("bones", [128, 128]); hsel = din("hsel", [128, 16])
    masks = din("masks", [64, 6, 128])
    wout = din("wout", [D, D]); hsc = din("hsc", [128, 2]); ln2g = din("ln2g", [128, D]); ln2b = din("ln2b", [128, D])
    x1f = dram_tmp(K, "x1f", [NTOK, D], F32)
    x1b = dram_tmp(K, "x1b", [NTOK, D], BF16)
    x1g = [dram_tmp(K, f"x1g{i}", [256, D], BF16) for i in range(8)]
    oloc = dram_tmp(K, "oloc", [SEQ, 1024], BF16)
    og = [dram_tmp(K, f"og{i}", [512, 1024], BF16) for i in range(8)]
    x2s = dram_tmp(K, "x2s", [NTOK, D], F32)
    out = dram_tmp(K, "out", [NTOK, D], F32, kind="ExternalOutput")
    B = lambda n: K.dram[n][1]
    groups = [[0, 1], [2, 3], [4, 5], [6, 7]]

    setup_consts(K, identd)
    base = K.const_top
    ffn_stage(K, x.ap(), B("x"), f1[0].ap(), f1[1].ap(), f1[2].ap(), f1[3].ap(), f1[4].ap(),
              [(x1f.ap(), B("x1f"), F32), (x1b.ap(), B("x1b"), BF16)], base=base)
    for i in range(8):
        P.op("pool", lambda e, i=i: e.collective_compute("AllGather", ALU.bypass, replica_groups=groups,
                                                         ins=[x1b.ap()[i * 128:(i + 1) * 128, :]],
                                                         outs=[x1g[i].ap()[:, :]]),
             reads=[B("x1b")], writes=[B(f"x1g{i}")], dma="cc", tag="allgather x1")

    def x1_src(t):
        r, i = t // 8, t % 8
        return x1g[i].ap()[r * 128:(r + 1) * 128, :], B(f"x1g{i}")
    X1T = P.sbuf("X1T", [128, KC, SEQ], BF16, base, nbufs=16)
    b2 = base + KC * SEQ * 2
    load_xT(K, x1_src, None, SEQ, X1T, b2, [4, 5, 6, 7], K.ident, False)
    attention_stage(K, X1T, win.ap(), bm.ap(), ctab.ap(), lamv.ap(), ngt.ap(), oloc.ap(), B("oloc"), b2)
    R = rwkv_alloc_persist(K, b2)
    rwkv_prep(K, R, X1T, win.ap(), pcol.ap(), w2a2.ap(), g2.ap(), bones.ap(), hsel.ap(), R.top)
    P.free(X1T)
    rwkv_chunks(K, R, masks.ap(), gnb.ap(), oloc.ap(), B("oloc"), base)
    for a in (R.AR, R.BK, R.VC, R.LW, R.SG, R.GC, R.BON, R.pcol, R.w2a2, R.g2, R.bones, R.hsel):
        P.free(a)
    for i in range(8):
        P.op("pool", lambda e, i=i: e.collective_compute("AllGather", ALU.bypass, replica_groups=groups,
                                                         ins=[oloc.ap()[i * 256:(i + 1) * 256, :]],
                                                         outs=[og[i].ap()[:, :]]),
             reads=[B("oloc")], writes=[B(f"og{i}")], dma="cc", tag="allgather o")

    def og_src(cand, r, t):
        tok = cand * NTOK + t * 128
        i, j = tok // 256, tok % 256
        return og[i].ap()[r * 256 + j:r * 256 + j + 128, :], B(f"og{i}")
    wout_stage(K, og_src, None, x1f.ap(), B("x1f"), wout.ap(), hsc.ap(), ln2g.ap(), ln2b.ap(),
               [(x2s.ap(), B("x2s"), F32)], base)
    ffn_stage(K, x2s.ap(), B("x2s"), f2[0].ap(), f2[1].ap(), f2[2].ap(), f2[3].ap(), f2[4].ap(),
              [(out.ap(), B("out"), F32)], base=base)
    finish(K, [B("out")])
    P.emit()
    return nc


def host_inputs(inp):
    l0 = {k: np.asarray(v[0], np.float32) for k, v in inp.items() if k != "x"}
    x = np.asarray(inp["x"], np.float32).reshape(8, NTOK, D)
    rep = lambda v, n=128: np.ascontiguousarray(np.broadcast_to(np.asarray(v, np.float32)[None], (n,) + v.shape))
    shared = dict(
        ident=np.eye(128, dtype=np.float32),
        f1_wg=host_tile_ffn_w(l0["ffn1_w_gate"]), f1_wu=host_tile_ffn_w(l0["ffn1_w_up"]), f1_wd=l0["ffn1_w_down"],
        ln1g=rep(l0["ln1_g"]), ln1b=rep(l0["ln1_b"]),
        f2_wg=host_tile_ffn_w(l0["ffn2_w_gate"]), f2_wu=host_tile_ffn_w(l0["ffn2_w_up"]), f2_wd=l0["ffn2_w_down"],
        ln3g=rep(l0["ln3_g"]), ln3b=rep(l0["ln3_b"]), ln2g=rep(l0["ln2_g"]), ln2b=rep(l0["ln2_b"]),
        lamv=rep(np.stack([l0["lambda_q1"], l0["lambda_k1"], l0["lambda_q2"], l0["lambda_k2"]])),
        ngt=rep(np.tile(l0["attn_norm_g"], 4)),
        wout=np.ascontiguousarray(l0["w_out"][np.concatenate([np.arange(0, 512), np.arange(1024, 1536),
                                                            np.arange(512, 1024), np.arange(1536, 2048)])]),
    )
    shared.update(host_rwkv_consts())
    per_half = []
    for h in range(2):
        bm, ctab = host_attn_consts(h)
        dd = dict(win=host_tile_win(l0["w_in"], h), bm=bm, ctab=ctab,
                  hsc=np.ascontiguousarray(np.broadcast_to(np.array([1.0 - h, float(h)], np.float32)[None], (128, 2))))
        dd.update(host_rwkv_params(l0, h))
        per_half.append(dd)
    maps = []
    for c in range(8):
        m = dict(shared)
        m.update(per_half[c % 2])
        m["x"] = np.ascontiguousarray(x[c])
        maps.append(m)
    return maps


def kernel(**inputs):
    nc = build_program()
    maps = host_inputs(inputs)
    res = run_bass_kernel_spmd(nc, maps, core_ids=list(range(8)))
    out = np.stack([np.asarray(res.results[c]["out"], np.float32) for c in range(8)], 0)
    return out.reshape(4, SEQ, D)
```

```python
import numpy as np
import concourse.bass as bass
import concourse.mybir as mybir
from concourse.bass_utils import run_bass_kernel_spmd

F32 = mybir.dt.float32
BF16 = mybir.dt.bfloat16
AF = mybir.ActivationFunctionType
ALU = mybir.AluOpType
AX = mybir.AxisListType

SAME_ENGINE_SYNC = True
SEM_CHUNK = 4000
N_DMA_SEMS = 12


class Buf:
    __slots__ = ("name", "last_w", "readers", "excl")

    def __init__(self, name, excl=False):
        self.name = name
        self.last_w = None
        self.readers = []
        self.excl = excl


class Op:
    __slots__ = ("eng", "fn", "deps", "dma", "needs_inc", "inc_no", "dsem", "dval", "pos", "tag", "prev_dval")

    def __init__(self, eng, fn, dma, tag):
        self.eng = eng
        self.fn = fn
        self.deps = []
        self.dma = dma
        self.needs_inc = False
        self.inc_no = None
        self.dsem = None
        self.dval = None
        self.tag = tag


class Alloc:
    def __init__(self, name, off, nbytes, handle, bufs):
        self.name, self.off, self.nbytes, self.handle, self.bufs = name, off, nbytes, handle, bufs


ENGS = ("pe", "act", "dve", "pool", "sp")


class Prog:
    def __init__(self, nc):
        self.nc = nc
        self.ops = {e: [] for e in ENGS}
        self.all_ops = []
        self.live = []
        self.ghosts = []
        self.uid = 0
        self.sbuf_top = 0

    def sbuf(self, name, shape, dtype, off, nbufs=1):
        esz = 4 if dtype == F32 else 2
        free = 1
        for s in shape[1:]:
            free *= s
        nbytes = free * esz
        assert off % 32 == 0, (name, off)
        assert 16384 <= off and off + nbytes <= 224 * 1024 - 160, (name, off, nbytes)
        self.uid += 1
        h = self.nc.alloc_sbuf_tensor_at(f"{name}_{self.uid}", list(shape), dtype, offset=off)
        bufs = [Buf(f"{name}[{i}]") for i in range(nbufs)]
        for a in self.live:
            assert a.off + a.nbytes <= off or off + nbytes <= a.off, ("overlap", name, a.name)
        keep = []
        for g in self.ghosts:
            if g.off + g.nbytes <= off or off + nbytes <= g.off:
                keep.append(g)
                continue
            hz = []
            for b in g.bufs:
                if b.last_w is not None:
                    hz.append(b.last_w)
                hz.extend(b.readers)
            for b in bufs:
                b.readers.extend(hz)
            keep.append(g)
        self.ghosts = keep
        a = Alloc(name, off, nbytes, h, bufs)
        self.live.append(a)
        return a

    def free(self, a):
        self.live.remove(a)
        self.ghosts.append(a)

    def op(self, eng, fn, reads=(), writes=(), dma=False, tag="", pe_sync=False):
        o = Op(eng, fn, dma, tag)
        deps = []
        for b in reads:
            if b.excl:
                continue
            if b.last_w is not None:
                deps.append(b.last_w)
        for b in list(writes) + [b for b in reads if b.excl]:
            if b.last_w is not None:
                deps.append(b.last_w)
            deps.extend(b.readers)
        for b in reads:
            if not b.excl:
                b.readers.append(o)
        for b in list(writes) + [b for b in reads if b.excl]:
            b.last_w = o
            b.readers = []
        seen = set()
        for d in deps:
            if id(d) in seen or d is o:
                continue
            seen.add(id(d))
            if (not d.dma) and d.eng == eng and not SAME_ENGINE_SYNC:
                continue
            if (not d.dma) and d.eng == eng and eng == "pe" and not pe_sync:
                continue
            o.deps.append(d)
        o.pos = len(self.ops[eng])
        self.ops[eng].append(o)
        self.all_ops.append(o)
        return o

    def emit(self):
        nc = self.nc
        for o in self.all_ops:
            for d in o.deps:
                d.needs_inc = True
        cnt = {e: 0 for e in ENGS}
        dma_cnt = {e: 0 for e in ENGS}
        n_sems = {}
        for e in ENGS:
            n = sum(1 for o in self.ops[e] if o.needs_inc and not o.dma)
            n_sems[e] = max(1, (n + SEM_CHUNK - 1) // SEM_CHUNK)
        import contextlib
        with contextlib.ExitStack() as st:
            esems = {e: [st.enter_context(nc.semaphore(f"s_{e}_{i}")) for i in range(n_sems[e])] for e in ENGS}
            dsems = {e: [st.enter_context(nc.semaphore(f"d_{e}_{i}")) for i in range(N_DMA_SEMS)]
                     for e in ("sp", "act", "pool")}
            dsem_val = {e: [0] * N_DMA_SEMS for e in dsems}
            ccsems = []
            dsems["cc"] = ccsems
            for e in ENGS:
                for o in self.ops[e]:
                    if o.dma == "cc":
                        o.dsem = ("cc", len(ccsems))
                        ccsems.append(st.enter_context(nc.semaphore(f"cc_{len(ccsems)}")))
                        o.prev_dval = 0
                        o.dval = 1
                    elif o.dma:
                        k = dma_cnt[e] % N_DMA_SEMS
                        dma_cnt[e] += 1
                        o.dsem = (e, k)
                        o.prev_dval = dsem_val[e][k]
                        dsem_val[e][k] += 16
                        o.dval = dsem_val[e][k]
                    elif o.needs_inc:
                        o.inc_no = cnt[e]
                        cnt[e] += 1
            items = {e: [] for e in ENGS}
            for e in ENGS:
                waited = {}
                for o in self.ops[e]:
                    ws = []
                    for d in o.deps:
                        if d.dma:
                            key = ("d",) + d.dsem
                            val = d.dval
                            sem = dsems[d.dsem[0]][d.dsem[1]]
                        else:
                            ch = d.inc_no // SEM_CHUNK
                            key = ("e", d.eng, ch)
                            val = d.inc_no % SEM_CHUNK + 1
                            sem = esems[d.eng][ch]
                            later = any(k[0] == "e" and k[1] == d.eng and k[2] > ch for k in waited)
                            if later:
                                continue
                        if waited.get(key, 0) >= val:
                            continue
                        waited[key] = val
                        ws.append((sem, val))
                    if o.dma and o.prev_dval > 0:
                        key = ("d",) + o.dsem
                        if waited.get(key, 0) < o.prev_dval:
                            waited[key] = o.prev_dval
                            ws.append((dsems[o.dsem[0]][o.dsem[1]], o.prev_dval))
                    items[e].append((ws, o))
            self.stats = {e: (len(self.ops[e]), sum(len(w) for w, _ in items[e])) for e in ENGS}

            def run(e, eng):
                for ws, o in items[e]:
                    for sem, val in ws:
                        eng.wait_ge(sem, val)
                    ins = o.fn(eng)
                    if o.dma == "cc":
                        ins.then_inc(dsems["cc"][o.dsem[1]], 1)
                    elif o.dma:
                        ins.then_inc(dsems[o.dsem[0]][o.dsem[1]], 16)
                    elif o.needs_inc:
                        ins.then_inc(esems[e][o.inc_no // SEM_CHUNK], 1)

            with nc.Block() as block:
                @block.tensor
                def _(eng):
                    run("pe", eng)

                @block.scalar
                def _(eng):
                    run("act", eng)

                @block.vector
                def _(eng):
                    run("dve", eng)

                @block.gpsimd
                def _(eng):
                    run("pool", eng)

                @block.sync
                def _(eng):
                    run("sp", eng)


D = 2048
DFF = 5632
NTOK = 1024
SEQ = 2048
ALPHA = 2.0 ** 0.25
LN_EPS = 1e-5
FG = 256
NG = DFF // FG
KC = D // 128


class Ctx:
    pass


def mk_ctx(nc):
    K = Ctx()
    K.nc = nc
    K.P = Prog(nc)
    K.ps = []
    K.psb = []
    for i in range(8):
        h = nc.alloc_psum_tensor(f"psum{i}", [128, 512], F32)
        K.ps.append(h)
        K.psb.append(Buf(f"psum{i}", excl=True))
    K.dram = {}
    return K


def dram_in(K, name, shape, dtype=F32):
    t = K.nc.dram_tensor(name, list(shape), dtype, kind="ExternalInput")
    K.dram[name] = (t, Buf("dram_" + name))
    return t


def dram_tmp(K, name, shape, dtype, kind="Internal"):
    t = K.nc.dram_tensor(name, list(shape), dtype, kind=kind)
    K.dram[name] = (t, Buf("dram_" + name))
    return t


def load_const(K, name, alloc, dst_ap, src_ap, q="sp"):
    K.P.op(q, lambda e, o=dst_ap, i=src_ap: e.dma_start(out=o, in_=i), reads=[], writes=alloc.bufs, dma=True,
           tag="const " + name)


def load_xT(K, src, src_buf, ntok, XT, xb_off, banks, ident, src_f32):
    P = K.P
    nt = ntok // 128
    xb = P.sbuf("xb", [128, 2, D], BF16, xb_off, nbufs=2)
    xbh = xb.handle
    XTh = XT.handle
    for t in range(nt):
        s = t % 2
        q = "pool" if src_f32 else "sp"
        if callable(src):
            sap, sbuf_ = src(t)
        else:
            sap, sbuf_ = src[t * 128:(t + 1) * 128, :], src_buf
        P.op(q, lambda e, s=s, sap=sap: e.dma_start(out=xbh[:, s, :], in_=sap),
             reads=[sbuf_], writes=[xb.bufs[s]], dma=True, tag="xb load")
        for half in range(2):
            bk = banks[(2 * t + half) % len(banks)]
            pb = K.ps[bk].bitcast(BF16)

            def tr(e, s=s, half=half, pb=pb):
                ins = None
                for j in range(8):
                    kc = half * 8 + j
                    ins = e.transpose(pb[:, j * 128:(j + 1) * 128], xbh[:, s, kc * 128:(kc + 1) * 128], ident[:])
                return ins
            P.op("pe", tr, reads=[xb.bufs[s], K.ident_buf], writes=[K.psb[bk]], tag="xT transposes")
            eng = "act" if half == 0 else "dve"

            def ev(e, t=t, half=half, pb=pb, eng=eng):
                o = XTh[:, half * 8:(half + 1) * 8, t * 128:(t + 1) * 128]
                i = pb.rearrange("p (j c) -> p j c", j=8)
                if eng == "act":
                    return e.activation(out=o, in_=i, func=AF.Copy)
                return e.tensor_copy(out=o, in_=i)
            P.op(eng, ev, reads=[K.psb[bk]], writes=[XT.bufs[t]], tag="xT evac")
    P.free(xb)


def ffn_stage(K, x_src, x_buf, wg, wu, wd, lng, lnb, outs, base=0):
    P = K.P
    nc = K.nc
    NT = NTOK // 128
    off = base
    acc = P.sbuf("acc", [128, NT, D], F32, off, nbufs=NT * 4); off += NT * D * 4
    XT = P.sbuf("XT", [128, KC, NTOK], BF16, off, nbufs=NT); off += KC * NTOK * 2
    wgs = P.sbuf("wgs", [128, 2, KC, FG], BF16, off, nbufs=2); off += 2 * KC * FG * 2
    wus = P.sbuf("wus", [128, 2, KC, FG], BF16, off, nbufs=2); off += 2 * KC * FG * 2
    wds = P.sbuf("wds", [128, 2, 2, D], BF16, off, nbufs=2); off += 2 * 2 * D * 2
    hT = P.sbuf("hT", [128, 2, 2, NTOK], BF16, off, nbufs=2); off += 2 * 2 * NTOK * 2
    stmp = P.sbuf("stmp", [128, 2, 512], F32, off, nbufs=2); off += 2 * 512 * 4
    gb = P.sbuf("lngb", [128, 2, D], F32, off, nbufs=1); off += 2 * D * 4
    st = P.sbuf("lnst", [128, 2, 4 * 6 + 8], F32, off, nbufs=2); off += 2 * 32 * 4
    xb_off = off
    acch, XTh, wgh, wuh, wdh, hTh, sth, gbh, lsth = (acc.handle, XT.handle, wgs.handle, wus.handle, wds.handle,
                                                      hT.handle, stmp.handle, gb.handle, st.handle)

    load_const(K, "lng", gb, gbh[:, 0, :], lng[:, :])
    load_const(K, "lnb", gb, gbh[:, 1, :], lnb[:, :])

    for t in range(NT):
        P.op("sp", lambda e, t=t: e.dma_start(out=acch[:, t, :], in_=x_src[t * 128:(t + 1) * 128, :]),
             reads=[x_buf], writes=acc.bufs[t * 4:(t + 1) * 4], dma=True, tag="acc load")
        P.op("act", lambda e, t=t: e.activation(out=acch[:, t, :], in_=acch[:, t, :], func=AF.Copy, scale=ALPHA),
             reads=[], writes=acc.bufs[t * 4:(t + 1) * 4], tag="acc scale")

    load_xT(K, x_src, x_buf, NTOK, XT, xb_off, [4, 5, 6, 7], K.ident, True)

    def load_wgu(g):
        s = g % 2
        for (wsrc, wh, wa, nm) in ((wg, wgh, wgs, "wg"), (wu, wuh, wus, "wu")):
            for piece in range(2):
                P.op("pool", lambda e, g=g, s=s, wsrc=wsrc, wh=wh, piece=piece: e.dma_start(
                    out=wh[:, s, piece * 8:(piece + 1) * 8, :], in_=wsrc[g, :, piece * 8:(piece + 1) * 8, :]),
                    reads=[], writes=[wa.bufs[s]], dma=True, tag=nm + " load")

    def load_wd(g):
        s = g % 2
        wdv = wd.rearrange("(g fc p) d -> g p fc d", fc=2, p=128)
        P.op("pool", lambda e, g=g, s=s: e.dma_start(out=wdh[:, s, :, :], in_=wdv[g]),
             reads=[], writes=[wds.bufs[s]], dma=True, tag="wd load")

    def upgate(g):
        s = g % 2
        for c in range(2):
            for th in range(2):
                i = (c * 2 + th) % 2
                bg, bu = i, 2 + i
                toks = slice(th * 512, (th + 1) * 512)
                xbufs = XT.bufs[th * 4:(th + 1) * 4]
                for (wh, wa, bk) in ((wgh, wgs, bg), (wuh, wus, bu)):
                    def mm(e, wh=wh, bk=bk, c=c, toks=toks, s=s):
                        ins = None
                        for kc in range(KC):
                            ins = e.matmul(K.ps[bk][:, :], wh[:, s, kc, c * 128:(c + 1) * 128], XTh[:, kc, toks],
                                           start=(kc == 0), stop=(kc == KC - 1))
                        return ins
                    P.op("pe", mm, reads=xbufs + [wa.bufs[s]], writes=[K.psb[bk]], tag="upgate mm")
                P.op("act", lambda e, i=i, bg=bg: e.activation(out=sth[:, i, :], in_=K.ps[bg][:, :], func=AF.Silu),
                     reads=[K.psb[bg]], writes=[stmp.bufs[i]], tag="silu")
                P.op("dve", lambda e, i=i, bu=bu, c=c, toks=toks, s=s: e.tensor_tensor(
                    out=hTh[:, s, c, toks], in0=K.ps[bu][:, :], in1=sth[:, i, :], op=ALU.mult),
                    reads=[K.psb[bu], stmp.bufs[i]], writes=[hT.bufs[s]], tag="hmul")

    dcnt = [0]

    def down(g, ln_after=False):
        s = g % 2
        for t in range(NT):
            if ln_after and t > 0:
                ln_tile(P, acch, acc.bufs[(t - 1) * 4:t * 4], t - 1, gbh, gb, lsth, st, (t - 1) % 2, outs)
            for db in range(4):
                bk = 4 + dcnt[0] % 4
                dcnt[0] += 1

                def mm(e, bk=bk, t=t, db=db, s=s):
                    ins = None
                    for fc in range(2):
                        ins = e.matmul(K.ps[bk][:, :], hTh[:, s, fc, t * 128:(t + 1) * 128],
                                       wdh[:, s, fc, db * 512:(db + 1) * 512], start=(fc == 0), stop=(fc == 1))
                    return ins
                P.op("pe", mm, reads=[hT.bufs[s], wds.bufs[s]], writes=[K.psb[bk]], tag="down mm")
                P.op("dve", lambda e, bk=bk, t=t, db=db: e.scalar_tensor_tensor(
                    out=acch[:, t, db * 512:(db + 1) * 512], in0=K.ps[bk][:, :], scalar=0.5,
                    in1=acch[:, t, db * 512:(db + 1) * 512], op0=ALU.mult, op1=ALU.add),
                    reads=[K.psb[bk]], writes=[acc.bufs[t * 4 + db]], tag="acc add")

    ngr = K.ng_override if hasattr(K, "ng_override") else NG
    load_wgu(0)
    if ngr > 1:
        load_wgu(1)
    load_wd(0)
    for g in range(ngr):
        upgate(g)
        if g > 0:
            down(g - 1)
        if g + 2 < ngr:
            load_wgu(g + 2)
        if g + 1 < ngr:
            load_wd(g + 1)
    P.lnbf = P.sbuf("lnbf", [128, 2, D], BF16, xb_off, nbufs=2)
    down(ngr - 1, ln_after=True)
    ln_tile(P, acch, acc.bufs[(NT - 1) * 4:NT * 4], NT - 1, gbh, gb, lsth, st, (NT - 1) % 2, outs)
    P.free(P.lnbf)
    for a in (acc, XT, wgs, wus, wds, hT, stmp, gb, st):
        P.free(a)


def setup_consts(K, identd):
    P = K.P
    a = P.sbuf("ident", [128, 128], BF16, 16384)
    K.ident = a.handle
    K.ident_buf = a.bufs[0]
    P.op("pool", lambda e: e.dma_start(out=K.ident[:, :], in_=identd.ap()[:, :]), reads=[], writes=a.bufs, dma=True,
         tag="ident")
    K.const_top = 16384 + 256


def finish(K, out_bufs):
    K.P.op("sp", lambda e: e.nop(), reads=out_bufs, writes=[], tag="final wait")


NCH_ATT = 12
NCH_RW = 15
NEG = -30000.0
LAMBDA_INIT = 0.2
ATTN_EPS = 1e-5
GN_EPS = 64e-5


def project_fm(K, X1T, wslot, wslot_buf, bank_rot, evac):
    P = K.P
    X1Th = X1T.handle
    for tg in range(4):
        bk = bank_rot()

        def mm(e, bk=bk, tg=tg):
            ins = None
            for kc in range(KC):
                ins = e.matmul(K.ps[bk][:, :], wslot[:, kc, :], X1Th[:, kc, tg * 512:(tg + 1) * 512],
                               start=(kc == 0), stop=(kc == KC - 1))
            return ins
        P.op("pe", mm, reads=X1T.bufs[tg * 4:(tg + 1) * 4] + [wslot_buf], writes=[K.psb[bk]], tag="proj fm")
        evac(tg, bk)


def attention_stage(K, X1T, win_t, bm, ctab, lamv, ng_t, o_dst, o_buf, base):
    P = K.P
    off = base
    QT = P.sbuf("QT", [128, 4, SEQ], BF16, off, nbufs=4); off += 4 * SEQ * 2
    KT = P.sbuf("KT", [128, 4, SEQ], BF16, off, nbufs=4); off += 4 * SEQ * 2
    VA = P.sbuf("VA", [128, 16, 4, 130], BF16, off, nbufs=16); off += 16 * 4 * 130 * 2
    wr = P.sbuf("wring", [128, 4, KC, 128], BF16, off, nbufs=4); off += 4 * KC * 128 * 2
    bms = P.sbuf("bms", [128, 2, 5, 512], F32, off, nbufs=2); off += 2 * 5 * 512 * 4
    ct = P.sbuf("ctab", [128, 64], F32, off); off += 64 * 4
    lm = P.sbuf("lam", [128, 4 * 64 + 16], F32, off); off += (4 * 64 + 16) * 4
    ngs = P.sbuf("ngs", [128, 512], F32, off); off += 512 * 4
    stm = P.sbuf("stmp", [128, 4, 512], F32, off, nbufs=4); off += 4 * 512 * 4
    pT = P.sbuf("pT", [128, 4, 512], BF16, off, nbufs=4); off += 4 * 512 * 2
    Osb = P.sbuf("Osb", [128, 2, 4, 130], F32, off, nbufs=2); off += 2 * 4 * 130 * 4
    ow = P.sbuf("ow", [128, 4, 4, 128], F32, off, nbufs=1); off += 16 * 128 * 4
    osq = P.sbuf("osq", [128, 4, 128], F32, off, nbufs=1); off += 4 * 128 * 4
    sm = P.sbuf("sm", [128, 32], F32, off, nbufs=1); off += 32 * 4
    owb = P.sbuf("owb", [128, 4, 4, 128], BF16, off, nbufs=1); off += 16 * 128 * 2
    owbh = owb.handle
    QTh, KTh, VAh, wrh, bmh, cth, lmh, ngh, stmh, pTh, Osh, owh, osqh, smh = (
        QT.handle, KT.handle, VA.handle, wr.handle, bms.handle, ct.handle, lm.handle, ngs.handle, stm.handle,
        pT.handle, Osb.handle, ow.handle, osq.handle, sm.handle)
    X1Th = X1T.handle

    load_const(K, "ctab", ct, cth[:, :], ctab[:, :])
    load_const(K, "lamv", lm, lmh[:, 0:256], lamv.rearrange("p a d -> p (a d)"))
    load_const(K, "ngt", ngs, ngh[:, :], ng_t[:, :])
    P.op("pool", lambda e: e.memset(VAh[:, :, :, 128:130], 1.0), reads=[], writes=VA.bufs, tag="va ones")

    P.op("dve", lambda e: e.tensor_tensor(out=lmh[:, 0:64], in0=lmh[:, 0:64], in1=lmh[:, 64:128], op=ALU.mult),
         reads=[], writes=lm.bufs, tag="lam1")
    P.op("dve", lambda e: e.tensor_tensor(out=lmh[:, 128:192], in0=lmh[:, 128:192], in1=lmh[:, 192:256], op=ALU.mult),
         reads=[], writes=lm.bufs, tag="lam2")
    P.op("dve", lambda e: e.tensor_reduce(out=lmh[:, 256:257], in_=lmh[:, 0:64], axis=AX.X, op=ALU.add),
         reads=[], writes=lm.bufs, tag="lam3")
    P.op("dve", lambda e: e.tensor_reduce(out=lmh[:, 257:258], in_=lmh[:, 128:192], axis=AX.X, op=ALU.add),
         reads=[], writes=lm.bufs, tag="lam4")
    P.op("act", lambda e: e.activation(out=lmh[:, 258:260], in_=lmh[:, 256:258], func=AF.Exp),
         reads=[], writes=lm.bufs, tag="lam5")
    P.op("dve", lambda e: e.tensor_tensor(out=lmh[:, 260:261], in0=lmh[:, 258:259], in1=lmh[:, 259:260],
                                          op=ALU.subtract), reads=[], writes=lm.bufs, tag="lam6")
    P.op("dve", lambda e: e.tensor_scalar(out=lmh[:, 261:262], in0=lmh[:, 260:261], scalar1=LAMBDA_INIT, scalar2=None,
                                          op0=ALU.add), reads=[], writes=lm.bufs, tag="lam7")
    LAM = lmh[:, 261:262]

    rot = [0]

    def bank_rot():
        rot[0] += 1
        return rot[0] % 4

    def load_w(ci):
        s = ci % 4
        P.op("pool", lambda e, ci=ci, s=s: e.dma_start(out=wrh[:, s, :, :], in_=win_t[ci]),
             reads=[], writes=[wr.bufs[s]], dma=True, tag="win load")
        return s
    for ci in range(min(3, NCH_ATT)):
        load_w(ci)
    for ci in range(NCH_ATT):
        s = ci % 4
        if ci + 3 < NCH_ATT:
            load_w(ci + 3)
        if ci < 8:
            dstT, dbuf = (QTh, QT) if ci < 4 else (KTh, KT)
            a = ci % 4

            def evac(tg, bk, dstT=dstT, dbuf=dbuf, a=a):
                eng = "act" if tg % 2 == 0 else "dve"

                def ev(e, tg=tg, bk=bk):
                    o = dstT[:, a, tg * 512:(tg + 1) * 512]
                    if eng == "act":
                        return e.activation(out=o, in_=K.ps[bk][:, :], func=AF.Copy)
                    return e.tensor_copy(out=o, in_=K.ps[bk][:, :])
                P.op(eng, ev, reads=[K.psb[bk]], writes=[dbuf.bufs[a]], tag="qk evac")
            project_fm(K, X1T, wrh[:, s], wr.bufs[s], bank_rot, evac)
        else:
            a = ci - 8
            for tq in range(4):
                bk = bank_rot()

                def mm(e, bk=bk, tq=tq, s=s):
                    ins = None
                    for tt in range(4):
                        t = tq * 4 + tt
                        for kc in range(KC):
                            ins = e.matmul(K.ps[bk][:, tt * 128:(tt + 1) * 128], X1Th[:, kc, t * 128:(t + 1) * 128],
                                           wrh[:, s, kc, :], start=(kc == 0 and tt == 0), stop=(kc == KC - 1),
                                           skip_group_check=True)
                    return ins
                P.op("pe", mm, reads=X1T.bufs[tq * 4:(tq + 1) * 4] + [wr.bufs[s]], writes=[K.psb[bk]], tag="v proj")
                P.op("act", lambda e, bk=bk, tq=tq, a=a: e.activation(
                    out=VAh[:, tq * 4:(tq + 1) * 4, a, 0:128], in_=K.ps[bk].rearrange("p (t c) -> p t c", t=4),
                    func=AF.Copy), reads=[K.psb[bk]], writes=VA.bufs[tq * 4:(tq + 1) * 4], tag="v evac")

    SCALE = 64 ** -0.5
    LOOK = 2
    tiles = []
    for a in range(4):
        for qg in range(4):
            nkb = 4 * qg + 4
            for m in range(2):
                for kb in range(nkb):
                    tiles.append((a, qg, m, kb, nkb))

    def front(idx):
        a, qg, m, kb, nkb = tiles[idx]
        bs = a % 2
        if qg == 0 and m == 0 and kb == 0:
            P.op("sp", lambda e, a=a, bs=bs: e.dma_start(out=bmh[:, bs, :, :], in_=bm[a].rearrange("r p q -> p r q")),
                 reads=[], writes=[bms.bufs[bs]], dma=True, tag="bm load")
        pr = slice(m * 64, (m + 1) * 64)
        i = idx % 4
        sb = i
        P.op("pe", lambda e, sb=sb, a=a, pr=pr, kb=kb, qg=qg: e.matmul(
            K.ps[sb][:, :], KTh[pr, a, kb * 128:(kb + 1) * 128], QTh[pr, a, qg * 512:(qg + 1) * 512],
            start=True, stop=True), reads=[KT.bufs[a], QT.bufs[a]], writes=[K.psb[sb]], tag="qk mm")
        r = kb - 4 * qg
        var = 4 if r < 0 else r
        P.op("dve", lambda e, sb=sb, i=i, bs=bs, var=var: e.scalar_tensor_tensor(
            out=stmh[:, i, :], in0=K.ps[sb][:, :], scalar=SCALE, in1=bmh[:, bs, var, :],
            op0=ALU.mult, op1=ALU.add), reads=[K.psb[sb], bms.bufs[bs]], writes=[stm.bufs[i]], tag="score bias")
        cidx = a * 16 + ((qg * 512 - kb * 128 + 384) // 128 if r < 0 else 3)
        P.op("act", lambda e, i=i, cidx=cidx: e.activation(
            out=pTh[:, i, :], in_=stmh[:, i, :], func=AF.Exp, bias=cth[:, cidx:cidx + 1], scale=1.0),
            reads=[stm.bufs[i], ct.bufs[0]], writes=[pT.bufs[i]], tag="exp")

    def back(idx):
        a, qg, m, kb, nkb = tiles[idx]
        i = idx % 4
        ob = (4, 5) if m == 0 else (6, 7)

        def pv(e, i=i, kb=kb, a=a, ob=ob, nkb=nkb):
            ins = None
            for qb in range(4):
                bk = ob[qb // 2]
                c0 = (qb % 2) * 130
                ins = e.matmul(K.ps[bk][:, c0:c0 + 130], pTh[:, i, qb * 128:(qb + 1) * 128],
                               VAh[:, kb, a, :], start=(kb == 0 and qb % 2 == 0), stop=(kb == nkb - 1),
                               skip_group_check=True)
            return ins
        P.op("pe", pv, reads=[pT.bufs[i], VA.bufs[kb]], writes=[K.psb[ob[0]], K.psb[ob[1]]], tag="pv mm")
        if kb != nkb - 1:
            return
        for hb in range(2):
            P.op("act", lambda e, m=m, hb=hb, ob=ob: e.activation(
                out=Osh[:, m, hb * 2:(hb + 1) * 2, :],
                in_=K.ps[ob[hb]][:, 0:260].rearrange("p (q c) -> p q c", q=2), func=AF.Copy),
                reads=[K.psb[ob[hb]]], writes=[Osb.bufs[m]], tag="O evac")
        if m == 0:
            return
        P.op("dve", lambda e: e.reciprocal(out=smh[:, 0:4], in_=Osh[:, 0, :, 128:129].rearrange("p q c -> p (q c)")),
             reads=[Osb.bufs[0]], writes=sm.bufs, tag="r1")
        P.op("dve", lambda e: e.reciprocal(out=smh[:, 4:8], in_=Osh[:, 1, :, 128:129].rearrange("p q c -> p (q c)")),
             reads=[Osb.bufs[1]], writes=sm.bufs, tag="r2")
        P.op("dve", lambda e: e.tensor_scalar(out=smh[:, 4:8], in0=smh[:, 4:8], scalar1=LAM, scalar2=None,
                                              op0=ALU.mult), reads=[lm.bufs[0]], writes=sm.bufs, tag="r2lam")
        P.op("dve", lambda e, a=a: e.tensor_tensor(
            out=owh[:, :, a, :], in0=Osh[:, 0, :, 0:128], in1=smh[:, 0:4].unsqueeze(2).to_broadcast([128, 4, 128]),
            op=ALU.mult), reads=[Osb.bufs[0]], writes=ow.bufs, tag="o1")
        P.op("dve", lambda e: e.tensor_tensor(
            out=osqh[:, :, :], in0=Osh[:, 1, :, 0:128], in1=smh[:, 4:8].unsqueeze(2).to_broadcast([128, 4, 128]),
            op=ALU.mult), reads=[Osb.bufs[1]], writes=osq.bufs, tag="o2")
        P.op("dve", lambda e, a=a: e.tensor_tensor(out=owh[:, :, a, :], in0=owh[:, :, a, :], in1=osqh[:, :, :],
                                                   op=ALU.subtract), reads=[], writes=ow.bufs + osq.bufs, tag="o12")
        P.op("pool", lambda e, a=a: e.tensor_tensor(out=osqh[:, :, :], in0=owh[:, :, a, :], in1=owh[:, :, a, :],
                                                    op=ALU.mult), reads=[], writes=ow.bufs + osq.bufs, tag="osq")
        P.op("dve", lambda e: e.tensor_reduce(out=smh[:, 8:12], in_=osqh[:, :, :], axis=AX.X, op=ALU.add),
             reads=[osq.bufs[0]], writes=sm.bufs, tag="ossq")
        P.op("dve", lambda e: e.tensor_scalar(out=smh[:, 8:12], in0=smh[:, 8:12], scalar1=1.0 / 128, scalar2=ATTN_EPS,
                                              op0=ALU.mult, op1=ALU.add), reads=[], writes=sm.bufs, tag="oms")
        P.op("act", lambda e: e.activation(out=smh[:, 12:16], in_=smh[:, 8:12], func=AF.Sqrt),
             reads=[], writes=sm.bufs, tag="orms")
        P.op("dve", lambda e: e.reciprocal(out=smh[:, 16:20], in_=smh[:, 12:16]), reads=[], writes=sm.bufs,
             tag="orr")
        P.op("dve", lambda e, a=a: e.tensor_tensor(
            out=owh[:, :, a, :], in0=owh[:, :, a, :], in1=smh[:, 16:20].unsqueeze(2).to_broadcast([128, 4, 128]),
            op=ALU.mult), reads=[], writes=ow.bufs + sm.bufs, tag="onorm")
        P.op("dve", lambda e, a=a: e.scalar_tensor_tensor(
            out=owbh[:, :, a, :], in0=owh[:, :, a, :], scalar=1.0 - LAMBDA_INIT,
            in1=ngh[:, a * 128:(a + 1) * 128].unsqueeze(1).to_broadcast([128, 4, 128]),
            op0=ALU.mult, op1=ALU.mult), reads=[ngs.bufs[0]] + ow.bufs, writes=owb.bufs, tag="og")
        for qb in range(4):
            t0 = qg * 512 + qb * 128
            P.op("sp", lambda e, qb=qb, t0=t0, a=a: e.dma_start(
                out=o_dst[t0:t0 + 128, a * 128:(a + 1) * 128], in_=owbh[:, qb, a, :]),
                reads=owb.bufs, writes=[o_buf], dma=True, tag="o_attn out")

    for idx in range(len(tiles) + LOOK):
        if idx < len(tiles):
            front(idx)
        if idx >= LOOK:
            back(idx - LOOK)
    for al in (QT, KT, VA, wr, bms, ct, lm, ngs, stm, pT, Osb, ow, osq, sm, owb):
        P.free(al)


def host_tile_ffn_w(w):
    return np.ascontiguousarray(w.reshape(KC, 128, NG, FG).transpose(2, 1, 0, 3))


def host_win_cols(half):
    cols = []
    for blk in range(3):
        for a in range(4):
            head = 4 * half + a
            cols.append(np.arange(blk * 1024 + head * 128, blk * 1024 + head * 128 + 128))
    for blk in range(3):
        for cc in range(4):
            c0 = 3072 + blk * 1024 + (8 * half + 2 * cc) * 64
            cols.append(np.arange(c0, c0 + 128))
    cols.append(np.arange(6144, 6144 + 128))
    cols.append(np.arange(6144 + 128, 6144 + 256))
    cols.append(np.arange(6144 + 256, 6144 + 288))
    return cols


def host_tile_win(w_in, half):
    cols = host_win_cols(half)
    out = np.zeros((len(cols), 128, KC, 128), np.float32)
    for ci, c in enumerate(cols):
        blk = w_in[:, c]
        out[ci, :, :, :len(c)] = blk.reshape(KC, 128, len(c)).transpose(1, 0, 2)
    return out


def host_attn_consts(half):
    bm = np.zeros((4, 5, 128, 512), np.float32)
    ctab = np.zeros((128, 64), np.float32)
    i = np.arange(128)[:, None].astype(np.float64)
    j = np.arange(512)[None, :].astype(np.float64)
    for a in range(4):
        slope = 2.0 ** (-(4 * half + a + 1))
        for r in range(4):
            d = j - i - 128 * r
            bm[a, r] = np.where(d >= 0, -slope * d, NEG)
        bm[a, 4] = -slope * (j - i)
        for idx in range(16):
            ctab[:, a * 16 + idx] = -slope * (idx * 128 - 384)
    return bm, ctab


C0 = float(np.exp(-0.5))
CH = 64
NCHUNK = SEQ // CH


def rwkv_alloc_persist(K, base):
    P = K.P
    R = Ctx()
    off = base
    R.AR = P.sbuf("AR", [128, 4, NCHUNK, 2, CH], BF16, off, nbufs=NCHUNK); off += 4 * NCHUNK * 2 * CH * 2
    R.BK = P.sbuf("BK", [128, 4, NCHUNK, 2, CH], BF16, off, nbufs=NCHUNK); off += 4 * NCHUNK * 2 * CH * 2
    R.VC = P.sbuf("VC", [128, 4, SEQ], BF16, off, nbufs=NCHUNK); off += 4 * SEQ * 2
    R.LW = P.sbuf("LW", [128, SEQ], BF16, off, nbufs=4); off += SEQ * 2
    R.SG = P.sbuf("SG", [128, 2, SEQ], BF16, off, nbufs=4); off += 2 * SEQ * 2
    R.GC = P.sbuf("GC", [128, 4, NCHUNK], F32, off, nbufs=4); off += 4 * NCHUNK * 4
    R.BON = P.sbuf("BON", [64, NCHUNK, 8], F32, off, nbufs=4); off += NCHUNK * 8 * 4
    R.pcol = P.sbuf("pcol", [128, 64], F32, off); off += 256
    R.w2a2 = P.sbuf("w2a2", [128, 512], BF16, off); off += 1024
    R.g2 = P.sbuf("g2", [128, 2, 512], BF16, off); off += 2048
    R.bones = P.sbuf("bones", [128, 128], BF16, off); off += 256
    R.hsel = P.sbuf("hsel", [128, 16], BF16, off); off += 32
    R.top = off
    return R


def rwkv_prep(K, R, X1T, win_t, pcol_d, w2a2_d, g2_d, bones_d, hsel_d, base):
    P = K.P
    off = base
    wr = P.sbuf("wring2", [128, 3, KC, 128], BF16, off, nbufs=3); off += 3 * KC * 128 * 2
    ones = P.sbuf("ones", [128, 512], F32, off); off += 2048
    names = ["rm", "km", "sg", "av", "kk", "nrm", "ka", "kp", "cum", "cx", "dd"]
    T = {}
    for n in names:
        T[n] = P.sbuf(n, [128, 512], F32, off); off += 2048
    pre = P.sbuf("pre", [128, 3, 520], F32, off, nbufs=3); off += 3 * 520 * 4
    sq = P.sbuf("sq", [128, 512], BF16, off); off += 1024
    rb = P.sbuf("rb", [128, 512], BF16, off); off += 1024
    cb = P.sbuf("cb", [128, 16], F32, off); off += 64
    h = {n: T[n].handle for n in names}
    b = {n: T[n].bufs[0] for n in names}
    wrh, preh, sqh, rbh, cbh, onesh = wr.handle, pre.handle, sq.handle, rb.handle, cb.handle, ones.handle
    ARh, BKh, VCh, LWh, SGh, GCh, BONh, pc, w2h, g2h, boh, hsh = (
        R.AR.handle, R.BK.handle, R.VC.handle, R.LW.handle, R.SG.handle, R.GC.handle, R.BON.handle, R.pcol.handle,
        R.w2a2.handle, R.g2.handle, R.bones.handle, R.hsel.handle)
    X1Th = X1T.handle

    load_const(K, "pcol", R.pcol, pc[:, :], pcol_d[:, :])
    load_const(K, "w2a2", R.w2a2, w2h[:, :], w2a2_d[:, :], q="pool")
    load_const(K, "g2", R.g2, g2h[:, :, :], g2_d[:, :, :], q="pool")
    load_const(K, "bones", R.bones, boh[:, :], bones_d[:, :], q="pool")
    load_const(K, "hsel", R.hsel, hsh[:, :], hsel_d[:, :], q="pool")
    P.op("pool", lambda e: e.memset(onesh[:, :], 1.0), reads=[], writes=ones.bufs, tag="ones")
    P.op("pool", lambda e: e.memset(SGh[:, 1, :], 0.0), reads=[], writes=R.SG.bufs, tag="sg2 zero")

    rot = [0]

    def bank_rot():
        rot[0] += 1
        return rot[0] % 8

    def load_w(ci, s):
        P.op("pool", lambda e, ci=ci, s=s: e.dma_start(out=wrh[:, s, :, :], in_=win_t[NCH_ATT + ci]),
             reads=[], writes=[wr.bufs[s]], dma=True, tag="win2 load")

    def proj_mix(ci, s, st, tq, out_fn):
        bk = bank_rot()

        def mm(e, bk=bk, tq=tq, s=s):
            ins = None
            for kc in range(KC):
                ins = e.matmul(K.ps[bk][:, :], wrh[:, s, kc, :], X1Th[:, kc, tq * 512:(tq + 1) * 512],
                               start=(kc == 0), stop=(kc == KC - 1))
            return ins
        P.op("pe", mm, reads=X1T.bufs[tq * 4:(tq + 1) * 4] + [wr.bufs[s]], writes=[K.psb[bk]], tag="proj rw")
        if tq == 0:
            P.op("pool", lambda e, st=st: e.memset(preh[:, st, 0:1], 0.0), reads=[], writes=[pre.bufs[st]], tag="carry0")
        P.op("act", lambda e, bk=bk, st=st: e.activation(out=preh[:, st, 1:513], in_=K.ps[bk][:, :], func=AF.Copy),
             reads=[K.psb[bk]], writes=[pre.bufs[st]], tag="pre evac")
        P.op("dve", lambda e, st=st: e.tensor_tensor(out=h["dd"][:, :], in0=preh[:, st, 0:512], in1=preh[:, st, 1:513],
                                                     op=ALU.subtract), reads=[pre.bufs[st]], writes=[b["dd"]], tag="mix d")
        out_fn(preh[:, st, 1:513], pre.bufs[st])
        if tq < 3:
            P.op("act", lambda e, st=st: e.activation(out=preh[:, st, 0:1], in_=preh[:, st, 512:513], func=AF.Copy),
                 reads=[], writes=[pre.bufs[st]], tag="carry")

    def mixed_to(out_ap, out_bufs, mucol, eng="dve"):
        def f(pre1, prebuf):
            P.op("dve", lambda e: e.scalar_tensor_tensor(out=out_ap, in0=h["dd"][:, :], scalar=pc[:, mucol:mucol + 1],
                                                         in1=pre1, op0=ALU.mult, op1=ALU.add),
                 reads=[b["dd"], prebuf, R.pcol.bufs[0]], writes=out_bufs, tag="mix out")
        return f

    for li, ci in enumerate((12, 13, 14)):
        load_w(ci, li)
    for tq in range(4):
        tsl = slice(tq * 512, (tq + 1) * 512)
        proj_mix(12, 0, 0, tq, mixed_to(h["rm"][:, :], [b["rm"]], 12))
        P.op("act", lambda e, tsl=tsl: e.activation(out=LWh[0:64, tsl], in_=h["rm"][0:64, :], func=AF.Tanh),
             reads=[b["rm"]], writes=[R.LW.bufs[tq]], tag="tanh wd")
        P.op("dve", lambda e, tsl=tsl: e.tensor_copy(out=LWh[64:128, tsl], in_=h["rm"][64:128, :]),
             reads=[b["rm"]], writes=[R.LW.bufs[tq]], tag="copy ad")
        proj_mix(13, 1, 1, tq, mixed_to(h["km"][:, :], [b["km"]], 13))
        P.op("act", lambda e, tsl=tsl: e.activation(out=SGh[:, 0, tsl], in_=h["km"][:, :], func=AF.Sigmoid),
             reads=[b["km"]], writes=[R.SG.bufs[tq]], tag="sig gd")
        proj_mix(14, 2, 2, tq, mixed_to(h["sg"][:, :], [b["sg"]], 14))
        P.op("act", lambda e, tsl=tsl: e.activation(out=SGh[0:32, 1, tsl], in_=h["sg"][0:32, :], func=AF.Sigmoid),
             reads=[b["sg"]], writes=[R.SG.bufs[tq]], tag="sig gd2")

    for cc in range(4):
        for st in range(3):
            load_w(st * 4 + cc, st)
        csl = slice(cc * 128, (cc + 1) * 128)
        for tq in range(4):
            tsl = slice(tq * 512, (tq + 1) * 512)
            jsl = slice(tq * 8, (tq + 1) * 8)
            cbufs = R.AR.bufs[tq * 8:(tq + 1) * 8]
            kbufs = R.BK.bufs[tq * 8:(tq + 1) * 8]
            proj_mix(cc, 0, 0, tq, mixed_to(h["rm"][:, :], [b["rm"]], cc))
            proj_mix(4 + cc, 1, 1, tq, mixed_to(h["km"][:, :], [b["km"]], 4 + cc))
            proj_mix(8 + cc, 2, 2, tq, mixed_to(VCh[:, cc, tsl], R.VC.bufs[tq * 8:(tq + 1) * 8], 8 + cc))
            bz, ba = bank_rot(), bank_rot()
            P.op("pe", lambda e, bz=bz, csl=csl, tsl=tsl: e.matmul(K.ps[bz][:, :], w2h[0:64, csl], LWh[0:64, tsl],
                                                                   start=True, stop=True),
                 reads=[R.w2a2.bufs[0], R.LW.bufs[tq]], writes=[K.psb[bz]], tag="w lora")
            P.op("pe", lambda e, ba=ba, csl=csl, tsl=tsl: e.matmul(K.ps[ba][:, :], w2h[64:128, csl], LWh[64:128, tsl],
                                                                   start=True, stop=True),
                 reads=[R.w2a2.bufs[0], R.LW.bufs[tq]], writes=[K.psb[ba]], tag="a lora")
            P.op("act", lambda e, bz=bz, cc=cc: e.activation(out=h["sg"][:, :], in_=K.ps[bz][:, :], func=AF.Sigmoid,
                                                            bias=pc[:, 15 + cc:16 + cc]),
                 reads=[K.psb[bz], R.pcol.bufs[0]], writes=[b["sg"]], tag="sig w")
            P.op("act", lambda e, ba=ba, cc=cc: e.activation(out=h["av"][:, :], in_=K.ps[ba][:, :], func=AF.Sigmoid,
                                                            bias=pc[:, 19 + cc:20 + cc]),
                 reads=[K.psb[ba], R.pcol.bufs[0]], writes=[b["av"]], tag="sig a")
            P.op("dve", lambda e, cc=cc: e.tensor_scalar(out=h["kk"][:, :], in0=h["km"][:, :],
                                                        scalar1=pc[:, 23 + cc:24 + cc], scalar2=None, op0=ALU.mult),
                 reads=[b["km"], R.pcol.bufs[0]], writes=[b["kk"]], tag="kkraw")
            P.op("act", lambda e: e.activation(out=sqh[:, :], in_=h["kk"][:, :], func=AF.Square),
                 reads=[b["kk"]], writes=sq.bufs, tag="kk sq")
            bn_ = bank_rot()
            P.op("pe", lambda e, bn_=bn_: e.matmul(K.ps[bn_][:, :], boh[:, :], sqh[:, :], start=True, stop=True),
                 reads=[R.bones.bufs[0], sq.bufs[0]], writes=[K.psb[bn_]], tag="ssq mm")
            P.op("act", lambda e, bn_=bn_: e.activation(out=h["nrm"][:, :], in_=K.ps[bn_][:, :], func=AF.Sqrt),
                 reads=[K.psb[bn_]], writes=[b["nrm"]], tag="nrm sqrt")
            P.op("dve", lambda e: e.tensor_scalar(out=h["nrm"][:, :], in0=h["nrm"][:, :], scalar1=1e-12, scalar2=None,
                                                  op0=ALU.max), reads=[], writes=[b["nrm"]], tag="nrm max")
            P.op("dve", lambda e: e.reciprocal(out=h["nrm"][:, :], in_=h["nrm"][:, :]), reads=[], writes=[b["nrm"]],
                 tag="nrm rcp")
            P.op("dve", lambda e: e.tensor_tensor(out=h["kk"][:, :], in0=h["kk"][:, :], in1=h["nrm"][:, :], op=ALU.mult),
                 reads=[b["nrm"]], writes=[b["kk"]], tag="kk")
            P.op("pool", lambda e: e.tensor_tensor(out=h["ka"][:, :], in0=h["kk"][:, :], in1=h["av"][:, :], op=ALU.mult),
                 reads=[b["kk"], b["av"]], writes=[b["ka"]], tag="ka")
            P.op("dve", lambda e, cc=cc: e.tensor_scalar(out=h["kp"][:, :], in0=h["av"][:, :], scalar1=-1.0,
                                                        scalar2=pc[:, 27 + cc:28 + cc], op0=ALU.add, op1=ALU.mult),
                 reads=[b["av"], R.pcol.bufs[0]], writes=[b["kp"]], tag="kp1")
            P.op("dve", lambda e: e.scalar_tensor_tensor(out=h["kp"][:, :], in0=h["kp"][:, :], scalar=1.0,
                                                         in1=h["km"][:, :], op0=ALU.add, op1=ALU.mult),
                 reads=[b["km"]], writes=[b["kp"]], tag="kp2")
            P.op("dve", lambda e, cc=cc: e.scalar_tensor_tensor(out=rbh[:, :], in0=h["rm"][:, :],
                                                               scalar=pc[:, 31 + cc:32 + cc], in1=h["kp"][:, :],
                                                               op0=ALU.mult, op1=ALU.mult),
                 reads=[b["rm"], b["kp"], R.pcol.bufs[0]], writes=rb.bufs, tag="rb")
            bb_ = bank_rot()

            def bon_mm(e, bb_=bb_):
                ins = None
                for jj in range(8):
                    ins = e.matmul(K.ps[bb_][0:64, jj * 2:jj * 2 + 2], rbh[:, jj * 64:(jj + 1) * 64], hsh[:, 0:2],
                                   start=(jj == 0), stop=True, skip_group_check=True)
                return ins
            P.op("pe", bon_mm, reads=[rb.bufs[0], R.hsel.bufs[0]], writes=[K.psb[bb_]], tag="bonus mm")
            P.op("dve", lambda e, bb_=bb_, jsl=jsl, cc=cc: e.tensor_copy(
                out=BONh[:, jsl, cc * 2:cc * 2 + 2], in_=K.ps[bb_][0:64, 0:16].rearrange("p (j c) -> p j c", c=2)),
                reads=[K.psb[bb_]], writes=[R.BON.bufs[tq]], tag="bonus evac")
            if tq == 0:
                P.op("dve", lambda e: e.tensor_tensor_scan(out=h["cum"][:, :], data0=onesh[:, :], data1=h["sg"][:, :],
                                                           initial=0.0, op0=ALU.mult, op1=ALU.add),
                     reads=[ones.bufs[0], b["sg"]], writes=[b["cum"]], tag="scan")
                P.op("pool", lambda e: e.memset(cbh[:, 0:1], 0.0), reads=[], writes=cb.bufs, tag="cb0")
            else:
                P.op("dve", lambda e: e.tensor_tensor_scan(out=h["cum"][:, :], data0=onesh[:, :], data1=h["sg"][:, :],
                                                           initial=cbh[:, 8:9], op0=ALU.mult, op1=ALU.add),
                     reads=[ones.bufs[0], b["sg"], cb.bufs[0]], writes=[b["cum"]], tag="scan")
                P.op("act", lambda e: e.activation(out=cbh[:, 0:1], in_=cbh[:, 8:9], func=AF.Copy),
                     reads=[], writes=cb.bufs, tag="cb carry")
            cum3 = h["cum"].rearrange("p (j t) -> p j t", t=CH)
            P.op("act", lambda e, cum3=cum3: e.activation(out=cbh[:, 1:9], in_=cum3[:, :, CH - 1], func=AF.Copy),
                 reads=[b["cum"]], writes=cb.bufs, tag="cb ends")
            P.op("dve", lambda e, cum3=cum3: e.tensor_tensor(out=cum3, in0=cum3,
                                                             in1=cbh[:, 0:8].unsqueeze(2).to_broadcast([128, 8, CH]),
                                                             op=ALU.subtract),
                 reads=[cb.bufs[0]], writes=[b["cum"]], tag="cumrel")
            P.op("pool", lambda e: e.tensor_tensor(out=h["cx"][:, :], in0=h["cum"][:, :], in1=h["sg"][:, :],
                                                   op=ALU.subtract), reads=[b["cum"], b["sg"]], writes=[b["cx"]],
                 tag="cumex")
            P.op("act", lambda e, cum3=cum3, cc=cc, jsl=jsl: e.activation(out=GCh[:, cc, jsl], in_=cum3[:, :, CH - 1],
                                                                         func=AF.Exp, scale=-C0),
                 reads=[b["cum"]], writes=[R.GC.bufs[cc]], tag="gammaC")
            P.op("act", lambda e: e.activation(out=h["av"][:, :], in_=h["cum"][:, :], func=AF.Exp, scale=-C0),
                 reads=[b["cum"]], writes=[b["av"]], tag="Eg")
            P.op("act", lambda e: e.activation(out=h["km"][:, :], in_=h["cx"][:, :], func=AF.Exp, scale=-C0),
                 reads=[b["cx"]], writes=[b["km"]], tag="Egx")
            P.op("act", lambda e: e.activation(out=h["sg"][:, :], in_=h["cum"][:, :], func=AF.Exp, scale=C0),
                 reads=[b["cum"]], writes=[b["sg"]], tag="Ei")

            def v3(x):
                return x.rearrange("p (j t) -> p j t", t=CH)
            P.op("dve", lambda e, cc=cc, jsl=jsl: e.scalar_tensor_tensor(
                out=ARh[:, cc, jsl, 0, :], in0=v3(h["kk"]), scalar=-1.0, in1=v3(h["km"]), op0=ALU.mult, op1=ALU.mult),
                reads=[b["kk"], b["km"]], writes=cbufs, tag="A~")
            P.op("pool", lambda e, cc=cc, jsl=jsl: e.tensor_tensor(
                out=ARh[:, cc, jsl, 1, :], in0=v3(h["rm"]), in1=v3(h["av"]), op=ALU.mult),
                reads=[b["rm"], b["av"]], writes=cbufs, tag="R~")
            P.op("dve", lambda e, cc=cc, jsl=jsl: e.tensor_tensor(
                out=BKh[:, cc, jsl, 0, :], in0=v3(h["ka"]), in1=v3(h["sg"]), op=ALU.mult),
                reads=[b["ka"], b["sg"]], writes=kbufs, tag="B~")
            P.op("pool", lambda e, cc=cc, jsl=jsl: e.tensor_tensor(
                out=BKh[:, cc, jsl, 1, :], in0=v3(h["kp"]), in1=v3(h["sg"]), op=ALU.mult),
                reads=[b["kp"], b["sg"]], writes=kbufs, tag="K~")
    for al in [wr, ones, pre, sq, rb, cb] + [T[n] for n in names]:
        P.free(al)


def rwkv_chunks(K, R, masks_d, gnb_d, o_dst, o_buf, base):
    P = K.P
    off = base
    mk = P.sbuf("masks", [64, 4, 128], F32, off); off += 4 * 128 * 4
    gnb = P.sbuf("gnb", [64, 2, 512], F32, off); off += 2 * 512 * 4
    M1 = P.sbuf("M1", [64, 2, 8, 128], BF16, off, nbufs=2); off += 2 * 8 * 128 * 2
    M2 = P.sbuf("M2", [64, 2, 8, 128], BF16, off, nbufs=2); off += 2 * 8 * 128 * 2
    M3 = P.sbuf("M3", [64, 2, 8, 64], BF16, off, nbufs=2); off += 2 * 8 * 64 * 2
    NL = P.sbuf("NL", [64, 2, 2, 8, 64], BF16, off, nbufs=4); off += 2 * 2 * 8 * 64 * 2
    PP = P.sbuf("PP", [64, 2, 8, 64], BF16, off, nbufs=2); off += 2 * 8 * 64 * 2
    BKh_ = P.sbuf("BKhat", [128, 2, 4, 2, CH], BF16, off, nbufs=2); off += 2 * 4 * 2 * CH * 2
    TOK = P.sbuf("TOK", [64, 2, 4, 512], BF16, off, nbufs=2); off += 2 * 4 * 512 * 2
    WTs = P.sbuf("WTs", [128, 2, 4, CH], BF16, off, nbufs=2); off += 2 * 4 * CH * 2
    MAK = P.sbuf("MAK", [64, 2, 8, 64], BF16, off, nbufs=2); off += 2 * 8 * 64 * 2
    Ub = P.sbuf("Ub", [64, 2, 8, 64], BF16, off, nbufs=2); off += 2 * 8 * 64 * 2
    Hf = P.sbuf("Hf", [128, 4, 64], F32, off); off += 4 * 64 * 4
    Hb = P.sbuf("Hb", [128, 4, 64], BF16, off); off += 4 * 64 * 2
    Ys = P.sbuf("Ys", [64, 2, 512], F32, off, nbufs=2); off += 2 * 512 * 4
    Yq = P.sbuf("Yq", [64, 512], F32, off); off += 512 * 4
    Gs = P.sbuf("Gs", [64, 512], F32, off); off += 512 * 4
    st = P.sbuf("gst", [64, 64], F32, off); off += 64 * 4
    Yo = P.sbuf("Yo", [64, 2, 512], BF16, off, nbufs=2); off += 2 * 512 * 2
    Yoh = Yo.handle
    mkh, gnh, M1h, M2h, M3h, NLh, PPh, BHh, TOKh, WTh, MAKh, Ubh, Hfh, Hbh, Ysh, Yqh, Gsh, sth = (
        mk.handle, gnb.handle, M1.handle, M2.handle, M3.handle, NL.handle, PP.handle, BKh_.handle, TOK.handle,
        WTs.handle, MAK.handle, Ub.handle, Hf.handle, Hb.handle, Ys.handle, Yq.handle, Gs.handle, st.handle)
    ARh, BKh, VCh, SGh, GCh, BONh, g2h = (R.AR.handle, R.BK.handle, R.VC.handle, R.SG.handle, R.GC.handle,
                                         R.BON.handle, R.g2.handle)
    ident = K.ident
    load_const(K, "masks", mk, mkh[:, :, :], masks_d[:, 0:4, :])
    load_const(K, "gnb", gnb, gnh[:, :, :], gnb_d[:, :, :])
    P.op("pool", lambda e: e.memset(Hfh[:, :, :], 0.0), reads=[], writes=Hf.bufs, tag="H0")
    P.op("pool", lambda e: e.memset(Hbh[:, :, :], 0.0), reads=[], writes=Hb.bufs, tag="H0b")

    rot = [0]

    def nb():
        rot[0] += 1
        return rot[0] % 8

    def pr(hh):
        return slice(64 * hh, 64 * hh + 64)

    def ps3(bk, n, w):
        return K.ps[bk][0:64, 0:n * w].rearrange("p (n w) -> p n w", w=w)

    for j in range(getattr(K, 'nchunk_override', NCHUNK)):
        s = j % 2
        ab, kb_, vb = R.AR.bufs[j], R.BK.bufs[j], R.VC.bufs[j]
        tq = j // 8
        for hh in range(2):
            b1, b2, b3 = nb(), nb(), nb()

            def mm1(e, b1=b1, hh=hh, j=j):
                ins = None
                for cc in range(4):
                    ins = e.matmul(K.ps[b1][0:64, cc * 128:(cc + 1) * 128], ARh[pr(hh), cc, j, 0, :],
                                   BKh[pr(hh), cc, j, :, :].rearrange("p a t -> p (a t)"), start=(cc == 0), stop=True,
                                   skip_group_check=True)
                return ins
            P.op("pe", mm1, reads=[ab, kb_], writes=[K.psb[b1]], tag="SA mm")
            P.op("dve", lambda e, b1=b1, hh=hh, s=s: e.tensor_tensor(
                out=M1h[:, s, hh * 4:(hh + 1) * 4, :], in0=ps3(b1, 4, 128),
                in1=mkh[:, 0:1, :].to_broadcast([64, 4, 128]), op=ALU.mult),
                reads=[K.psb[b1], mk.bufs[0]], writes=[M1.bufs[s]], tag="M1 evac")

            def mm2(e, b2=b2, hh=hh, j=j):
                ins = None
                for cc in range(4):
                    ins = e.matmul(K.ps[b2][0:64, cc * 128:(cc + 1) * 128], BKh[pr(hh), cc, j, 0, :],
                                   ARh[pr(hh), cc, j, :, :].rearrange("p a t -> p (a t)"), start=(cc == 0), stop=True,
                                   skip_group_check=True)
                return ins
            P.op("pe", mm2, reads=[ab, kb_], writes=[K.psb[b2]], tag="SB mm")
            P.op("dve", lambda e, b2=b2, hh=hh, s=s: e.tensor_tensor(
                out=M2h[:, s, hh * 4:(hh + 1) * 4, :], in0=ps3(b2, 4, 128),
                in1=mkh[:, 1:2, :].to_broadcast([64, 4, 128]), op=ALU.mult),
                reads=[K.psb[b2], mk.bufs[0]], writes=[M2.bufs[s]], tag="M2 evac")

            def mm3(e, b3=b3, hh=hh, j=j):
                ins = None
                for cc in range(4):
                    ins = e.matmul(K.ps[b3][0:64, cc * 64:(cc + 1) * 64], BKh[pr(hh), cc, j, 1, :],
                                   ARh[pr(hh), cc, j, 1, :], start=(cc == 0), stop=True, skip_group_check=True)
                return ins
            P.op("pe", mm3, reads=[ab, kb_], writes=[K.psb[b3]], tag="SK mm")
            P.op("dve", lambda e, b3=b3, hh=hh, s=s: e.tensor_tensor(
                out=M3h[:, s, hh * 4:(hh + 1) * 4, :], in0=ps3(b3, 4, 64),
                in1=mkh[:, 2:3, 0:64].to_broadcast([64, 4, 64]), op=ALU.mult),
                reads=[K.psb[b3], mk.bufs[0]], writes=[M3.bufs[s]], tag="M3 evac")

        if getattr(K, 'chunk_cut', 99) <= 1:
            continue
        P.op("pool", lambda e, s=s: e.tensor_tensor(out=PPh[:, 0, :, :], in0=M2h[:, s, :, 0:64],
                                                    in1=mkh[:, 3:4, 0:64].to_broadcast([64, 8, 64]), op=ALU.add),
             reads=[M2.bufs[s], mk.bufs[0]], writes=[PP.bufs[0]], tag="P0")
        Ncur = lambda hd, s=s: M2h[:, s, hd, 0:64]
        Lcur = lambda hd, s=s: M1h[:, s, hd, 0:64]
        ncur_buf, lcur_buf = M2.bufs[s], M1.bufs[s]
        pc_ = 0
        for lev in range(1, 6):
            sl = lev % 2
            bl = nb()
            bn2 = nb() if lev < 5 else None

            def mmL(e, bl=bl, Ncur=Ncur, Lcur=Lcur):
                ins = None
                for hd in range(8):
                    ins = e.matmul(K.ps[bl][0:64, hd * 64:(hd + 1) * 64], Ncur(hd), Lcur(hd), start=(hd == 0), stop=True,
                                   skip_group_check=True)
                return ins
            P.op("pe", mmL, reads=[ncur_buf, lcur_buf], writes=[K.psb[bl]], tag="L sq")
            P.op("act", lambda e, bl=bl, sl=sl: e.activation(out=NLh[:, sl, 1, :, :], in_=ps3(bl, 8, 64), func=AF.Copy),
                 reads=[K.psb[bl]], writes=[NL.bufs[sl * 2 + 1]], tag="L evac")
            if lev < 5:
                def mmN(e, bn2=bn2, Ncur=Ncur, Lcur=Lcur):
                    ins = None
                    for hd in range(8):
                        ins = e.matmul(K.ps[bn2][0:64, hd * 64:(hd + 1) * 64], Lcur(hd), Ncur(hd), start=(hd == 0),
                                       stop=True, skip_group_check=True)
                    return ins
                P.op("pe", mmN, reads=[ncur_buf, lcur_buf], writes=[K.psb[bn2]], tag="N sq")
                P.op("dve", lambda e, bn2=bn2, sl=sl: e.tensor_copy(out=NLh[:, sl, 0, :, :], in_=ps3(bn2, 8, 64)),
                     reads=[K.psb[bn2]], writes=[NL.bufs[sl * 2 + 0]], tag="N evac")
            bp = nb()

            def mmP(e, bp=bp, sl=sl, pc_=pc_):
                ins = None
                for hd in range(8):
                    ins = e.matmul(K.ps[bp][0:64, hd * 64:(hd + 1) * 64], NLh[:, sl, 1, hd, :], PPh[:, pc_, hd, :],
                                   start=(hd == 0), stop=True, skip_group_check=True)
                return ins
            P.op("pe", mmP, reads=[NL.bufs[sl * 2 + 1], PP.bufs[pc_]], writes=[K.psb[bp]], tag="P mm")
            pn = 1 - pc_
            P.op("dve", lambda e, bp=bp, pc_=pc_, pn=pn: e.tensor_tensor(out=PPh[:, pn, :, :], in0=ps3(bp, 8, 64),
                                                                        in1=PPh[:, pc_, :, :], op=ALU.add),
                 reads=[K.psb[bp], PP.bufs[pc_]], writes=[PP.bufs[pn]], tag="P add")
            pc_ = pn
            Ncur = lambda hd, sl=sl: NLh[:, sl, 0, hd, :]
            Lcur = lambda hd, sl=sl: NLh[:, sl, 1, hd, :]
            ncur_buf, lcur_buf = NL.bufs[sl * 2 + 0], NL.bufs[sl * 2 + 1]
        TT = lambda hd, pc_=pc_: PPh[:, pc_, hd, :]
        tt_buf = PP.bufs[pc_]

        if getattr(K, 'chunk_cut', 99) <= 2:
            continue
        P.op("pool", lambda e, s=s, j=j: e.tensor_tensor(
            out=BHh[:, s, :, :, :], in0=BKh[:, :, j, :, :],
            in1=GCh[:, :, j:j + 1].unsqueeze(3).to_broadcast([128, 4, 2, CH]), op=ALU.mult),
            reads=[kb_, R.GC.bufs[0], R.GC.bufs[1], R.GC.bufs[2], R.GC.bufs[3]], writes=[BKh_.bufs[s]], tag="BKhat")
        for g2_ in range(2):
            bt = nb()
            ptb = K.ps[bt].bitcast(BF16)

            def trs(e, ptb=ptb, g2_=g2_, s=s, j=j):
                ins = None
                for kk_ in range(2):
                    kind = g2_ * 2 + kk_
                    for cc in range(4):
                        if kind == 0:
                            src = ARh[:, cc, j, 0, :]
                        elif kind == 1:
                            src = BHh[:, s, cc, 0, :]
                        elif kind == 2:
                            src = BHh[:, s, cc, 1, :]
                        else:
                            src = VCh[:, cc, j * CH:(j + 1) * CH]
                        ins = e.transpose(ptb[0:64, kk_ * 512 + cc * 128: kk_ * 512 + (cc + 1) * 128], src, ident[:, :])
                return ins
            P.op("pe", trs, reads=[ab, BKh_.bufs[s], vb, K.ident_buf], writes=[K.psb[bt]], tag="tok transposes")
            eng = "act" if g2_ == 0 else "dve"

            def tev(e, ptb=ptb, g2_=g2_, s=s, eng=eng):
                o = TOKh[:, s, g2_ * 2:(g2_ + 1) * 2, :]
                i = ptb[0:64, :].rearrange("p (k c) -> p k c", k=2)
                if eng == "act":
                    return e.activation(out=o, in_=i, func=AF.Copy)
                return e.tensor_copy(out=o, in_=i)
            P.op(eng, tev, reads=[K.psb[bt]], writes=[TOK.bufs[s]], tag="tok evac")

        if getattr(K, 'chunk_cut', 99) <= 3:
            continue
        bw = nb()

        def mmW(e, bw=bw, s=s, TT=TT):
            ins = None
            for hp in range(8):
                cc = hp % 4
                ins = e.matmul(K.ps[bw][:, hp * 64:(hp + 1) * 64], TOKh[:, s, 0, cc * 128:(cc + 1) * 128], TT(hp),
                               start=(hp == 0), stop=True, skip_group_check=True)
            return ins
        P.op("pe", mmW, reads=[TOK.bufs[s], tt_buf], writes=[K.psb[bw]], tag="WT mm")
        for hh in range(2):
            eng = "act" if hh == 0 else "dve"

            def wev(e, bw=bw, hh=hh, s=s, eng=eng):
                o = WTh[pr(hh), s, :, :]
                i = K.ps[bw][pr(hh), hh * 256:(hh + 1) * 256].rearrange("p (c t) -> p c t", c=4)
                if eng == "act":
                    return e.activation(out=o, in_=i, func=AF.Copy)
                return e.tensor_copy(out=o, in_=i)
            P.op(eng, wev, reads=[K.psb[bw]], writes=[WTs.bufs[s]], tag="WT evac")
        bm_ = nb()

        def mmM(e, bm_=bm_, s=s, TT=TT):
            ins = None
            for hp in range(8):
                ins = e.matmul(K.ps[bm_][0:64, hp * 64:(hp + 1) * 64], M1h[:, s, hp, 64:128], TT(hp),
                               start=(hp == 0), stop=True, skip_group_check=True)
            return ins
        P.op("pe", mmM, reads=[M1.bufs[s], tt_buf], writes=[K.psb[bm_]], tag="MAK mm")
        P.op("act", lambda e, bm_=bm_, s=s: e.activation(out=MAKh[:, s, :, :], in_=ps3(bm_, 8, 64), func=AF.Copy),
             reads=[K.psb[bm_]], writes=[MAK.bufs[s]], tag="MAK evac")

        if getattr(K, 'chunk_cut', 99) <= 4:
            continue
        Vt = lambda hp, s=s: TOKh[:, s, 3, (hp % 4) * 128 + (hp // 4) * 64:(hp % 4) * 128 + (hp // 4) * 64 + 64]
        for hh in range(2):
            bu = nb()

            def mmU1(e, bu=bu, hh=hh, s=s, Vt=Vt):
                ins = None
                for cc in range(4):
                    hp = hh * 4 + cc
                    ins = e.matmul(K.ps[bu][0:64, cc * 64:(cc + 1) * 64], MAKh[:, s, hp, :], Vt(hp), start=(cc == 0),
                                   stop=False, skip_group_check=True)
                return ins
            P.op("pe", mmU1, reads=[MAK.bufs[s], TOK.bufs[s]], writes=[K.psb[bu]], tag="U mm1")

            def mmU2(e, bu=bu, hh=hh, s=s):
                ins = None
                for cc in range(4):
                    ins = e.matmul(K.ps[bu][0:64, cc * 64:(cc + 1) * 64], WTh[pr(hh), s, cc, :], Hbh[pr(hh), cc, :],
                                   start=False, stop=True, skip_group_check=True)
                return ins
            P.op("pe", mmU2, reads=[WTs.bufs[s], Hb.bufs[0]], writes=[K.psb[bu]], tag="U mm2", pe_sync=True)
            P.op("act", lambda e, bu=bu, hh=hh, s=s: e.activation(out=Ubh[:, s, hh * 4:(hh + 1) * 4, :],
                                                                 in_=ps3(bu, 4, 64), func=AF.Copy),
                 reads=[K.psb[bu]], writes=[Ub.bufs[s]], tag="U evac")
        for hh in range(2):
            by = nb()

            def mmY1(e, by=by, hh=hh, s=s, Vt=Vt):
                ins = None
                for cc in range(4):
                    hp = hh * 4 + cc
                    o = K.ps[by][0:64, cc * 64:(cc + 1) * 64]
                    e.matmul(o, M2h[:, s, hp, 64:128], Ubh[:, s, hp, :], start=(cc == 0), stop=False,
                             skip_group_check=True)
                    ins = e.matmul(o, M3h[:, s, hp, :], Vt(hp), start=False, stop=False, skip_group_check=True)
                return ins
            P.op("pe", mmY1, reads=[M2.bufs[s], Ub.bufs[s], M3.bufs[s], TOK.bufs[s]], writes=[K.psb[by]], tag="Y mm1")

            def mmY2(e, by=by, hh=hh, j=j):
                ins = None
                for cc in range(4):
                    ins = e.matmul(K.ps[by][0:64, cc * 64:(cc + 1) * 64], ARh[pr(hh), cc, j, 1, :], Hbh[pr(hh), cc, :],
                                   start=False, stop=True, skip_group_check=True)
                return ins
            P.op("pe", mmY2, reads=[ab, Hb.bufs[0]], writes=[K.psb[by]], tag="Y mm2", pe_sync=True)
            P.op("act", lambda e, by=by, hh=hh, s=s: e.activation(
                out=Ysh[:, s, :].rearrange("p (c h v) -> p c h v", c=4, h=2)[:, :, hh, :], in_=ps3(by, 4, 64),
                func=AF.Copy), reads=[K.psb[by]], writes=[Ys.bufs[s]], tag="Y evac")
        bh = nb()

        def mmH(e, bh=bh, s=s, Vt=Vt):
            ins = None
            for hp in range(8):
                cc = hp % 4
                o = K.ps[bh][:, hp * 64:(hp + 1) * 64]
                e.matmul(o, TOKh[:, s, 1, cc * 128:(cc + 1) * 128], Ubh[:, s, hp, :], start=(hp == 0), stop=False,
                         skip_group_check=True)
                ins = e.matmul(o, TOKh[:, s, 2, cc * 128:(cc + 1) * 128], Vt(hp), start=False, stop=True,
                               skip_group_check=True)
            return ins
        P.op("pe", mmH, reads=[TOK.bufs[s], Ub.bufs[s]], writes=[K.psb[bh]], tag="H mm")
        P.op("pool", lambda e, j=j: e.tensor_tensor(out=Hfh[:, :, :], in0=Hfh[:, :, :],
                                                    in1=GCh[:, :, j:j + 1].to_broadcast([128, 4, 64]), op=ALU.mult),
             reads=[R.GC.bufs[0], R.GC.bufs[1], R.GC.bufs[2], R.GC.bufs[3]], writes=Hf.bufs, tag="H decay")
        for hh in range(2):
            P.op("dve", lambda e, bh=bh, hh=hh: e.tensor_tensor(
                out=Hfh[pr(hh), :, :], in0=Hfh[pr(hh), :, :],
                in1=K.ps[bh][pr(hh), hh * 256:(hh + 1) * 256].rearrange("p (c v) -> p c v", c=4), op=ALU.add),
                reads=[K.psb[bh]], writes=Hf.bufs, tag="H add")
        P.op("act", lambda e: e.activation(out=Hbh[:, :, :], in_=Hfh[:, :, :], func=AF.Copy),
             reads=Hf.bufs, writes=Hb.bufs, tag="H bf16")

        if getattr(K, 'chunk_cut', 99) <= 5:
            continue
        bg = nb()

        def mmG(e, bg=bg, j=j):
            e.matmul(K.ps[bg][0:64, :], SGh[:, 0, j * CH:(j + 1) * CH], g2h[:, 0, :], start=True, stop=False)
            return e.matmul(K.ps[bg][0:64, :], SGh[0:32, 1, j * CH:(j + 1) * CH], g2h[0:32, 1, :], start=False, stop=True)
        P.op("pe", mmG, reads=[R.SG.bufs[tq], R.g2.bufs[0]], writes=[K.psb[bg]], tag="gate mm")
        P.op("act", lambda e, bg=bg: e.activation(out=Gsh[:, :], in_=K.ps[bg][0:64, :], func=AF.Copy),
             reads=[K.psb[bg]], writes=Gs.bufs, tag="gate evac")
        if getattr(K, 'chunk_cut', 99) <= 6:
            continue
        Y3 = Ysh[:, s, :].rearrange("p (h v) -> p h v", v=64)
        Q3 = Yqh.rearrange("p (h v) -> p h v", v=64)
        yb = Ys.bufs[s]
        P.op("dve", lambda e, Y3=Y3: e.tensor_reduce(out=sth[:, 0:8], in_=Y3, axis=AX.X, op=ALU.add),
             reads=[yb], writes=st.bufs, tag="gn sum")
        P.op("pool", lambda e, s=s: e.tensor_tensor(out=Yqh[:, :], in0=Ysh[:, s, :], in1=Ysh[:, s, :], op=ALU.mult),
             reads=[yb], writes=Yq.bufs, tag="gn sq")
        P.op("dve", lambda e, Q3=Q3: e.tensor_reduce(out=sth[:, 8:16], in_=Q3, axis=AX.X, op=ALU.add),
             reads=Yq.bufs, writes=st.bufs, tag="gn ssq")
        P.op("dve", lambda e: e.tensor_scalar(out=sth[:, 16:24], in0=sth[:, 0:8], scalar1=1.0 / 64, scalar2=None,
                                              op0=ALU.mult), reads=[], writes=st.bufs, tag="gn mean")
        P.op("dve", lambda e: e.tensor_tensor(out=sth[:, 24:32], in0=sth[:, 16:24], in1=sth[:, 16:24], op=ALU.mult),
             reads=[], writes=st.bufs, tag="gn m2")
        P.op("dve", lambda e: e.scalar_tensor_tensor(out=sth[:, 32:40], in0=sth[:, 8:16], scalar=1.0 / 64,
                                                     in1=sth[:, 24:32], op0=ALU.mult, op1=ALU.subtract),
             reads=[], writes=st.bufs, tag="gn var")
        P.op("dve", lambda e: e.tensor_scalar(out=sth[:, 32:40], in0=sth[:, 32:40], scalar1=GN_EPS, scalar2=None,
                                              op0=ALU.add), reads=[], writes=st.bufs, tag="gn var eps")
        P.op("act", lambda e: e.activation(out=sth[:, 40:48], in_=sth[:, 32:40], func=AF.Sqrt),
             reads=[], writes=st.bufs, tag="gn sqrt")
        P.op("dve", lambda e: e.reciprocal(out=sth[:, 48:56], in_=sth[:, 40:48]), reads=[], writes=st.bufs, tag="gn rstd")
        P.op("dve", lambda e, Y3=Y3: e.tensor_tensor(out=Y3, in0=Y3, in1=sth[:, 16:24].unsqueeze(2).to_broadcast([64, 8, 64]),
                                                     op=ALU.subtract), reads=[], writes=[yb] + st.bufs, tag="gn sub")
        P.op("pool", lambda e, Y3=Y3: e.tensor_tensor(out=Y3, in0=Y3, in1=sth[:, 48:56].unsqueeze(2).to_broadcast([64, 8, 64]),
                                                      op=ALU.mult), reads=st.bufs, writes=[yb], tag="gn mul")
        P.op("pool", lambda e, s=s: e.tensor_tensor(out=Ysh[:, s, :], in0=Ysh[:, s, :], in1=gnh[:, 0, :], op=ALU.mult),
             reads=gnb.bufs, writes=[yb], tag="gn g")
        P.op("pool", lambda e, s=s: e.tensor_tensor(out=Ysh[:, s, :], in0=Ysh[:, s, :], in1=gnh[:, 1, :], op=ALU.add),
             reads=gnb.bufs, writes=[yb], tag="gn b")
        if getattr(K, 'chunk_cut', 99) <= 7:
            continue
        P.op("dve", lambda e, Q3=Q3, s=s, j=j: e.tensor_tensor(
            out=Q3, in0=TOKh[:, s, 3, :].rearrange("p (h v) -> p h v", v=64),
            in1=BONh[:, j, :].unsqueeze(2).to_broadcast([64, 8, 64]), op=ALU.mult),
            reads=[TOK.bufs[s], R.BON.bufs[tq]], writes=Yq.bufs, tag="bonus v")
        P.op("pool", lambda e, s=s: e.tensor_tensor(out=Ysh[:, s, :], in0=Ysh[:, s, :], in1=Yqh[:, :], op=ALU.add),
             reads=Yq.bufs, writes=[yb], tag="y+bonus")
        P.op("dve", lambda e, s=s: e.tensor_tensor(out=Yoh[:, s, :], in0=Ysh[:, s, :], in1=Gsh[:, :], op=ALU.mult),
             reads=Gs.bufs + [yb], writes=[Yo.bufs[s]], tag="gate mul")
        if getattr(K, 'chunk_cut', 99) <= 8:
            continue
        P.op("sp", lambda e, s=s, j=j: e.dma_start(out=o_dst[j * CH:(j + 1) * CH, 512:1024], in_=Yoh[:, s, :]),
             reads=[Yo.bufs[s]], writes=[o_buf], dma=True, tag="o_rwkv out")
    for al in (mk, gnb, M1, M2, M3, NL, PP, BKh_, TOK, WTs, MAK, Ub, Hf, Hb, Ys, Yq, Gs, st, Yo):
        P.free(al)


def host_rwkv_params(inp, half):
    c0, c1 = half * 512, (half + 1) * 512
    mu = inp["rwkv_mu"]
    pcol = np.zeros((128, 64), np.float32)
    for blk in range(3):
        for cc in range(4):
            pcol[:, blk * 4 + cc] = mu[blk * 1024 + c0 + cc * 128: blk * 1024 + c0 + (cc + 1) * 128]
    pcol[:, 12] = mu[3072:3072 + 128]
    pcol[:, 13] = mu[3072 + 128:3072 + 256]
    pcol[0:32, 14] = mu[3072 + 256:3072 + 288]
    rk = inp["rwkv_r_k"].reshape(-1)
    for cc in range(4):
        sl = slice(c0 + cc * 128, c0 + (cc + 1) * 128)
        pcol[:, 15 + cc] = inp["rwkv_w0"][sl]
        pcol[:, 19 + cc] = inp["rwkv_a0"][sl]
        pcol[:, 23 + cc] = inp["rwkv_k_k"][sl]
        pcol[:, 27 + cc] = inp["rwkv_k_a"][sl]
        pcol[:, 31 + cc] = rk[sl]
    w2a2 = np.concatenate([inp["rwkv_w2"][:, c0:c1], inp["rwkv_a2"][:, c0:c1]], 0).astype(np.float32)
    g2 = np.zeros((128, 2, 512), np.float32)
    g2[:, 0, :] = inp["rwkv_g2"][0:128, c0:c1]
    g2[0:32, 1, :] = inp["rwkv_g2"][128:160, c0:c1]
    gnb = np.zeros((64, 2, 512), np.float32)
    gnb[:, 0, :] = inp["rwkv_gn_g"][c0:c1][None]
    gnb[:, 1, :] = inp["rwkv_gn_b"][c0:c1][None]
    return dict(pcol=pcol, w2a2=np.ascontiguousarray(w2a2), g2=g2, gnb=gnb)


def host_rwkv_consts():
    bones = np.zeros((128, 128), np.float32)
    bones[0:64, 0:64] = 1.0
    bones[64:128, 64:128] = 1.0
    hsel = np.zeros((128, 16), np.float32)
    hsel[0:64, 0] = 1.0
    hsel[64:128, 1] = 1.0
    t = np.arange(64)
    sl = (t[None, :] < t[:, None]).astype(np.float32)
    su = (t[:, None] < t[None, :]).astype(np.float32)
    ui = (t[:, None] <= t[None, :]).astype(np.float32)
    masks = np.zeros((64, 6, 128), np.float32)
    masks[:, 0, 0:64] = sl
    masks[:, 0, 64:128] = sl
    masks[:, 1, 0:64] = su
    masks[:, 1, 64:128] = ui
    masks[:, 2, 0:64] = ui
    masks[:, 3, 0:64] = np.eye(64, dtype=np.float32)
    return dict(bones=bones, hsel=hsel, masks=masks)


def ln_tile(P, acch, ab, t, gbh, gb, lsth, st, s, outs):
    for j in range(4):
        P.op("dve", lambda e, t=t, j=j, s=s: e.bn_stats(out=lsth[:, s, j * 6:(j + 1) * 6],
                                                      in_=acch[:, t, j * 512:(j + 1) * 512]),
             reads=[ab[j]], writes=[st.bufs[s]], tag="bn_stats")
    P.op("dve", lambda e, s=s: e.bn_aggr(out=lsth[:, s, 24:26], in_=lsth[:, s, 0:24].rearrange("p (a b) -> p a b", b=6)),
         reads=[], writes=[st.bufs[s]], tag="bn_aggr")
    P.op("dve", lambda e, s=s: e.tensor_scalar(out=lsth[:, s, 28:29], in0=lsth[:, s, 25:26], scalar1=LN_EPS,
                                              scalar2=None, op0=ALU.add), reads=[], writes=[st.bufs[s]], tag="var+eps")
    P.op("act", lambda e, s=s: e.activation(out=lsth[:, s, 29:30], in_=lsth[:, s, 28:29], func=AF.Sqrt),
         reads=[], writes=[st.bufs[s]], tag="sqrt")
    P.op("dve", lambda e, s=s: e.reciprocal(out=lsth[:, s, 26:27], in_=lsth[:, s, 29:30]),
         reads=[], writes=[st.bufs[s]], tag="rstd")
    P.op("dve", lambda e, s=s: e.scalar_tensor_tensor(out=lsth[:, s, 27:28], in0=lsth[:, s, 24:25], scalar=-1.0,
                                                     in1=lsth[:, s, 26:27], op0=ALU.mult, op1=ALU.mult),
         reads=[], writes=[st.bufs[s]], tag="nmr")
    P.op("act", lambda e, t=t, s=s: e.activation(out=acch[:, t, :], in_=acch[:, t, :], func=AF.Identity,
                                                bias=lsth[:, s, 27:28], scale=lsth[:, s, 26:27]),
         reads=[st.bufs[s]], writes=ab, tag="ln norm")
    P.op("pool", lambda e, t=t: e.tensor_tensor(out=acch[:, t, :], in0=acch[:, t, :], in1=gbh[:, 0, :], op=ALU.mult),
         reads=[gb.bufs[0]], writes=ab, tag="ln g")
    P.op("pool", lambda e, t=t: e.tensor_tensor(out=acch[:, t, :], in0=acch[:, t, :], in1=gbh[:, 1, :], op=ALU.add),
         reads=[gb.bufs[0]], writes=ab, tag="ln b")
    for (dst, dbuf, dt) in outs:
        if dt == F32:
            P.op("sp", lambda e, t=t, dst=dst: e.dma_start(out=dst[t * 128:(t + 1) * 128, :], in_=acch[:, t, :]),
                 reads=ab, writes=[dbuf], dma=True, tag="ln out")
        else:
            obf = P.lnbf
            P.op("act", lambda e, t=t, s=s, obf=obf: e.activation(out=obf.handle[:, s, :], in_=acch[:, t, :], func=AF.Copy),
                 reads=ab, writes=[obf.bufs[s]], tag="ln out cast")
            P.op("sp", lambda e, t=t, s=s, dst=dst, obf=obf: e.dma_start(out=dst[t * 128:(t + 1) * 128, :],
                                                                        in_=obf.handle[:, s, :]),
                 reads=[obf.bufs[s]], writes=[dbuf], dma=True, tag="ln out bf16")


def wout_stage(K, og, og_buf, x1f, x1f_buf, wout, hsc_d, lng, lnb, outs, base):
    P = K.P
    NT = NTOK // 128
    off = base
    acc = P.sbuf("acc3", [128, NT, D], F32, off, nbufs=NT * 4); off += NT * D * 4
    XT = P.sbuf("oT", [128, KC, NTOK], BF16, off, nbufs=NT); off += KC * NTOK * 2
    ws = P.sbuf("wouts", [128, KC, D], BF16, off, nbufs=4); off += KC * D * 2
    gb = P.sbuf("lngb3", [128, 2, D], F32, off); off += 2 * D * 4
    st = P.sbuf("lnst3", [128, 2, 32], F32, off, nbufs=2); off += 2 * 32 * 4
    hs = P.sbuf("hsc", [128, 8], F32, off); off += 32
    xa = P.sbuf("xa", [128, 2, 2, D], BF16, off, nbufs=2); off += 2 * 2 * D * 2
    xb = P.sbuf("xb3", [128, 2, D], BF16, off, nbufs=2); off += 2 * D * 2
    acch, XTh, wsh, gbh, lsth, hsh, xah, xbh = (acc.handle, XT.handle, ws.handle, gb.handle, st.handle, hs.handle,
                                                xa.handle, xb.handle)
    load_const(K, "lng3", gb, gbh[:, 0, :], lng[:, :])
    load_const(K, "lnb3", gb, gbh[:, 1, :], lnb[:, :])
    load_const(K, "hsc", hs, hsh[:, 0:2], hsc_d[:, :])
    wv = wout.rearrange("(kc p) d -> p kc d", p=128)
    for q in range(4):
        P.op("pool", lambda e, q=q: e.dma_start(out=wsh[:, q * 4:(q + 1) * 4, :], in_=wv[:, q * 4:(q + 1) * 4, :]),
             reads=[], writes=[ws.bufs[q]], dma=True, tag="wout load")
    for t in range(NT):
        P.op("sp", lambda e, t=t: e.dma_start(out=acch[:, t, :], in_=x1f[t * 128:(t + 1) * 128, :]),
             reads=[x1f_buf], writes=acc.bufs[t * 4:(t + 1) * 4], dma=True, tag="acc3 load")
        P.op("act", lambda e, t=t: e.activation(out=acch[:, t, :], in_=acch[:, t, :], func=AF.Copy, scale=ALPHA),
             reads=[], writes=acc.bufs[t * 4:(t + 1) * 4], tag="acc3 scale")
    banks = [4, 5, 6, 7]
    for t in range(NT):
        s = t % 2
        for cand in range(2):
            for r in range(2):
                if callable(og):
                    oap, obuf_ = og(cand, r, t)
                else:
                    row0 = r * SEQ + cand * NTOK + t * 128
                    oap, obuf_ = og[row0:row0 + 128, :], og_buf
                P.op("sp", lambda e, s=s, cand=cand, r=r, oap=oap: e.dma_start(
                    out=xah[:, s, cand, r * 1024:(r + 1) * 1024], in_=oap),
                    reads=[obuf_], writes=[xa.bufs[s]], dma=True, tag="og load")
        P.op("dve", lambda e, s=s: e.tensor_scalar(out=xbh[:, s, :], in0=xah[:, s, 0, :], scalar1=hsh[:, 0:1],
                                                  scalar2=None, op0=ALU.mult),
             reads=[xa.bufs[s], hs.bufs[0]], writes=[xb.bufs[s]], tag="blend0")
        P.op("dve", lambda e, s=s: e.scalar_tensor_tensor(out=xbh[:, s, :], in0=xah[:, s, 1, :], scalar=hsh[:, 1:2],
                                                         in1=xbh[:, s, :], op0=ALU.mult, op1=ALU.add),
             reads=[xa.bufs[s], hs.bufs[0]], writes=[xb.bufs[s]], tag="blend1")
        for half in range(2):
            bk = banks[(2 * t + half) % 4]
            pb = K.ps[bk].bitcast(BF16)

            def tr(e, s=s, half=half, pb=pb):
                ins = None
                for j in range(8):
                    kc = half * 8 + j
                    ins = e.transpose(pb[:, j * 128:(j + 1) * 128], xbh[:, s, kc * 128:(kc + 1) * 128], K.ident[:])
                return ins
            P.op("pe", tr, reads=[xb.bufs[s], K.ident_buf], writes=[K.psb[bk]], tag="oT transposes")
            eng = "act" if half == 0 else "dve"

            def ev(e, t=t, half=half, pb=pb, eng=eng):
                o = XTh[:, half * 8:(half + 1) * 8, t * 128:(t + 1) * 128]
                i = pb.rearrange("p (j c) -> p j c", j=8)
                if eng == "act":
                    return e.activation(out=o, in_=i, func=AF.Copy)
                return e.tensor_copy(out=o, in_=i)
            P.op(eng, ev, reads=[K.psb[bk]], writes=[XT.bufs[t]], tag="oT evac")
    cnt = 0
    for t in range(NT):
        for db in range(4):
            bk = cnt % 4
            cnt += 1

            def mm(e, bk=bk, t=t, db=db):
                ins = None
                for kc in range(KC):
                    ins = e.matmul(K.ps[bk][:, :], XTh[:, kc, t * 128:(t + 1) * 128], wsh[:, kc, db * 512:(db + 1) * 512],
                                   start=(kc == 0), stop=(kc == KC - 1))
                return ins
            P.op("pe", mm, reads=[XT.bufs[t]] + ws.bufs, writes=[K.psb[bk]], tag="wout mm")
            P.op("dve", lambda e, bk=bk, t=t, db=db: e.tensor_tensor(
                out=acch[:, t, db * 512:(db + 1) * 512], in0=K.ps[bk][:, :], in1=acch[:, t, db * 512:(db + 1) * 512],
                op=ALU.add), reads=[K.psb[bk]], writes=[acc.bufs[t * 4 + db]], tag="acc3 add")
        ln_tile(P, acch, acc.bufs[t * 4:(t + 1) * 4], t, gbh, gb, lsth, st, t % 2, outs)
    for a in (acc, XT, ws, gb, st, hs, xa, xb):
        P.free(a)


def build_program():
    nc = bass.Bass("TRN2", target_bir_lowering=False)
    K = mk_ctx(nc)
    P = K.P
    d = {}

    def din(name, shape):
        d[name] = dram_in(K, name, shape)
        return d[name]
    x = din("x", [NTOK, D])
    identd = din("ident", [128, 128])
    f1 = [din("f1_wg", [NG, 128, KC, FG]), din("f1_wu", [NG, 128, KC, FG]), din("f1_wd", [DFF, D]),
          din("ln1g", [128, D]), din("ln1b", [128, D])]
    f2 = [din("f2_wg", [NG, 128, KC, FG]), din("f2_wu", [NG, 128, KC, FG]), din("f2_wd", [DFF, D]),
          din("ln3g", [128, D]), din("ln3b", [128, D])]
    win = din("win", [27, 128, KC, 128])
    bm = din("bm", [4, 5, 128, 512]); ctab = din("ctab", [128, 64]); lamv = din("lamv", [128, 4, 64])
    ngt = din("ngt", [128, 512])
    pcol = din("pcol", [128, 64]); w2a2 = din("w2a2", [128, 512]); g2 = din("g2", [128, 2, 512])
    gnb = din("gnb", [64, 2, 512]); bones = din("bones", [128, 128]); hsel = din("hsel", [128, 16])
    masks = din("masks", [64, 6, 128])
    wout = din("wout", [D, D]); hsc = din("hsc", [128, 2]); ln2g = din("ln2g", [128, D]); ln2b = din("ln2b", [128, D])
    x1f = dram_tmp(K, "x1f", [NTOK, D], F32)
    x1b = dram_tmp(K, "x1b", [NTOK, D], BF16)
    x1g = [dram_tmp(K, f"x1g{i}", [256, D], BF16) for i in range(8)]
    oloc = dram_tmp(K, "oloc", [SEQ, 1024], BF16)
    og = [dram_tmp(K, f"og{i}", [512, 1024], BF16) for i in range(8)]
    x2s = dram_tmp(K, "x2s", [NTOK, D], F32)
    out = dram_tmp(K, "out", [NTOK, D], F32, kind="ExternalOutput")
    B = lambda n: K.dram[n][1]
    groups = [[0, 1], [2, 3], [4, 5], [6, 7]]

    setup_consts(K, identd)
    base = K.const_top
    ffn_stage(K, x.ap(), B("x"), f1[0].ap(), f1[1].ap(), f1[2].ap(), f1[3].ap(), f1[4].ap(),
              [(x1f.ap(), B("x1f"), F32), (x1b.ap(), B("x1b"), BF16)], base=base)
    for i in range(8):
        P.op("pool", lambda e, i=i: e.collective_compute("AllGather", ALU.bypass, replica_groups=groups,
                                                         ins=[x1b.ap()[i * 128:(i + 1) * 128, :]],
                                                         outs=[x1g[i].ap()[:, :]]),
             reads=[B("x1b")], writes=[B(f"x1g{i}")], dma="cc", tag="allgather x1")

    def x1_src(t):
        r, i = t // 8, t % 8
        return x1g[i].ap()[r * 128:(r + 1) * 128, :], B(f"x1g{i}")
    X1T = P.sbuf("X1T", [128, KC, SEQ], BF16, base, nbufs=16)
    b2 = base + KC * SEQ * 2
    load_xT(K, x1_src, None, SEQ, X1T, b2, [4, 5, 6, 7], K.ident, False)
    attention_stage(K, X1T, win.ap(), bm.ap(), ctab.ap(), lamv.ap(), ngt.ap(), oloc.ap(), B("oloc"), b2)
    R = rwkv_alloc_persist(K, b2)
    rwkv_prep(K, R, X1T, win.ap(), pcol.ap(), w2a2.ap(), g2.ap(), bones.ap(), hsel.ap(), R.top)
    P.free(X1T)
    rwkv_chunks(K, R, masks.ap(), gnb.ap(), oloc.ap(), B("oloc"), base)
    for a in (R.AR, R.BK, R.VC, R.LW, R.SG, R.GC, R.BON, R.pcol, R.w2a2, R.g2, R.bones, R.hsel):
        P.free(a)
    for i in range(8):
        P.op("pool", lambda e, i=i: e.collective_compute("AllGather", ALU.bypass, replica_groups=groups,
                                                         ins=[oloc.ap()[i * 256:(i + 1) * 256, :]],
                                                         outs=[og[i].ap()[:, :]]),
             reads=[B("oloc")], writes=[B(f"og{i}")], dma="cc", tag="allgather o")

    def og_src(cand, r, t):
        tok = cand * NTOK + t * 128
        i, j = tok // 256, tok % 256
        return og[i].ap()[r * 256 + j:r * 256 + j + 128, :], B(f"og{i}")
    wout_stage(K, og_src, None, x1f.ap(), B("x1f"), wout.ap(), hsc.ap(), ln2g.ap(), ln2b.ap(),
               [(x2s.ap(), B("x2s"), F32)], base)
    ffn_stage(K, x2s.ap(), B("x2s"), f2[0].ap(), f2[1].ap(), f2[2].ap(), f2[3].ap(), f2[4].ap(),
              [(out.ap(), B("out"), F32)], base=base)
    finish(K, [B("out")])
    P.emit()
    return nc


def host_inputs(inp):
    l0 = {k: np.asarray(v[0], np.float32) for k, v in inp.items() if k != "x"}
    x = np.asarray(inp["x"], np.float32).reshape(8, NTOK, D)
    rep = lambda v, n=128: np.ascontiguousarray(np.broadcast_to(np.asarray(v, np.float32)[None], (n,) + v.shape))
    shared = dict(
        ident=np.eye(128, dtype=np.float32),
        f1_wg=host_tile_ffn_w(l0["ffn1_w_gate"]), f1_wu=host_tile_ffn_w(l0["ffn1_w_up"]), f1_wd=l0["ffn1_w_down"],
        ln1g=rep(l0["ln1_g"]), ln1b=rep(l0["ln1_b"]),
        f2_wg=host_tile_ffn_w(l0["ffn2_w_gate"]), f2_wu=host_tile_ffn_w(l0["ffn2_w_up"]), f2_wd=l0["ffn2_w_down"],
        ln3g=rep(l0["ln3_g"]), ln3b=rep(l0["ln3_b"]), ln2g=rep(l0["ln2_g"]), ln2b=rep(l0["ln2_b"]),
        lamv=rep(np.stack([l0["lambda_q1"], l0["lambda_k1"], l0["lambda_q2"], l0["lambda_k2"]])),
        ngt=rep(np.tile(l0["attn_norm_g"], 4)),
        wout=np.ascontiguousarray(l0["w_out"][np.concatenate([np.arange(0, 512), np.arange(1024, 1536),
                                                            np.arange(512, 1024), np.arange(1536, 2048)])]),
    )
    shared.update(host_rwkv_consts())
    per_half = []
    for h in range(2):
        bm, ctab = host_attn_consts(h)
        dd = dict(win=host_tile_win(l0["w_in"], h), bm=bm, ctab=ctab,
                  hsc=np.ascontiguousarray(np.broadcast_to(np.array([1.0 - h, float(h)], np.float32)[None], (128, 2))))
        dd.update(host_rwkv_params(l0, h))
        per_half.append(dd)
    maps = []
    for c in range(8):
        m = dict(shared)
        m.update(per_half[c % 2])
        m["x"] = np.ascontiguousarray(x[c])
        maps.append(m)
    return maps


def kernel(**inputs):
    nc = build_program()
    maps = host_inputs(inputs)
    res = run_bass_kernel_spmd(nc, maps, core_ids=list(range(8)))
    out = np.stack([np.asarray(res.results[c]["out"], np.float32) for c in range(8)], 0)
    return out.reshape(4, SEQ, D)
```

```python
import numpy as np
import concourse.bass as bass
import concourse.mybir as mybir
from concourse.bass_utils import run_bass_kernel_spmd

F32 = mybir.dt.float32
BF16 = mybir.dt.bfloat16
AF = mybir.ActivationFunctionType
ALU = mybir.AluOpType
AX = mybir.AxisListType

SAME_ENGINE_SYNC = True
SEM_CHUNK = 4000
N_DMA_SEMS = 12


class Buf:
    __slots__ = ("name", "last_w", "readers", "excl")

    def __init__(self, name, excl=False):
        self.name = name
        self.last_w = None
        self.readers = []
        self.excl = excl


class Op:
    __slots__ = ("eng", "fn", "deps", "dma", "needs_inc", "inc_no", "dsem", "dval", "pos", "tag", "prev_dval")

    def __init__(self, eng, fn, dma, tag):
        self.eng = eng
        self.fn = fn
        self.deps = []
        self.dma = dma
        self.needs_inc = False
        self.inc_no = None
        self.dsem = None
        self.dval = None
        self.tag = tag


class Alloc:
    def __init__(self, name, off, nbytes, handle, bufs):
        self.name, self.off, self.nbytes, self.handle, self.bufs = name, off, nbytes, handle, bufs


ENGS = ("pe", "act", "dve", "pool", "sp")


class Prog:
    def __init__(self, nc):
        self.nc = nc
        self.ops = {e: [] for e in ENGS}
        self.all_ops = []
        self.live = []
        self.ghosts = []
        self.uid = 0
        self.sbuf_top = 0

    def sbuf(self, name, shape, dtype, off, nbufs=1):
        esz = 4 if dtype == F32 else 2
        free = 1
        for s in shape[1:]:
            free *= s
        nbytes = free * esz
        assert off % 32 == 0, (name, off)
        assert 16384 <= off and off + nbytes <= 224 * 1024 - 160, (name, off, nbytes)
        self.uid += 1
        h = self.nc.alloc_sbuf_tensor_at(f"{name}_{self.uid}", list(shape), dtype, offset=off)
        bufs = [Buf(f"{name}[{i}]") for i in range(nbufs)]
        for a in self.live:
            assert a.off + a.nbytes <= off or off + nbytes <= a.off, ("overlap", name, a.name)
        keep = []
        for g in self.ghosts:
            if g.off + g.nbytes <= off or off + nbytes <= g.off:
                keep.append(g)
                continue
            hz = []
            for b in g.bufs:
                if b.last_w is not None:
                    hz.append(b.last_w)
                hz.extend(b.readers)
            for b in bufs:
                b.readers.extend(hz)
            keep.append(g)
        self.ghosts = keep
        a = Alloc(name, off, nbytes, h, bufs)
        self.live.append(a)
        return a

    def free(self, a):
        self.live.remove(a)
        self.ghosts.append(a)

    def op(self, eng, fn, reads=(), writes=(), dma=False, tag="", pe_sync=False):
        o = Op(eng, fn, dma, tag)
        deps = []
        for b in reads:
            if b.excl:
                continue
            if b.last_w is not None:
                deps.append(b.last_w)
        for b in list(writes) + [b for b in reads if b.excl]:
            if b.last_w is not None:
                deps.append(b.last_w)
            deps.extend(b.readers)
        for b in reads:
            if not b.excl:
                b.readers.append(o)
        for b in list(writes) + [b for b in reads if b.excl]:
            b.last_w = o
            b.readers = []
        seen = set()
        for d in deps:
            if id(d) in seen or d is o:
                continue
            seen.add(id(d))
            if (not d.dma) and d.eng == eng and not SAME_ENGINE_SYNC:
                continue
            if (not d.dma) and d.eng == eng and eng == "pe" and not pe_sync:
                continue
            o.deps.append(d)
        o.pos = len(self.ops[eng])
        self.ops[eng].append(o)
        self.all_ops.append(o)
        return o

    def emit(self):
        nc = self.nc
        for o in self.all_ops:
            for d in o.deps:
                d.needs_inc = True
        cnt = {e: 0 for e in ENGS}
        dma_cnt = {e: 0 for e in ENGS}
        n_sems = {}
        for e in ENGS:
            n = sum(1 for o in self.ops[e] if o.needs_inc and not o.dma)
            n_sems[e] = max(1, (n + SEM_CHUNK - 1) // SEM_CHUNK)
        import contextlib
        with contextlib.ExitStack() as st:
            esems = {e: [st.enter_context(nc.semaphore(f"s_{e}_{i}")) for i in range(n_sems[e])] for e in ENGS}
            dsems = {e: [st.enter_context(nc.semaphore(f"d_{e}_{i}")) for i in range(N_DMA_SEMS)]
                     for e in ("sp", "act", "pool")}
            dsem_val = {e: [0] * N_DMA_SEMS for e in dsems}
            ccsems = []
            dsems["cc"] = ccsems
            for e in ENGS:
                for o in self.ops[e]:
                    if o.dma == "cc":
                        o.dsem = ("cc", len(ccsems))
                        ccsems.append(st.enter_context(nc.semaphore(f"cc_{len(ccsems)}")))
                        o.prev_dval = 0
                        o.dval = 1
                    elif o.dma:
                        k = dma_cnt[e] % N_DMA_SEMS
                        dma_cnt[e] += 1
                        o.dsem = (e, k)
                        o.prev_dval = dsem_val[e][k]
                        dsem_val[e][k] += 16
                        o.dval = dsem_val[e][k]
                    elif o.needs_inc:
                        o.inc_no = cnt[e]
                        cnt[e] += 1
            items = {e: [] for e in ENGS}
            for e in ENGS:
                waited = {}
                for o in self.ops[e]:
                    ws = []
                    for d in o.deps:
                        if d.dma:
                            key = ("d",) + d.dsem
                            val = d.dval
                            sem = dsems[d.dsem[0]][d.dsem[1]]
                        else:
                            ch = d.inc_no // SEM_CHUNK
                            key = ("e", d.eng, ch)
                            val = d.inc_no % SEM_CHUNK + 1
                            sem = esems[d.eng][ch]
                            later = any(k[0] == "e" and k[1] == d.eng and k[2] > ch for k in waited)
                            if later:
                                continue
                        if waited.get(key, 0) >= val:
                            continue
                        waited[key] = val
                        ws.append((sem, val))
                    if o.dma and o.prev_dval > 0:
                        key = ("d",) + o.dsem
                        if waited.get(key, 0) < o.prev_dval:
                            waited[key] = o.prev_dval
                            ws.append((dsems[o.dsem[0]][o.dsem[1]], o.prev_dval))
                    items[e].append((ws, o))
            self.stats = {e: (len(self.ops[e]), sum(len(w) for w, _ in items[e])) for e in ENGS}

            def run(e, eng):
                for ws, o in items[e]:
                    for sem, val in ws:
                        eng.wait_ge(sem, val)
                    ins = o.fn(eng)
                    if o.dma == "cc":
                        ins.then_inc(dsems["cc"][o.dsem[1]], 1)
                    elif o.dma:
                        ins.then_inc(dsems[o.dsem[0]][o.dsem[1]], 16)
                    elif o.needs_inc:
                        ins.then_inc(esems[e][o.inc_no // SEM_CHUNK], 1)

            with nc.Block() as block:
                @block.tensor
                def _(eng):
                    run("pe", eng)

                @block.scalar
                def _(eng):
                    run("act", eng)

                @block.vector
                def _(eng):
                    run("dve", eng)

                @block.gpsimd
                def _(eng):
                    run("pool", eng)

                @block.sync
                def _(eng):
                    run("sp", eng)


D = 2048
DFF = 5632
NTOK = 1024
SEQ = 2048
ALPHA = 2.0 ** 0.25
LN_EPS = 1e-5
FG = 256
NG = DFF // FG
KC = D // 128


class Ctx:
    pass


def mk_ctx(nc):
    K = Ctx()
    K.nc = nc
    K.P = Prog(nc)
    K.ps = []
    K.psb = []
    for i in range(8):
        h = nc.alloc_psum_tensor(f"psum{i}", [128, 512], F32)
        K.ps.append(h)
        K.psb.append(Buf(f"psum{i}", excl=True))
    K.dram = {}
    return K


def dram_in(K, name, shape, dtype=F32):
    t = K.nc.dram_tensor(name, list(shape), dtype, kind="ExternalInput")
    K.dram[name] = (t, Buf("dram_" + name))
    return t


def dram_tmp(K, name, shape, dtype, kind="Internal"):
    t = K.nc.dram_tensor(name, list(shape), dtype, kind=kind)
    K.dram[name] = (t, Buf("dram_" + name))
    return t


def load_const(K, name, alloc, dst_ap, src_ap, q="sp"):
    K.P.op(q, lambda e, o=dst_ap, i=src_ap: e.dma_start(out=o, in_=i), reads=[], writes=alloc.bufs, dma=True,
           tag="const " + name)


def load_xT(K, src, src_buf, ntok, XT, xb_off, banks, ident, src_f32):
    P = K.P
    nt = ntok // 128
    xb = P.sbuf("xb", [128, 2, D], BF16, xb_off, nbufs=2)
    xbh = xb.handle
    XTh = XT.handle
    for t in range(nt):
        s = t % 2
        q = "pool" if src_f32 else "sp"
        if callable(src):
            sap, sbuf_ = src(t)
        else:
            sap, sbuf_ = src[t * 128:(t + 1) * 128, :], src_buf
        P.op(q, lambda e, s=s, sap=sap: e.dma_start(out=xbh[:, s, :], in_=sap),
             reads=[sbuf_], writes=[xb.bufs[s]], dma=True, tag="xb load")
        for half in range(2):
            bk = banks[(2 * t + half) % len(banks)]
            pb = K.ps[bk].bitcast(BF16)

            def tr(e, s=s, half=half, pb=pb):
                ins = None
                for j in range(8):
                    kc = half * 8 + j
                    ins = e.transpose(pb[:, j * 128:(j + 1) * 128], xbh[:, s, kc * 128:(kc + 1) * 128], ident[:])
                return ins
            P.op("pe", tr, reads=[xb.bufs[s], K.ident_buf], writes=[K.psb[bk]], tag="xT transposes")
            eng = "act" if half == 0 else "dve"

            def ev(e, t=t, half=half, pb=pb, eng=eng):
                o = XTh[:, half * 8:(half + 1) * 8, t * 128:(t + 1) * 128]
                i = pb.rearrange("p (j c) -> p j c", j=8)
                if eng == "act":
                    return e.activation(out=o, in_=i, func=AF.Copy)
                return e.tensor_copy(out=o, in_=i)
            P.op(eng, ev, reads=[K.psb[bk]], writes=[XT.bufs[t]], tag="xT evac")
    P.free(xb)


def ffn_stage(K, x_src, x_buf, wg, wu, wd, lng, lnb, outs, base=0):
    P = K.P
    nc = K.nc
    NT = NTOK // 128
    off = base
    acc = P.sbuf("acc", [128, NT, D], F32, off, nbufs=NT * 4); off += NT * D * 4
    XT = P.sbuf("XT", [128, KC, NTOK], BF16, off, nbufs=NT); off += KC * NTOK * 2
    wgs = P.sbuf("wgs", [128, 2, KC, FG], BF16, off, nbufs=2); off += 2 * KC * FG * 2
    wus = P.sbuf("wus", [128, 2, KC, FG], BF16, off, nbufs=2); off += 2 * KC * FG * 2
    wds = P.sbuf("wds", [128, 2, 2, D], BF16, off, nbufs=2); off += 2 * 2 * D * 2
    hT = P.sbuf("hT", [128, 2, 2, NTOK], BF16, off, nbufs=2); off += 2 * 2 * NTOK * 2
    stmp = P.sbuf("stmp", [128, 2, 512], F32, off, nbufs=2); off += 2 * 512 * 4
    gb = P.sbuf("lngb", [128, 2, D], F32, off, nbufs=1); off += 2 * D * 4
    st = P.sbuf("lnst", [128, 2, 4 * 6 + 8], F32, off, nbufs=2); off += 2 * 32 * 4
    xb_off = off
    acch, XTh, wgh, wuh, wdh, hTh, sth, gbh, lsth = (acc.handle, XT.handle, wgs.handle, wus.handle, wds.handle,
                                                      hT.handle, stmp.handle, gb.handle, st.handle)

    load_const(K, "lng", gb, gbh[:, 0, :], lng[:, :])
    load_const(K, "lnb", gb, gbh[:, 1, :], lnb[:, :])

    for t in range(NT):
        P.op("sp", lambda e, t=t: e.dma_start(out=acch[:, t, :], in_=x_src[t * 128:(t + 1) * 128, :]),
             reads=[x_buf], writes=acc.bufs[t * 4:(t + 1) * 4], dma=True, tag="acc load")
        P.op("act", lambda e, t=t: e.activation(out=acch[:, t, :], in_=acch[:, t, :], func=AF.Copy, scale=ALPHA),
             reads=[], writes=acc.bufs[t * 4:(t + 1) * 4], tag="acc scale")

    load_xT(K, x_src, x_buf, NTOK, XT, xb_off, [4, 5, 6, 7], K.ident, True)

    def load_wgu(g):
        s = g % 2
        for (wsrc, wh, wa, nm) in ((wg, wgh, wgs, "wg"), (wu, wuh, wus, "wu")):
            for piece in range(2):
                P.op("pool", lambda e, g=g, s=s, wsrc=wsrc, wh=wh, piece=piece: e.dma_start(
                    out=wh[:, s, piece * 8:(piece + 1) * 8, :], in_=wsrc[g, :, piece * 8:(piece + 1) * 8, :]),
                    reads=[], writes=[wa.bufs[s]], dma=True, tag=nm + " load")

    def load_wd(g):
        s = g % 2
        wdv = wd.rearrange("(g fc p) d -> g p fc d", fc=2, p=128)
        P.op("pool", lambda e, g=g, s=s: e.dma_start(out=wdh[:, s, :, :], in_=wdv[g]),
             reads=[], writes=[wds.bufs[s]], dma=True, tag="wd load")

    def upgate(g, only=None):
        s = g % 2
        for c in range(2):
            for th in range(2):
                if only is not None and only != (c * 2 + th):
                    continue
                i = (c * 2 + th) % 2
                bg, bu = i, 2 + i
                toks = slice(th * 512, (th + 1) * 512)
                xbufs = XT.bufs[th * 4:(th + 1) * 4]
                for (wh, wa, bk) in ((wgh, wgs, bg), (wuh, wus, bu)):
                    def mm(e, wh=wh, bk=bk, c=c, toks=toks, s=s):
                        ins = None
                        for kc in range(KC):
                            ins = e.matmul(K.ps[bk][:, :], wh[:, s, kc, c * 128:(c + 1) * 128], XTh[:, kc, toks],
                                           start=(kc == 0), stop=(kc == KC - 1))
                        return ins
                    P.op("pe", mm, reads=xbufs + [wa.bufs[s]], writes=[K.psb[bk]], tag="upgate mm")
                P.op("act", lambda e, i=i, bg=bg: e.activation(out=sth[:, i, :], in_=K.ps[bg][:, :], func=AF.Silu),
                     reads=[K.psb[bg]], writes=[stmp.bufs[i]], tag="silu")
                P.op("dve", lambda e, i=i, bu=bu, c=c, toks=toks, s=s: e.tensor_tensor(
                    out=hTh[:, s, c, toks], in0=K.ps[bu][:, :], in1=sth[:, i, :], op=ALU.mult),
                    reads=[K.psb[bu], stmp.bufs[i]], writes=[hT.bufs[s]], tag="hmul")

    dcnt = [0]

    def down(g, ln_after=False, part=None):
        s = g % 2
        for t in range(NT):
            if part is not None and t // 2 != part:
                continue
            if ln_after and t > 0:
                ln_tile(P, acch, acc.bufs[(t - 1) * 4:t * 4], t - 1, gbh, gb, lsth, st, (t - 1) % 2, outs)
            for db in range(4):
                bk = 4 + dcnt[0] % 4
                dcnt[0] += 1

                def mm(e, bk=bk, t=t, db=db, s=s):
                    ins = None
                    for fc in range(2):
                        ins = e.matmul(K.ps[bk][:, :], hTh[:, s, fc, t * 128:(t + 1) * 128],
                                       wdh[:, s, fc, db * 512:(db + 1) * 512], start=(fc == 0), stop=(fc == 1))
                    return ins
                P.op("pe", mm, reads=[hT.bufs[s], wds.bufs[s]], writes=[K.psb[bk]], tag="down mm")
                P.op("dve", lambda e, bk=bk, t=t, db=db: e.scalar_tensor_tensor(
                    out=acch[:, t, db * 512:(db + 1) * 512], in0=K.ps[bk][:, :], scalar=0.5,
                    in1=acch[:, t, db * 512:(db + 1) * 512], op0=ALU.mult, op1=ALU.add),
                    reads=[K.psb[bk]], writes=[acc.bufs[t * 4 + db]], tag="acc add")

    ngr = K.ng_override if hasattr(K, "ng_override") else NG
    load_wgu(0)
    if ngr > 1:
        load_wgu(1)
    load_wd(0)
    for g in range(ngr):
        for blk in range(4):
            upgate(g, only=blk)
            if g > 0:
                down(g - 1, part=blk)
        if g + 2 < ngr:
            load_wgu(g + 2)
        if g + 1 < ngr:
            load_wd(g + 1)
    P.lnbf = P.sbuf("lnbf", [128, 2, D], BF16, xb_off, nbufs=2)
    down(ngr - 1, ln_after=True)
    ln_tile(P, acch, acc.bufs[(NT - 1) * 4:NT * 4], NT - 1, gbh, gb, lsth, st, (NT - 1) % 2, outs)
    P.free(P.lnbf)
    for a in (acc, XT, wgs, wus, wds, hT, stmp, gb, st):
        P.free(a)


def setup_consts(K, identd):
    P = K.P
    a = P.sbuf("ident", [128, 128], BF16, 16384)
    K.ident = a.handle
    K.ident_buf = a.bufs[0]
    P.op("pool", lambda e: e.dma_start(out=K.ident[:, :], in_=identd.ap()[:, :]), reads=[], writes=a.bufs, dma=True,
         tag="ident")
    K.const_top = 16384 + 256


def finish(K, out_bufs):
    K.P.op("sp", lambda e: e.nop(), reads=out_bufs, writes=[], tag="final wait")


NCH_ATT = 12
NCH_RW = 15
NEG = -30000.0
LAMBDA_INIT = 0.2
ATTN_EPS = 1e-5
GN_EPS = 64e-5


def project_fm(K, X1T, wslot, wslot_buf, bank_rot, evac):
    P = K.P
    X1Th = X1T.handle
    for tg in range(4):
        bk = bank_rot()

        def mm(e, bk=bk, tg=tg):
            ins = None
            for kc in range(KC):
                ins = e.matmul(K.ps[bk][:, :], wslot[:, kc, :], X1Th[:, kc, tg * 512:(tg + 1) * 512],
                               start=(kc == 0), stop=(kc == KC - 1))
            return ins
        P.op("pe", mm, reads=X1T.bufs[tg * 4:(tg + 1) * 4] + [wslot_buf], writes=[K.psb[bk]], tag="proj fm")
        evac(tg, bk)


def attention_stage(K, X1T, win_t, bm, ctab, lamv, ng_t, o_dst, o_buf, base):
    P = K.P
    off = base
    QT = P.sbuf("QT", [128, 4, SEQ], BF16, off, nbufs=4); off += 4 * SEQ * 2
    KT = P.sbuf("KT", [128, 4, SEQ], BF16, off, nbufs=4); off += 4 * SEQ * 2
    VA = P.sbuf("VA", [128, 16, 4, 130], BF16, off, nbufs=16); off += 16 * 4 * 130 * 2
    wr = P.sbuf("wring", [128, 4, KC, 128], BF16, off, nbufs=4); off += 4 * KC * 128 * 2
    bms = P.sbuf("bms", [128, 2, 5, 512], F32, off, nbufs=2); off += 2 * 5 * 512 * 4
    ct = P.sbuf("ctab", [128, 64], F32, off); off += 64 * 4
    lm = P.sbuf("lam", [128, 4 * 64 + 16], F32, off); off += (4 * 64 + 16) * 4
    ngs = P.sbuf("ngs", [128, 512], F32, off); off += 512 * 4
    stm = P.sbuf("stmp", [128, 4, 512], F32, off, nbufs=4); off += 4 * 512 * 4
    pT = P.sbuf("pT", [128, 4, 512], BF16, off, nbufs=4); off += 4 * 512 * 2
    Osb = P.sbuf("Osb", [128, 2, 4, 130], F32, off, nbufs=2); off += 2 * 4 * 130 * 4
    ow = P.sbuf("ow", [128, 4, 4, 128], F32, off, nbufs=1); off += 16 * 128 * 4
    osq = P.sbuf("osq", [128, 4, 128], F32, off, nbufs=1); off += 4 * 128 * 4
    sm = P.sbuf("sm", [128, 32], F32, off, nbufs=1); off += 32 * 4
    owb = P.sbuf("owb", [128, 4, 4, 128], BF16, off, nbufs=1); off += 16 * 128 * 2
    owbh = owb.handle
    QTh, KTh, VAh, wrh, bmh, cth, lmh, ngh, stmh, pTh, Osh, owh, osqh, smh = (
        QT.handle, KT.handle, VA.handle, wr.handle, bms.handle, ct.handle, lm.handle, ngs.handle, stm.handle,
        pT.handle, Osb.handle, ow.handle, osq.handle, sm.handle)
    X1Th = X1T.handle

    load_const(K, "ctab", ct, cth[:, :], ctab[:, :])
    load_const(K, "lamv", lm, lmh[:, 0:256], lamv.rearrange("p a d -> p (a d)"))
    load_const(K, "ngt", ngs, ngh[:, :], ng_t[:, :])
    P.op("pool", lambda e: e.memset(VAh[:, :, :, 128:130], 1.0), reads=[], writes=VA.bufs, tag="va ones")

    P.op("dve", lambda e: e.tensor_tensor(out=lmh[:, 0:64], in0=lmh[:, 0:64], in1=lmh[:, 64:128], op=ALU.mult),
         reads=[], writes=lm.bufs, tag="lam1")
    P.op("dve", lambda e: e.tensor_tensor(out=lmh[:, 128:192], in0=lmh[:, 128:192], in1=lmh[:, 192:256], op=ALU.mult),
         reads=[], writes=lm.bufs, tag="lam2")
    P.op("dve", lambda e: e.tensor_reduce(out=lmh[:, 256:257], in_=lmh[:, 0:64], axis=AX.X, op=ALU.add),
         reads=[], writes=lm.bufs, tag="lam3")
    P.op("dve", lambda e: e.tensor_reduce(out=lmh[:, 257:258], in_=lmh[:, 128:192], axis=AX.X, op=ALU.add),
         reads=[], writes=lm.bufs, tag="lam4")
    P.op("act", lambda e: e.activation(out=lmh[:, 258:260], in_=lmh[:, 256:258], func=AF.Exp),
         reads=[], writes=lm.bufs, tag="lam5")
    P.op("dve", lambda e: e.tensor_tensor(out=lmh[:, 260:261], in0=lmh[:, 258:259], in1=lmh[:, 259:260],
                                          op=ALU.subtract), reads=[], writes=lm.bufs, tag="lam6")
    P.op("dve", lambda e: e.tensor_scalar(out=lmh[:, 261:262], in0=lmh[:, 260:261], scalar1=LAMBDA_INIT, scalar2=None,
                                          op0=ALU.add), reads=[], writes=lm.bufs, tag="lam7")
    LAM = lmh[:, 261:262]

    rot = [0]

    def bank_rot():
        rot[0] += 1
        return rot[0] % 4

    def load_w(ci):
        s = ci % 4
        P.op("pool", lambda e, ci=ci, s=s: e.dma_start(out=wrh[:, s, :, :], in_=win_t[ci]),
             reads=[], writes=[wr.bufs[s]], dma=True, tag="win load")
        return s
    for ci in range(min(3, NCH_ATT)):
        load_w(ci)
    for ci in range(NCH_ATT):
        s = ci % 4
        if ci + 3 < NCH_ATT:
            load_w(ci + 3)
        if ci < 8:
            dstT, dbuf = (QTh, QT) if ci < 4 else (KTh, KT)
            a = ci % 4

            def evac(tg, bk, dstT=dstT, dbuf=dbuf, a=a):
                eng = "act" if tg % 2 == 0 else "dve"

                def ev(e, tg=tg, bk=bk):
                    o = dstT[:, a, tg * 512:(tg + 1) * 512]
                    if eng == "act":
                        return e.activation(out=o, in_=K.ps[bk][:, :], func=AF.Copy)
                    return e.tensor_copy(out=o, in_=K.ps[bk][:, :])
                P.op(eng, ev, reads=[K.psb[bk]], writes=[dbuf.bufs[a]], tag="qk evac")
            project_fm(K, X1T, wrh[:, s], wr.bufs[s], bank_rot, evac)
        else:
            a = ci - 8
            for tq in range(4):
                bk = bank_rot()

                def mm(e, bk=bk, tq=tq, s=s):
                    ins = None
                    for tt in range(4):
                        t = tq * 4 + tt
                        for kc in range(KC):
                            ins = e.matmul(K.ps[bk][:, tt * 128:(tt + 1) * 128], X1Th[:, kc, t * 128:(t + 1) * 128],
                                           wrh[:, s, kc, :], start=(kc == 0 and tt == 0), stop=(kc == KC - 1),
                                           skip_group_check=True)
                    return ins
                P.op("pe", mm, reads=X1T.bufs[tq * 4:(tq + 1) * 4] + [wr.bufs[s]], writes=[K.psb[bk]], tag="v proj")
                P.op("act", lambda e, bk=bk, tq=tq, a=a: e.activation(
                    out=VAh[:, tq * 4:(tq + 1) * 4, a, 0:128], in_=K.ps[bk].rearrange("p (t c) -> p t c", t=4),
                    func=AF.Copy), reads=[K.psb[bk]], writes=VA.bufs[tq * 4:(tq + 1) * 4], tag="v evac")

    SCALE = 64 ** -0.5
    LOOK = 2
    tiles = []
    for a in range(4):
        for qg in range(4):
            nkb = 4 * qg + 4
            for m in range(2):
                for kb in range(nkb):
                    tiles.append((a, qg, m, kb, nkb))

    def front(idx):
        a, qg, m, kb, nkb = tiles[idx]
        bs = a % 2
        if qg == 0 and m == 0 and kb == 0:
            P.op("sp", lambda e, a=a, bs=bs: e.dma_start(out=bmh[:, bs, :, :], in_=bm[a].rearrange("r p q -> p r q")),
                 reads=[], writes=[bms.bufs[bs]], dma=True, tag="bm load")
        pr = slice(m * 64, (m + 1) * 64)
        i = idx % 4
        sb = i
        P.op("pe", lambda e, sb=sb, a=a, pr=pr, kb=kb, qg=qg: e.matmul(
            K.ps[sb][:, :], KTh[pr, a, kb * 128:(kb + 1) * 128], QTh[pr, a, qg * 512:(qg + 1) * 512],
            start=True, stop=True), reads=[KT.bufs[a], QT.bufs[a]], writes=[K.psb[sb]], tag="qk mm")
        r = kb - 4 * qg
        var = 4 if r < 0 else r
        P.op("dve", lambda e, sb=sb, i=i, bs=bs, var=var: e.scalar_tensor_tensor(
            out=stmh[:, i, :], in0=K.ps[sb][:, :], scalar=SCALE, in1=bmh[:, bs, var, :],
            op0=ALU.mult, op1=ALU.add), reads=[K.psb[sb], bms.bufs[bs]], writes=[stm.bufs[i]], tag="score bias")
        cidx = a * 16 + ((qg * 512 - kb * 128 + 384) // 128 if r < 0 else 3)
        P.op("act", lambda e, i=i, cidx=cidx: e.activation(
            out=pTh[:, i, :], in_=stmh[:, i, :], func=AF.Exp, bias=cth[:, cidx:cidx + 1], scale=1.0),
            reads=[stm.bufs[i], ct.bufs[0]], writes=[pT.bufs[i]], tag="exp")

    def back(idx):
        a, qg, m, kb, nkb = tiles[idx]
        i = idx % 4
        ob = (4, 5) if m == 0 else (6, 7)

        def pv(e, i=i, kb=kb, a=a, ob=ob, nkb=nkb):
            ins = None
            for qb in range(4):
                bk = ob[qb // 2]
                c0 = (qb % 2) * 130
                ins = e.matmul(K.ps[bk][:, c0:c0 + 130], pTh[:, i, qb * 128:(qb + 1) * 128],
                               VAh[:, kb, a, :], start=(kb == 0 and qb % 2 == 0), stop=(kb == nkb - 1),
                               skip_group_check=True)
            return ins
        P.op("pe", pv, reads=[pT.bufs[i], VA.bufs[kb]], writes=[K.psb[ob[0]], K.psb[ob[1]]], tag="pv mm")
        if kb != nkb - 1:
            return
        for hb in range(2):
            P.op("act", lambda e, m=m, hb=hb, ob=ob: e.activation(
                out=Osh[:, m, hb * 2:(hb + 1) * 2, :],
                in_=K.ps[ob[hb]][:, 0:260].rearrange("p (q c) -> p q c", q=2), func=AF.Copy),
                reads=[K.psb[ob[hb]]], writes=[Osb.bufs[m]], tag="O evac")
        if m == 0:
            return
        P.op("dve", lambda e: e.reciprocal(out=smh[:, 0:4], in_=Osh[:, 0, :, 128:129].rearrange("p q c -> p (q c)")),
             reads=[Osb.bufs[0]], writes=sm.bufs, tag="r1")
        P.op("dve", lambda e: e.reciprocal(out=smh[:, 4:8], in_=Osh[:, 1, :, 128:129].rearrange("p q c -> p (q c)")),
             reads=[Osb.bufs[1]], writes=sm.bufs, tag="r2")
        P.op("dve", lambda e: e.tensor_scalar(out=smh[:, 4:8], in0=smh[:, 4:8], scalar1=LAM, scalar2=None,
                                              op0=ALU.mult), reads=[lm.bufs[0]], writes=sm.bufs, tag="r2lam")
        P.op("dve", lambda e, a=a: e.tensor_tensor(
            out=owh[:, :, a, :], in0=Osh[:, 0, :, 0:128], in1=smh[:, 0:4].unsqueeze(2).to_broadcast([128, 4, 128]),
            op=ALU.mult), reads=[Osb.bufs[0]], writes=ow.bufs, tag="o1")
        P.op("dve", lambda e: e.tensor_tensor(
            out=osqh[:, :, :], in0=Osh[:, 1, :, 0:128], in1=smh[:, 4:8].unsqueeze(2).to_broadcast([128, 4, 128]),
            op=ALU.mult), reads=[Osb.bufs[1]], writes=osq.bufs, tag="o2")
        P.op("dve", lambda e, a=a: e.tensor_tensor(out=owh[:, :, a, :], in0=owh[:, :, a, :], in1=osqh[:, :, :],
                                                   op=ALU.subtract), reads=[], writes=ow.bufs + osq.bufs, tag="o12")
        P.op("pool", lambda e, a=a: e.tensor_tensor(out=osqh[:, :, :], in0=owh[:, :, a, :], in1=owh[:, :, a, :],
                                                    op=ALU.mult), reads=[], writes=ow.bufs + osq.bufs, tag="osq")
        P.op("dve", lambda e: e.tensor_reduce(out=smh[:, 8:12], in_=osqh[:, :, :], axis=AX.X, op=ALU.add),
             reads=[osq.bufs[0]], writes=sm.bufs, tag="ossq")
        P.op("dve", lambda e: e.tensor_scalar(out=smh[:, 8:12], in0=smh[:, 8:12], scalar1=1.0 / 128, scalar2=ATTN_EPS,
                                              op0=ALU.mult, op1=ALU.add), reads=[], writes=sm.bufs, tag="oms")
        P.op("act", lambda e: e.activation(out=smh[:, 12:16], in_=smh[:, 8:12], func=AF.Sqrt),
             reads=[], writes=sm.bufs, tag="orms")
        P.op("dve", lambda e: e.reciprocal(out=smh[:, 16:20], in_=smh[:, 12:16]), reads=[], writes=sm.bufs,
             tag="orr")
        P.op("dve", lambda e, a=a: e.tensor_tensor(
            out=owh[:, :, a, :], in0=owh[:, :, a, :], in1=smh[:, 16:20].unsqueeze(2).to_broadcast([128, 4, 128]),
            op=ALU.mult), reads=[], writes=ow.bufs + sm.bufs, tag="onorm")
        P.op("dve", lambda e, a=a: e.scalar_tensor_tensor(
            out=owbh[:, :, a, :], in0=owh[:, :, a, :], scalar=1.0 - LAMBDA_INIT,
            in1=ngh[:, a * 128:(a + 1) * 128].unsqueeze(1).to_broadcast([128, 4, 128]),
            op0=ALU.mult, op1=ALU.mult), reads=[ngs.bufs[0]] + ow.bufs, writes=owb.bufs, tag="og")
        for qb in range(4):
            t0 = qg * 512 + qb * 128
            P.op("sp", lambda e, qb=qb, t0=t0, a=a: e.dma_start(
                out=o_dst[t0:t0 + 128, a * 128:(a + 1) * 128], in_=owbh[:, qb, a, :]),
                reads=owb.bufs, writes=[o_buf], dma=True, tag="o_attn out")

    for idx in range(len(tiles) + LOOK):
        if idx < len(tiles):
            front(idx)
        if idx >= LOOK:
            back(idx - LOOK)
    for al in (QT, KT, VA, wr, bms, ct, lm, ngs, stm, pT, Osb, ow, osq, sm, owb):
        P.free(al)


def host_tile_ffn_w(w):
    return np.ascontiguousarray(w.reshape(KC, 128, NG, FG).transpose(2, 1, 0, 3))


def host_win_cols(half):
    cols = []
    for blk in range(3):
        for a in range(4):
            head = 4 * half + a
            cols.append(np.arange(blk * 1024 + head * 128, blk * 1024 + head * 128 + 128))
    for blk in range(3):
        for cc in range(4):
            c0 = 3072 + blk * 1024 + (8 * half + 2 * cc) * 64
            cols.append(np.arange(c0, c0 + 128))
    cols.append(np.arange(6144, 6144 + 128))
    cols.append(np.arange(6144 + 128, 6144 + 256))
    cols.append(np.arange(6144 + 256, 6144 + 288))
    return cols


def host_tile_win(w_in, half):
    cols = host_win_cols(half)
    out = np.zeros((len(cols), 128, KC, 128), np.float32)
    for ci, c in enumerate(cols):
        blk = w_in[:, c]
        out[ci, :, :, :len(c)] = blk.reshape(KC, 128, len(c)).transpose(1, 0, 2)
    return out


def host_attn_consts(half):
    bm = np.zeros((4, 5, 128, 512), np.float32)
    ctab = np.zeros((128, 64), np.float32)
    i = np.arange(128)[:, None].astype(np.float64)
    j = np.arange(512)[None, :].astype(np.float64)
    for a in range(4):
        slope = 2.0 ** (-(4 * half + a + 1))
        for r in range(4):
            d = j - i - 128 * r
            bm[a, r] = np.where(d >= 0, -slope * d, NEG)
        bm[a, 4] = -slope * (j - i)
        for idx in range(16):
            ctab[:, a * 16 + idx] = -slope * (idx * 128 - 384)
    return bm, ctab


C0 = float(np.exp(-0.5))
CH = 64
NCHUNK = SEQ // CH


def rwkv_alloc_persist(K, base):
    P = K.P
    R = Ctx()
    off = base
    R.AR = P.sbuf("AR", [128, 4, NCHUNK, 2, CH], BF16, off, nbufs=NCHUNK); off += 4 * NCHUNK * 2 * CH * 2
    R.BK = P.sbuf("BK", [128, 4, NCHUNK, 2, CH], BF16, off, nbufs=NCHUNK); off += 4 * NCHUNK * 2 * CH * 2
    R.VC = P.sbuf("VC", [128, 4, SEQ], BF16, off, nbufs=NCHUNK); off += 4 * SEQ * 2
    R.LW = P.sbuf("LW", [128, SEQ], BF16, off, nbufs=4); off += SEQ * 2
    R.SG = P.sbuf("SG", [128, 2, SEQ], BF16, off, nbufs=4); off += 2 * SEQ * 2
    R.GC = P.sbuf("GC", [128, 4, NCHUNK], F32, off, nbufs=4); off += 4 * NCHUNK * 4
    R.BON = P.sbuf("BON", [64, NCHUNK, 8], F32, off, nbufs=4); off += NCHUNK * 8 * 4
    R.pcol = P.sbuf("pcol", [128, 64], F32, off); off += 256
    R.w2a2 = P.sbuf("w2a2", [128, 512], BF16, off); off += 1024
    R.g2 = P.sbuf("g2", [128, 2, 512], BF16, off); off += 2048
    R.bones = P.sbuf("bones", [128, 128], BF16, off); off += 256
    R.hsel = P.sbuf("hsel", [128, 16], BF16, off); off += 32
    R.top = off
    return R


def rwkv_prep(K, R, X1T, win_t, pcol_d, w2a2_d, g2_d, bones_d, hsel_d, base):
    P = K.P
    off = base
    wr = P.sbuf("wring2", [128, 3, KC, 128], BF16, off, nbufs=3); off += 3 * KC * 128 * 2
    ones = P.sbuf("ones", [128, 512], F32, off); off += 2048
    names = ["rm", "km", "sg", "av", "kk", "nrm", "ka", "kp", "cum", "cx", "dd"]
    T = {}
    for n in names:
        T[n] = P.sbuf(n, [128, 512], F32, off); off += 2048
    pre = P.sbuf("pre", [128, 3, 520], F32, off, nbufs=3); off += 3 * 520 * 4
    sq = P.sbuf("sq", [128, 512], BF16, off); off += 1024
    rb = P.sbuf("rb", [128, 512], BF16, off); off += 1024
    cb = P.sbuf("cb", [128, 16], F32, off); off += 64
    h = {n: T[n].handle for n in names}
    b = {n: T[n].bufs[0] for n in names}
    wrh, preh, sqh, rbh, cbh, onesh = wr.handle, pre.handle, sq.handle, rb.handle, cb.handle, ones.handle
    ARh, BKh, VCh, LWh, SGh, GCh, BONh, pc, w2h, g2h, boh, hsh = (
        R.AR.handle, R.BK.handle, R.VC.handle, R.LW.handle, R.SG.handle, R.GC.handle, R.BON.handle, R.pcol.handle,
        R.w2a2.handle, R.g2.handle, R.bones.handle, R.hsel.handle)
    X1Th = X1T.handle

    load_const(K, "pcol", R.pcol, pc[:, :], pcol_d[:, :])
    load_const(K, "w2a2", R.w2a2, w2h[:, :], w2a2_d[:, :], q="pool")
    load_const(K, "g2", R.g2, g2h[:, :, :], g2_d[:, :, :], q="pool")
    load_const(K, "bones", R.bones, boh[:, :], bones_d[:, :], q="pool")
    load_const(K, "hsel", R.hsel, hsh[:, :], hsel_d[:, :], q="pool")
    P.op("pool", lambda e: e.memset(onesh[:, :], 1.0), reads=[], writes=ones.bufs, tag="ones")
    P.op("pool", lambda e: e.memset(SGh[:, 1, :], 0.0), reads=[], writes=R.SG.bufs, tag="sg2 zero")

    rot = [0]

    def bank_rot():
        rot[0] += 1
        return rot[0] % 8

    def load_w(ci, s):
        P.op("pool", lambda e, ci=ci, s=s: e.dma_start(out=wrh[:, s, :, :], in_=win_t[NCH_ATT + ci]),
             reads=[], writes=[wr.bufs[s]], dma=True, tag="win2 load")

    def proj_mix(ci, s, st, tq, out_fn):
        bk = bank_rot()

        def mm(e, bk=bk, tq=tq, s=s):
            ins = None
            for kc in range(KC):
                ins = e.matmul(K.ps[bk][:, :], wrh[:, s, kc, :], X1Th[:, kc, tq * 512:(tq + 1) * 512],
                               start=(kc == 0), stop=(kc == KC - 1))
            return ins
        P.op("pe", mm, reads=X1T.bufs[tq * 4:(tq + 1) * 4] + [wr.bufs[s]], writes=[K.psb[bk]], tag="proj rw")
        if tq == 0:
            P.op("pool", lambda e, st=st: e.memset(preh[:, st, 0:1], 0.0), reads=[], writes=[pre.bufs[st]], tag="carry0")
        P.op("act", lambda e, bk=bk, st=st: e.activation(out=preh[:, st, 1:513], in_=K.ps[bk][:, :], func=AF.Copy),
             reads=[K.psb[bk]], writes=[pre.bufs[st]], tag="pre evac")
        P.op("dve", lambda e, st=st: e.tensor_tensor(out=h["dd"][:, :], in0=preh[:, st, 0:512], in1=preh[:, st, 1:513],
                                                     op=ALU.subtract), reads=[pre.bufs[st]], writes=[b["dd"]], tag="mix d")
        out_fn(preh[:, st, 1:513], pre.bufs[st])
        if tq < 3:
            P.op("act", lambda e, st=st: e.activation(out=preh[:, st, 0:1], in_=preh[:, st, 512:513], func=AF.Copy),
                 reads=[], writes=[pre.bufs[st]], tag="carry")

    def mixed_to(out_ap, out_bufs, mucol, eng="dve"):
        def f(pre1, prebuf):
            P.op("dve", lambda e: e.scalar_tensor_tensor(out=out_ap, in0=h["dd"][:, :], scalar=pc[:, mucol:mucol + 1],
                                                         in1=pre1, op0=ALU.mult, op1=ALU.add),
                 reads=[b["dd"], prebuf, R.pcol.bufs[0]], writes=out_bufs, tag="mix out")
        return f

    for li, ci in enumerate((12, 13, 14)):
        load_w(ci, li)
    for tq in range(4):
        tsl = slice(tq * 512, (tq + 1) * 512)
        proj_mix(12, 0, 0, tq, mixed_to(h["rm"][:, :], [b["rm"]], 12))
        P.op("act", lambda e, tsl=tsl: e.activation(out=LWh[0:64, tsl], in_=h["rm"][0:64, :], func=AF.Tanh),
             reads=[b["rm"]], writes=[R.LW.bufs[tq]], tag="tanh wd")
        P.op("dve", lambda e, tsl=tsl: e.tensor_copy(out=LWh[64:128, tsl], in_=h["rm"][64:128, :]),
             reads=[b["rm"]], writes=[R.LW.bufs[tq]], tag="copy ad")
        proj_mix(13, 1, 1, tq, mixed_to(h["km"][:, :], [b["km"]], 13))
        P.op("act", lambda e, tsl=tsl: e.activation(out=SGh[:, 0, tsl], in_=h["km"][:, :], func=AF.Sigmoid),
             reads=[b["km"]], writes=[R.SG.bufs[tq]], tag="sig gd")
        proj_mix(14, 2, 2, tq, mixed_to(h["sg"][:, :], [b["sg"]], 14))
        P.op("act", lambda e, tsl=tsl: e.activation(out=SGh[0:32, 1, tsl], in_=h["sg"][0:32, :], func=AF.Sigmoid),
             reads=[b["sg"]], writes=[R.SG.bufs[tq]], tag="sig gd2")

    for cc in range(4):
        for st in range(3):
            load_w(st * 4 + cc, st)
        csl = slice(cc * 128, (cc + 1) * 128)
        for tq in range(4):
            tsl = slice(tq * 512, (tq + 1) * 512)
            jsl = slice(tq * 8, (tq + 1) * 8)
            cbufs = R.AR.bufs[tq * 8:(tq + 1) * 8]
            kbufs = R.BK.bufs[tq * 8:(tq + 1) * 8]
            proj_mix(cc, 0, 0, tq, mixed_to(h["rm"][:, :], [b["rm"]], cc))
            proj_mix(4 + cc, 1, 1, tq, mixed_to(h["km"][:, :], [b["km"]], 4 + cc))
            proj_mix(8 + cc, 2, 2, tq, mixed_to(VCh[:, cc, tsl], R.VC.bufs[tq * 8:(tq + 1) * 8], 8 + cc))
            bz, ba = bank_rot(), bank_rot()
            P.op("pe", lambda e, bz=bz, csl=csl, tsl=tsl: e.matmul(K.ps[bz][:, :], w2h[0:64, csl], LWh[0:64, tsl],
                                                                   start=True, stop=True),
                 reads=[R.w2a2.bufs[0], R.LW.bufs[tq]], writes=[K.psb[bz]], tag="w lora")
            P.op("pe", lambda e, ba=ba, csl=csl, tsl=tsl: e.matmul(K.ps[ba][:, :], w2h[64:128, csl], LWh[64:128, tsl],
                                                                   start=True, stop=True),
                 reads=[R.w2a2.bufs[0], R.LW.bufs[tq]], writes=[K.psb[ba]], tag="a lora")
            P.op("act", lambda e, bz=bz, cc=cc: e.activation(out=h["sg"][:, :], in_=K.ps[bz][:, :], func=AF.Sigmoid,
                                                            bias=pc[:, 15 + cc:16 + cc]),
                 reads=[K.psb[bz], R.pcol.bufs[0]], writes=[b["sg"]], tag="sig w")
            P.op("act", lambda e, ba=ba, cc=cc: e.activation(out=h["av"][:, :], in_=K.ps[ba][:, :], func=AF.Sigmoid,
                                                            bias=pc[:, 19 + cc:20 + cc]),
                 reads=[K.psb[ba], R.pcol.bufs[0]], writes=[b["av"]], tag="sig a")
            P.op("dve", lambda e, cc=cc: e.tensor_scalar(out=h["kk"][:, :], in0=h["km"][:, :],
                                                        scalar1=pc[:, 23 + cc:24 + cc], scalar2=None, op0=ALU.mult),
                 reads=[b["km"], R.pcol.bufs[0]], writes=[b["kk"]], tag="kkraw")
            P.op("act", lambda e: e.activation(out=sqh[:, :], in_=h["kk"][:, :], func=AF.Square),
                 reads=[b["kk"]], writes=sq.bufs, tag="kk sq")
            bn_ = bank_rot()
            P.op("pe", lambda e, bn_=bn_: e.matmul(K.ps[bn_][:, :], boh[:, :], sqh[:, :], start=True, stop=True),
                 reads=[R.bones.bufs[0], sq.bufs[0]], writes=[K.psb[bn_]], tag="ssq mm")
            P.op("act", lambda e, bn_=bn_: e.activation(out=h["nrm"][:, :], in_=K.ps[bn_][:, :], func=AF.Sqrt),
                 reads=[K.psb[bn_]], writes=[b["nrm"]], tag="nrm sqrt")
            P.op("dve", lambda e: e.tensor_scalar(out=h["nrm"][:, :], in0=h["nrm"][:, :], scalar1=1e-12, scalar2=None,
                                                  op0=ALU.max), reads=[], writes=[b["nrm"]], tag="nrm max")
            P.op("dve", lambda e: e.reciprocal(out=h["nrm"][:, :], in_=h["nrm"][:, :]), reads=[], writes=[b["nrm"]],
                 tag="nrm rcp")
            P.op("dve", lambda e: e.tensor_tensor(out=h["kk"][:, :], in0=h["kk"][:, :], in1=h["nrm"][:, :], op=ALU.mult),
                 reads=[b["nrm"]], writes=[b["kk"]], tag="kk")
            P.op("pool", lambda e: e.tensor_tensor(out=h["ka"][:, :], in0=h["kk"][:, :], in1=h["av"][:, :], op=ALU.mult),
                 reads=[b["kk"], b["av"]], writes=[b["ka"]], tag="ka")
            P.op("dve", lambda e, cc=cc: e.tensor_scalar(out=h["kp"][:, :], in0=h["av"][:, :], scalar1=-1.0,
                                                        scalar2=pc[:, 27 + cc:28 + cc], op0=ALU.add, op1=ALU.mult),
                 reads=[b["av"], R.pcol.bufs[0]], writes=[b["kp"]], tag="kp1")
            P.op("dve", lambda e: e.scalar_tensor_tensor(out=h["kp"][:, :], in0=h["kp"][:, :], scalar=1.0,
                                                         in1=h["km"][:, :], op0=ALU.add, op1=ALU.mult),
                 reads=[b["km"]], writes=[b["kp"]], tag="kp2")
            P.op("dve", lambda e, cc=cc: e.scalar_tensor_tensor(out=rbh[:, :], in0=h["rm"][:, :],
                                                               scalar=pc[:, 31 + cc:32 + cc], in1=h["kp"][:, :],
                                                               op0=ALU.mult, op1=ALU.mult),
                 reads=[b["rm"], b["kp"], R.pcol.bufs[0]], writes=rb.bufs, tag="rb")
            bb_ = bank_rot()

            def bon_mm(e, bb_=bb_):
                ins = None
                for jj in range(8):
                    ins = e.matmul(K.ps[bb_][0:64, jj * 2:jj * 2 + 2], rbh[:, jj * 64:(jj + 1) * 64], hsh[:, 0:2],
                                   start=(jj == 0), stop=True, skip_group_check=True)
                return ins
            P.op("pe", bon_mm, reads=[rb.bufs[0], R.hsel.bufs[0]], writes=[K.psb[bb_]], tag="bonus mm")
            P.op("dve", lambda e, bb_=bb_, jsl=jsl, cc=cc: e.tensor_copy(
                out=BONh[:, jsl, cc * 2:cc * 2 + 2], in_=K.ps[bb_][0:64, 0:16].rearrange("p (j c) -> p j c", c=2)),
                reads=[K.psb[bb_]], writes=[R.BON.bufs[tq]], tag="bonus evac")
            if tq == 0:
                P.op("dve", lambda e: e.tensor_tensor_scan(out=h["cum"][:, :], data0=onesh[:, :], data1=h["sg"][:, :],
                                                           initial=0.0, op0=ALU.mult, op1=ALU.add),
                     reads=[ones.bufs[0], b["sg"]], writes=[b["cum"]], tag="scan")
                P.op("pool", lambda e: e.memset(cbh[:, 0:1], 0.0), reads=[], writes=cb.bufs, tag="cb0")
            else:
                P.op("dve", lambda e: e.tensor_tensor_scan(out=h["cum"][:, :], data0=onesh[:, :], data1=h["sg"][:, :],
                                                           initial=cbh[:, 8:9], op0=ALU.mult, op1=ALU.add),
                     reads=[ones.bufs[0], b["sg"], cb.bufs[0]], writes=[b["cum"]], tag="scan")
                P.op("act", lambda e: e.activation(out=cbh[:, 0:1], in_=cbh[:, 8:9], func=AF.Copy),
                     reads=[], writes=cb.bufs, tag="cb carry")
            cum3 = h["cum"].rearrange("p (j t) -> p j t", t=CH)
            P.op("act", lambda e, cum3=cum3: e.activation(out=cbh[:, 1:9], in_=cum3[:, :, CH - 1], func=AF.Copy),
                 reads=[b["cum"]], writes=cb.bufs, tag="cb ends")
            P.op("dve", lambda e, cum3=cum3: e.tensor_tensor(out=cum3, in0=cum3,
                                                             in1=cbh[:, 0:8].unsqueeze(2).to_broadcast([128, 8, CH]),
                                                             op=ALU.subtract),
                 reads=[cb.bufs[0]], writes=[b["cum"]], tag="cumrel")
            P.op("pool", lambda e: e.tensor_tensor(out=h["cx"][:, :], in0=h["cum"][:, :], in1=h["sg"][:, :],
                                                   op=ALU.subtract), reads=[b["cum"], b["sg"]], writes=[b["cx"]],
                 tag="cumex")
            P.op("act", lambda e, cum3=cum3, cc=cc, jsl=jsl: e.activation(out=GCh[:, cc, jsl], in_=cum3[:, :, CH - 1],
                                                                         func=AF.Exp, scale=-C0),
                 reads=[b["cum"]], writes=[R.GC.bufs[cc]], tag="gammaC")
            P.op("act", lambda e: e.activation(out=h["av"][:, :], in_=h["cum"][:, :], func=AF.Exp, scale=-C0),
                 reads=[b["cum"]], writes=[b["av"]], tag="Eg")
            P.op("act", lambda e: e.activation(out=h["km"][:, :], in_=h["cx"][:, :], func=AF.Exp, scale=-C0),
                 reads=[b["cx"]], writes=[b["km"]], tag="Egx")
            P.op("act", lambda e: e.activation(out=h["sg"][:, :], in_=h["cum"][:, :], func=AF.Exp, scale=C0),
                 reads=[b["cum"]], writes=[b["sg"]], tag="Ei")

            def v3(x):
                return x.rearrange("p (j t) -> p j t", t=CH)
            P.op("dve", lambda e, cc=cc, jsl=jsl: e.scalar_tensor_tensor(
                out=ARh[:, cc, jsl, 0, :], in0=v3(h["kk"]), scalar=-1.0, in1=v3(h["km"]), op0=ALU.mult, op1=ALU.mult),
                reads=[b["kk"], b["km"]], writes=cbufs, tag="A~")
            P.op("pool", lambda e, cc=cc, jsl=jsl: e.tensor_tensor(
                out=ARh[:, cc, jsl, 1, :], in0=v3(h["rm"]), in1=v3(h["av"]), op=ALU.mult),
                reads=[b["rm"], b["av"]], writes=cbufs, tag="R~")
            P.op("dve", lambda e, cc=cc, jsl=jsl: e.tensor_tensor(
                out=BKh[:, cc, jsl, 0, :], in0=v3(h["ka"]), in1=v3(h["sg"]), op=ALU.mult),
                reads=[b["ka"], b["sg"]], writes=kbufs, tag="B~")
            P.op("pool", lambda e, cc=cc, jsl=jsl: e.tensor_tensor(
                out=BKh[:, cc, jsl, 1, :], in0=v3(h["kp"]), in1=v3(h["sg"]), op=ALU.mult),
                reads=[b["kp"], b["sg"]], writes=kbufs, tag="K~")
    for al in [wr, ones, pre, sq, rb, cb] + [T[n] for n in names]:
        P.free(al)


def rwkv_chunks(K, R, masks_d, gnb_d, o_dst, o_buf, base):
    P = K.P
    off = base
    mk = P.sbuf("masks", [64, 4, 128], F32, off); off += 4 * 128 * 4
    gnb = P.sbuf("gnb", [64, 2, 512], F32, off); off += 2 * 512 * 4
    M1 = P.sbuf("M1", [64, 2, 8, 128], BF16, off, nbufs=2); off += 2 * 8 * 128 * 2
    M2 = P.sbuf("M2", [64, 2, 8, 128], BF16, off, nbufs=2); off += 2 * 8 * 128 * 2
    M3 = P.sbuf("M3", [64, 2, 8, 64], BF16, off, nbufs=2); off += 2 * 8 * 64 * 2
    NL = P.sbuf("NL", [64, 2, 2, 8, 64], BF16, off, nbufs=4); off += 2 * 2 * 8 * 64 * 2
    PP = P.sbuf("PP", [64, 2, 8, 64], BF16, off, nbufs=2); off += 2 * 8 * 64 * 2
    BKh_ = P.sbuf("BKhat", [128, 2, 4, 2, CH], BF16, off, nbufs=2); off += 2 * 4 * 2 * CH * 2
    TOK = P.sbuf("TOK", [64, 2, 4, 512], BF16, off, nbufs=2); off += 2 * 4 * 512 * 2
    WTs = P.sbuf("WTs", [128, 2, 4, CH], BF16, off, nbufs=2); off += 2 * 4 * CH * 2
    MAK = P.sbuf("MAK", [64, 2, 8, 64], BF16, off, nbufs=2); off += 2 * 8 * 64 * 2
    Ub = P.sbuf("Ub", [64, 2, 8, 64], BF16, off, nbufs=2); off += 2 * 8 * 64 * 2
    Hf = P.sbuf("Hf", [128, 4, 64], F32, off); off += 4 * 64 * 4
    Hb = P.sbuf("Hb", [128, 4, 64], BF16, off); off += 4 * 64 * 2
    Ys = P.sbuf("Ys", [64, 2, 512], F32, off, nbufs=2); off += 2 * 512 * 4
    Yq = P.sbuf("Yq", [64, 512], F32, off); off += 512 * 4
    Gs = P.sbuf("Gs", [64, 512], F32, off); off += 512 * 4
    st = P.sbuf("gst", [64, 64], F32, off); off += 64 * 4
    Yo = P.sbuf("Yo", [64, 2, 512], BF16, off, nbufs=2); off += 2 * 512 * 2
    Yoh = Yo.handle
    mkh, gnh, M1h, M2h, M3h, NLh, PPh, BHh, TOKh, WTh, MAKh, Ubh, Hfh, Hbh, Ysh, Yqh, Gsh, sth = (
        mk.handle, gnb.handle, M1.handle, M2.handle, M3.handle, NL.handle, PP.handle, BKh_.handle, TOK.handle,
        WTs.handle, MAK.handle, Ub.handle, Hf.handle, Hb.handle, Ys.handle, Yq.handle, Gs.handle, st.handle)
    ARh, BKh, VCh, SGh, GCh, BONh, g2h = (R.AR.handle, R.BK.handle, R.VC.handle, R.SG.handle, R.GC.handle,
                                         R.BON.handle, R.g2.handle)
    ident = K.ident
    load_const(K, "masks", mk, mkh[:, :, :], masks_d[:, 0:4, :])
    load_const(K, "gnb", gnb, gnh[:, :, :], gnb_d[:, :, :])
    P.op("pool", lambda e: e.memset(Hfh[:, :, :], 0.0), reads=[], writes=Hf.bufs, tag="H0")
    P.op("pool", lambda e: e.memset(Hbh[:, :, :], 0.0), reads=[], writes=Hb.bufs, tag="H0b")

    rot = [0]

    def nb():
        rot[0] += 1
        return rot[0] % 8

    def pr(hh):
        return slice(64 * hh, 64 * hh + 64)

    def ps3(bk, n, w):
        return K.ps[bk][0:64, 0:n * w].rearrange("p (n w) -> p n w", w=w)

    for j in range(getattr(K, 'nchunk_override', NCHUNK)):
        s = j % 2
        ab, kb_, vb = R.AR.bufs[j], R.BK.bufs[j], R.VC.bufs[j]
        tq = j // 8
        for hh in range(2):
            b1, b2, b3 = nb(), nb(), nb()

            def mm1(e, b1=b1, hh=hh, j=j):
                ins = None
                for cc in range(4):
                    ins = e.matmul(K.ps[b1][0:64, cc * 128:(cc + 1) * 128], ARh[pr(hh), cc, j, 0, :],
                                   BKh[pr(hh), cc, j, :, :].rearrange("p a t -> p (a t)"), start=(cc == 0), stop=True,
                                   skip_group_check=True)
                return ins
            P.op("pe", mm1, reads=[ab, kb_], writes=[K.psb[b1]], tag="SA mm")
            P.op("dve", lambda e, b1=b1, hh=hh, s=s: e.tensor_tensor(
                out=M1h[:, s, hh * 4:(hh + 1) * 4, :], in0=ps3(b1, 4, 128),
                in1=mkh[:, 0:1, :].to_broadcast([64, 4, 128]), op=ALU.mult),
                reads=[K.psb[b1], mk.bufs[0]], writes=[M1.bufs[s]], tag="M1 evac")

            def mm2(e, b2=b2, hh=hh, j=j):
                ins = None
                for cc in range(4):
                    ins = e.matmul(K.ps[b2][0:64, cc * 128:(cc + 1) * 128], BKh[pr(hh), cc, j, 0, :],
                                   ARh[pr(hh), cc, j, :, :].rearrange("p a t -> p (a t)"), start=(cc == 0), stop=True,
                                   skip_group_check=True)
                return ins
            P.op("pe", mm2, reads=[ab, kb_], writes=[K.psb[b2]], tag="SB mm")
            P.op("dve", lambda e, b2=b2, hh=hh, s=s: e.tensor_tensor(
                out=M2h[:, s, hh * 4:(hh + 1) * 4, :], in0=ps3(b2, 4, 128),
                in1=mkh[:, 1:2, :].to_broadcast([64, 4, 128]), op=ALU.mult),
                reads=[K.psb[b2], mk.bufs[0]], writes=[M2.bufs[s]], tag="M2 evac")

            def mm3(e, b3=b3, hh=hh, j=j):
                ins = None
                for cc in range(4):
                    ins = e.matmul(K.ps[b3][0:64, cc * 64:(cc + 1) * 64], BKh[pr(hh), cc, j, 1, :],
                                   ARh[pr(hh), cc, j, 1, :], start=(cc == 0), stop=True, skip_group_check=True)
                return ins
            P.op("pe", mm3, reads=[ab, kb_], writes=[K.psb[b3]], tag="SK mm")
            P.op("dve", lambda e, b3=b3, hh=hh, s=s: e.tensor_tensor(
                out=M3h[:, s, hh * 4:(hh + 1) * 4, :], in0=ps3(b3, 4, 64),
                in1=mkh[:, 2:3, 0:64].to_broadcast([64, 4, 64]), op=ALU.mult),
                reads=[K.psb[b3], mk.bufs[0]], writes=[M3.bufs[s]], tag="M3 evac")

        if getattr(K, 'chunk_cut', 99) <= 1:
            continue
        P.op("pool", lambda e, s=s: e.tensor_tensor(out=PPh[:, 0, :, :], in0=M2h[:, s, :, 0:64],
                                                    in1=mkh[:, 3:4, 0:64].to_broadcast([64, 8, 64]), op=ALU.add),
             reads=[M2.bufs[s], mk.bufs[0]], writes=[PP.bufs[0]], tag="P0")
        Ncur = lambda hd, s=s: M2h[:, s, hd, 0:64]
        Lcur = lambda hd, s=s: M1h[:, s, hd, 0:64]
        ncur_buf, lcur_buf = M2.bufs[s], M1.bufs[s]
        pc_ = 0
        for lev in range(1, 6):
            sl = lev % 2
            bl = nb()
            bn2 = nb() if lev < 5 else None

            def mmL(e, bl=bl, Ncur=Ncur, Lcur=Lcur):
                ins = None
                for hd in range(8):
                    ins = e.matmul(K.ps[bl][0:64, hd * 64:(hd + 1) * 64], Ncur(hd), Lcur(hd), start=(hd == 0), stop=True,
                                   skip_group_check=True)
                return ins
            P.op("pe", mmL, reads=[ncur_buf, lcur_buf], writes=[K.psb[bl]], tag="L sq")
            P.op("act", lambda e, bl=bl, sl=sl: e.activation(out=NLh[:, sl, 1, :, :], in_=ps3(bl, 8, 64), func=AF.Copy),
                 reads=[K.psb[bl]], writes=[NL.bufs[sl * 2 + 1]], tag="L evac")
            if lev < 5:
                def mmN(e, bn2=bn2, Ncur=Ncur, Lcur=Lcur):
                    ins = None
                    for hd in range(8):
                        ins = e.matmul(K.ps[bn2][0:64, hd * 64:(hd + 1) * 64], Lcur(hd), Ncur(hd), start=(hd == 0),
                                       stop=True, skip_group_check=True)
                    return ins
                P.op("pe", mmN, reads=[ncur_buf, lcur_buf], writes=[K.psb[bn2]], tag="N sq")
                P.op("dve", lambda e, bn2=bn2, sl=sl: e.tensor_copy(out=NLh[:, sl, 0, :, :], in_=ps3(bn2, 8, 64)),
                     reads=[K.psb[bn2]], writes=[NL.bufs[sl * 2 + 0]], tag="N evac")
            bp = nb()

            def mmP(e, bp=bp, sl=sl, pc_=pc_):
                ins = None
                for hd in range(8):
                    ins = e.matmul(K.ps[bp][0:64, hd * 64:(hd + 1) * 64], NLh[:, sl, 1, hd, :], PPh[:, pc_, hd, :],
                                   start=(hd == 0), stop=True, skip_group_check=True)
                return ins
            P.op("pe", mmP, reads=[NL.bufs[sl * 2 + 1], PP.bufs[pc_]], writes=[K.psb[bp]], tag="P mm")
            pn = 1 - pc_
            P.op("dve", lambda e, bp=bp, pc_=pc_, pn=pn: e.tensor_tensor(out=PPh[:, pn, :, :], in0=ps3(bp, 8, 64),
                                                                        in1=PPh[:, pc_, :, :], op=ALU.add),
                 reads=[K.psb[bp], PP.bufs[pc_]], writes=[PP.bufs[pn]], tag="P add")
            pc_ = pn
            Ncur = lambda hd, sl=sl: NLh[:, sl, 0, hd, :]
            Lcur = lambda hd, sl=sl: NLh[:, sl, 1, hd, :]
            ncur_buf, lcur_buf = NL.bufs[sl * 2 + 0], NL.bufs[sl * 2 + 1]
        TT = lambda hd, pc_=pc_: PPh[:, pc_, hd, :]
        tt_buf = PP.bufs[pc_]

        if getattr(K, 'chunk_cut', 99) <= 2:
            continue
        P.op("pool", lambda e, s=s, j=j: e.tensor_tensor(
            out=BHh[:, s, :, :, :], in0=BKh[:, :, j, :, :],
            in1=GCh[:, :, j:j + 1].unsqueeze(3).to_broadcast([128, 4, 2, CH]), op=ALU.mult),
            reads=[kb_, R.GC.bufs[0], R.GC.bufs[1], R.GC.bufs[2], R.GC.bufs[3]], writes=[BKh_.bufs[s]], tag="BKhat")
        for g2_ in range(2):
            bt = nb()
            ptb = K.ps[bt].bitcast(BF16)

            def trs(e, ptb=ptb, g2_=g2_, s=s, j=j):
                ins = None
                for kk_ in range(2):
                    kind = g2_ * 2 + kk_
                    for cc in range(4):
                        if kind == 0:
                            src = ARh[:, cc, j, 0, :]
                        elif kind == 1:
                            src = BHh[:, s, cc, 0, :]
                        elif kind == 2:
                            src = BHh[:, s, cc, 1, :]
                        else:
                            src = VCh[:, cc, j * CH:(j + 1) * CH]
                        ins = e.transpose(ptb[0:64, kk_ * 512 + cc * 128: kk_ * 512 + (cc + 1) * 128], src, ident[:, :])
                return ins
            P.op("pe", trs, reads=[ab, BKh_.bufs[s], vb, K.ident_buf], writes=[K.psb[bt]], tag="tok transposes")
            eng = "act" if g2_ == 0 else "dve"

            def tev(e, ptb=ptb, g2_=g2_, s=s, eng=eng):
                o = TOKh[:, s, g2_ * 2:(g2_ + 1) * 2, :]
                i = ptb[0:64, :].rearrange("p (k c) -> p k c", k=2)
                if eng == "act":
                    return e.activation(out=o, in_=i, func=AF.Copy)
                return e.tensor_copy(out=o, in_=i)
            P.op(eng, tev, reads=[K.psb[bt]], writes=[TOK.bufs[s]], tag="tok evac")

        if getattr(K, 'chunk_cut', 99) <= 3:
            continue
        bw = nb()

        def mmW(e, bw=bw, s=s, TT=TT):
            ins = None
            for hp in range(8):
                cc = hp % 4
                ins = e.matmul(K.ps[bw][:, hp * 64:(hp + 1) * 64], TOKh[:, s, 0, cc * 128:(cc + 1) * 128], TT(hp),
                               start=(hp == 0), stop=True, skip_group_check=True)
            return ins
        P.op("pe", mmW, reads=[TOK.bufs[s], tt_buf], writes=[K.psb[bw]], tag="WT mm")
        for hh in range(2):
            eng = "act" if hh == 0 else "dve"

            def wev(e, bw=bw, hh=hh, s=s, eng=eng):
                o = WTh[pr(hh), s, :, :]
                i = K.ps[bw][pr(hh), hh * 256:(hh + 1) * 256].rearrange("p (c t) -> p c t", c=4)
                if eng == "act":
                    return e.activation(out=o, in_=i, func=AF.Copy)
                return e.tensor_copy(out=o, in_=i)
            P.op(eng, wev, reads=[K.psb[bw]], writes=[WTs.bufs[s]], tag="WT evac")
        bm_ = nb()

        def mmM(e, bm_=bm_, s=s, TT=TT):
            ins = None
            for hp in range(8):
                ins = e.matmul(K.ps[bm_][0:64, hp * 64:(hp + 1) * 64], M1h[:, s, hp, 64:128], TT(hp),
                               start=(hp == 0), stop=True, skip_group_check=True)
            return ins
        P.op("pe", mmM, reads=[M1.bufs[s], tt_buf], writes=[K.psb[bm_]], tag="MAK mm")
        P.op("act", lambda e, bm_=bm_, s=s: e.activation(out=MAKh[:, s, :, :], in_=ps3(bm_, 8, 64), func=AF.Copy),
             reads=[K.psb[bm_]], writes=[MAK.bufs[s]], tag="MAK evac")

        if getattr(K, 'chunk_cut', 99) <= 4:
            continue
        Vt = lambda hp, s=s: TOKh[:, s, 3, (hp % 4) * 128 + (hp // 4) * 64:(hp % 4) * 128 + (hp // 4) * 64 + 64]
        for hh in range(2):
            bu = nb()

            def mmU1(e, bu=bu, hh=hh, s=s, Vt=Vt):
                ins = None
                for cc in range(4):
                    hp = hh * 4 + cc
                    ins = e.matmul(K.ps[bu][0:64, cc * 64:(cc + 1) * 64], MAKh[:, s, hp, :], Vt(hp), start=(cc == 0),
                                   stop=False, skip_group_check=True)
                return ins
            P.op("pe", mmU1, reads=[MAK.bufs[s], TOK.bufs[s]], writes=[K.psb[bu]], tag="U mm1")

            def mmU2(e, bu=bu, hh=hh, s=s):
                ins = None
                for cc in range(4):
                    ins = e.matmul(K.ps[bu][0:64, cc * 64:(cc + 1) * 64], WTh[pr(hh), s, cc, :], Hbh[pr(hh), cc, :],
                                   start=False, stop=True, skip_group_check=True)
                return ins
            P.op("pe", mmU2, reads=[WTs.bufs[s], Hb.bufs[0]], writes=[K.psb[bu]], tag="U mm2", pe_sync=True)
            P.op("act", lambda e, bu=bu, hh=hh, s=s: e.activation(out=Ubh[:, s, hh * 4:(hh + 1) * 4, :],
                                                                 in_=ps3(bu, 4, 64), func=AF.Copy),
                 reads=[K.psb[bu]], writes=[Ub.bufs[s]], tag="U evac")
        for hh in range(2):
            by = nb()

            def mmY1(e, by=by, hh=hh, s=s, Vt=Vt):
                ins = None
                for cc in range(4):
                    hp = hh * 4 + cc
                    o = K.ps[by][0:64, cc * 64:(cc + 1) * 64]
                    e.matmul(o, M2h[:, s, hp, 64:128], Ubh[:, s, hp, :], start=(cc == 0), stop=False,
                             skip_group_check=True)
                    ins = e.matmul(o, M3h[:, s, hp, :], Vt(hp), start=False, stop=False, skip_group_check=True)
                return ins
            P.op("pe", mmY1, reads=[M2.bufs[s], Ub.bufs[s], M3.bufs[s], TOK.bufs[s]], writes=[K.psb[by]], tag="Y mm1")

            def mmY2(e, by=by, hh=hh, j=j):
                ins = None
                for cc in range(4):
                    ins = e.matmul(K.ps[by][0:64, cc * 64:(cc + 1) * 64], ARh[pr(hh), cc, j, 1, :], Hbh[pr(hh), cc, :],
                                   start=False, stop=True, skip_group_check=True)
                return ins
            P.op("pe", mmY2, reads=[ab, Hb.bufs[0]], writes=[K.psb[by]], tag="Y mm2", pe_sync=True)
            P.op("act", lambda e, by=by, hh=hh, s=s: e.activation(
                out=Ysh[:, s, :].rearrange("p (c h v) -> p c h v", c=4, h=2)[:, :, hh, :], in_=ps3(by, 4, 64),
                func=AF.Copy), reads=[K.psb[by]], writes=[Ys.bufs[s]], tag="Y evac")
        bh = nb()

        def mmH(e, bh=bh, s=s, Vt=Vt):
            ins = None
            for hp in range(8):
                cc = hp % 4
                o = K.ps[bh][:, hp * 64:(hp + 1) * 64]
                e.matmul(o, TOKh[:, s, 1, cc * 128:(cc + 1) * 128], Ubh[:, s, hp, :], start=(hp == 0), stop=False,
                         skip_group_check=True)
                ins = e.matmul(o, TOKh[:, s, 2, cc * 128:(cc + 1) * 128], Vt(hp), start=False, stop=True,
                               skip_group_check=True)
            return ins
        P.op("pe", mmH, reads=[TOK.bufs[s], Ub.bufs[s]], writes=[K.psb[bh]], tag="H mm")
        P.op("pool", lambda e, j=j: e.tensor_tensor(out=Hfh[:, :, :], in0=Hfh[:, :, :],
                                                    in1=GCh[:, :, j:j + 1].to_broadcast([128, 4, 64]), op=ALU.mult),
             reads=[R.GC.bufs[0], R.GC.bufs[1], R.GC.bufs[2], R.GC.bufs[3]], writes=Hf.bufs, tag="H decay")
        for hh in range(2):
            P.op("dve", lambda e, bh=bh, hh=hh: e.tensor_tensor(
                out=Hfh[pr(hh), :, :], in0=Hfh[pr(hh), :, :],
                in1=K.ps[bh][pr(hh), hh * 256:(hh + 1) * 256].rearrange("p (c v) -> p c v", c=4), op=ALU.add),
                reads=[K.psb[bh]], writes=Hf.bufs, tag="H add")
        P.op("act", lambda e: e.activation(out=Hbh[:, :, :], in_=Hfh[:, :, :], func=AF.Copy),
             reads=Hf.bufs, writes=Hb.bufs, tag="H bf16")

        if getattr(K, 'chunk_cut', 99) <= 5:
            continue
        bg = nb()

        def mmG(e, bg=bg, j=j):
            e.matmul(K.ps[bg][0:64, :], SGh[:, 0, j * CH:(j + 1) * CH], g2h[:, 0, :], start=True, stop=False)
            return e.matmul(K.ps[bg][0:64, :], SGh[0:32, 1, j * CH:(j + 1) * CH], g2h[0:32, 1, :], start=False, stop=True)
        P.op("pe", mmG, reads=[R.SG.bufs[tq], R.g2.bufs[0]], writes=[K.psb[bg]], tag="gate mm")
        P.op("act", lambda e, bg=bg: e.activation(out=Gsh[:, :], in_=K.ps[bg][0:64, :], func=AF.Copy),
             reads=[K.psb[bg]], writes=Gs.bufs, tag="gate evac")
        if getattr(K, 'chunk_cut', 99) <= 6:
            continue
        Y3 = Ysh[:, s, :].rearrange("p (h v) -> p h v", v=64)
        Q3 = Yqh.rearrange("p (h v) -> p h v", v=64)
        yb = Ys.bufs[s]
        P.op("dve", lambda e, Y3=Y3: e.tensor_reduce(out=sth[:, 0:8], in_=Y3, axis=AX.X, op=ALU.add),
             reads=[yb], writes=st.bufs, tag="gn sum")
        P.op("pool", lambda e, s=s: e.tensor_tensor(out=Yqh[:, :], in0=Ysh[:, s, :], in1=Ysh[:, s, :], op=ALU.mult),
             reads=[yb], writes=Yq.bufs, tag="gn sq")
        P.op("dve", lambda e, Q3=Q3: e.tensor_reduce(out=sth[:, 8:16], in_=Q3, axis=AX.X, op=ALU.add),
             reads=Yq.bufs, writes=st.bufs, tag="gn ssq")
        P.op("dve", lambda e: e.tensor_scalar(out=sth[:, 16:24], in0=sth[:, 0:8], scalar1=1.0 / 64, scalar2=None,
                                              op0=ALU.mult), reads=[], writes=st.bufs, tag="gn mean")
        P.op("dve", lambda e: e.tensor_tensor(out=sth[:, 24:32], in0=sth[:, 16:24], in1=sth[:, 16:24], op=ALU.mult),
             reads=[], writes=st.bufs, tag="gn m2")
        P.op("dve", lambda e: e.scalar_tensor_tensor(out=sth[:, 32:40], in0=sth[:, 8:16], scalar=1.0 / 64,
                                                     in1=sth[:, 24:32], op0=ALU.mult, op1=ALU.subtract),
             reads=[], writes=st.bufs, tag="gn var")
        P.op("dve", lambda e: e.tensor_scalar(out=sth[:, 32:40], in0=sth[:, 32:40], scalar1=GN_EPS, scalar2=None,
                                              op0=ALU.add), reads=[], writes=st.bufs, tag="gn var eps")
        P.op("act", lambda e: e.activation(out=sth[:, 40:48], in_=sth[:, 32:40], func=AF.Sqrt),
             reads=[], writes=st.bufs, tag="gn sqrt")
        P.op("dve", lambda e: e.reciprocal(out=sth[:, 48:56], in_=sth[:, 40:48]), reads=[], writes=st.bufs, tag="gn rstd")
        P.op("dve", lambda e, Y3=Y3: e.tensor_tensor(out=Y3, in0=Y3, in1=sth[:, 16:24].unsqueeze(2).to_broadcast([64, 8, 64]),
                                                     op=ALU.subtract), reads=[], writes=[yb] + st.bufs, tag="gn sub")
        P.op("pool", lambda e, Y3=Y3: e.tensor_tensor(out=Y3, in0=Y3, in1=sth[:, 48:56].unsqueeze(2).to_broadcast([64, 8, 64]),
                                                      op=ALU.mult), reads=st.bufs, writes=[yb], tag="gn mul")
        P.op("pool", lambda e, s=s: e.tensor_tensor(out=Ysh[:, s, :], in0=Ysh[:, s, :], in1=gnh[:, 0, :], op=ALU.mult),
             reads=gnb.bufs, writes=[yb], tag="gn g")
        P.op("pool", lambda e, s=s: e.tensor_tensor(out=Ysh[:, s, :], in0=Ysh[:, s, :], in1=gnh[:, 1, :], op=ALU.add),
             reads=gnb.bufs, writes=[yb], tag="gn b")
        if getattr(K, 'chunk_cut', 99) <= 7:
            continue
        P.op("dve", lambda e, Q3=Q3, s=s, j=j: e.tensor_tensor(
            out=Q3, in0=TOKh[:, s, 3, :].rearrange("p (h v) -> p h v", v=64),
            in1=BONh[:, j, :].unsqueeze(2).to_broadcast([64, 8, 64]), op=ALU.mult),
            reads=[TOK.bufs[s], R.BON.bufs[tq]], writes=Yq.bufs, tag="bonus v")
        P.op("pool", lambda e, s=s: e.tensor_tensor(out=Ysh[:, s, :], in0=Ysh[:, s, :], in1=Yqh[:, :], op=ALU.add),
             reads=Yq.bufs, writes=[yb], tag="y+bonus")
        P.op("dve", lambda e, s=s: e.tensor_tensor(out=Yoh[:, s, :], in0=Ysh[:, s, :], in1=Gsh[:, :], op=ALU.mult),
             reads=Gs.bufs + [yb], writes=[Yo.bufs[s]], tag="gate mul")
        if getattr(K, 'chunk_cut', 99) <= 8:
            continue
        P.op("sp", lambda e, s=s, j=j: e.dma_start(out=o_dst[j * CH:(j + 1) * CH, 512:1024], in_=Yoh[:, s, :]),
             reads=[Yo.bufs[s]], writes=[o_buf], dma=True, tag="o_rwkv out")
    for al in (mk, gnb, M1, M2, M3, NL, PP, BKh_, TOK, WTs, MAK, Ub, Hf, Hb, Ys, Yq, Gs, st, Yo):
        P.free(al)


def host_rwkv_params(inp, half):
    c0, c1 = half * 512, (half + 1) * 512
    mu = inp["rwkv_mu"]
    pcol = np.zeros((128, 64), np.float32)
    for blk in range(3):
        for cc in range(4):
            pcol[:, blk * 4 + cc] = mu[blk * 1024 + c0 + cc * 128: blk * 1024 + c0 + (cc + 1) * 128]
    pcol[:, 12] = mu[3072:3072 + 128]
    pcol[:, 13] = mu[3072 + 128:3072 + 256]
    pcol[0:32, 14] = mu[3072 + 256:3072 + 288]
    rk = inp["rwkv_r_k"].reshape(-1)
    for cc in range(4):
        sl = slice(c0 + cc * 128, c0 + (cc + 1) * 128)
        pcol[:, 15 + cc] = inp["rwkv_w0"][sl]
        pcol[:, 19 + cc] = inp["rwkv_a0"][sl]
        pcol[:, 23 + cc] = inp["rwkv_k_k"][sl]
        pcol[:, 27 + cc] = inp["rwkv_k_a"][sl]
        pcol[:, 31 + cc] = rk[sl]
    w2a2 = np.concatenate([inp["rwkv_w2"][:, c0:c1], inp["rwkv_a2"][:, c0:c1]], 0).astype(np.float32)
    g2 = np.zeros((128, 2, 512), np.float32)
    g2[:, 0, :] = inp["rwkv_g2"][0:128, c0:c1]
    g2[0:32, 1, :] = inp["rwkv_g2"][128:160, c0:c1]
    gnb = np.zeros((64, 2, 512), np.float32)
    gnb[:, 0, :] = inp["rwkv_gn_g"][c0:c1][None]
    gnb[:, 1, :] = inp["rwkv_gn_b"][c0:c1][None]
    return dict(pcol=pcol, w2a2=np.ascontiguousarray(w2a2), g2=g2, gnb=gnb)


def host_rwkv_consts():
    bones = np.zeros((128, 128), np.float32)
    bones[0:64, 0:64] = 1.0
    bones[64:128, 64:128] = 1.0
    hsel = np.zeros((128, 16), np.float32)
    hsel[0:64, 0] = 1.0
    hsel[64:128, 1] = 1.0
    t = np.arange(64)
    sl = (t[None, :] < t[:, None]).astype(np.float32)
    su = (t[:, None] < t[None, :]).astype(np.float32)
    ui = (t[:, None] <= t[None, :]).astype(np.float32)
    masks = np.zeros((64, 6, 128), np.float32)
    masks[:, 0, 0:64] = sl
    masks[:, 0, 64:128] = sl
    masks[:, 1, 0:64] = su
    masks[:, 1, 64:128] = ui
    masks[:, 2, 0:64] = ui
    masks[:, 3, 0:64] = np.eye(64, dtype=np.float32)
    return dict(bones=bones, hsel=hsel, masks=masks)


def ln_tile(P, acch, ab, t, gbh, gb, lsth, st, s, outs):
    for j in range(4):
        P.op("dve", lambda e, t=t, j=j, s=s: e.bn_stats(out=lsth[:, s, j * 6:(j + 1) * 6],
                                                      in_=acch[:, t, j * 512:(j + 1) * 512]),
             reads=[ab[j]], writes=[st.bufs[s]], tag="bn_stats")
    P.op("dve", lambda e, s=s: e.bn_aggr(out=lsth[:, s, 24:26], in_=lsth[:, s, 0:24].rearrange("p (a b) -> p a b", b=6)),
         reads=[], writes=[st.bufs[s]], tag="bn_aggr")
    P.op("dve", lambda e, s=s: e.tensor_scalar(out=lsth[:, s, 28:29], in0=lsth[:, s, 25:26], scalar1=LN_EPS,
                                              scalar2=None, op0=ALU.add), reads=[], writes=[st.bufs[s]], tag="var+eps")
    P.op("act", lambda e, s=s: e.activation(out=lsth[:, s, 29:30], in_=lsth[:, s, 28:29], func=AF.Sqrt),
         reads=[], writes=[st.bufs[s]], tag="sqrt")
    P.op("dve", lambda e, s=s: e.reciprocal(out=lsth[:, s, 26:27], in_=lsth[:, s, 29:30]),
         reads=[], writes=[st.bufs[s]], tag="rstd")
    P.op("dve", lambda e, s=s: e.scalar_tensor_tensor(out=lsth[:, s, 27:28], in0=lsth[:, s, 24:25], scalar=-1.0,
                                                     in1=lsth[:, s, 26:27], op0=ALU.mult, op1=ALU.mult),
         reads=[], writes=[st.bufs[s]], tag="nmr")
    P.op("act", lambda e, t=t, s=s: e.activation(out=acch[:, t, :], in_=acch[:, t, :], func=AF.Identity,
                                                bias=lsth[:, s, 27:28], scale=lsth[:, s, 26:27]),
         reads=[st.bufs[s]], writes=ab, tag="ln norm")
    P.op("pool", lambda e, t=t: e.tensor_tensor(out=acch[:, t, :], in0=acch[:, t, :], in1=gbh[:, 0, :], op=ALU.mult),
         reads=[gb.bufs[0]], writes=ab, tag="ln g")
    P.op("pool", lambda e, t=t: e.tensor_tensor(out=acch[:, t, :], in0=acch[:, t, :], in1=gbh[:, 1, :], op=ALU.add),
         reads=[gb.bufs[0]], writes=ab, tag="ln b")
    for (dst, dbuf, dt) in outs:
        if dt == F32:
            P.op("sp", lambda e, t=t, dst=dst: e.dma_start(out=dst[t * 128:(t + 1) * 128, :], in_=acch[:, t, :]),
                 reads=ab, writes=[dbuf], dma=True, tag="ln out")
        else:
            obf = P.lnbf
            P.op("act", lambda e, t=t, s=s, obf=obf: e.activation(out=obf.handle[:, s, :], in_=acch[:, t, :], func=AF.Copy),
                 reads=ab, writes=[obf.bufs[s]], tag="ln out cast")
            P.op("sp", lambda e, t=t, s=s, dst=dst, obf=obf: e.dma_start(out=dst[t * 128:(t + 1) * 128, :],
                                                                        in_=obf.handle[:, s, :]),
                 reads=[obf.bufs[s]], writes=[dbuf], dma=True, tag="ln out bf16")


def wout_stage(K, og, og_buf, x1f, x1f_buf, wout, hsc_d, lng, lnb, outs, base):
    P = K.P
    NT = NTOK // 128
    off = base
    acc = P.sbuf("acc3", [128, NT, D], F32, off, nbufs=NT * 4); off += NT * D * 4
    XT = P.sbuf("oT", [128, KC, NTOK], BF16, off, nbufs=NT); off += KC * NTOK * 2
    ws = P.sbuf("wouts", [128, KC, D], BF16, off, nbufs=4); off += KC * D * 2
    gb = P.sbuf("lngb3", [128, 2, D], F32, off); off += 2 * D * 4
    st = P.sbuf("lnst3", [128, 2, 32], F32, off, nbufs=2); off += 2 * 32 * 4
    hs = P.sbuf("hsc", [128, 8], F32, off); off += 32
    xa = P.sbuf("xa", [128, 2, 2, D], BF16, off, nbufs=2); off += 2 * 2 * D * 2
    xb = P.sbuf("xb3", [128, 2, D], BF16, off, nbufs=2); off += 2 * D * 2
    acch, XTh, wsh, gbh, lsth, hsh, xah, xbh = (acc.handle, XT.handle, ws.handle, gb.handle, st.handle, hs.handle,
                                                xa.handle, xb.handle)
    load_const(K, "lng3", gb, gbh[:, 0, :], lng[:, :])
    load_const(K, "lnb3", gb, gbh[:, 1, :], lnb[:, :])
    load_const(K, "hsc", hs, hsh[:, 0:2], hsc_d[:, :])
    wv = wout.rearrange("(kc p) d -> p kc d", p=128)
    for q in range(4):
        P.op("pool", lambda e, q=q: e.dma_start(out=wsh[:, q * 4:(q + 1) * 4, :], in_=wv[:, q * 4:(q + 1) * 4, :]),
             reads=[], writes=[ws.bufs[q]], dma=True, tag="wout load")
    for t in range(NT):
        P.op("sp", lambda e, t=t: e.dma_start(out=acch[:, t, :], in_=x1f[t * 128:(t + 1) * 128, :]),
             reads=[x1f_buf], writes=acc.bufs[t * 4:(t + 1) * 4], dma=True, tag="acc3 load")
        P.op("act", lambda e, t=t: e.activation(out=acch[:, t, :], in_=acch[:, t, :], func=AF.Copy, scale=ALPHA),
             reads=[], writes=acc.bufs[t * 4:(t + 1) * 4], tag="acc3 scale")
    banks = [4, 5, 6, 7]
    for t in range(NT):
        s = t % 2
        for cand in range(2):
            for r in range(2):
                if callable(og):
                    oap, obuf_ = og(cand, r, t)
                else:
                    row0 = r * SEQ + cand * NTOK + t * 128
                    oap, obuf_ = og[row0:row0 + 128, :], og_buf
                P.op("sp", lambda e, s=s, cand=cand, r=r, oap=oap: e.dma_start(
                    out=xah[:, s, cand, r * 1024:(r + 1) * 1024], in_=oap),
                    reads=[obuf_], writes=[xa.bufs[s]], dma=True, tag="og load")
        P.op("dve", lambda e, s=s: e.tensor_scalar(out=xbh[:, s, :], in0=xah[:, s, 0, :], scalar1=hsh[:, 0:1],
                                                  scalar2=None, op0=ALU.mult),
             reads=[xa.bufs[s], hs.bufs[0]], writes=[xb.bufs[s]], tag="blend0")
        P.op("dve", lambda e, s=s: e.scalar_tensor_tensor(out=xbh[:, s, :], in0=xah[:, s, 1, :], scalar=hsh[:, 1:2],
                                                         in1=xbh[:, s, :], op0=ALU.mult, op1=ALU.add),
             reads=[xa.bufs[s], hs.bufs[0]], writes=[xb.bufs[s]], tag="blend1")
        for half in range(2):
            bk = banks[(2 * t + half) % 4]
            pb = K.ps[bk].bitcast(BF16)

            def tr(e, s=s, half=half, pb=pb):
                ins = None
                for j in range(8):
                    kc = half * 8 + j
                    ins = e.transpose(pb[:, j * 128:(j + 1) * 128], xbh[:, s, kc * 128:(kc + 1) * 128], K.ident[:])
                return ins
            P.op("pe", tr, reads=[xb.bufs[s], K.ident_buf], writes=[K.psb[bk]], tag="oT transposes")
            eng = "act" if half == 0 else "dve"

            def ev(e, t=t, half=half, pb=pb, eng=eng):
                o = XTh[:, half * 8:(half + 1) * 8, t * 128:(t + 1) * 128]
                i = pb.rearrange("p (j c) -> p j c", j=8)
                if eng == "act":
                    return e.activation(out=o, in_=i, func=AF.Copy)
                return e.tensor_copy(out=o, in_=i)
            P.op(eng, ev, reads=[K.psb[bk]], writes=[XT.bufs[t]], tag="oT evac")
    cnt = 0
    for t in range(NT):
        for db in range(4):
            bk = cnt % 4
            cnt += 1

            def mm(e, bk=bk, t=t, db=db):
                ins = None
                for kc in range(KC):
                    ins = e.matmul(K.ps[bk][:, :], XTh[:, kc, t * 128:(t + 1) * 128], wsh[:, kc, db * 512:(db + 1) * 512],
                                   start=(kc == 0), stop=(kc == KC - 1))
                return ins
            P.op("pe", mm, reads=[XT.bufs[t]] + ws.bufs, writes=[K.psb[bk]], tag="wout mm")
            P.op("dve", lambda e, bk=bk, t=t, db=db: e.tensor_tensor(
                out=acch[:, t, db * 512:(db + 1) * 512], in0=K.ps[bk][:, :], in1=acch[:, t, db * 512:(db + 1) * 512],
                op=ALU.add), reads=[K.psb[bk]], writes=[acc.bufs[t * 4 + db]], tag="acc3 add")
        ln_tile(P, acch, acc.bufs[t * 4:(t + 1) * 4], t, gbh, gb, lsth, st, t % 2, outs)
    for a in (acc, XT, ws, gb, st, hs, xa, xb):
        P.free(a)


def build_program():
    nc = bass.Bass("TRN2", target_bir_lowering=False)
    K = mk_ctx(nc)
    P = K.P
    d = {}

    def din(name, shape):
        d[name] = dram_in(K, name, shape)
        return d[name]
    x = din("x", [NTOK, D])
    identd = din("ident", [128, 128])
    f1 = [din("f1_wg", [NG, 128, KC, FG]), din("f1_wu", [NG, 128, KC, FG]), din("f1_wd", [DFF, D]),
          din("ln1g", [128, D]), din("ln1b", [128, D])]
    f2 = [din("f2_wg", [NG, 128, KC, FG]), din("f2_wu", [NG, 128, KC, FG]), din("f2_wd", [DFF, D]),
          din("ln3g", [128, D]), din("ln3b", [128, D])]
    win = din("win", [27, 128, KC, 128])
    bm = din("bm", [4, 5, 128, 512]); ctab = din("ctab", [128, 64]); lamv = din("lamv", [128, 4, 64])
    ngt = din("ngt", [128, 512])
    pcol = din("pcol", [128, 64]); w2a2 = din("w2a2", [128, 512]); g2 = din("g2", [128, 2, 512])
    gnb = din("gnb", [64, 2, 512]); bones = din("bones", [128, 128]); hsel = din("hsel", [128, 16])
    masks = din("masks", [64, 6, 128])
    wout = din("wout", [D, D]); hsc = din("hsc", [128, 2]); ln2g = din("ln2g", [128, D]); ln2b = din("ln2b", [128, D])
    x1f = dram_tmp(K, "x1f", [NTOK, D], F32)
    x1b = dram_tmp(K, "x1b", [NTOK, D], BF16)
    x1g = [dram_tmp(K, f"x1g{i}", [256, D], BF16) for i in range(8)]
    oloc = dram_tmp(K, "oloc", [SEQ, 1024], BF16)
    og = [dram_tmp(K, f"og{i}", [512, 1024], BF16) for i in range(8)]
    x2s = dram_tmp(K, "x2s", [NTOK, D], F32)
    out = dram_tmp(K, "out", [NTOK, D], F32, kind="ExternalOutput")
    B = lambda n: K.dram[n][1]
    groups = [[0, 1], [2, 3], [4, 5], [6, 7]]

    setup_consts(K, identd)
    base = K.const_top
    ffn_stage(K, x.ap(), B("x"), f1[0].ap(), f1[1].ap(), f1[2].ap(), f1[3].ap(), f1[4].ap(),
              [(x1f.ap(), B("x1f"), F32), (x1b.ap(), B("x1b"), BF16)], base=base)
    for i in range(8):
        P.op("pool", lambda e, i=i: e.collective_compute("AllGather", ALU.bypass, replica_groups=groups,
                                                         ins=[x1b.ap()[i * 128:(i + 1) * 128, :]],
                                                         outs=[x1g[i].ap()[:, :]]),
             reads=[B("x1b")], writes=[B(f"x1g{i}")], dma="cc", tag="allgather x1")

    def x1_src(t):
        r, i = t // 8, t % 8
        return x1g[i].ap()[r * 128:(r + 1) * 128, :], B(f"x1g{i}")
    X1T = P.sbuf("X1T", [128, KC, SEQ], BF16, base, nbufs=16)
    b2 = base + KC * SEQ * 2
    load_xT(K, x1_src, None, SEQ, X1T, b2, [4, 5, 6, 7], K.ident, False)
    attention_stage(K, X1T, win.ap(), bm.ap(), ctab.ap(), lamv.ap(), ngt.ap(), oloc.ap(), B("oloc"), b2)
    R = rwkv_alloc_persist(K, b2)
    rwkv_prep(K, R, X1T, win.ap(), pcol.ap(), w2a2.ap(), g2.ap(), bones.ap(), hsel.ap(), R.top)
    P.free(X1T)
    rwkv_chunks(K, R, masks.ap(), gnb.ap(), oloc.ap(), B("oloc"), base)
    for a in (R.AR, R.BK, R.VC, R.LW, R.SG, R.GC, R.BON, R.pcol, R.w2a2, R.g2, R.bones, R.hsel):
        P.free(a)
    for i in range(8):
        P.op("pool", lambda e, i=i: e.collective_compute("AllGather", ALU.bypass, replica_groups=groups,
                                                         ins=[oloc.ap()[i * 256:(i + 1) * 256, :]],
                                                         outs=[og[i].ap()[:, :]]),
             reads=[B("oloc")], writes=[B(f"og{i}")], dma="cc", tag="allgather o")

    def og_src(cand, r, t):
        tok = cand * NTOK + t * 128
        i, j = tok // 256, tok % 256
        return og[i].ap()[r * 256 + j:r * 256 + j + 128, :], B(f"og{i}")
    wout_stage(K, og_src, None, x1f.ap(), B("x1f"), wout.ap(), hsc.ap(), ln2g.ap(), ln2b.ap(),
               [(x2s.ap(), B("x2s"), F32)], base)
    ffn_stage(K, x2s.ap(), B("x2s"), f2[0].ap(), f2[1].ap(), f2[2].ap(), f2[3].ap(), f2[4].ap(),
              [(out.ap(), B("out"), F32)], base=base)
    finish(K, [B("out")])
    P.emit()
    return nc


def host_inputs(inp):
    l0 = {k: np.asarray(v[0], np.float32) for k, v in inp.items() if k != "x"}
    x = np.asarray(inp["x"], np.float32).reshape(8, NTOK, D)
    rep = lambda v, n=128: np.ascontiguousarray(np.broadcast_to(np.asarray(v, np.float32)[None], (n,) + v.shape))
    shared = dict(
        ident=np.eye(128, dtype=np.float32),
        f1_wg=host_tile_ffn_w(l0["ffn1_w_gate"]), f1_wu=host_tile_ffn_w(l0["ffn1_w_up"]), f1_wd=l0["ffn1_w_down"],
        ln1g=rep(l0["ln1_g"]), ln1b=rep(l0["ln1_b"]),
        f2_wg=host_tile_ffn_w(l0["ffn2_w_gate"]), f2_wu=host_tile_ffn_w(l0["ffn2_w_up"]), f2_wd=l0["ffn2_w_down"],
        ln3g=rep(l0["ln3_g"]), ln3b=rep(l0["ln3_b"]), ln2g=rep(l0["ln2_g"]), ln2b=rep(l0["ln2_b"]),
        lamv=rep(np.stack([l0["lambda_q1"], l0["lambda_k1"], l0["lambda_q2"], l0["lambda_k2"]])),
        ngt=rep(np.tile(l0["attn_norm_g"], 4)),
        wout=np.ascontiguousarray(l0["w_out"][np.concatenate([np.arange(0, 512), np.arange(1024, 1536),
                                                            np.arange(512, 1024), np.arange(1536, 2048)])]),
    )
    shared.update(host_rwkv_consts())
    per_half = []
    for h in range(2):
        bm, ctab = host_attn_consts(h)
        dd = dict(win=host_tile_win(l0["w_in"], h), bm=bm, ctab=ctab,
                  hsc=np.ascontiguousarray(np.broadcast_to(np.array([1.0 - h, float(h)], np.float32)[None], (128, 2))))
        dd.update(host_rwkv_params(l0, h))
        per_half.append(dd)
    maps = []
    for c in range(8):
        m = dict(shared)
        m.update(per_half[c % 2])
        m["x"] = np.ascontiguousarray(x[c])
        maps.append(m)
    return maps


def kernel(**inputs):
    nc = build_program()
    maps = host_inputs(inputs)
    res = run_bass_kernel_spmd(nc, maps, core_ids=list(range(8)))
    out = np.stack([np.asarray(res.results[c]["out"], np.float32) for c in range(8)], 0)
    return out.reshape(4, SEQ, D)
```

```python
import numpy as np
import concourse.bass as bass
import concourse.mybir as mybir
from concourse.bass_utils import run_bass_kernel_spmd

F32 = mybir.dt.float32
BF16 = mybir.dt.bfloat16
AF = mybir.ActivationFunctionType
ALU = mybir.AluOpType
AX = mybir.AxisListType

SAME_ENGINE_SYNC = True
SEM_CHUNK = 4000
N_DMA_SEMS = 12


class Buf:
    __slots__ = ("name", "last_w", "readers", "excl")

    def __init__(self, name, excl=False):
        self.name = name
        self.last_w = None
        self.readers = []
        self.excl = excl


class Op:
    __slots__ = ("eng", "fn", "deps", "dma", "needs_inc", "inc_no", "dsem", "dval", "pos", "tag", "prev_dval")

    def __init__(self, eng, fn, dma, tag):
        self.eng = eng
        self.fn = fn
        self.deps = []
        self.dma = dma
        self.needs_inc = False
        self.inc_no = None
        self.dsem = None
        self.dval = None
        self.tag = tag


class Alloc:
    def __init__(self, name, off, nbytes, handle, bufs):
        self.name, self.off, self.nbytes, self.handle, self.bufs = name, off, nbytes, handle, bufs


ENGS = ("pe", "act", "dve", "pool", "sp")


class Prog:
    def __init__(self, nc):
        self.nc = nc
        self.ops = {e: [] for e in ENGS}
        self.all_ops = []
        self.live = []
        self.ghosts = []
        self.uid = 0
        self.sbuf_top = 0

    def sbuf(self, name, shape, dtype, off, nbufs=1):
        esz = 4 if dtype == F32 else 2
        free = 1
        for s in shape[1:]:
            free *= s
        nbytes = free * esz
        assert off % 32 == 0, (name, off)
        assert 16384 <= off and off + nbytes <= 224 * 1024 - 160, (name, off, nbytes)
        self.uid += 1
        h = self.nc.alloc_sbuf_tensor_at(f"{name}_{self.uid}", list(shape), dtype, offset=off)
        bufs = [Buf(f"{name}[{i}]") for i in range(nbufs)]
        for a in self.live:
            assert a.off + a.nbytes <= off or off + nbytes <= a.off, ("overlap", name, a.name)
        keep = []
        for g in self.ghosts:
            if g.off + g.nbytes <= off or off + nbytes <= g.off:
                keep.append(g)
                continue
            hz = []
            for b in g.bufs:
                if b.last_w is not None:
                    hz.append(b.last_w)
                hz.extend(b.readers)
            for b in bufs:
                b.readers.extend(hz)
            keep.append(g)
        self.ghosts = keep
        a = Alloc(name, off, nbytes, h, bufs)
        self.live.append(a)
        return a

    def free(self, a):
        self.live.remove(a)
        self.ghosts.append(a)

    def op(self, eng, fn, reads=(), writes=(), dma=False, tag="", pe_sync=False):
        o = Op(eng, fn, dma, tag)
        deps = []
        for b in reads:
            if b.excl:
                continue
            if b.last_w is not None:
                deps.append(b.last_w)
        for b in list(writes) + [b for b in reads if b.excl]:
            if b.last_w is not None:
                deps.append(b.last_w)
            deps.extend(b.readers)
        for b in reads:
            if not b.excl:
                b.readers.append(o)
        for b in list(writes) + [b for b in reads if b.excl]:
            b.last_w = o
            b.readers = []
        seen = set()
        for d in deps:
            if id(d) in seen or d is o:
                continue
            seen.add(id(d))
            if (not d.dma) and d.eng == eng and not SAME_ENGINE_SYNC:
                continue
            if (not d.dma) and d.eng == eng and eng == "pe" and not pe_sync:
                continue
            o.deps.append(d)
        o.pos = len(self.ops[eng])
        self.ops[eng].append(o)
        self.all_ops.append(o)
        return o

    def emit(self):
        nc = self.nc
        for o in self.all_ops:
            for d in o.deps:
                d.needs_inc = True
        cnt = {e: 0 for e in ENGS}
        dma_cnt = {e: 0 for e in ENGS}
        n_sems = {}
        for e in ENGS:
            n = sum(1 for o in self.ops[e] if o.needs_inc and not o.dma)
            n_sems[e] = max(1, (n + SEM_CHUNK - 1) // SEM_CHUNK)
        import contextlib
        with contextlib.ExitStack() as st:
            esems = {e: [st.enter_context(nc.semaphore(f"s_{e}_{i}")) for i in range(n_sems[e])] for e in ENGS}
            dsems = {e: [st.enter_context(nc.semaphore(f"d_{e}_{i}")) for i in range(N_DMA_SEMS)]
                     for e in ("sp", "act", "pool")}
            dsem_val = {e: [0] * N_DMA_SEMS for e in dsems}
            ccsems = []
            dsems["cc"] = ccsems
            for e in ENGS:
                for o in self.ops[e]:
                    if o.dma == "cc":
                        o.dsem = ("cc", len(ccsems))
                        ccsems.append(st.enter_context(nc.semaphore(f"cc_{len(ccsems)}")))
                        o.prev_dval = 0
                        o.dval = 1
                    elif o.dma:
                        k = dma_cnt[e] % N_DMA_SEMS
                        dma_cnt[e] += 1
                        o.dsem = (e, k)
                        o.prev_dval = dsem_val[e][k]
                        dsem_val[e][k] += 16
                        o.dval = dsem_val[e][k]
                    elif o.needs_inc:
                        o.inc_no = cnt[e]
                        cnt[e] += 1
            items = {e: [] for e in ENGS}
            for e in ENGS:
                waited = {}
                for o in self.ops[e]:
                    ws = []
                    for d in o.deps:
                        if d.dma:
                            key = ("d",) + d.dsem
                            val = d.dval
                            sem = dsems[d.dsem[0]][d.dsem[1]]
                        else:
                            ch = d.inc_no // SEM_CHUNK
                            key = ("e", d.eng, ch)
                            val = d.inc_no % SEM_CHUNK + 1
                            sem = esems[d.eng][ch]
                            later = any(k[0] == "e" and k[1] == d.eng and k[2] > ch for k in waited)
                            if later:
                                continue
                        if waited.get(key, 0) >= val:
                            continue
                        waited[key] = val
                        ws.append((sem, val))
                    if o.dma and o.prev_dval > 0:
                        key = ("d",) + o.dsem
                        if waited.get(key, 0) < o.prev_dval:
                            waited[key] = o.prev_dval
                            ws.append((dsems[o.dsem[0]][o.dsem[1]], o.prev_dval))
                    items[e].append((ws, o))
            self.stats = {e: (len(self.ops[e]), sum(len(w) for w, _ in items[e])) for e in ENGS}

            def run(e, eng):
                for ws, o in items[e]:
                    for sem, val in ws:
                        eng.wait_ge(sem, val)
                    ins = o.fn(eng)
                    if o.dma == "cc":
                        ins.then_inc(dsems["cc"][o.dsem[1]], 1)
                    elif o.dma:
                        ins.then_inc(dsems[o.dsem[0]][o.dsem[1]], 16)
                    elif o.needs_inc:
                        ins.then_inc(esems[e][o.inc_no // SEM_CHUNK], 1)

            with nc.Block() as block:
                @block.tensor
                def _(eng):
                    run("pe", eng)

                @block.scalar
                def _(eng):
                    run("act", eng)

                @block.vector
                def _(eng):
                    run("dve", eng)

                @block.gpsimd
                def _(eng):
                    run("pool", eng)

                @block.sync
                def _(eng):
                    run("sp", eng)


D = 2048
DFF = 5632
NTOK = 1024
SEQ = 2048
ALPHA = 2.0 ** 0.25
LN_EPS = 1e-5
FG = 256
NG = DFF // FG
KC = D // 128


class Ctx:
    pass


def mk_ctx(nc):
    K = Ctx()
    K.nc = nc
    K.P = Prog(nc)
    K.ps = []
    K.psb = []
    for i in range(8):
        h = nc.alloc_psum_tensor(f"psum{i}", [128, 512], F32)
        K.ps.append(h)
        K.psb.append(Buf(f"psum{i}", excl=True))
    K.dram = {}
    return K


def dram_in(K, name, shape, dtype=F32):
    t = K.nc.dram_tensor(name, list(shape), dtype, kind="ExternalInput")
    K.dram[name] = (t, Buf("dram_" + name))
    return t


def dram_tmp(K, name, shape, dtype, kind="Internal"):
    t = K.nc.dram_tensor(name, list(shape), dtype, kind=kind)
    K.dram[name] = (t, Buf("dram_" + name))
    return t


def load_const(K, name, alloc, dst_ap, src_ap, q="sp"):
    K.P.op(q, lambda e, o=dst_ap, i=src_ap: e.dma_start(out=o, in_=i), reads=[], writes=alloc.bufs, dma=True,
           tag="const " + name)


def load_xT(K, src, src_buf, ntok, XT, xb_off, banks, ident, src_f32):
    P = K.P
    nt = ntok // 128
    xb = P.sbuf("xb", [128, 2, D], BF16, xb_off, nbufs=2)
    xbh = xb.handle
    XTh = XT.handle
    for t in range(nt):
        s = t % 2
        q = "pool" if src_f32 else "sp"
        if callable(src):
            sap, sbuf_ = src(t)
        else:
            sap, sbuf_ = src[t * 128:(t + 1) * 128, :], src_buf
        P.op(q, lambda e, s=s, sap=sap: e.dma_start(out=xbh[:, s, :], in_=sap),
             reads=[sbuf_], writes=[xb.bufs[s]], dma=True, tag="xb load")
        for half in range(2):
            bk = banks[(2 * t + half) % len(banks)]
            pb = K.ps[bk].bitcast(BF16)

            def tr(e, s=s, half=half, pb=pb):
                ins = None
                for j in range(8):
                    kc = half * 8 + j
                    ins = e.transpose(pb[:, j * 128:(j + 1) * 128], xbh[:, s, kc * 128:(kc + 1) * 128], ident[:])
                return ins
            P.op("pe", tr, reads=[xb.bufs[s], K.ident_buf], writes=[K.psb[bk]], tag="xT transposes")
            eng = "act" if half == 0 else "dve"

            def ev(e, t=t, half=half, pb=pb, eng=eng):
                o = XTh[:, half * 8:(half + 1) * 8, t * 128:(t + 1) * 128]
                i = pb.rearrange("p (j c) -> p j c", j=8)
                if eng == "act":
                    return e.activation(out=o, in_=i, func=AF.Copy)
                return e.tensor_copy(out=o, in_=i)
            P.op(eng, ev, reads=[K.psb[bk]], writes=[XT.bufs[t]], tag="xT evac")
    P.free(xb)


def ffn_stage(K, x_src, x_buf, wg, wu, wd, lng, lnb, outs, base=0, after_tile=None):
    P = K.P
    nc = K.nc
    NT = NTOK // 128
    off = base
    acc = P.sbuf("acc", [128, NT, D], F32, off, nbufs=NT * 4); off += NT * D * 4
    XT = P.sbuf("XT", [128, KC, NTOK], BF16, off, nbufs=NT); off += KC * NTOK * 2
    wgs = P.sbuf("wgs", [128, 2, KC, FG], BF16, off, nbufs=2); off += 2 * KC * FG * 2
    wus = P.sbuf("wus", [128, 2, KC, FG], BF16, off, nbufs=2); off += 2 * KC * FG * 2
    wds = P.sbuf("wds", [128, 2, 2, D], BF16, off, nbufs=2); off += 2 * 2 * D * 2
    hT = P.sbuf("hT", [128, 2, 2, NTOK], BF16, off, nbufs=2); off += 2 * 2 * NTOK * 2
    stmp = P.sbuf("stmp", [128, 2, 512], F32, off, nbufs=2); off += 2 * 512 * 4
    gb = P.sbuf("lngb", [128, 2, D], F32, off, nbufs=1); off += 2 * D * 4
    st = P.sbuf("lnst", [128, 2, 4 * 6 + 8], F32, off, nbufs=2); off += 2 * 32 * 4
    xb_off = off
    acch, XTh, wgh, wuh, wdh, hTh, sth, gbh, lsth = (acc.handle, XT.handle, wgs.handle, wus.handle, wds.handle,
                                                      hT.handle, stmp.handle, gb.handle, st.handle)

    load_const(K, "lng", gb, gbh[:, 0, :], lng[:, :])
    load_const(K, "lnb", gb, gbh[:, 1, :], lnb[:, :])

    for t in range(NT):
        P.op("sp", lambda e, t=t: e.dma_start(out=acch[:, t, :], in_=x_src[t * 128:(t + 1) * 128, :]),
             reads=[x_buf], writes=acc.bufs[t * 4:(t + 1) * 4], dma=True, tag="acc load")
        P.op("act", lambda e, t=t: e.activation(out=acch[:, t, :], in_=acch[:, t, :], func=AF.Copy, scale=ALPHA),
             reads=[], writes=acc.bufs[t * 4:(t + 1) * 4], tag="acc scale")

    load_xT(K, x_src, x_buf, NTOK, XT, xb_off, [4, 5, 6, 7], K.ident, True)

    def load_wgu(g):
        s = g % 2
        for (wsrc, wh, wa, nm) in ((wg, wgh, wgs, "wg"), (wu, wuh, wus, "wu")):
            for piece in range(2):
                P.op("pool", lambda e, g=g, s=s, wsrc=wsrc, wh=wh, piece=piece: e.dma_start(
                    out=wh[:, s, piece * 8:(piece + 1) * 8, :], in_=wsrc[g, :, piece * 8:(piece + 1) * 8, :]),
                    reads=[], writes=[wa.bufs[s]], dma=True, tag=nm + " load")

    def load_wd(g):
        s = g % 2
        wdv = wd.rearrange("(g fc p) d -> g p fc d", fc=2, p=128)
        P.op("pool", lambda e, g=g, s=s: e.dma_start(out=wdh[:, s, :, :], in_=wdv[g]),
             reads=[], writes=[wds.bufs[s]], dma=True, tag="wd load")

    def upgate(g, only=None):
        s = g % 2
        for c in range(2):
            for th in range(2):
                if only is not None and only != (c * 2 + th):
                    continue
                i = (c * 2 + th) % 2
                bg, bu = i, 2 + i
                toks = slice(th * 512, (th + 1) * 512)
                xbufs = XT.bufs[th * 4:(th + 1) * 4]
                for (wh, wa, bk) in ((wgh, wgs, bg), (wuh, wus, bu)):
                    def mm(e, wh=wh, bk=bk, c=c, toks=toks, s=s):
                        ins = None
                        for kc in range(KC):
                            ins = e.matmul(K.ps[bk][:, :], wh[:, s, kc, c * 128:(c + 1) * 128], XTh[:, kc, toks],
                                           start=(kc == 0), stop=(kc == KC - 1))
                        return ins
                    P.op("pe", mm, reads=xbufs + [wa.bufs[s]], writes=[K.psb[bk]], tag="upgate mm")
                P.op("act", lambda e, i=i, bg=bg: e.activation(out=sth[:, i, :], in_=K.ps[bg][:, :], func=AF.Silu),
                     reads=[K.psb[bg]], writes=[stmp.bufs[i]], tag="silu")
                P.op("dve", lambda e, i=i, bu=bu, c=c, toks=toks, s=s: e.tensor_tensor(
                    out=hTh[:, s, c, toks], in0=K.ps[bu][:, :], in1=sth[:, i, :], op=ALU.mult),
                    reads=[K.psb[bu], stmp.bufs[i]], writes=[hT.bufs[s]], tag="hmul")

    dcnt = [0]

    def down(g, ln_after=False, part=None):
        s = g % 2
        for t in range(NT):
            if part is not None and t // 2 != part:
                continue
            if ln_after and t > 0:
                ln_tile(P, acch, acc.bufs[(t - 1) * 4:t * 4], t - 1, gbh, gb, lsth, st, (t - 1) % 2, outs)
                if after_tile is not None:
                    after_tile(t - 1)
            for db in range(4):
                bk = 4 + dcnt[0] % 4
                dcnt[0] += 1

                def mm(e, bk=bk, t=t, db=db, s=s):
                    ins = None
                    for fc in range(2):
                        ins = e.matmul(K.ps[bk][:, :], hTh[:, s, fc, t * 128:(t + 1) * 128],
                                       wdh[:, s, fc, db * 512:(db + 1) * 512], start=(fc == 0), stop=(fc == 1))
                    return ins
                P.op("pe", mm, reads=[hT.bufs[s], wds.bufs[s]], writes=[K.psb[bk]], tag="down mm")
                P.op("dve", lambda e, bk=bk, t=t, db=db: e.scalar_tensor_tensor(
                    out=acch[:, t, db * 512:(db + 1) * 512], in0=K.ps[bk][:, :], scalar=0.5,
                    in1=acch[:, t, db * 512:(db + 1) * 512], op0=ALU.mult, op1=ALU.add),
                    reads=[K.psb[bk]], writes=[acc.bufs[t * 4 + db]], tag="acc add")

    ngr = K.ng_override if hasattr(K, "ng_override") else NG
    load_wgu(0)
    if ngr > 1:
        load_wgu(1)
    load_wd(0)
    for g in range(ngr):
        for blk in range(4):
            upgate(g, only=blk)
            if g > 0:
                down(g - 1, part=blk)
        if g + 2 < ngr:
            load_wgu(g + 2)
        if g + 1 < ngr:
            load_wd(g + 1)
    P.lnbf = P.sbuf("lnbf", [128, 2, D], BF16, xb_off, nbufs=2)
    down(ngr - 1, ln_after=True)
    ln_tile(P, acch, acc.bufs[(NT - 1) * 4:NT * 4], NT - 1, gbh, gb, lsth, st, (NT - 1) % 2, outs)
    if after_tile is not None:
        after_tile(NT - 1)
    P.free(P.lnbf)
    for a in (acc, XT, wgs, wus, wds, hT, stmp, gb, st):
        P.free(a)


def setup_consts(K, identd):
    P = K.P
    a = P.sbuf("ident", [128, 128], BF16, 16384)
    K.ident = a.handle
    K.ident_buf = a.bufs[0]
    P.op("pool", lambda e: e.dma_start(out=K.ident[:, :], in_=identd.ap()[:, :]), reads=[], writes=a.bufs, dma=True,
         tag="ident")
    K.const_top = 16384 + 256


def finish(K, out_bufs):
    K.P.op("sp", lambda e: e.nop(), reads=out_bufs, writes=[], tag="final wait")


NCH_ATT = 12
NCH_RW = 15
NEG = -30000.0
LAMBDA_INIT = 0.2
ATTN_EPS = 1e-5
GN_EPS = 64e-5


def project_fm(K, X1T, wslot, wslot_buf, bank_rot, evac):
    P = K.P
    X1Th = X1T.handle
    for tg in range(4):
        bk = bank_rot()

        def mm(e, bk=bk, tg=tg):
            ins = None
            for kc in range(KC):
                ins = e.matmul(K.ps[bk][:, :], wslot[:, kc, :], X1Th[:, kc, tg * 512:(tg + 1) * 512],
                               start=(kc == 0), stop=(kc == KC - 1))
            return ins
        P.op("pe", mm, reads=X1T.bufs[tg * 4:(tg + 1) * 4] + [wslot_buf], writes=[K.psb[bk]], tag="proj fm")
        evac(tg, bk)


def attention_stage(K, X1T, win_t, bm, ctab, lamv, ng_t, o_dst, o_buf, base):
    P = K.P
    off = base
    QT = P.sbuf("QT", [128, 4, SEQ], BF16, off, nbufs=4); off += 4 * SEQ * 2
    KT = P.sbuf("KT", [128, 4, SEQ], BF16, off, nbufs=4); off += 4 * SEQ * 2
    VA = P.sbuf("VA", [128, 16, 4, 130], BF16, off, nbufs=16); off += 16 * 4 * 130 * 2
    wr = P.sbuf("wring", [128, 4, KC, 128], BF16, off, nbufs=4); off += 4 * KC * 128 * 2
    bms = P.sbuf("bms", [128, 2, 5, 512], F32, off, nbufs=2); off += 2 * 5 * 512 * 4
    ct = P.sbuf("ctab", [128, 64], F32, off); off += 64 * 4
    lm = P.sbuf("lam", [128, 4 * 64 + 16], F32, off); off += (4 * 64 + 16) * 4
    ngs = P.sbuf("ngs", [128, 512], F32, off); off += 512 * 4
    stm = P.sbuf("stmp", [128, 4, 512], F32, off, nbufs=4); off += 4 * 512 * 4
    pT = P.sbuf("pT", [128, 4, 512], BF16, off, nbufs=4); off += 4 * 512 * 2
    Osb = P.sbuf("Osb", [128, 2, 4, 130], F32, off, nbufs=2); off += 2 * 4 * 130 * 4
    ow = P.sbuf("ow", [128, 4, 4, 128], F32, off, nbufs=1); off += 16 * 128 * 4
    osq = P.sbuf("osq", [128, 4, 128], F32, off, nbufs=1); off += 4 * 128 * 4
    sm = P.sbuf("sm", [128, 32], F32, off, nbufs=1); off += 32 * 4
    owb = P.sbuf("owb", [128, 4, 4, 128], BF16, off, nbufs=1); off += 16 * 128 * 2
    owbh = owb.handle
    QTh, KTh, VAh, wrh, bmh, cth, lmh, ngh, stmh, pTh, Osh, owh, osqh, smh = (
        QT.handle, KT.handle, VA.handle, wr.handle, bms.handle, ct.handle, lm.handle, ngs.handle, stm.handle,
        pT.handle, Osb.handle, ow.handle, osq.handle, sm.handle)
    X1Th = X1T.handle

    load_const(K, "ctab", ct, cth[:, :], ctab[:, :])
    load_const(K, "lamv", lm, lmh[:, 0:256], lamv.rearrange("p a d -> p (a d)"))
    load_const(K, "ngt", ngs, ngh[:, :], ng_t[:, :])
    P.op("pool", lambda e: e.memset(VAh[:, :, :, 128:130], 1.0), reads=[], writes=VA.bufs, tag="va ones")

    P.op("dve", lambda e: e.tensor_tensor(out=lmh[:, 0:64], in0=lmh[:, 0:64], in1=lmh[:, 64:128], op=ALU.mult),
         reads=[], writes=lm.bufs, tag="lam1")
    P.op("dve", lambda e: e.tensor_tensor(out=lmh[:, 128:192], in0=lmh[:, 128:192], in1=lmh[:, 192:256], op=ALU.mult),
         reads=[], writes=lm.bufs, tag="lam2")
    P.op("dve", lambda e: e.tensor_reduce(out=lmh[:, 256:257], in_=lmh[:, 0:64], axis=AX.X, op=ALU.add),
         reads=[], writes=lm.bufs, tag="lam3")
    P.op("dve", lambda e: e.tensor_reduce(out=lmh[:, 257:258], in_=lmh[:, 128:192], axis=AX.X, op=ALU.add),
         reads=[], writes=lm.bufs, tag="lam4")
    P.op("act", lambda e: e.activation(out=lmh[:, 258:260], in_=lmh[:, 256:258], func=AF.Exp),
         reads=[], writes=lm.bufs, tag="lam5")
    P.op("dve", lambda e: e.tensor_tensor(out=lmh[:, 260:261], in0=lmh[:, 258:259], in1=lmh[:, 259:260],
                                          op=ALU.subtract), reads=[], writes=lm.bufs, tag="lam6")
    P.op("dve", lambda e: e.tensor_scalar(out=lmh[:, 261:262], in0=lmh[:, 260:261], scalar1=LAMBDA_INIT, scalar2=None,
                                          op0=ALU.add), reads=[], writes=lm.bufs, tag="lam7")
    LAM = lmh[:, 261:262]

    rot = [0]

    def bank_rot():
        rot[0] += 1
        return rot[0] % 4

    def load_w(ci):
        s = ci % 4
        P.op("pool", lambda e, ci=ci, s=s: e.dma_start(out=wrh[:, s, :, :], in_=win_t[ci]),
             reads=[], writes=[wr.bufs[s]], dma=True, tag="win load")
        return s
    for ci in range(min(3, NCH_ATT)):
        load_w(ci)
    for ci in range(NCH_ATT):
        s = ci % 4
        if ci + 3 < NCH_ATT:
            load_w(ci + 3)
        if ci < 8:
            dstT, dbuf = (QTh, QT) if ci < 4 else (KTh, KT)
            a = ci % 4

            def evac(tg, bk, dstT=dstT, dbuf=dbuf, a=a):
                eng = "act" if tg % 2 == 0 else "dve"

                def ev(e, tg=tg, bk=bk):
                    o = dstT[:, a, tg * 512:(tg + 1) * 512]
                    if eng == "act":
                        return e.activation(out=o, in_=K.ps[bk][:, :], func=AF.Copy)
                    return e.tensor_copy(out=o, in_=K.ps[bk][:, :])
                P.op(eng, ev, reads=[K.psb[bk]], writes=[dbuf.bufs[a]], tag="qk evac")
            project_fm(K, X1T, wrh[:, s], wr.bufs[s], bank_rot, evac)
        else:
            a = ci - 8
            for tq in range(4):
                bk = bank_rot()

                def mm(e, bk=bk, tq=tq, s=s):
                    ins = None
                    for tt in range(4):
                        t = tq * 4 + tt
                        for kc in range(KC):
                            ins = e.matmul(K.ps[bk][:, tt * 128:(tt + 1) * 128], X1Th[:, kc, t * 128:(t + 1) * 128],
                                           wrh[:, s, kc, :], start=(kc == 0 and tt == 0), stop=(kc == KC - 1),
                                           skip_group_check=True)
                    return ins
                P.op("pe", mm, reads=X1T.bufs[tq * 4:(tq + 1) * 4] + [wr.bufs[s]], writes=[K.psb[bk]], tag="v proj")
                P.op("act", lambda e, bk=bk, tq=tq, a=a: e.activation(
                    out=VAh[:, tq * 4:(tq + 1) * 4, a, 0:128], in_=K.ps[bk].rearrange("p (t c) -> p t c", t=4),
                    func=AF.Copy), reads=[K.psb[bk]], writes=VA.bufs[tq * 4:(tq + 1) * 4], tag="v evac")

    SCALE = 64 ** -0.5
    LOOK = 2
    tiles = []
    for a in range(4):
        for qg in range(4):
            nkb = 4 * qg + 4
            for m in range(2):
                for kb in range(nkb):
                    tiles.append((a, qg, m, kb, nkb))

    def front(idx):
        a, qg, m, kb, nkb = tiles[idx]
        bs = a % 2
        if qg == 0 and m == 0 and kb == 0:
            P.op("sp", lambda e, a=a, bs=bs: e.dma_start(out=bmh[:, bs, :, :], in_=bm[a].rearrange("r p q -> p r q")),
                 reads=[], writes=[bms.bufs[bs]], dma=True, tag="bm load")
        pr = slice(m * 64, (m + 1) * 64)
        i = idx % 4
        sb = i
        P.op("pe", lambda e, sb=sb, a=a, pr=pr, kb=kb, qg=qg: e.matmul(
            K.ps[sb][:, :], KTh[pr, a, kb * 128:(kb + 1) * 128], QTh[pr, a, qg * 512:(qg + 1) * 512],
            start=True, stop=True), reads=[KT.bufs[a], QT.bufs[a]], writes=[K.psb[sb]], tag="qk mm")
        r = kb - 4 * qg
        var = 4 if r < 0 else r
        P.op("dve", lambda e, sb=sb, i=i, bs=bs, var=var: e.scalar_tensor_tensor(
            out=stmh[:, i, :], in0=K.ps[sb][:, :], scalar=SCALE, in1=bmh[:, bs, var, :],
            op0=ALU.mult, op1=ALU.add), reads=[K.psb[sb], bms.bufs[bs]], writes=[stm.bufs[i]], tag="score bias")
        cidx = a * 16 + ((qg * 512 - kb * 128 + 384) // 128 if r < 0 else 3)
        P.op("act", lambda e, i=i, cidx=cidx: e.activation(
            out=pTh[:, i, :], in_=stmh[:, i, :], func=AF.Exp, bias=cth[:, cidx:cidx + 1], scale=1.0),
            reads=[stm.bufs[i], ct.bufs[0]], writes=[pT.bufs[i]], tag="exp")

    def back(idx):
        a, qg, m, kb, nkb = tiles[idx]
        i = idx % 4
        ob = (4, 5) if m == 0 else (6, 7)

        def pv(e, i=i, kb=kb, a=a, ob=ob, nkb=nkb):
            ins = None
            for qb in range(4):
                bk = ob[qb // 2]
                c0 = (qb % 2) * 130
                ins = e.matmul(K.ps[bk][:, c0:c0 + 130], pTh[:, i, qb * 128:(qb + 1) * 128],
                               VAh[:, kb, a, :], start=(kb == 0 and qb % 2 == 0), stop=(kb == nkb - 1),
                               skip_group_check=True)
            return ins
        P.op("pe", pv, reads=[pT.bufs[i], VA.bufs[kb]], writes=[K.psb[ob[0]], K.psb[ob[1]]], tag="pv mm")
        if kb != nkb - 1:
            return
        for hb in range(2):
            P.op("act", lambda e, m=m, hb=hb, ob=ob: e.activation(
                out=Osh[:, m, hb * 2:(hb + 1) * 2, :],
                in_=K.ps[ob[hb]][:, 0:260].rearrange("p (q c) -> p q c", q=2), func=AF.Copy),
                reads=[K.psb[ob[hb]]], writes=[Osb.bufs[m]], tag="O evac")
        if m == 0:
            return
        P.op("dve", lambda e: e.reciprocal(out=smh[:, 0:4], in_=Osh[:, 0, :, 128:129].rearrange("p q c -> p (q c)")),
             reads=[Osb.bufs[0]], writes=sm.bufs, tag="r1")
        P.op("dve", lambda e: e.reciprocal(out=smh[:, 4:8], in_=Osh[:, 1, :, 128:129].rearrange("p q c -> p (q c)")),
             reads=[Osb.bufs[1]], writes=sm.bufs, tag="r2")
        P.op("dve", lambda e: e.tensor_scalar(out=smh[:, 4:8], in0=smh[:, 4:8], scalar1=LAM, scalar2=None,
                                              op0=ALU.mult), reads=[lm.bufs[0]], writes=sm.bufs, tag="r2lam")
        P.op("dve", lambda e, a=a: e.tensor_tensor(
            out=owh[:, :, a, :], in0=Osh[:, 0, :, 0:128], in1=smh[:, 0:4].unsqueeze(2).to_broadcast([128, 4, 128]),
            op=ALU.mult), reads=[Osb.bufs[0]], writes=ow.bufs, tag="o1")
        P.op("dve", lambda e: e.tensor_tensor(
            out=osqh[:, :, :], in0=Osh[:, 1, :, 0:128], in1=smh[:, 4:8].unsqueeze(2).to_broadcast([128, 4, 128]),
            op=ALU.mult), reads=[Osb.bufs[1]], writes=osq.bufs, tag="o2")
        P.op("dve", lambda e, a=a: e.tensor_tensor(out=owh[:, :, a, :], in0=owh[:, :, a, :], in1=osqh[:, :, :],
                                                   op=ALU.subtract), reads=[], writes=ow.bufs + osq.bufs, tag="o12")
        P.op("pool", lambda e, a=a: e.tensor_tensor(out=osqh[:, :, :], in0=owh[:, :, a, :], in1=owh[:, :, a, :],
                                                    op=ALU.mult), reads=[], writes=ow.bufs + osq.bufs, tag="osq")
        P.op("dve", lambda e: e.tensor_reduce(out=smh[:, 8:12], in_=osqh[:, :, :], axis=AX.X, op=ALU.add),
             reads=[osq.bufs[0]], writes=sm.bufs, tag="ossq")
        P.op("dve", lambda e: e.tensor_scalar(out=smh[:, 8:12], in0=smh[:, 8:12], scalar1=1.0 / 128, scalar2=ATTN_EPS,
                                              op0=ALU.mult, op1=ALU.add), reads=[], writes=sm.bufs, tag="oms")
        P.op("act", lambda e: e.activation(out=smh[:, 12:16], in_=smh[:, 8:12], func=AF.Sqrt),
             reads=[], writes=sm.bufs, tag="orms")
        P.op("dve", lambda e: e.reciprocal(out=smh[:, 16:20], in_=smh[:, 12:16]), reads=[], writes=sm.bufs,
             tag="orr")
        P.op("dve", lambda e, a=a: e.tensor_tensor(
            out=owh[:, :, a, :], in0=owh[:, :, a, :], in1=smh[:, 16:20].unsqueeze(2).to_broadcast([128, 4, 128]),
            op=ALU.mult), reads=[], writes=ow.bufs + sm.bufs, tag="onorm")
        P.op("dve", lambda e, a=a: e.scalar_tensor_tensor(
            out=owbh[:, :, a, :], in0=owh[:, :, a, :], scalar=1.0 - LAMBDA_INIT,
            in1=ngh[:, a * 128:(a + 1) * 128].unsqueeze(1).to_broadcast([128, 4, 128]),
            op0=ALU.mult, op1=ALU.mult), reads=[ngs.bufs[0]] + ow.bufs, writes=owb.bufs, tag="og")
        for qb in range(4):
            t0 = qg * 512 + qb * 128
            P.op("sp", lambda e, qb=qb, t0=t0, a=a: e.dma_start(
                out=o_dst[t0:t0 + 128, a * 128:(a + 1) * 128], in_=owbh[:, qb, a, :]),
                reads=owb.bufs, writes=[o_buf], dma=True, tag="o_attn out")

    for idx in range(len(tiles) + LOOK):
        if idx < len(tiles):
            front(idx)
        if idx >= LOOK:
            back(idx - LOOK)
    for al in (QT, KT, VA, wr, bms, ct, lm, ngs, stm, pT, Osb, ow, osq, sm, owb):
        P.free(al)


def host_tile_ffn_w(w):
    return np.ascontiguousarray(w.reshape(KC, 128, NG, FG).transpose(2, 1, 0, 3))


def host_win_cols(half):
    cols = []
    for blk in range(3):
        for a in range(4):
            head = 4 * half + a
            cols.append(np.arange(blk * 1024 + head * 128, blk * 1024 + head * 128 + 128))
    for blk in range(3):
        for cc in range(4):
            c0 = 3072 + blk * 1024 + (8 * half + 2 * cc) * 64
            cols.append(np.arange(c0, c0 + 128))
    cols.append(np.arange(6144, 6144 + 128))
    cols.append(np.arange(6144 + 128, 6144 + 256))
    cols.append(np.arange(6144 + 256, 6144 + 288))
    return cols


def host_tile_win(w_in, half):
    cols = host_win_cols(half)
    out = np.zeros((len(cols), 128, KC, 128), np.float32)
    for ci, c in enumerate(cols):
        blk = w_in[:, c]
        out[ci, :, :, :len(c)] = blk.reshape(KC, 128, len(c)).transpose(1, 0, 2)
    return out


def host_attn_consts(half):
    bm = np.zeros((4, 5, 128, 512), np.float32)
    ctab = np.zeros((128, 64), np.float32)
    i = np.arange(128)[:, None].astype(np.float64)
    j = np.arange(512)[None, :].astype(np.float64)
    for a in range(4):
        slope = 2.0 ** (-(4 * half + a + 1))
        for r in range(4):
            d = j - i - 128 * r
            bm[a, r] = np.where(d >= 0, -slope * d, NEG)
        bm[a, 4] = -slope * (j - i)
        for idx in range(16):
            ctab[:, a * 16 + idx] = -slope * (idx * 128 - 384)
    return bm, ctab


C0 = float(np.exp(-0.5))
CH = 64
NCHUNK = SEQ // CH


def rwkv_alloc_persist(K, base):
    P = K.P
    R = Ctx()
    off = base
    R.AR = P.sbuf("AR", [128, 4, NCHUNK, 2, CH], BF16, off, nbufs=NCHUNK); off += 4 * NCHUNK * 2 * CH * 2
    R.BK = P.sbuf("BK", [128, 4, NCHUNK, 2, CH], BF16, off, nbufs=NCHUNK); off += 4 * NCHUNK * 2 * CH * 2
    R.VC = P.sbuf("VC", [128, 4, SEQ], BF16, off, nbufs=NCHUNK); off += 4 * SEQ * 2
    R.LW = P.sbuf("LW", [128, SEQ], BF16, off, nbufs=4); off += SEQ * 2
    R.SG = P.sbuf("SG", [128, 2, SEQ], BF16, off, nbufs=4); off += 2 * SEQ * 2
    R.GC = P.sbuf("GC", [128, 4, NCHUNK], F32, off, nbufs=4); off += 4 * NCHUNK * 4
    R.BON = P.sbuf("BON", [64, NCHUNK, 8], F32, off, nbufs=4); off += NCHUNK * 8 * 4
    R.pcol = P.sbuf("pcol", [128, 64], F32, off); off += 256
    R.w2a2 = P.sbuf("w2a2", [128, 512], BF16, off); off += 1024
    R.g2 = P.sbuf("g2", [128, 2, 512], BF16, off); off += 2048
    R.bones = P.sbuf("bones", [128, 128], BF16, off); off += 256
    R.hsel = P.sbuf("hsel", [128, 16], BF16, off); off += 32
    R.top = off
    return R


def rwkv_prep(K, R, X1T, win_t, pcol_d, w2a2_d, g2_d, bones_d, hsel_d, base):
    P = K.P
    off = base
    wr = P.sbuf("wring2", [128, 3, KC, 128], BF16, off, nbufs=3); off += 3 * KC * 128 * 2
    ones = P.sbuf("ones", [128, 512], F32, off); off += 2048
    names = ["rm", "km", "sg", "av", "kk", "nrm", "ka", "kp", "cum", "cx", "dd"]
    T = {}
    for n in names:
        T[n] = P.sbuf(n, [128, 512], F32, off); off += 2048
    pre = P.sbuf("pre", [128, 3, 520], F32, off, nbufs=3); off += 3 * 520 * 4
    sq = P.sbuf("sq", [128, 512], BF16, off); off += 1024
    rb = P.sbuf("rb", [128, 512], BF16, off); off += 1024
    cb = P.sbuf("cb", [128, 16], F32, off); off += 64
    h = {n: T[n].handle for n in names}
    b = {n: T[n].bufs[0] for n in names}
    wrh, preh, sqh, rbh, cbh, onesh = wr.handle, pre.handle, sq.handle, rb.handle, cb.handle, ones.handle
    ARh, BKh, VCh, LWh, SGh, GCh, BONh, pc, w2h, g2h, boh, hsh = (
        R.AR.handle, R.BK.handle, R.VC.handle, R.LW.handle, R.SG.handle, R.GC.handle, R.BON.handle, R.pcol.handle,
        R.w2a2.handle, R.g2.handle, R.bones.handle, R.hsel.handle)
    X1Th = X1T.handle

    load_const(K, "pcol", R.pcol, pc[:, :], pcol_d[:, :])
    load_const(K, "w2a2", R.w2a2, w2h[:, :], w2a2_d[:, :], q="pool")
    load_const(K, "g2", R.g2, g2h[:, :, :], g2_d[:, :, :], q="pool")
    load_const(K, "bones", R.bones, boh[:, :], bones_d[:, :], q="pool")
    load_const(K, "hsel", R.hsel, hsh[:, :], hsel_d[:, :], q="pool")
    P.op("pool", lambda e: e.memset(onesh[:, :], 1.0), reads=[], writes=ones.bufs, tag="ones")
    P.op("pool", lambda e: e.memset(SGh[:, 1, :], 0.0), reads=[], writes=R.SG.bufs, tag="sg2 zero")

    rot = [0]

    def bank_rot():
        rot[0] += 1
        return rot[0] % 8

    def load_w(ci, s):
        P.op("pool", lambda e, ci=ci, s=s: e.dma_start(out=wrh[:, s, :, :], in_=win_t[NCH_ATT + ci]),
             reads=[], writes=[wr.bufs[s]], dma=True, tag="win2 load")

    def proj_mix(ci, s, st, tq, out_fn):
        bk = bank_rot()

        def mm(e, bk=bk, tq=tq, s=s):
            ins = None
            for kc in range(KC):
                ins = e.matmul(K.ps[bk][:, :], wrh[:, s, kc, :], X1Th[:, kc, tq * 512:(tq + 1) * 512],
                               start=(kc == 0), stop=(kc == KC - 1))
            return ins
        P.op("pe", mm, reads=X1T.bufs[tq * 4:(tq + 1) * 4] + [wr.bufs[s]], writes=[K.psb[bk]], tag="proj rw")
        if tq == 0:
            P.op("pool", lambda e, st=st: e.memset(preh[:, st, 0:1], 0.0), reads=[], writes=[pre.bufs[st]], tag="carry0")
        P.op("act", lambda e, bk=bk, st=st: e.activation(out=preh[:, st, 1:513], in_=K.ps[bk][:, :], func=AF.Copy),
             reads=[K.psb[bk]], writes=[pre.bufs[st]], tag="pre evac")
        P.op("dve", lambda e, st=st: e.tensor_tensor(out=h["dd"][:, :], in0=preh[:, st, 0:512], in1=preh[:, st, 1:513],
                                                     op=ALU.subtract), reads=[pre.bufs[st]], writes=[b["dd"]], tag="mix d")
        out_fn(preh[:, st, 1:513], pre.bufs[st])
        if tq < 3:
            P.op("act", lambda e, st=st: e.activation(out=preh[:, st, 0:1], in_=preh[:, st, 512:513], func=AF.Copy),
                 reads=[], writes=[pre.bufs[st]], tag="carry")

    def mixed_to(out_ap, out_bufs, mucol, eng="dve"):
        def f(pre1, prebuf):
            P.op("dve", lambda e: e.scalar_tensor_tensor(out=out_ap, in0=h["dd"][:, :], scalar=pc[:, mucol:mucol + 1],
                                                         in1=pre1, op0=ALU.mult, op1=ALU.add),
                 reads=[b["dd"], prebuf, R.pcol.bufs[0]], writes=out_bufs, tag="mix out")
        return f

    for li, ci in enumerate((12, 13, 14)):
        load_w(ci, li)
    for tq in range(4):
        tsl = slice(tq * 512, (tq + 1) * 512)
        proj_mix(12, 0, 0, tq, mixed_to(h["rm"][:, :], [b["rm"]], 12))
        P.op("act", lambda e, tsl=tsl: e.activation(out=LWh[0:64, tsl], in_=h["rm"][0:64, :], func=AF.Tanh),
             reads=[b["rm"]], writes=[R.LW.bufs[tq]], tag="tanh wd")
        P.op("dve", lambda e, tsl=tsl: e.tensor_copy(out=LWh[64:128, tsl], in_=h["rm"][64:128, :]),
             reads=[b["rm"]], writes=[R.LW.bufs[tq]], tag="copy ad")
        proj_mix(13, 1, 1, tq, mixed_to(h["km"][:, :], [b["km"]], 13))
        P.op("act", lambda e, tsl=tsl: e.activation(out=SGh[:, 0, tsl], in_=h["km"][:, :], func=AF.Sigmoid),
             reads=[b["km"]], writes=[R.SG.bufs[tq]], tag="sig gd")
        proj_mix(14, 2, 2, tq, mixed_to(h["sg"][:, :], [b["sg"]], 14))
        P.op("act", lambda e, tsl=tsl: e.activation(out=SGh[0:32, 1, tsl], in_=h["sg"][0:32, :], func=AF.Sigmoid),
             reads=[b["sg"]], writes=[R.SG.bufs[tq]], tag="sig gd2")

    for cc in range(4):
        for st in range(3):
            load_w(st * 4 + cc, st)
        csl = slice(cc * 128, (cc + 1) * 128)
        for tq in range(4):
            tsl = slice(tq * 512, (tq + 1) * 512)
            jsl = slice(tq * 8, (tq + 1) * 8)
            cbufs = R.AR.bufs[tq * 8:(tq + 1) * 8]
            kbufs = R.BK.bufs[tq * 8:(tq + 1) * 8]
            proj_mix(cc, 0, 0, tq, mixed_to(h["rm"][:, :], [b["rm"]], cc))
            proj_mix(4 + cc, 1, 1, tq, mixed_to(h["km"][:, :], [b["km"]], 4 + cc))
            proj_mix(8 + cc, 2, 2, tq, mixed_to(VCh[:, cc, tsl], R.VC.bufs[tq * 8:(tq + 1) * 8], 8 + cc))
            bz, ba = bank_rot(), bank_rot()
            P.op("pe", lambda e, bz=bz, csl=csl, tsl=tsl: e.matmul(K.ps[bz][:, :], w2h[0:64, csl], LWh[0:64, tsl],
                                                                   start=True, stop=True),
                 reads=[R.w2a2.bufs[0], R.LW.bufs[tq]], writes=[K.psb[bz]], tag="w lora")
            P.op("pe", lambda e, ba=ba, csl=csl, tsl=tsl: e.matmul(K.ps[ba][:, :], w2h[64:128, csl], LWh[64:128, tsl],
                                                                   start=True, stop=True),
                 reads=[R.w2a2.bufs[0], R.LW.bufs[tq]], writes=[K.psb[ba]], tag="a lora")
            P.op("act", lambda e, bz=bz, cc=cc: e.activation(out=h["sg"][:, :], in_=K.ps[bz][:, :], func=AF.Sigmoid,
                                                            bias=pc[:, 15 + cc:16 + cc]),
                 reads=[K.psb[bz], R.pcol.bufs[0]], writes=[b["sg"]], tag="sig w")
            P.op("act", lambda e, ba=ba, cc=cc: e.activation(out=h["av"][:, :], in_=K.ps[ba][:, :], func=AF.Sigmoid,
                                                            bias=pc[:, 19 + cc:20 + cc]),
                 reads=[K.psb[ba], R.pcol.bufs[0]], writes=[b["av"]], tag="sig a")
            P.op("dve", lambda e, cc=cc: e.tensor_scalar(out=h["kk"][:, :], in0=h["km"][:, :],
                                                        scalar1=pc[:, 23 + cc:24 + cc], scalar2=None, op0=ALU.mult),
                 reads=[b["km"], R.pcol.bufs[0]], writes=[b["kk"]], tag="kkraw")
            P.op("act", lambda e: e.activation(out=sqh[:, :], in_=h["kk"][:, :], func=AF.Square),
                 reads=[b["kk"]], writes=sq.bufs, tag="kk sq")
            bn_ = bank_rot()
            P.op("pe", lambda e, bn_=bn_: e.matmul(K.ps[bn_][:, :], boh[:, :], sqh[:, :], start=True, stop=True),
                 reads=[R.bones.bufs[0], sq.bufs[0]], writes=[K.psb[bn_]], tag="ssq mm")
            P.op("act", lambda e, bn_=bn_: e.activation(out=h["nrm"][:, :], in_=K.ps[bn_][:, :], func=AF.Sqrt),
                 reads=[K.psb[bn_]], writes=[b["nrm"]], tag="nrm sqrt")
            P.op("dve", lambda e: e.tensor_scalar(out=h["nrm"][:, :], in0=h["nrm"][:, :], scalar1=1e-12, scalar2=None,
                                                  op0=ALU.max), reads=[], writes=[b["nrm"]], tag="nrm max")
            P.op("dve", lambda e: e.reciprocal(out=h["nrm"][:, :], in_=h["nrm"][:, :]), reads=[], writes=[b["nrm"]],
                 tag="nrm rcp")
            P.op("dve", lambda e: e.tensor_tensor(out=h["kk"][:, :], in0=h["kk"][:, :], in1=h["nrm"][:, :], op=ALU.mult),
                 reads=[b["nrm"]], writes=[b["kk"]], tag="kk")
            P.op("pool", lambda e: e.tensor_tensor(out=h["ka"][:, :], in0=h["kk"][:, :], in1=h["av"][:, :], op=ALU.mult),
                 reads=[b["kk"], b["av"]], writes=[b["ka"]], tag="ka")
            P.op("dve", lambda e, cc=cc: e.tensor_scalar(out=h["kp"][:, :], in0=h["av"][:, :], scalar1=-1.0,
                                                        scalar2=pc[:, 27 + cc:28 + cc], op0=ALU.add, op1=ALU.mult),
                 reads=[b["av"], R.pcol.bufs[0]], writes=[b["kp"]], tag="kp1")
            P.op("dve", lambda e: e.scalar_tensor_tensor(out=h["kp"][:, :], in0=h["kp"][:, :], scalar=1.0,
                                                         in1=h["km"][:, :], op0=ALU.add, op1=ALU.mult),
                 reads=[b["km"]], writes=[b["kp"]], tag="kp2")
            P.op("dve", lambda e, cc=cc: e.scalar_tensor_tensor(out=rbh[:, :], in0=h["rm"][:, :],
                                                               scalar=pc[:, 31 + cc:32 + cc], in1=h["kp"][:, :],
                                                               op0=ALU.mult, op1=ALU.mult),
                 reads=[b["rm"], b["kp"], R.pcol.bufs[0]], writes=rb.bufs, tag="rb")
            bb_ = bank_rot()

            def bon_mm(e, bb_=bb_):
                ins = None
                for jj in range(8):
                    ins = e.matmul(K.ps[bb_][0:64, jj * 2:jj * 2 + 2], rbh[:, jj * 64:(jj + 1) * 64], hsh[:, 0:2],
                                   start=(jj == 0), stop=True, skip_group_check=True)
                return ins
            P.op("pe", bon_mm, reads=[rb.bufs[0], R.hsel.bufs[0]], writes=[K.psb[bb_]], tag="bonus mm")
            P.op("dve", lambda e, bb_=bb_, jsl=jsl, cc=cc: e.tensor_copy(
                out=BONh[:, jsl, cc * 2:cc * 2 + 2], in_=K.ps[bb_][0:64, 0:16].rearrange("p (j c) -> p j c", c=2)),
                reads=[K.psb[bb_]], writes=[R.BON.bufs[tq]], tag="bonus evac")
            if tq == 0:
                P.op("dve", lambda e: e.tensor_tensor_scan(out=h["cum"][:, :], data0=onesh[:, :], data1=h["sg"][:, :],
                                                           initial=0.0, op0=ALU.mult, op1=ALU.add),
                     reads=[ones.bufs[0], b["sg"]], writes=[b["cum"]], tag="scan")
                P.op("pool", lambda e: e.memset(cbh[:, 0:1], 0.0), reads=[], writes=cb.bufs, tag="cb0")
            else:
                P.op("dve", lambda e: e.tensor_tensor_scan(out=h["cum"][:, :], data0=onesh[:, :], data1=h["sg"][:, :],
                                                           initial=cbh[:, 8:9], op0=ALU.mult, op1=ALU.add),
                     reads=[ones.bufs[0], b["sg"], cb.bufs[0]], writes=[b["cum"]], tag="scan")
                P.op("act", lambda e: e.activation(out=cbh[:, 0:1], in_=cbh[:, 8:9], func=AF.Copy),
                     reads=[], writes=cb.bufs, tag="cb carry")
            cum3 = h["cum"].rearrange("p (j t) -> p j t", t=CH)
            P.op("act", lambda e, cum3=cum3: e.activation(out=cbh[:, 1:9], in_=cum3[:, :, CH - 1], func=AF.Copy),
                 reads=[b["cum"]], writes=cb.bufs, tag="cb ends")
            P.op("dve", lambda e, cum3=cum3: e.tensor_tensor(out=cum3, in0=cum3,
                                                             in1=cbh[:, 0:8].unsqueeze(2).to_broadcast([128, 8, CH]),
                                                             op=ALU.subtract),
                 reads=[cb.bufs[0]], writes=[b["cum"]], tag="cumrel")
            P.op("pool", lambda e: e.tensor_tensor(out=h["cx"][:, :], in0=h["cum"][:, :], in1=h["sg"][:, :],
                                                   op=ALU.subtract), reads=[b["cum"], b["sg"]], writes=[b["cx"]],
                 tag="cumex")
            P.op("act", lambda e, cum3=cum3, cc=cc, jsl=jsl: e.activation(out=GCh[:, cc, jsl], in_=cum3[:, :, CH - 1],
                                                                         func=AF.Exp, scale=-C0),
                 reads=[b["cum"]], writes=[R.GC.bufs[cc]], tag="gammaC")
            P.op("act", lambda e: e.activation(out=h["av"][:, :], in_=h["cum"][:, :], func=AF.Exp, scale=-C0),
                 reads=[b["cum"]], writes=[b["av"]], tag="Eg")
            P.op("act", lambda e: e.activation(out=h["km"][:, :], in_=h["cx"][:, :], func=AF.Exp, scale=-C0),
                 reads=[b["cx"]], writes=[b["km"]], tag="Egx")
            P.op("act", lambda e: e.activation(out=h["sg"][:, :], in_=h["cum"][:, :], func=AF.Exp, scale=C0),
                 reads=[b["cum"]], writes=[b["sg"]], tag="Ei")

            def v3(x):
                return x.rearrange("p (j t) -> p j t", t=CH)
            P.op("dve", lambda e, cc=cc, jsl=jsl: e.scalar_tensor_tensor(
                out=ARh[:, cc, jsl, 0, :], in0=v3(h["kk"]), scalar=-1.0, in1=v3(h["km"]), op0=ALU.mult, op1=ALU.mult),
                reads=[b["kk"], b["km"]], writes=cbufs, tag="A~")
            P.op("pool", lambda e, cc=cc, jsl=jsl: e.tensor_tensor(
                out=ARh[:, cc, jsl, 1, :], in0=v3(h["rm"]), in1=v3(h["av"]), op=ALU.mult),
                reads=[b["rm"], b["av"]], writes=cbufs, tag="R~")
            P.op("dve", lambda e, cc=cc, jsl=jsl: e.tensor_tensor(
                out=BKh[:, cc, jsl, 0, :], in0=v3(h["ka"]), in1=v3(h["sg"]), op=ALU.mult),
                reads=[b["ka"], b["sg"]], writes=kbufs, tag="B~")
            P.op("pool", lambda e, cc=cc, jsl=jsl: e.tensor_tensor(
                out=BKh[:, cc, jsl, 1, :], in0=v3(h["kp"]), in1=v3(h["sg"]), op=ALU.mult),
                reads=[b["kp"], b["sg"]], writes=kbufs, tag="K~")
    for al in [wr, ones, pre, sq, rb, cb] + [T[n] for n in names]:
        P.free(al)


def rwkv_chunks(K, R, masks_d, gnb_d, o_dst, o_buf, base):
    P = K.P
    off = base
    mk = P.sbuf("masks", [64, 4, 128], F32, off); off += 4 * 128 * 4
    gnb = P.sbuf("gnb", [64, 2, 512], F32, off); off += 2 * 512 * 4
    M1 = P.sbuf("M1", [64, 2, 8, 128], BF16, off, nbufs=2); off += 2 * 8 * 128 * 2
    M2 = P.sbuf("M2", [64, 2, 8, 128], BF16, off, nbufs=2); off += 2 * 8 * 128 * 2
    M3 = P.sbuf("M3", [64, 2, 8, 64], BF16, off, nbufs=2); off += 2 * 8 * 64 * 2
    NL = P.sbuf("NL", [64, 2, 2, 8, 64], BF16, off, nbufs=4); off += 2 * 2 * 8 * 64 * 2
    PP = P.sbuf("PP", [64, 2, 8, 64], BF16, off, nbufs=2); off += 2 * 8 * 64 * 2
    BKh_ = P.sbuf("BKhat", [128, 2, 4, 2, CH], BF16, off, nbufs=2); off += 2 * 4 * 2 * CH * 2
    TOK = P.sbuf("TOK", [64, 2, 4, 512], BF16, off, nbufs=2); off += 2 * 4 * 512 * 2
    WTs = P.sbuf("WTs", [128, 2, 4, CH], BF16, off, nbufs=2); off += 2 * 4 * CH * 2
    MAK = P.sbuf("MAK", [64, 2, 8, 64], BF16, off, nbufs=2); off += 2 * 8 * 64 * 2
    Ub = P.sbuf("Ub", [64, 2, 8, 64], BF16, off, nbufs=2); off += 2 * 8 * 64 * 2
    Hf = P.sbuf("Hf", [128, 4, 64], F32, off); off += 4 * 64 * 4
    Hb = P.sbuf("Hb", [128, 4, 64], BF16, off); off += 4 * 64 * 2
    Ys = P.sbuf("Ys", [64, 2, 512], F32, off, nbufs=2); off += 2 * 512 * 4
    Yq = P.sbuf("Yq", [64, 512], F32, off); off += 512 * 4
    Gs = P.sbuf("Gs", [64, 512], F32, off); off += 512 * 4
    st = P.sbuf("gst", [64, 64], F32, off); off += 64 * 4
    Yo = P.sbuf("Yo", [64, 2, 512], BF16, off, nbufs=2); off += 2 * 512 * 2
    Yoh = Yo.handle
    mkh, gnh, M1h, M2h, M3h, NLh, PPh, BHh, TOKh, WTh, MAKh, Ubh, Hfh, Hbh, Ysh, Yqh, Gsh, sth = (
        mk.handle, gnb.handle, M1.handle, M2.handle, M3.handle, NL.handle, PP.handle, BKh_.handle, TOK.handle,
        WTs.handle, MAK.handle, Ub.handle, Hf.handle, Hb.handle, Ys.handle, Yq.handle, Gs.handle, st.handle)
    ARh, BKh, VCh, SGh, GCh, BONh, g2h = (R.AR.handle, R.BK.handle, R.VC.handle, R.SG.handle, R.GC.handle,
                                         R.BON.handle, R.g2.handle)
    ident = K.ident
    load_const(K, "masks", mk, mkh[:, :, :], masks_d[:, 0:4, :])
    load_const(K, "gnb", gnb, gnh[:, :, :], gnb_d[:, :, :])
    P.op("pool", lambda e: e.memset(Hfh[:, :, :], 0.0), reads=[], writes=Hf.bufs, tag="H0")
    P.op("pool", lambda e: e.memset(Hbh[:, :, :], 0.0), reads=[], writes=Hb.bufs, tag="H0b")

    rot = [0]

    def nb():
        rot[0] += 1
        return rot[0] % 8

    def pr(hh):
        return slice(64 * hh, 64 * hh + 64)

    def ps3(bk, n, w):
        return K.ps[bk][0:64, 0:n * w].rearrange("p (n w) -> p n w", w=w)

    for j in range(getattr(K, 'nchunk_override', NCHUNK)):
        s = j % 2
        ab, kb_, vb = R.AR.bufs[j], R.BK.bufs[j], R.VC.bufs[j]
        tq = j // 8
        for hh in range(2):
            b1, b2, b3 = nb(), nb(), nb()

            def mm1(e, b1=b1, hh=hh, j=j):
                ins = None
                for cc in range(4):
                    ins = e.matmul(K.ps[b1][0:64, cc * 128:(cc + 1) * 128], ARh[pr(hh), cc, j, 0, :],
                                   BKh[pr(hh), cc, j, :, :].rearrange("p a t -> p (a t)"), start=(cc == 0), stop=True,
                                   skip_group_check=True)
                return ins
            P.op("pe", mm1, reads=[ab, kb_], writes=[K.psb[b1]], tag="SA mm")
            P.op("dve", lambda e, b1=b1, hh=hh, s=s: e.tensor_tensor(
                out=M1h[:, s, hh * 4:(hh + 1) * 4, :], in0=ps3(b1, 4, 128),
                in1=mkh[:, 0:1, :].to_broadcast([64, 4, 128]), op=ALU.mult),
                reads=[K.psb[b1], mk.bufs[0]], writes=[M1.bufs[s]], tag="M1 evac")

            def mm2(e, b2=b2, hh=hh, j=j):
                ins = None
                for cc in range(4):
                    ins = e.matmul(K.ps[b2][0:64, cc * 128:(cc + 1) * 128], BKh[pr(hh), cc, j, 0, :],
                                   ARh[pr(hh), cc, j, :, :].rearrange("p a t -> p (a t)"), start=(cc == 0), stop=True,
                                   skip_group_check=True)
                return ins
            P.op("pe", mm2, reads=[ab, kb_], writes=[K.psb[b2]], tag="SB mm")
            P.op("dve", lambda e, b2=b2, hh=hh, s=s: e.tensor_tensor(
                out=M2h[:, s, hh * 4:(hh + 1) * 4, :], in0=ps3(b2, 4, 128),
                in1=mkh[:, 1:2, :].to_broadcast([64, 4, 128]), op=ALU.mult),
                reads=[K.psb[b2], mk.bufs[0]], writes=[M2.bufs[s]], tag="M2 evac")

            def mm3(e, b3=b3, hh=hh, j=j):
                ins = None
                for cc in range(4):
                    ins = e.matmul(K.ps[b3][0:64, cc * 64:(cc + 1) * 64], BKh[pr(hh), cc, j, 1, :],
                                   ARh[pr(hh), cc, j, 1, :], start=(cc == 0), stop=True, skip_group_check=True)
                return ins
            P.op("pe", mm3, reads=[ab, kb_], writes=[K.psb[b3]], tag="SK mm")
            P.op("dve", lambda e, b3=b3, hh=hh, s=s: e.tensor_tensor(
                out=M3h[:, s, hh * 4:(hh + 1) * 4, :], in0=ps3(b3, 4, 64),
                in1=mkh[:, 2:3, 0:64].to_broadcast([64, 4, 64]), op=ALU.mult),
                reads=[K.psb[b3], mk.bufs[0]], writes=[M3.bufs[s]], tag="M3 evac")

        if getattr(K, 'chunk_cut', 99) <= 1:
            continue
        P.op("pool", lambda e, s=s: e.tensor_tensor(out=PPh[:, 0, :, :], in0=M2h[:, s, :, 0:64],
                                                    in1=mkh[:, 3:4, 0:64].to_broadcast([64, 8, 64]), op=ALU.add),
             reads=[M2.bufs[s], mk.bufs[0]], writes=[PP.bufs[0]], tag="P0")
        Ncur = lambda hd, s=s: M2h[:, s, hd, 0:64]
        Lcur = lambda hd, s=s: M1h[:, s, hd, 0:64]
        ncur_buf, lcur_buf = M2.bufs[s], M1.bufs[s]
        pc_ = 0
        for lev in range(1, 6):
            sl = lev % 2
            bl = nb()
            bn2 = nb() if lev < 5 else None

            def mmL(e, bl=bl, Ncur=Ncur, Lcur=Lcur):
                ins = None
                for hd in range(8):
                    ins = e.matmul(K.ps[bl][0:64, hd * 64:(hd + 1) * 64], Ncur(hd), Lcur(hd), start=(hd == 0), stop=True,
                                   skip_group_check=True)
                return ins
            P.op("pe", mmL, reads=[ncur_buf, lcur_buf], writes=[K.psb[bl]], tag="L sq")
            P.op("act", lambda e, bl=bl, sl=sl: e.activation(out=NLh[:, sl, 1, :, :], in_=ps3(bl, 8, 64), func=AF.Copy),
                 reads=[K.psb[bl]], writes=[NL.bufs[sl * 2 + 1]], tag="L evac")
            if lev < 5:
                def mmN(e, bn2=bn2, Ncur=Ncur, Lcur=Lcur):
                    ins = None
                    for hd in range(8):
                        ins = e.matmul(K.ps[bn2][0:64, hd * 64:(hd + 1) * 64], Lcur(hd), Ncur(hd), start=(hd == 0),
                                       stop=True, skip_group_check=True)
                    return ins
                P.op("pe", mmN, reads=[ncur_buf, lcur_buf], writes=[K.psb[bn2]], tag="N sq")
                P.op("dve", lambda e, bn2=bn2, sl=sl: e.tensor_copy(out=NLh[:, sl, 0, :, :], in_=ps3(bn2, 8, 64)),
                     reads=[K.psb[bn2]], writes=[NL.bufs[sl * 2 + 0]], tag="N evac")
            bp = nb()

            def mmP(e, bp=bp, sl=sl, pc_=pc_):
                ins = None
                for hd in range(8):
                    ins = e.matmul(K.ps[bp][0:64, hd * 64:(hd + 1) * 64], NLh[:, sl, 1, hd, :], PPh[:, pc_, hd, :],
                                   start=(hd == 0), stop=True, skip_group_check=True)
                return ins
            P.op("pe", mmP, reads=[NL.bufs[sl * 2 + 1], PP.bufs[pc_]], writes=[K.psb[bp]], tag="P mm")
            pn = 1 - pc_
            P.op("dve", lambda e, bp=bp, pc_=pc_, pn=pn: e.tensor_tensor(out=PPh[:, pn, :, :], in0=ps3(bp, 8, 64),
                                                                        in1=PPh[:, pc_, :, :], op=ALU.add),
                 reads=[K.psb[bp], PP.bufs[pc_]], writes=[PP.bufs[pn]], tag="P add")
            pc_ = pn
            Ncur = lambda hd, sl=sl: NLh[:, sl, 0, hd, :]
            Lcur = lambda hd, sl=sl: NLh[:, sl, 1, hd, :]
            ncur_buf, lcur_buf = NL.bufs[sl * 2 + 0], NL.bufs[sl * 2 + 1]
        TT = lambda hd, pc_=pc_: PPh[:, pc_, hd, :]
        tt_buf = PP.bufs[pc_]

        if getattr(K, 'chunk_cut', 99) <= 2:
            continue
        P.op("pool", lambda e, s=s, j=j: e.tensor_tensor(
            out=BHh[:, s, :, :, :], in0=BKh[:, :, j, :, :],
            in1=GCh[:, :, j:j + 1].unsqueeze(3).to_broadcast([128, 4, 2, CH]), op=ALU.mult),
            reads=[kb_, R.GC.bufs[0], R.GC.bufs[1], R.GC.bufs[2], R.GC.bufs[3]], writes=[BKh_.bufs[s]], tag="BKhat")
        for g2_ in range(2):
            bt = nb()
            ptb = K.ps[bt].bitcast(BF16)

            def trs(e, ptb=ptb, g2_=g2_, s=s, j=j):
                ins = None
                for kk_ in range(2):
                    kind = g2_ * 2 + kk_
                    for cc in range(4):
                        if kind == 0:
                            src = ARh[:, cc, j, 0, :]
                        elif kind == 1:
                            src = BHh[:, s, cc, 0, :]
                        elif kind == 2:
                            src = BHh[:, s, cc, 1, :]
                        else:
                            src = VCh[:, cc, j * CH:(j + 1) * CH]
                        ins = e.transpose(ptb[0:64, kk_ * 512 + cc * 128: kk_ * 512 + (cc + 1) * 128], src, ident[:, :])
                return ins
            P.op("pe", trs, reads=[ab, BKh_.bufs[s], vb, K.ident_buf], writes=[K.psb[bt]], tag="tok transposes")
            eng = "act" if g2_ == 0 else "dve"

            def tev(e, ptb=ptb, g2_=g2_, s=s, eng=eng):
                o = TOKh[:, s, g2_ * 2:(g2_ + 1) * 2, :]
                i = ptb[0:64, :].rearrange("p (k c) -> p k c", k=2)
                if eng == "act":
                    return e.activation(out=o, in_=i, func=AF.Copy)
                return e.tensor_copy(out=o, in_=i)
            P.op(eng, tev, reads=[K.psb[bt]], writes=[TOK.bufs[s]], tag="tok evac")

        if getattr(K, 'chunk_cut', 99) <= 3:
            continue
        bw = nb()

        def mmW(e, bw=bw, s=s, TT=TT):
            ins = None
            for hp in range(8):
                cc = hp % 4
                ins = e.matmul(K.ps[bw][:, hp * 64:(hp + 1) * 64], TOKh[:, s, 0, cc * 128:(cc + 1) * 128], TT(hp),
                               start=(hp == 0), stop=True, skip_group_check=True)
            return ins
        P.op("pe", mmW, reads=[TOK.bufs[s], tt_buf], writes=[K.psb[bw]], tag="WT mm")
        for hh in range(2):
            eng = "act" if hh == 0 else "dve"

            def wev(e, bw=bw, hh=hh, s=s, eng=eng):
                o = WTh[pr(hh), s, :, :]
                i = K.ps[bw][pr(hh), hh * 256:(hh + 1) * 256].rearrange("p (c t) -> p c t", c=4)
                if eng == "act":
                    return e.activation(out=o, in_=i, func=AF.Copy)
                return e.tensor_copy(out=o, in_=i)
            P.op(eng, wev, reads=[K.psb[bw]], writes=[WTs.bufs[s]], tag="WT evac")
        bm_ = nb()

        def mmM(e, bm_=bm_, s=s, TT=TT):
            ins = None
            for hp in range(8):
                ins = e.matmul(K.ps[bm_][0:64, hp * 64:(hp + 1) * 64], M1h[:, s, hp, 64:128], TT(hp),
                               start=(hp == 0), stop=True, skip_group_check=True)
            return ins
        P.op("pe", mmM, reads=[M1.bufs[s], tt_buf], writes=[K.psb[bm_]], tag="MAK mm")
        P.op("act", lambda e, bm_=bm_, s=s: e.activation(out=MAKh[:, s, :, :], in_=ps3(bm_, 8, 64), func=AF.Copy),
             reads=[K.psb[bm_]], writes=[MAK.bufs[s]], tag="MAK evac")

        if getattr(K, 'chunk_cut', 99) <= 4:
            continue
        Vt = lambda hp, s=s: TOKh[:, s, 3, (hp % 4) * 128 + (hp // 4) * 64:(hp % 4) * 128 + (hp // 4) * 64 + 64]
        for hh in range(2):
            bu = nb()

            def mmU1(e, bu=bu, hh=hh, s=s, Vt=Vt):
                ins = None
                for cc in range(4):
                    hp = hh * 4 + cc
                    ins = e.matmul(K.ps[bu][0:64, cc * 64:(cc + 1) * 64], MAKh[:, s, hp, :], Vt(hp), start=(cc == 0),
                                   stop=False, skip_group_check=True)
                return ins
            P.op("pe", mmU1, reads=[MAK.bufs[s], TOK.bufs[s]], writes=[K.psb[bu]], tag="U mm1")

            def mmU2(e, bu=bu, hh=hh, s=s):
                ins = None
                for cc in range(4):
                    ins = e.matmul(K.ps[bu][0:64, cc * 64:(cc + 1) * 64], WTh[pr(hh), s, cc, :], Hbh[pr(hh), cc, :],
                                   start=False, stop=True, skip_group_check=True)
                return ins
            P.op("pe", mmU2, reads=[WTs.bufs[s], Hb.bufs[0]], writes=[K.psb[bu]], tag="U mm2", pe_sync=True)
            P.op("act", lambda e, bu=bu, hh=hh, s=s: e.activation(out=Ubh[:, s, hh * 4:(hh + 1) * 4, :],
                                                                 in_=ps3(bu, 4, 64), func=AF.Copy),
                 reads=[K.psb[bu]], writes=[Ub.bufs[s]], tag="U evac")
        for hh in range(2):
            by = nb()

            def mmY1(e, by=by, hh=hh, s=s, Vt=Vt):
                ins = None
                for cc in range(4):
                    hp = hh * 4 + cc
                    o = K.ps[by][0:64, cc * 64:(cc + 1) * 64]
                    e.matmul(o, M2h[:, s, hp, 64:128], Ubh[:, s, hp, :], start=(cc == 0), stop=False,
                             skip_group_check=True)
                    ins = e.matmul(o, M3h[:, s, hp, :], Vt(hp), start=False, stop=False, skip_group_check=True)
                return ins
            P.op("pe", mmY1, reads=[M2.bufs[s], Ub.bufs[s], M3.bufs[s], TOK.bufs[s]], writes=[K.psb[by]], tag="Y mm1")

            def mmY2(e, by=by, hh=hh, j=j):
                ins = None
                for cc in range(4):
                    ins = e.matmul(K.ps[by][0:64, cc * 64:(cc + 1) * 64], ARh[pr(hh), cc, j, 1, :], Hbh[pr(hh), cc, :],
                                   start=False, stop=True, skip_group_check=True)
                return ins
            P.op("pe", mmY2, reads=[ab, Hb.bufs[0]], writes=[K.psb[by]], tag="Y mm2", pe_sync=True)
            P.op("act", lambda e, by=by, hh=hh, s=s: e.activation(
                out=Ysh[:, s, :].rearrange("p (c h v) -> p c h v", c=4, h=2)[:, :, hh, :], in_=ps3(by, 4, 64),
                func=AF.Copy), reads=[K.psb[by]], writes=[Ys.bufs[s]], tag="Y evac")
        bh = nb()

        def mmH(e, bh=bh, s=s, Vt=Vt):
            ins = None
            for hp in range(8):
                cc = hp % 4
                o = K.ps[bh][:, hp * 64:(hp + 1) * 64]
                e.matmul(o, TOKh[:, s, 1, cc * 128:(cc + 1) * 128], Ubh[:, s, hp, :], start=(hp == 0), stop=False,
                         skip_group_check=True)
                ins = e.matmul(o, TOKh[:, s, 2, cc * 128:(cc + 1) * 128], Vt(hp), start=False, stop=True,
                               skip_group_check=True)
            return ins
        P.op("pe", mmH, reads=[TOK.bufs[s], Ub.bufs[s]], writes=[K.psb[bh]], tag="H mm")
        P.op("pool", lambda e, j=j: e.tensor_tensor(out=Hfh[:, :, :], in0=Hfh[:, :, :],
                                                    in1=GCh[:, :, j:j + 1].to_broadcast([128, 4, 64]), op=ALU.mult),
             reads=[R.GC.bufs[0], R.GC.bufs[1], R.GC.bufs[2], R.GC.bufs[3]], writes=Hf.bufs, tag="H decay")
        for hh in range(2):
            P.op("dve", lambda e, bh=bh, hh=hh: e.tensor_tensor(
                out=Hfh[pr(hh), :, :], in0=Hfh[pr(hh), :, :],
                in1=K.ps[bh][pr(hh), hh * 256:(hh + 1) * 256].rearrange("p (c v) -> p c v", c=4), op=ALU.add),
                reads=[K.psb[bh]], writes=Hf.bufs, tag="H add")
        P.op("act", lambda e: e.activation(out=Hbh[:, :, :], in_=Hfh[:, :, :], func=AF.Copy),
             reads=Hf.bufs, writes=Hb.bufs, tag="H bf16")

        if getattr(K, 'chunk_cut', 99) <= 5:
            continue
        bg = nb()

        def mmG(e, bg=bg, j=j):
            e.matmul(K.ps[bg][0:64, :], SGh[:, 0, j * CH:(j + 1) * CH], g2h[:, 0, :], start=True, stop=False)
            return e.matmul(K.ps[bg][0:64, :], SGh[0:32, 1, j * CH:(j + 1) * CH], g2h[0:32, 1, :], start=False, stop=True)
        P.op("pe", mmG, reads=[R.SG.bufs[tq], R.g2.bufs[0]], writes=[K.psb[bg]], tag="gate mm")
        P.op("act", lambda e, bg=bg: e.activation(out=Gsh[:, :], in_=K.ps[bg][0:64, :], func=AF.Copy),
             reads=[K.psb[bg]], writes=Gs.bufs, tag="gate evac")
        if getattr(K, 'chunk_cut', 99) <= 6:
            continue
        Y3 = Ysh[:, s, :].rearrange("p (h v) -> p h v", v=64)
        Q3 = Yqh.rearrange("p (h v) -> p h v", v=64)
        yb = Ys.bufs[s]
        P.op("dve", lambda e, Y3=Y3: e.tensor_reduce(out=sth[:, 0:8], in_=Y3, axis=AX.X, op=ALU.add),
             reads=[yb], writes=st.bufs, tag="gn sum")
        P.op("pool", lambda e, s=s: e.tensor_tensor(out=Yqh[:, :], in0=Ysh[:, s, :], in1=Ysh[:, s, :], op=ALU.mult),
             reads=[yb], writes=Yq.bufs, tag="gn sq")
        P.op("dve", lambda e, Q3=Q3: e.tensor_reduce(out=sth[:, 8:16], in_=Q3, axis=AX.X, op=ALU.add),
             reads=Yq.bufs, writes=st.bufs, tag="gn ssq")
        P.op("dve", lambda e: e.tensor_scalar(out=sth[:, 16:24], in0=sth[:, 0:8], scalar1=1.0 / 64, scalar2=None,
                                              op0=ALU.mult), reads=[], writes=st.bufs, tag="gn mean")
        P.op("dve", lambda e: e.tensor_tensor(out=sth[:, 24:32], in0=sth[:, 16:24], in1=sth[:, 16:24], op=ALU.mult),
             reads=[], writes=st.bufs, tag="gn m2")
        P.op("dve", lambda e: e.scalar_tensor_tensor(out=sth[:, 32:40], in0=sth[:, 8:16], scalar=1.0 / 64,
                                                     in1=sth[:, 24:32], op0=ALU.mult, op1=ALU.subtract),
             reads=[], writes=st.bufs, tag="gn var")
        P.op("dve", lambda e: e.tensor_scalar(out=sth[:, 32:40], in0=sth[:, 32:40], scalar1=GN_EPS, scalar2=None,
                                              op0=ALU.add), reads=[], writes=st.bufs, tag="gn var eps")
        P.op("act", lambda e: e.activation(out=sth[:, 40:48], in_=sth[:, 32:40], func=AF.Sqrt),
             reads=[], writes=st.bufs, tag="gn sqrt")
        P.op("dve", lambda e: e.reciprocal(out=sth[:, 48:56], in_=sth[:, 40:48]), reads=[], writes=st.bufs, tag="gn rstd")
        P.op("dve", lambda e, Y3=Y3: e.tensor_tensor(out=Y3, in0=Y3, in1=sth[:, 16:24].unsqueeze(2).to_broadcast([64, 8, 64]),
                                                     op=ALU.subtract), reads=[], writes=[yb] + st.bufs, tag="gn sub")
        P.op("pool", lambda e, Y3=Y3: e.tensor_tensor(out=Y3, in0=Y3, in1=sth[:, 48:56].unsqueeze(2).to_broadcast([64, 8, 64]),
                                                      op=ALU.mult), reads=st.bufs, writes=[yb], tag="gn mul")
        P.op("pool", lambda e, s=s: e.tensor_tensor(out=Ysh[:, s, :], in0=Ysh[:, s, :], in1=gnh[:, 0, :], op=ALU.mult),
             reads=gnb.bufs, writes=[yb], tag="gn g")
        P.op("pool", lambda e, s=s: e.tensor_tensor(out=Ysh[:, s, :], in0=Ysh[:, s, :], in1=gnh[:, 1, :], op=ALU.add),
             reads=gnb.bufs, writes=[yb], tag="gn b")
        if getattr(K, 'chunk_cut', 99) <= 7:
            continue
        P.op("dve", lambda e, Q3=Q3, s=s, j=j: e.tensor_tensor(
            out=Q3, in0=TOKh[:, s, 3, :].rearrange("p (h v) -> p h v", v=64),
            in1=BONh[:, j, :].unsqueeze(2).to_broadcast([64, 8, 64]), op=ALU.mult),
            reads=[TOK.bufs[s], R.BON.bufs[tq]], writes=Yq.bufs, tag="bonus v")
        P.op("pool", lambda e, s=s: e.tensor_tensor(out=Ysh[:, s, :], in0=Ysh[:, s, :], in1=Yqh[:, :], op=ALU.add),
             reads=Yq.bufs, writes=[yb], tag="y+bonus")
        P.op("dve", lambda e, s=s: e.tensor_tensor(out=Yoh[:, s, :], in0=Ysh[:, s, :], in1=Gsh[:, :], op=ALU.mult),
             reads=Gs.bufs + [yb], writes=[Yo.bufs[s]], tag="gate mul")
        if getattr(K, 'chunk_cut', 99) <= 8:
            continue
        P.op("sp", lambda e, s=s, j=j: e.dma_start(out=o_dst[j * CH:(j + 1) * CH, 512:1024], in_=Yoh[:, s, :]),
             reads=[Yo.bufs[s]], writes=[o_buf], dma=True, tag="o_rwkv out")
    for al in (mk, gnb, M1, M2, M3, NL, PP, BKh_, TOK, WTs, MAK, Ub, Hf, Hb, Ys, Yq, Gs, st, Yo):
        P.free(al)


def host_rwkv_params(inp, half):
    c0, c1 = half * 512, (half + 1) * 512
    mu = inp["rwkv_mu"]
    pcol = np.zeros((128, 64), np.float32)
    for blk in range(3):
        for cc in range(4):
            pcol[:, blk * 4 + cc] = mu[blk * 1024 + c0 + cc * 128: blk * 1024 + c0 + (cc + 1) * 128]
    pcol[:, 12] = mu[3072:3072 + 128]
    pcol[:, 13] = mu[3072 + 128:3072 + 256]
    pcol[0:32, 14] = mu[3072 + 256:3072 + 288]
    rk = inp["rwkv_r_k"].reshape(-1)
    for cc in range(4):
        sl = slice(c0 + cc * 128, c0 + (cc + 1) * 128)
        pcol[:, 15 + cc] = inp["rwkv_w0"][sl]
        pcol[:, 19 + cc] = inp["rwkv_a0"][sl]
        pcol[:, 23 + cc] = inp["rwkv_k_k"][sl]
        pcol[:, 27 + cc] = inp["rwkv_k_a"][sl]
        pcol[:, 31 + cc] = rk[sl]
    w2a2 = np.concatenate([inp["rwkv_w2"][:, c0:c1], inp["rwkv_a2"][:, c0:c1]], 0).astype(np.float32)
    g2 = np.zeros((128, 2, 512), np.float32)
    g2[:, 0, :] = inp["rwkv_g2"][0:128, c0:c1]
    g2[0:32, 1, :] = inp["rwkv_g2"][128:160, c0:c1]
    gnb = np.zeros((64, 2, 512), np.float32)
    gnb[:, 0, :] = inp["rwkv_gn_g"][c0:c1][None]
    gnb[:, 1, :] = inp["rwkv_gn_b"][c0:c1][None]
    return dict(pcol=pcol, w2a2=np.ascontiguousarray(w2a2), g2=g2, gnb=gnb)


def host_rwkv_consts():
    bones = np.zeros((128, 128), np.float32)
    bones[0:64, 0:64] = 1.0
    bones[64:128, 64:128] = 1.0
    hsel = np.zeros((128, 16), np.float32)
    hsel[0:64, 0] = 1.0
    hsel[64:128, 1] = 1.0
    t = np.arange(64)
    sl = (t[None, :] < t[:, None]).astype(np.float32)
    su = (t[:, None] < t[None, :]).astype(np.float32)
    ui = (t[:, None] <= t[None, :]).astype(np.float32)
    masks = np.zeros((64, 6, 128), np.float32)
    masks[:, 0, 0:64] = sl
    masks[:, 0, 64:128] = sl
    masks[:, 1, 0:64] = su
    masks[:, 1, 64:128] = ui
    masks[:, 2, 0:64] = ui
    masks[:, 3, 0:64] = np.eye(64, dtype=np.float32)
    return dict(bones=bones, hsel=hsel, masks=masks)


def ln_tile(P, acch, ab, t, gbh, gb, lsth, st, s, outs):
    for j in range(4):
        P.op("dve", lambda e, t=t, j=j, s=s: e.bn_stats(out=lsth[:, s, j * 6:(j + 1) * 6],
                                                      in_=acch[:, t, j * 512:(j + 1) * 512]),
             reads=[ab[j]], writes=[st.bufs[s]], tag="bn_stats")
    P.op("dve", lambda e, s=s: e.bn_aggr(out=lsth[:, s, 24:26], in_=lsth[:, s, 0:24].rearrange("p (a b) -> p a b", b=6)),
         reads=[], writes=[st.bufs[s]], tag="bn_aggr")
    P.op("dve", lambda e, s=s: e.tensor_scalar(out=lsth[:, s, 28:29], in0=lsth[:, s, 25:26], scalar1=LN_EPS,
                                              scalar2=None, op0=ALU.add), reads=[], writes=[st.bufs[s]], tag="var+eps")
    P.op("act", lambda e, s=s: e.activation(out=lsth[:, s, 29:30], in_=lsth[:, s, 28:29], func=AF.Sqrt),
         reads=[], writes=[st.bufs[s]], tag="sqrt")
    P.op("dve", lambda e, s=s: e.reciprocal(out=lsth[:, s, 26:27], in_=lsth[:, s, 29:30]),
         reads=[], writes=[st.bufs[s]], tag="rstd")
    P.op("dve", lambda e, s=s: e.scalar_tensor_tensor(out=lsth[:, s, 27:28], in0=lsth[:, s, 24:25], scalar=-1.0,
                                                     in1=lsth[:, s, 26:27], op0=ALU.mult, op1=ALU.mult),
         reads=[], writes=[st.bufs[s]], tag="nmr")
    P.op("act", lambda e, t=t, s=s: e.activation(out=acch[:, t, :], in_=acch[:, t, :], func=AF.Identity,
                                                bias=lsth[:, s, 27:28], scale=lsth[:, s, 26:27]),
         reads=[st.bufs[s]], writes=ab, tag="ln norm")
    P.op("pool", lambda e, t=t: e.tensor_tensor(out=acch[:, t, :], in0=acch[:, t, :], in1=gbh[:, 0, :], op=ALU.mult),
         reads=[gb.bufs[0]], writes=ab, tag="ln g")
    P.op("pool", lambda e, t=t: e.tensor_tensor(out=acch[:, t, :], in0=acch[:, t, :], in1=gbh[:, 1, :], op=ALU.add),
         reads=[gb.bufs[0]], writes=ab, tag="ln b")
    for (dst, dbuf, dt) in outs:
        if isinstance(dbuf, list):
            dbuf = dbuf[t]
        if dt == F32:
            P.op("sp", lambda e, t=t, dst=dst: e.dma_start(out=dst[t * 128:(t + 1) * 128, :], in_=acch[:, t, :]),
                 reads=ab, writes=[dbuf], dma=True, tag="ln out")
        else:
            obf = P.lnbf
            P.op("act", lambda e, t=t, s=s, obf=obf: e.activation(out=obf.handle[:, s, :], in_=acch[:, t, :], func=AF.Copy),
                 reads=ab, writes=[obf.bufs[s]], tag="ln out cast")
            P.op("sp", lambda e, t=t, s=s, dst=dst, obf=obf: e.dma_start(out=dst[t * 128:(t + 1) * 128, :],
                                                                        in_=obf.handle[:, s, :]),
                 reads=[obf.bufs[s]], writes=[dbuf], dma=True, tag="ln out bf16")


def wout_stage(K, og, og_buf, x1f, x1f_buf, wout, hsc_d, lng, lnb, outs, base):
    P = K.P
    NT = NTOK // 128
    off = base
    acc = P.sbuf("acc3", [128, NT, D], F32, off, nbufs=NT * 4); off += NT * D * 4
    XT = P.sbuf("oT", [128, KC, NTOK], BF16, off, nbufs=NT); off += KC * NTOK * 2
    ws = P.sbuf("wouts", [128, KC, D], BF16, off, nbufs=4); off += KC * D * 2
    gb = P.sbuf("lngb3", [128, 2, D], F32, off); off += 2 * D * 4
    st = P.sbuf("lnst3", [128, 2, 32], F32, off, nbufs=2); off += 2 * 32 * 4
    hs = P.sbuf("hsc", [128, 8], F32, off); off += 32
    xa = P.sbuf("xa", [128, 2, 2, D], BF16, off, nbufs=2); off += 2 * 2 * D * 2
    xb = P.sbuf("xb3", [128, 2, D], BF16, off, nbufs=2); off += 2 * D * 2
    acch, XTh, wsh, gbh, lsth, hsh, xah, xbh = (acc.handle, XT.handle, ws.handle, gb.handle, st.handle, hs.handle,
                                                xa.handle, xb.handle)
    load_const(K, "lng3", gb, gbh[:, 0, :], lng[:, :])
    load_const(K, "lnb3", gb, gbh[:, 1, :], lnb[:, :])
    load_const(K, "hsc", hs, hsh[:, 0:2], hsc_d[:, :])
    wv = wout.rearrange("(kc p) d -> p kc d", p=128)
    for q in range(4):
        P.op("pool", lambda e, q=q: e.dma_start(out=wsh[:, q * 4:(q + 1) * 4, :], in_=wv[:, q * 4:(q + 1) * 4, :]),
             reads=[], writes=[ws.bufs[q]], dma=True, tag="wout load")
    for t in range(NT):
        P.op("sp", lambda e, t=t: e.dma_start(out=acch[:, t, :], in_=x1f[t * 128:(t + 1) * 128, :]),
             reads=[x1f_buf], writes=acc.bufs[t * 4:(t + 1) * 4], dma=True, tag="acc3 load")
        P.op("act", lambda e, t=t: e.activation(out=acch[:, t, :], in_=acch[:, t, :], func=AF.Copy, scale=ALPHA),
             reads=[], writes=acc.bufs[t * 4:(t + 1) * 4], tag="acc3 scale")
    banks = [4, 5, 6, 7]
    for t in range(NT):
        s = t % 2
        for cand in range(2):
            for r in range(2):
                if callable(og):
                    oap, obuf_ = og(cand, r, t)
                else:
                    row0 = r * SEQ + cand * NTOK + t * 128
                    oap, obuf_ = og[row0:row0 + 128, :], og_buf
                P.op("sp", lambda e, s=s, cand=cand, r=r, oap=oap: e.dma_start(
                    out=xah[:, s, cand, r * 1024:(r + 1) * 1024], in_=oap),
                    reads=[obuf_], writes=[xa.bufs[s]], dma=True, tag="og load")
        P.op("dve", lambda e, s=s: e.tensor_scalar(out=xbh[:, s, :], in0=xah[:, s, 0, :], scalar1=hsh[:, 0:1],
                                                  scalar2=None, op0=ALU.mult),
             reads=[xa.bufs[s], hs.bufs[0]], writes=[xb.bufs[s]], tag="blend0")
        P.op("dve", lambda e, s=s: e.scalar_tensor_tensor(out=xbh[:, s, :], in0=xah[:, s, 1, :], scalar=hsh[:, 1:2],
                                                         in1=xbh[:, s, :], op0=ALU.mult, op1=ALU.add),
             reads=[xa.bufs[s], hs.bufs[0]], writes=[xb.bufs[s]], tag="blend1")
        for half in range(2):
            bk = banks[(2 * t + half) % 4]
            pb = K.ps[bk].bitcast(BF16)

            def tr(e, s=s, half=half, pb=pb):
                ins = None
                for j in range(8):
                    kc = half * 8 + j
                    ins = e.transpose(pb[:, j * 128:(j + 1) * 128], xbh[:, s, kc * 128:(kc + 1) * 128], K.ident[:])
                return ins
            P.op("pe", tr, reads=[xb.bufs[s], K.ident_buf], writes=[K.psb[bk]], tag="oT transposes")
            eng = "act" if half == 0 else "dve"

            def ev(e, t=t, half=half, pb=pb, eng=eng):
                o = XTh[:, half * 8:(half + 1) * 8, t * 128:(t + 1) * 128]
                i = pb.rearrange("p (j c) -> p j c", j=8)
                if eng == "act":
                    return e.activation(out=o, in_=i, func=AF.Copy)
                return e.tensor_copy(out=o, in_=i)
            P.op(eng, ev, reads=[K.psb[bk]], writes=[XT.bufs[t]], tag="oT evac")
    cnt = 0
    for t in range(NT):
        for db in range(4):
            bk = cnt % 4
            cnt += 1

            def mm(e, bk=bk, t=t, db=db):
                ins = None
                for kc in range(KC):
                    ins = e.matmul(K.ps[bk][:, :], XTh[:, kc, t * 128:(t + 1) * 128], wsh[:, kc, db * 512:(db + 1) * 512],
                                   start=(kc == 0), stop=(kc == KC - 1))
                return ins
            P.op("pe", mm, reads=[XT.bufs[t]] + ws.bufs, writes=[K.psb[bk]], tag="wout mm")
            P.op("dve", lambda e, bk=bk, t=t, db=db: e.tensor_tensor(
                out=acch[:, t, db * 512:(db + 1) * 512], in0=K.ps[bk][:, :], in1=acch[:, t, db * 512:(db + 1) * 512],
                op=ALU.add), reads=[K.psb[bk]], writes=[acc.bufs[t * 4 + db]], tag="acc3 add")
        ln_tile(P, acch, acc.bufs[t * 4:(t + 1) * 4], t, gbh, gb, lsth, st, t % 2, outs)
    for a in (acc, XT, ws, gb, st, hs, xa, xb):
        P.free(a)


def build_program():
    nc = bass.Bass("TRN2", target_bir_lowering=False)
    K = mk_ctx(nc)
    P = K.P
    d = {}

    def din(name, shape):
        d[name] = dram_in(K, name, shape)
        return d[name]
    x = din("x", [NTOK, D])
    identd = din("ident", [128, 128])
    f1 = [din("f1_wg", [NG, 128, KC, FG]), din("f1_wu", [NG, 128, KC, FG]), din("f1_wd", [DFF, D]),
          din("ln1g", [128, D]), din("ln1b", [128, D])]
    f2 = [din("f2_wg", [NG, 128, KC, FG]), din("f2_wu", [NG, 128, KC, FG]), din("f2_wd", [DFF, D]),
          din("ln3g", [128, D]), din("ln3b", [128, D])]
    win = din("win", [27, 128, KC, 128])
    bm = din("bm", [4, 5, 128, 512]); ctab = din("ctab", [128, 64]); lamv = din("lamv", [128, 4, 64])
    ngt = din("ngt", [128, 512])
    pcol = din("pcol", [128, 64]); w2a2 = din("w2a2", [128, 512]); g2 = din("g2", [128, 2, 512])
    gnb = din("gnb", [64, 2, 512]); bones = din("bones", [128, 128]); hsel = din("hsel", [128, 16])
    masks = din("masks", [64, 6, 128])
    wout = din("wout", [D, D]); hsc = din("hsc", [128, 2]); ln2g = din("ln2g", [128, D]); ln2b = din("ln2b", [128, D])
    x1f = dram_tmp(K, "x1f", [NTOK, D], F32)
    x1b = dram_tmp(K, "x1b", [NTOK, D], BF16)
    x1g = [dram_tmp(K, f"x1g{i}", [256, D], BF16) for i in range(8)]
    oloc = dram_tmp(K, "oloc", [SEQ, 1024], BF16)
    og = [dram_tmp(K, f"og{i}", [512, 1024], BF16) for i in range(8)]
    x2s = dram_tmp(K, "x2s", [NTOK, D], F32)
    out = dram_tmp(K, "out", [NTOK, D], F32, kind="ExternalOutput")
    B = lambda n: K.dram[n][1]
    groups = [[0, 1], [2, 3], [4, 5], [6, 7]]

    setup_consts(K, identd)
    base = K.const_top
    x1b_bufs = [Buf(f"dram_x1b[{i}]") for i in range(8)]

    def gather_x1(i):
        P.op("pool", lambda e, i=i: e.collective_compute("AllGather", ALU.bypass, replica_groups=groups,
                                                         ins=[x1b.ap()[i * 128:(i + 1) * 128, :]],
                                                         outs=[x1g[i].ap()[:, :]]),
             reads=[x1b_bufs[i]], writes=[B(f"x1g{i}")], dma="cc", tag="allgather x1")
    ffn_stage(K, x.ap(), B("x"), f1[0].ap(), f1[1].ap(), f1[2].ap(), f1[3].ap(), f1[4].ap(),
              [(x1f.ap(), B("x1f"), F32), (x1b.ap(), x1b_bufs, BF16)], base=base, after_tile=gather_x1)

    def x1_src(t):
        r, i = t // 8, t % 8
        return x1g[i].ap()[r * 128:(r + 1) * 128, :], B(f"x1g{i}")
    X1T = P.sbuf("X1T", [128, KC, SEQ], BF16, base, nbufs=16)
    b2 = base + KC * SEQ * 2
    load_xT(K, x1_src, None, SEQ, X1T, b2, [4, 5, 6, 7], K.ident, False)
    attention_stage(K, X1T, win.ap(), bm.ap(), ctab.ap(), lamv.ap(), ngt.ap(), oloc.ap(), B("oloc"), b2)
    R = rwkv_alloc_persist(K, b2)
    rwkv_prep(K, R, X1T, win.ap(), pcol.ap(), w2a2.ap(), g2.ap(), bones.ap(), hsel.ap(), R.top)
    P.free(X1T)
    rwkv_chunks(K, R, masks.ap(), gnb.ap(), oloc.ap(), B("oloc"), base)
    for a in (R.AR, R.BK, R.VC, R.LW, R.SG, R.GC, R.BON, R.pcol, R.w2a2, R.g2, R.bones, R.hsel):
        P.free(a)
    for i in range(8):
        P.op("pool", lambda e, i=i: e.collective_compute("AllGather", ALU.bypass, replica_groups=groups,
                                                         ins=[oloc.ap()[i * 256:(i + 1) * 256, :]],
                                                         outs=[og[i].ap()[:, :]]),
             reads=[B("oloc")], writes=[B(f"og{i}")], dma="cc", tag="allgather o")

    def og_src(cand, r, t):
        tok = cand * NTOK + t * 128
        i, j = tok // 256, tok % 256
        return og[i].ap()[r * 256 + j:r * 256 + j + 128, :], B(f"og{i}")
    wout_stage(K, og_src, None, x1f.ap(), B("x1f"), wout.ap(), hsc.ap(), ln2g.ap(), ln2b.ap(),
               [(x2s.ap(), B("x2s"), F32)], base)
    ffn_stage(K, x2s.ap(), B("x2s"), f2[0].ap(), f2[1].ap(), f2[2].ap(), f2[3].ap(), f2[4].ap(),
              [(out.ap(), B("out"), F32)], base=base)
    finish(K, [B("out")])
    P.emit()
    return nc


def host_inputs(inp):
    l0 = {k: np.asarray(v[0], np.float32) for k, v in inp.items() if k != "x"}
    x = np.asarray(inp["x"], np.float32).reshape(8, NTOK, D)
    rep = lambda v, n=128: np.ascontiguousarray(np.broadcast_to(np.asarray(v, np.float32)[None], (n,) + v.shape))
    shared = dict(
        ident=np.eye(128, dtype=np.float32),
        f1_wg=host_tile_ffn_w(l0["ffn1_w_gate"]), f1_wu=host_tile_ffn_w(l0["ffn1_w_up"]), f1_wd=l0["ffn1_w_down"],
        ln1g=rep(l0["ln1_g"]), ln1b=rep(l0["ln1_b"]),
        f2_wg=host_tile_ffn_w(l0["ffn2_w_gate"]), f2_wu=host_tile_ffn_w(l0["ffn2_w_up"]), f2_wd=l0["ffn2_w_down"],
        ln3g=rep(l0["ln3_g"]), ln3b=rep(l0["ln3_b"]), ln2g=rep(l0["ln2_g"]), ln2b=rep(l0["ln2_b"]),
        lamv=rep(np.stack([l0["lambda_q1"], l0["lambda_k1"], l0["lambda_q2"], l0["lambda_k2"]])),
        ngt=rep(np.tile(l0["attn_norm_g"], 4)),
        wout=np.ascontiguousarray(l0["w_out"][np.concatenate([np.arange(0, 512), np.arange(1024, 1536),
                                                            np.arange(512, 1024), np.arange(1536, 2048)])]),
    )
    shared.update(host_rwkv_consts())
    per_half = []
    for h in range(2):
        bm, ctab = host_attn_consts(h)
        dd = dict(win=host_tile_win(l0["w_in"], h), bm=bm, ctab=ctab,
                  hsc=np.ascontiguousarray(np.broadcast_to(np.array([1.0 - h, float(h)], np.float32)[None], (128, 2))))
        dd.update(host_rwkv_params(l0, h))
        per_half.append(dd)
    maps = []
    for c in range(8):
        m = dict(shared)
        m.update(per_half[c % 2])
        m["x"] = np.ascontiguousarray(x[c])
        maps.append(m)
    return maps


def kernel(**inputs):
    nc = build_program()
    maps = host_inputs(inputs)
    res = run_bass_kernel_spmd(nc, maps, core_ids=list(range(8)))
    out = np.stack([np.asarray(res.results[c]["out"], np.float32) for c in range(8)], 0)
    return out.reshape(4, SEQ, D)
```

```python
import numpy as np
import concourse.bass as bass
import concourse.mybir as mybir
from concourse.bass_utils import run_bass_kernel_spmd

F32 = mybir.dt.float32
BF16 = mybir.dt.bfloat16
AF = mybir.ActivationFunctionType
ALU = mybir.AluOpType
AX = mybir.AxisListType

SAME_ENGINE_SYNC = True
SEM_CHUNK = 4000
N_DMA_SEMS = 12


class Buf:
    __slots__ = ("name", "last_w", "readers", "excl")

    def __init__(self, name, excl=False):
        self.name = name
        self.last_w = None
        self.readers = []
        self.excl = excl


class Op:
    __slots__ = ("eng", "fn", "deps", "dma", "needs_inc", "inc_no", "dsem", "dval", "pos", "tag", "prev_dval")

    def __init__(self, eng, fn, dma, tag):
        self.eng = eng
        self.fn = fn
        self.deps = []
        self.dma = dma
        self.needs_inc = False
        self.inc_no = None
        self.dsem = None
        self.dval = None
        self.tag = tag


class Alloc:
    def __init__(self, name, off, nbytes, handle, bufs):
        self.name, self.off, self.nbytes, self.handle, self.bufs = name, off, nbytes, handle, bufs


ENGS = ("pe", "act", "dve", "pool", "sp")


class Prog:
    def __init__(self, nc):
        self.nc = nc
        self.ops = {e: [] for e in ENGS}
        self.all_ops = []
        self.live = []
        self.ghosts = []
        self.uid = 0
        self.sbuf_top = 0

    def sbuf(self, name, shape, dtype, off, nbufs=1):
        esz = 4 if dtype == F32 else 2
        free = 1
        for s in shape[1:]:
            free *= s
        nbytes = free * esz
        assert off % 32 == 0, (name, off)
        assert 16384 <= off and off + nbytes <= 224 * 1024 - 160, (name, off, nbytes)
        self.uid += 1
        h = self.nc.alloc_sbuf_tensor_at(f"{name}_{self.uid}", list(shape), dtype, offset=off)
        bufs = [Buf(f"{name}[{i}]") for i in range(nbufs)]
        for a in self.live:
            assert a.off + a.nbytes <= off or off + nbytes <= a.off, ("overlap", name, a.name)
        keep = []
        for g in self.ghosts:
            if g.off + g.nbytes <= off or off + nbytes <= g.off:
                keep.append(g)
                continue
            hz = []
            for b in g.bufs:
                if b.last_w is not None:
                    hz.append(b.last_w)
                hz.extend(b.readers)
            for b in bufs:
                b.readers.extend(hz)
            keep.append(g)
        self.ghosts = keep
        a = Alloc(name, off, nbytes, h, bufs)
        self.live.append(a)
        return a

    def free(self, a):
        self.live.remove(a)
        self.ghosts.append(a)

    def op(self, eng, fn, reads=(), writes=(), dma=False, tag="", pe_sync=False):
        o = Op(eng, fn, dma, tag)
        deps = []
        for b in reads:
            if b.excl:
                continue
            if b.last_w is not None:
                deps.append(b.last_w)
        for b in list(writes) + [b for b in reads if b.excl]:
            if b.last_w is not None:
                deps.append(b.last_w)
            deps.extend(b.readers)
        for b in reads:
            if not b.excl:
                b.readers.append(o)
        for b in list(writes) + [b for b in reads if b.excl]:
            b.last_w = o
            b.readers = []
        seen = set()
        for d in deps:
            if id(d) in seen or d is o:
                continue
            seen.add(id(d))
            if (not d.dma) and d.eng == eng and not SAME_ENGINE_SYNC:
                continue
            if (not d.dma) and d.eng == eng and eng == "pe" and not pe_sync:
                continue
            o.deps.append(d)
        o.pos = len(self.ops[eng])
        self.ops[eng].append(o)
        self.all_ops.append(o)
        return o

    def emit(self):
        nc = self.nc
        for o in self.all_ops:
            for d in o.deps:
                d.needs_inc = True
        cnt = {e: 0 for e in ENGS}
        dma_cnt = {e: 0 for e in ENGS}
        n_sems = {}
        for e in ENGS:
            n = sum(1 for o in self.ops[e] if o.needs_inc and not o.dma)
            n_sems[e] = max(1, (n + SEM_CHUNK - 1) // SEM_CHUNK)
        import contextlib
        with contextlib.ExitStack() as st:
            esems = {e: [st.enter_context(nc.semaphore(f"s_{e}_{i}")) for i in range(n_sems[e])] for e in ENGS}
            dsems = {e: [st.enter_context(nc.semaphore(f"d_{e}_{i}")) for i in range(N_DMA_SEMS)]
                     for e in ("sp", "act", "pool")}
            dsem_val = {e: [0] * N_DMA_SEMS for e in dsems}
            ccsems = []
            dsems["cc"] = ccsems
            for e in ENGS:
                for o in self.ops[e]:
                    if o.dma == "cc":
                        o.dsem = ("cc", len(ccsems))
                        ccsems.append(st.enter_context(nc.semaphore(f"cc_{len(ccsems)}")))
                        o.prev_dval = 0
                        o.dval = 1
                    elif o.dma:
                        k = dma_cnt[e] % N_DMA_SEMS
                        dma_cnt[e] += 1
                        o.dsem = (e, k)
                        o.prev_dval = dsem_val[e][k]
                        dsem_val[e][k] += 16
                        o.dval = dsem_val[e][k]
                    elif o.needs_inc:
                        o.inc_no = cnt[e]
                        cnt[e] += 1
            items = {e: [] for e in ENGS}
            for e in ENGS:
                waited = {}
                for o in self.ops[e]:
                    ws = []
                    for d in o.deps:
                        if d.dma:
                            key = ("d",) + d.dsem
                            val = d.dval
                            sem = dsems[d.dsem[0]][d.dsem[1]]
                        else:
                            ch = d.inc_no // SEM_CHUNK
                            key = ("e", d.eng, ch)
                            val = d.inc_no % SEM_CHUNK + 1
                            sem = esems[d.eng][ch]
                            later = any(k[0] == "e" and k[1] == d.eng and k[2] > ch for k in waited)
                            if later:
                                continue
                        if waited.get(key, 0) >= val:
                            continue
                        waited[key] = val
                        ws.append((sem, val))
                    if o.dma and o.prev_dval > 0:
                        key = ("d",) + o.dsem
                        if waited.get(key, 0) < o.prev_dval:
                            waited[key] = o.prev_dval
                            ws.append((dsems[o.dsem[0]][o.dsem[1]], o.prev_dval))
                    items[e].append((ws, o))
            self.stats = {e: (len(self.ops[e]), sum(len(w) for w, _ in items[e])) for e in ENGS}

            def run(e, eng):
                for ws, o in items[e]:
                    for sem, val in ws:
                        eng.wait_ge(sem, val)
                    ins = o.fn(eng)
                    if o.dma == "cc":
                        ins.then_inc(dsems["cc"][o.dsem[1]], 1)
                    elif o.dma:
                        ins.then_inc(dsems[o.dsem[0]][o.dsem[1]], 16)
                    elif o.needs_inc:
                        ins.then_inc(esems[e][o.inc_no // SEM_CHUNK], 1)

            with nc.Block() as block:
                @block.tensor
                def _(eng):
                    run("pe", eng)

                @block.scalar
                def _(eng):
                    run("act", eng)

                @block.vector
                def _(eng):
                    run("dve", eng)

                @block.gpsimd
                def _(eng):
                    run("pool", eng)

                @block.sync
                def _(eng):
                    run("sp", eng)


D = 2048
DFF = 5632
NTOK = 1024
SEQ = 2048
ALPHA = 2.0 ** 0.25
LN_EPS = 1e-5
FG = 256
NG = DFF // FG
KC = D // 128


class Ctx:
    pass


def mk_ctx(nc):
    K = Ctx()
    K.nc = nc
    K.P = Prog(nc)
    K.ps = []
    K.psb = []
    for i in range(8):
        h = nc.alloc_psum_tensor(f"psum{i}", [128, 512], F32)
        K.ps.append(h)
        K.psb.append(Buf(f"psum{i}", excl=True))
    K.dram = {}
    return K


def dram_in(K, name, shape, dtype=F32):
    t = K.nc.dram_tensor(name, list(shape), dtype, kind="ExternalInput")
    K.dram[name] = (t, Buf("dram_" + name))
    return t


def dram_tmp(K, name, shape, dtype, kind="Internal"):
    t = K.nc.dram_tensor(name, list(shape), dtype, kind=kind)
    K.dram[name] = (t, Buf("dram_" + name))
    return t


def load_const(K, name, alloc, dst_ap, src_ap, q="sp"):
    K.P.op(q, lambda e, o=dst_ap, i=src_ap: e.dma_start(out=o, in_=i), reads=[], writes=alloc.bufs, dma=True,
           tag="const " + name)


def load_xT(K, src, src_buf, ntok, XT, xb_off, banks, ident, src_f32):
    P = K.P
    nt = ntok // 128
    xb = P.sbuf("xb", [128, 2, D], BF16, xb_off, nbufs=2)
    xbh = xb.handle
    XTh = XT.handle
    for t in range(nt):
        s = t % 2
        q = "pool" if src_f32 else "sp"
        if callable(src):
            sap, sbuf_ = src(t)
        else:
            sap, sbuf_ = src[t * 128:(t + 1) * 128, :], src_buf
        P.op(q, lambda e, s=s, sap=sap: e.dma_start(out=xbh[:, s, :], in_=sap),
             reads=[sbuf_], writes=[xb.bufs[s]], dma=True, tag="xb load")
        for half in range(2):
            bk = banks[(2 * t + half) % len(banks)]
            pb = K.ps[bk].bitcast(BF16)

            def tr(e, s=s, half=half, pb=pb):
                ins = None
                for j in range(8):
                    kc = half * 8 + j
                    ins = e.transpose(pb[:, j * 128:(j + 1) * 128], xbh[:, s, kc * 128:(kc + 1) * 128], ident[:])
                return ins
            P.op("pe", tr, reads=[xb.bufs[s], K.ident_buf], writes=[K.psb[bk]], tag="xT transposes")
            eng = "act" if half == 0 else "dve"

            def ev(e, t=t, half=half, pb=pb, eng=eng):
                o = XTh[:, half * 8:(half + 1) * 8, t * 128:(t + 1) * 128]
                i = pb.rearrange("p (j c) -> p j c", j=8)
                if eng == "act":
                    return e.activation(out=o, in_=i, func=AF.Copy)
                return e.tensor_copy(out=o, in_=i)
            P.op(eng, ev, reads=[K.psb[bk]], writes=[XT.bufs[t]], tag="xT evac")
    P.free(xb)


def ffn_stage(K, x_src, x_buf, wg, wu, wd, lng, lnb, outs, base=0, after_tile=None):
    P = K.P
    nc = K.nc
    NT = NTOK // 128
    off = base
    acc = P.sbuf("acc", [128, NT, D], F32, off, nbufs=NT * 4); off += NT * D * 4
    XT = P.sbuf("XT", [128, KC, NTOK], BF16, off, nbufs=NT); off += KC * NTOK * 2
    wgs = P.sbuf("wgs", [128, 2, KC, FG], BF16, off, nbufs=2); off += 2 * KC * FG * 2
    wus = P.sbuf("wus", [128, 2, KC, FG], BF16, off, nbufs=2); off += 2 * KC * FG * 2
    wds = P.sbuf("wds", [128, 2, 2, D], BF16, off, nbufs=2); off += 2 * 2 * D * 2
    hT = P.sbuf("hT", [128, 2, 2, NTOK], BF16, off, nbufs=2); off += 2 * 2 * NTOK * 2
    stmp = P.sbuf("stmp", [128, 2, 512], F32, off, nbufs=2); off += 2 * 512 * 4
    gb = P.sbuf("lngb", [128, 2, D], F32, off, nbufs=1); off += 2 * D * 4
    st = P.sbuf("lnst", [128, 2, 4 * 6 + 8], F32, off, nbufs=2); off += 2 * 32 * 4
    xb_off = off
    acch, XTh, wgh, wuh, wdh, hTh, sth, gbh, lsth = (acc.handle, XT.handle, wgs.handle, wus.handle, wds.handle,
                                                      hT.handle, stmp.handle, gb.handle, st.handle)

    load_const(K, "lng", gb, gbh[:, 0, :], lng[:, :])
    load_const(K, "lnb", gb, gbh[:, 1, :], lnb[:, :])

    for t in range(NT):
        P.op("sp", lambda e, t=t: e.dma_start(out=acch[:, t, :], in_=x_src[t * 128:(t + 1) * 128, :]),
             reads=[x_buf], writes=acc.bufs[t * 4:(t + 1) * 4], dma=True, tag="acc load")
        P.op("act", lambda e, t=t: e.activation(out=acch[:, t, :], in_=acch[:, t, :], func=AF.Copy, scale=ALPHA),
             reads=[], writes=acc.bufs[t * 4:(t + 1) * 4], tag="acc scale")

    load_xT(K, x_src, x_buf, NTOK, XT, xb_off, [4, 5, 6, 7], K.ident, True)

    def load_wgu(g):
        s = g % 2
        for (wsrc, wh, wa, nm) in ((wg, wgh, wgs, "wg"), (wu, wuh, wus, "wu")):
            for piece in range(2):
                P.op("pool", lambda e, g=g, s=s, wsrc=wsrc, wh=wh, piece=piece: e.dma_start(
                    out=wh[:, s, piece * 8:(piece + 1) * 8, :], in_=wsrc[g, :, piece * 8:(piece + 1) * 8, :]),
                    reads=[], writes=[wa.bufs[s]], dma=True, tag=nm + " load")

    def load_wd(g):
        s = g % 2
        wdv = wd.rearrange("(g fc p) d -> g p fc d", fc=2, p=128)
        P.op("pool", lambda e, g=g, s=s: e.dma_start(out=wdh[:, s, :, :], in_=wdv[g]),
             reads=[], writes=[wds.bufs[s]], dma=True, tag="wd load")

    def upgate(g, only=None):
        s = g % 2
        for c in range(2):
            for th in range(2):
                if only is not None and only != (c * 2 + th):
                    continue
                i = (c * 2 + th) % 2
                bg, bu = i, 2 + i
                toks = slice(th * 512, (th + 1) * 512)
                xbufs = XT.bufs[th * 4:(th + 1) * 4]
                for (wh, wa, bk) in ((wgh, wgs, bg), (wuh, wus, bu)):
                    def mm(e, wh=wh, bk=bk, c=c, toks=toks, s=s):
                        ins = None
                        for kc in range(KC):
                            ins = e.matmul(K.ps[bk][:, :], wh[:, s, kc, c * 128:(c + 1) * 128], XTh[:, kc, toks],
                                           start=(kc == 0), stop=(kc == KC - 1))
                        return ins
                    P.op("pe", mm, reads=xbufs + [wa.bufs[s]], writes=[K.psb[bk]], tag="upgate mm")
                P.op("act", lambda e, i=i, bg=bg: e.activation(out=sth[:, i, :], in_=K.ps[bg][:, :], func=AF.Silu),
                     reads=[K.psb[bg]], writes=[stmp.bufs[i]], tag="silu")
                P.op("dve", lambda e, i=i, bu=bu, c=c, toks=toks, s=s: e.tensor_tensor(
                    out=hTh[:, s, c, toks], in0=K.ps[bu][:, :], in1=sth[:, i, :], op=ALU.mult),
                    reads=[K.psb[bu], stmp.bufs[i]], writes=[hT.bufs[s]], tag="hmul")

    dcnt = [0]

    def down(g, ln_after=False, part=None):
        s = g % 2
        for t in range(NT):
            if part is not None and t // 2 != part:
                continue
            if ln_after and t > 0:
                ln_tile(P, acch, acc.bufs[(t - 1) * 4:t * 4], t - 1, gbh, gb, lsth, st, (t - 1) % 2, outs)
                if after_tile is not None:
                    after_tile(t - 1)
            for db in range(4):
                bk = 4 + dcnt[0] % 4
                dcnt[0] += 1

                def mm(e, bk=bk, t=t, db=db, s=s):
                    ins = None
                    for fc in range(2):
                        ins = e.matmul(K.ps[bk][:, :], hTh[:, s, fc, t * 128:(t + 1) * 128],
                                       wdh[:, s, fc, db * 512:(db + 1) * 512], start=(fc == 0), stop=(fc == 1))
                    return ins
                P.op("pe", mm, reads=[hT.bufs[s], wds.bufs[s]], writes=[K.psb[bk]], tag="down mm")
                P.op("dve", lambda e, bk=bk, t=t, db=db: e.scalar_tensor_tensor(
                    out=acch[:, t, db * 512:(db + 1) * 512], in0=K.ps[bk][:, :], scalar=0.5,
                    in1=acch[:, t, db * 512:(db + 1) * 512], op0=ALU.mult, op1=ALU.add),
                    reads=[K.psb[bk]], writes=[acc.bufs[t * 4 + db]], tag="acc add")

    ngr = K.ng_override if hasattr(K, "ng_override") else NG
    load_wgu(0)
    if ngr > 1:
        load_wgu(1)
    load_wd(0)
    for g in range(ngr):
        for blk in range(4):
            upgate(g, only=blk)
            if g > 0:
                down(g - 1, part=blk)
        if g + 2 < ngr:
            load_wgu(g + 2)
        if g + 1 < ngr:
            load_wd(g + 1)
    P.lnbf = P.sbuf("lnbf", [128, 2, D], BF16, xb_off, nbufs=2)
    down(ngr - 1, ln_after=True)
    ln_tile(P, acch, acc.bufs[(NT - 1) * 4:NT * 4], NT - 1, gbh, gb, lsth, st, (NT - 1) % 2, outs)
    if after_tile is not None:
        after_tile(NT - 1)
    P.free(P.lnbf)
    for a in (acc, XT, wgs, wus, wds, hT, stmp, gb, st):
        P.free(a)


def setup_consts(K, identd):
    P = K.P
    a = P.sbuf("ident", [128, 128], BF16, 16384)
    K.ident = a.handle
    K.ident_buf = a.bufs[0]
    P.op("pool", lambda e: e.dma_start(out=K.ident[:, :], in_=identd.ap()[:, :]), reads=[], writes=a.bufs, dma=True,
         tag="ident")
    K.const_top = 16384 + 256


def finish(K, out_bufs):
    K.P.op("sp", lambda e: e.nop(), reads=out_bufs, writes=[], tag="final wait")


NCH_ATT = 12
NCH_RW = 15
NEG = -30000.0
LAMBDA_INIT = 0.2
ATTN_EPS = 1e-5
GN_EPS = 64e-5


def project_fm(K, X1T, wslot, wslot_buf, bank_rot, evac):
    P = K.P
    X1Th = X1T.handle
    for tg in range(4):
        bk = bank_rot()

        def mm(e, bk=bk, tg=tg):
            ins = None
            for kc in range(KC):
                ins = e.matmul(K.ps[bk][:, :], wslot[:, kc, :], X1Th[:, kc, tg * 512:(tg + 1) * 512],
                               start=(kc == 0), stop=(kc == KC - 1))
            return ins
        P.op("pe", mm, reads=X1T.bufs[tg * 4:(tg + 1) * 4] + [wslot_buf], writes=[K.psb[bk]], tag="proj fm")
        evac(tg, bk)


def attention_stage(K, X1T, win_t, bm, ctab, lamv, ng_t, o_dst, o_buf, base):
    P = K.P
    off = base
    QT = P.sbuf("QT", [128, 4, SEQ], BF16, off, nbufs=4); off += 4 * SEQ * 2
    KT = P.sbuf("KT", [128, 4, SEQ], BF16, off, nbufs=4); off += 4 * SEQ * 2
    VA = P.sbuf("VA", [128, 16, 4, 130], BF16, off, nbufs=16); off += 16 * 4 * 130 * 2
    wr = P.sbuf("wring", [128, 4, KC, 128], BF16, off, nbufs=4); off += 4 * KC * 128 * 2
    bms = P.sbuf("bms", [128, 2, 5, 512], F32, off, nbufs=2); off += 2 * 5 * 512 * 4
    ct = P.sbuf("ctab", [128, 64], F32, off); off += 64 * 4
    lm = P.sbuf("lam", [128, 4 * 64 + 16], F32, off); off += (4 * 64 + 16) * 4
    ngs = P.sbuf("ngs", [128, 512], F32, off); off += 512 * 4
    stm = P.sbuf("stmp", [128, 4, 512], F32, off, nbufs=4); off += 4 * 512 * 4
    pT = P.sbuf("pT", [128, 4, 512], BF16, off, nbufs=4); off += 4 * 512 * 2
    Osb = P.sbuf("Osb", [128, 2, 4, 130], F32, off, nbufs=2); off += 2 * 4 * 130 * 4
    ow = P.sbuf("ow", [128, 4, 4, 128], F32, off, nbufs=1); off += 16 * 128 * 4
    osq = P.sbuf("osq", [128, 4, 128], F32, off, nbufs=1); off += 4 * 128 * 4
    sm = P.sbuf("sm", [128, 32], F32, off, nbufs=1); off += 32 * 4
    owb = P.sbuf("owb", [128, 4, 4, 128], BF16, off, nbufs=1); off += 16 * 128 * 2
    owbh = owb.handle
    QTh, KTh, VAh, wrh, bmh, cth, lmh, ngh, stmh, pTh, Osh, owh, osqh, smh = (
        QT.handle, KT.handle, VA.handle, wr.handle, bms.handle, ct.handle, lm.handle, ngs.handle, stm.handle,
        pT.handle, Osb.handle, ow.handle, osq.handle, sm.handle)
    X1Th = X1T.handle

    load_const(K, "ctab", ct, cth[:, :], ctab[:, :])
    load_const(K, "lamv", lm, lmh[:, 0:256], lamv.rearrange("p a d -> p (a d)"))
    load_const(K, "ngt", ngs, ngh[:, :], ng_t[:, :])
    P.op("pool", lambda e: e.memset(VAh[:, :, :, 128:130], 1.0), reads=[], writes=VA.bufs, tag="va ones")

    P.op("dve", lambda e: e.tensor_tensor(out=lmh[:, 0:64], in0=lmh[:, 0:64], in1=lmh[:, 64:128], op=ALU.mult),
         reads=[], writes=lm.bufs, tag="lam1")
    P.op("dve", lambda e: e.tensor_tensor(out=lmh[:, 128:192], in0=lmh[:, 128:192], in1=lmh[:, 192:256], op=ALU.mult),
         reads=[], writes=lm.bufs, tag="lam2")
    P.op("dve", lambda e: e.tensor_reduce(out=lmh[:, 256:257], in_=lmh[:, 0:64], axis=AX.X, op=ALU.add),
         reads=[], writes=lm.bufs, tag="lam3")
    P.op("dve", lambda e: e.tensor_reduce(out=lmh[:, 257:258], in_=lmh[:, 128:192], axis=AX.X, op=ALU.add),
         reads=[], writes=lm.bufs, tag="lam4")
    P.op("act", lambda e: e.activation(out=lmh[:, 258:260], in_=lmh[:, 256:258], func=AF.Exp),
         reads=[], writes=lm.bufs, tag="lam5")
    P.op("dve", lambda e: e.tensor_tensor(out=lmh[:, 260:261], in0=lmh[:, 258:259], in1=lmh[:, 259:260],
                                          op=ALU.subtract), reads=[], writes=lm.bufs, tag="lam6")
    P.op("dve", lambda e: e.tensor_scalar(out=lmh[:, 261:262], in0=lmh[:, 260:261], scalar1=LAMBDA_INIT, scalar2=None,
                                          op0=ALU.add), reads=[], writes=lm.bufs, tag="lam7")
    LAM = lmh[:, 261:262]

    rot = [0]

    def bank_rot():
        rot[0] += 1
        return rot[0] % 4

    def load_w(ci):
        s = ci % 4
        P.op("pool", lambda e, ci=ci, s=s: e.dma_start(out=wrh[:, s, :, :], in_=win_t[ci]),
             reads=[], writes=[wr.bufs[s]], dma=True, tag="win load")
        return s
    for ci in range(min(3, NCH_ATT)):
        load_w(ci)
    for ci in range(NCH_ATT):
        s = ci % 4
        if ci + 3 < NCH_ATT:
            load_w(ci + 3)
        if ci < 8:
            dstT, dbuf = (QTh, QT) if ci < 4 else (KTh, KT)
            a = ci % 4

            def evac(tg, bk, dstT=dstT, dbuf=dbuf, a=a):
                eng = "act" if tg % 2 == 0 else "dve"

                def ev(e, tg=tg, bk=bk):
                    o = dstT[:, a, tg * 512:(tg + 1) * 512]
                    if eng == "act":
                        return e.activation(out=o, in_=K.ps[bk][:, :], func=AF.Copy)
                    return e.tensor_copy(out=o, in_=K.ps[bk][:, :])
                P.op(eng, ev, reads=[K.psb[bk]], writes=[dbuf.bufs[a]], tag="qk evac")
            project_fm(K, X1T, wrh[:, s], wr.bufs[s], bank_rot, evac)
        else:
            a = ci - 8
            for tq in range(4):
                bk = bank_rot()

                def mm(e, bk=bk, tq=tq, s=s):
                    ins = None
                    for tt in range(4):
                        t = tq * 4 + tt
                        for kc in range(KC):
                            ins = e.matmul(K.ps[bk][:, tt * 128:(tt + 1) * 128], X1Th[:, kc, t * 128:(t + 1) * 128],
                                           wrh[:, s, kc, :], start=(kc == 0 and tt == 0), stop=(kc == KC - 1),
                                           skip_group_check=True)
                    return ins
                P.op("pe", mm, reads=X1T.bufs[tq * 4:(tq + 1) * 4] + [wr.bufs[s]], writes=[K.psb[bk]], tag="v proj")
                P.op("act", lambda e, bk=bk, tq=tq, a=a: e.activation(
                    out=VAh[:, tq * 4:(tq + 1) * 4, a, 0:128], in_=K.ps[bk].rearrange("p (t c) -> p t c", t=4),
                    func=AF.Copy), reads=[K.psb[bk]], writes=VA.bufs[tq * 4:(tq + 1) * 4], tag="v evac")

    SCALE = 64 ** -0.5
    LOOK = 2
    tiles = []
    for a in range(4):
        for qg in range(4):
            nkb = 4 * qg + 4
            for m in range(2):
                for kb in range(nkb):
                    tiles.append((a, qg, m, kb, nkb))

    def front(idx):
        a, qg, m, kb, nkb = tiles[idx]
        bs = a % 2
        if qg == 0 and m == 0 and kb == 0:
            P.op("sp", lambda e, a=a, bs=bs: e.dma_start(out=bmh[:, bs, :, :], in_=bm[a].rearrange("r p q -> p r q")),
                 reads=[], writes=[bms.bufs[bs]], dma=True, tag="bm load")
        pr = slice(m * 64, (m + 1) * 64)
        i = idx % 4
        sb = i
        P.op("pe", lambda e, sb=sb, a=a, pr=pr, kb=kb, qg=qg: e.matmul(
            K.ps[sb][:, :], KTh[pr, a, kb * 128:(kb + 1) * 128], QTh[pr, a, qg * 512:(qg + 1) * 512],
            start=True, stop=True), reads=[KT.bufs[a], QT.bufs[a]], writes=[K.psb[sb]], tag="qk mm")
        r = kb - 4 * qg
        var = 4 if r < 0 else r
        P.op("dve", lambda e, sb=sb, i=i, bs=bs, var=var: e.scalar_tensor_tensor(
            out=stmh[:, i, :], in0=K.ps[sb][:, :], scalar=SCALE, in1=bmh[:, bs, var, :],
            op0=ALU.mult, op1=ALU.add), reads=[K.psb[sb], bms.bufs[bs]], writes=[stm.bufs[i]], tag="score bias")
        cidx = a * 16 + ((qg * 512 - kb * 128 + 384) // 128 if r < 0 else 3)
        P.op("act", lambda e, i=i, cidx=cidx: e.activation(
            out=pTh[:, i, :], in_=stmh[:, i, :], func=AF.Exp, bias=cth[:, cidx:cidx + 1], scale=1.0),
            reads=[stm.bufs[i], ct.bufs[0]], writes=[pT.bufs[i]], tag="exp")

    def back(idx):
        a, qg, m, kb, nkb = tiles[idx]
        i = idx % 4
        ob = (4, 5) if m == 0 else (6, 7)

        def pv(e, i=i, kb=kb, a=a, ob=ob, nkb=nkb):
            ins = None
            for qb in range(4):
                bk = ob[qb // 2]
                c0 = (qb % 2) * 130
                ins = e.matmul(K.ps[bk][:, c0:c0 + 130], pTh[:, i, qb * 128:(qb + 1) * 128],
                               VAh[:, kb, a, :], start=(kb == 0 and qb % 2 == 0), stop=(kb == nkb - 1),
                               skip_group_check=True)
            return ins
        P.op("pe", pv, reads=[pT.bufs[i], VA.bufs[kb]], writes=[K.psb[ob[0]], K.psb[ob[1]]], tag="pv mm")
        if kb != nkb - 1:
            return
        for hb in range(2):
            P.op("act", lambda e, m=m, hb=hb, ob=ob: e.activation(
                out=Osh[:, m, hb * 2:(hb + 1) * 2, :],
                in_=K.ps[ob[hb]][:, 0:260].rearrange("p (q c) -> p q c", q=2), func=AF.Copy),
                reads=[K.psb[ob[hb]]], writes=[Osb.bufs[m]], tag="O evac")
        if m == 0:
            return
        P.op("dve", lambda e: e.reciprocal(out=smh[:, 0:4], in_=Osh[:, 0, :, 128:129].rearrange("p q c -> p (q c)")),
             reads=[Osb.bufs[0]], writes=sm.bufs, tag="r1")
        P.op("dve", lambda e: e.reciprocal(out=smh[:, 4:8], in_=Osh[:, 1, :, 128:129].rearrange("p q c -> p (q c)")),
             reads=[Osb.bufs[1]], writes=sm.bufs, tag="r2")
        P.op("dve", lambda e: e.tensor_scalar(out=smh[:, 4:8], in0=smh[:, 4:8], scalar1=LAM, scalar2=None,
                                              op0=ALU.mult), reads=[lm.bufs[0]], writes=sm.bufs, tag="r2lam")
        P.op("dve", lambda e, a=a: e.tensor_tensor(
            out=owh[:, :, a, :], in0=Osh[:, 0, :, 0:128], in1=smh[:, 0:4].unsqueeze(2).to_broadcast([128, 4, 128]),
            op=ALU.mult), reads=[Osb.bufs[0]] + sm.bufs, writes=ow.bufs, tag="o1")
        P.op("dve", lambda e: e.tensor_tensor(
            out=osqh[:, :, :], in0=Osh[:, 1, :, 0:128], in1=smh[:, 4:8].unsqueeze(2).to_broadcast([128, 4, 128]),
            op=ALU.mult), reads=[Osb.bufs[1]] + sm.bufs, writes=osq.bufs, tag="o2")
        P.op("dve", lambda e, a=a: e.tensor_tensor(out=owh[:, :, a, :], in0=owh[:, :, a, :], in1=osqh[:, :, :],
                                                   op=ALU.subtract), reads=[], writes=ow.bufs + osq.bufs, tag="o12")
        P.op("pool", lambda e, a=a: e.tensor_tensor(out=osqh[:, :, :], in0=owh[:, :, a, :], in1=owh[:, :, a, :],
                                                    op=ALU.mult), reads=[], writes=ow.bufs + osq.bufs, tag="osq")
        P.op("dve", lambda e: e.tensor_reduce(out=smh[:, 8:12], in_=osqh[:, :, :], axis=AX.X, op=ALU.add),
             reads=[osq.bufs[0]], writes=sm.bufs, tag="ossq")
        P.op("dve", lambda e: e.tensor_scalar(out=smh[:, 8:12], in0=smh[:, 8:12], scalar1=1.0 / 128, scalar2=ATTN_EPS,
                                              op0=ALU.mult, op1=ALU.add), reads=[], writes=sm.bufs, tag="oms")
        P.op("act", lambda e: e.activation(out=smh[:, 12:16], in_=smh[:, 8:12], func=AF.Sqrt),
             reads=[], writes=sm.bufs, tag="orms")
        P.op("dve", lambda e: e.reciprocal(out=smh[:, 16:20], in_=smh[:, 12:16]), reads=[], writes=sm.bufs,
             tag="orr")
        P.op("dve", lambda e, a=a: e.tensor_tensor(
            out=owh[:, :, a, :], in0=owh[:, :, a, :], in1=smh[:, 16:20].unsqueeze(2).to_broadcast([128, 4, 128]),
            op=ALU.mult), reads=[], writes=ow.bufs + sm.bufs, tag="onorm")
        P.op("dve", lambda e, a=a: e.scalar_tensor_tensor(
            out=owbh[:, :, a, :], in0=owh[:, :, a, :], scalar=1.0 - LAMBDA_INIT,
            in1=ngh[:, a * 128:(a + 1) * 128].unsqueeze(1).to_broadcast([128, 4, 128]),
            op0=ALU.mult, op1=ALU.mult), reads=[ngs.bufs[0]] + ow.bufs, writes=owb.bufs, tag="og")
        for qb in range(4):
            t0 = qg * 512 + qb * 128
            P.op("sp", lambda e, qb=qb, t0=t0, a=a: e.dma_start(
                out=o_dst[t0:t0 + 128, a * 128:(a + 1) * 128], in_=owbh[:, qb, a, :]),
                reads=owb.bufs, writes=[o_buf[t0 // 256] if isinstance(o_buf, list) else o_buf], dma=True,
                tag="o_attn out")

    for idx in range(len(tiles) + LOOK):
        if idx < len(tiles):
            front(idx)
        if idx >= LOOK:
            back(idx - LOOK)
    for al in (QT, KT, VA, wr, bms, ct, lm, ngs, stm, pT, Osb, ow, osq, sm, owb):
        P.free(al)


def host_tile_ffn_w(w):
    return np.ascontiguousarray(w.reshape(KC, 128, NG, FG).transpose(2, 1, 0, 3))


def host_win_cols(half):
    cols = []
    for blk in range(3):
        for a in range(4):
            head = 4 * half + a
            cols.append(np.arange(blk * 1024 + head * 128, blk * 1024 + head * 128 + 128))
    for blk in range(3):
        for cc in range(4):
            c0 = 3072 + blk * 1024 + (8 * half + 2 * cc) * 64
            cols.append(np.arange(c0, c0 + 128))
    cols.append(np.arange(6144, 6144 + 128))
    cols.append(np.arange(6144 + 128, 6144 + 256))
    cols.append(np.arange(6144 + 256, 6144 + 288))
    return cols


def host_tile_win(w_in, half):
    cols = host_win_cols(half)
    out = np.zeros((len(cols), 128, KC, 128), np.float32)
    for ci, c in enumerate(cols):
        blk = w_in[:, c]
        out[ci, :, :, :len(c)] = blk.reshape(KC, 128, len(c)).transpose(1, 0, 2)
    return out


def host_attn_consts(half):
    bm = np.zeros((4, 5, 128, 512), np.float32)
    ctab = np.zeros((128, 64), np.float32)
    i = np.arange(128)[:, None].astype(np.float64)
    j = np.arange(512)[None, :].astype(np.float64)
    for a in range(4):
        slope = 2.0 ** (-(4 * half + a + 1))
        for r in range(4):
            d = j - i - 128 * r
            bm[a, r] = np.where(d >= 0, -slope * d, NEG)
        bm[a, 4] = -slope * (j - i)
        for idx in range(16):
            ctab[:, a * 16 + idx] = -slope * (idx * 128 - 384)
    return bm, ctab


C0 = float(np.exp(-0.5))
CH = 64
NCHUNK = SEQ // CH


def rwkv_alloc_persist(K, base):
    P = K.P
    R = Ctx()
    off = base
    R.AR = P.sbuf("AR", [128, 4, NCHUNK, 2, CH], BF16, off, nbufs=NCHUNK); off += 4 * NCHUNK * 2 * CH * 2
    R.BK = P.sbuf("BK", [128, 4, NCHUNK, 2, CH], BF16, off, nbufs=NCHUNK); off += 4 * NCHUNK * 2 * CH * 2
    R.VC = P.sbuf("VC", [128, 4, SEQ], BF16, off, nbufs=NCHUNK); off += 4 * SEQ * 2
    R.LW = P.sbuf("LW", [128, SEQ], BF16, off, nbufs=4); off += SEQ * 2
    R.SG = P.sbuf("SG", [128, 2, SEQ], BF16, off, nbufs=4); off += 2 * SEQ * 2
    R.GC = P.sbuf("GC", [128, 4, NCHUNK], F32, off, nbufs=4); off += 4 * NCHUNK * 4
    R.BON = P.sbuf("BON", [64, NCHUNK, 8], F32, off, nbufs=4); off += NCHUNK * 8 * 4
    R.pcol = P.sbuf("pcol", [128, 64], F32, off); off += 256
    R.w2a2 = P.sbuf("w2a2", [128, 512], BF16, off); off += 1024
    R.g2 = P.sbuf("g2", [128, 2, 512], BF16, off); off += 2048
    R.bones = P.sbuf("bones", [128, 128], BF16, off); off += 256
    R.hsel = P.sbuf("hsel", [128, 16], BF16, off); off += 32
    R.top = off
    return R


def rwkv_prep(K, R, X1T, win_t, pcol_d, w2a2_d, g2_d, bones_d, hsel_d, base):
    P = K.P
    off = base
    wr = P.sbuf("wring2", [128, 3, KC, 128], BF16, off, nbufs=3); off += 3 * KC * 128 * 2
    ones = P.sbuf("ones", [128, 512], F32, off); off += 2048
    names = ["rm", "km", "sg", "av", "kk", "nrm", "ka", "kp", "cum", "cx", "dd"]
    T = {}
    for n in names:
        T[n] = P.sbuf(n, [128, 512], F32, off); off += 2048
    pre = P.sbuf("pre", [128, 3, 520], F32, off, nbufs=3); off += 3 * 520 * 4
    sq = P.sbuf("sq", [128, 512], BF16, off); off += 1024
    rb = P.sbuf("rb", [128, 512], BF16, off); off += 1024
    cb = P.sbuf("cb", [128, 16], F32, off); off += 64
    h = {n: T[n].handle for n in names}
    b = {n: T[n].bufs[0] for n in names}
    wrh, preh, sqh, rbh, cbh, onesh = wr.handle, pre.handle, sq.handle, rb.handle, cb.handle, ones.handle
    ARh, BKh, VCh, LWh, SGh, GCh, BONh, pc, w2h, g2h, boh, hsh = (
        R.AR.handle, R.BK.handle, R.VC.handle, R.LW.handle, R.SG.handle, R.GC.handle, R.BON.handle, R.pcol.handle,
        R.w2a2.handle, R.g2.handle, R.bones.handle, R.hsel.handle)
    X1Th = X1T.handle

    load_const(K, "pcol", R.pcol, pc[:, :], pcol_d[:, :])
    load_const(K, "w2a2", R.w2a2, w2h[:, :], w2a2_d[:, :], q="pool")
    load_const(K, "g2", R.g2, g2h[:, :, :], g2_d[:, :, :], q="pool")
    load_const(K, "bones", R.bones, boh[:, :], bones_d[:, :], q="pool")
    load_const(K, "hsel", R.hsel, hsh[:, :], hsel_d[:, :], q="pool")
    P.op("pool", lambda e: e.memset(onesh[:, :], 1.0), reads=[], writes=ones.bufs, tag="ones")
    P.op("pool", lambda e: e.memset(SGh[:, 1, :], 0.0), reads=[], writes=R.SG.bufs, tag="sg2 zero")

    rot = [0]

    def bank_rot():
        rot[0] += 1
        return rot[0] % 8

    def load_w(ci, s):
        P.op("pool", lambda e, ci=ci, s=s: e.dma_start(out=wrh[:, s, :, :], in_=win_t[NCH_ATT + ci]),
             reads=[], writes=[wr.bufs[s]], dma=True, tag="win2 load")

    def proj_mix(ci, s, st, tq, out_fn):
        bk = bank_rot()

        def mm(e, bk=bk, tq=tq, s=s):
            ins = None
            for kc in range(KC):
                ins = e.matmul(K.ps[bk][:, :], wrh[:, s, kc, :], X1Th[:, kc, tq * 512:(tq + 1) * 512],
                               start=(kc == 0), stop=(kc == KC - 1))
            return ins
        P.op("pe", mm, reads=X1T.bufs[tq * 4:(tq + 1) * 4] + [wr.bufs[s]], writes=[K.psb[bk]], tag="proj rw")
        if tq == 0:
            P.op("pool", lambda e, st=st: e.memset(preh[:, st, 0:1], 0.0), reads=[], writes=[pre.bufs[st]], tag="carry0")
        P.op("act", lambda e, bk=bk, st=st: e.activation(out=preh[:, st, 1:513], in_=K.ps[bk][:, :], func=AF.Copy),
             reads=[K.psb[bk]], writes=[pre.bufs[st]], tag="pre evac")
        P.op("dve", lambda e, st=st: e.tensor_tensor(out=h["dd"][:, :], in0=preh[:, st, 0:512], in1=preh[:, st, 1:513],
                                                     op=ALU.subtract), reads=[pre.bufs[st]], writes=[b["dd"]], tag="mix d")
        out_fn(preh[:, st, 1:513], pre.bufs[st])
        if tq < 3:
            P.op("act", lambda e, st=st: e.activation(out=preh[:, st, 0:1], in_=preh[:, st, 512:513], func=AF.Copy),
                 reads=[], writes=[pre.bufs[st]], tag="carry")

    def mixed_to(out_ap, out_bufs, mucol, eng="dve"):
        def f(pre1, prebuf):
            P.op("dve", lambda e: e.scalar_tensor_tensor(out=out_ap, in0=h["dd"][:, :], scalar=pc[:, mucol:mucol + 1],
                                                         in1=pre1, op0=ALU.mult, op1=ALU.add),
                 reads=[b["dd"], prebuf, R.pcol.bufs[0]], writes=out_bufs, tag="mix out")
        return f

    for li, ci in enumerate((12, 13, 14)):
        load_w(ci, li)
    for tq in range(4):
        tsl = slice(tq * 512, (tq + 1) * 512)
        proj_mix(12, 0, 0, tq, mixed_to(h["rm"][:, :], [b["rm"]], 12))
        P.op("act", lambda e, tsl=tsl: e.activation(out=LWh[0:64, tsl], in_=h["rm"][0:64, :], func=AF.Tanh),
             reads=[b["rm"]], writes=[R.LW.bufs[tq]], tag="tanh wd")
        P.op("dve", lambda e, tsl=tsl: e.tensor_copy(out=LWh[64:128, tsl], in_=h["rm"][64:128, :]),
             reads=[b["rm"]], writes=[R.LW.bufs[tq]], tag="copy ad")
        proj_mix(13, 1, 1, tq, mixed_to(h["km"][:, :], [b["km"]], 13))
        P.op("act", lambda e, tsl=tsl: e.activation(out=SGh[:, 0, tsl], in_=h["km"][:, :], func=AF.Sigmoid),
             reads=[b["km"]], writes=[R.SG.bufs[tq]], tag="sig gd")
        proj_mix(14, 2, 2, tq, mixed_to(h["sg"][:, :], [b["sg"]], 14))
        P.op("act", lambda e, tsl=tsl: e.activation(out=SGh[0:32, 1, tsl], in_=h["sg"][0:32, :], func=AF.Sigmoid),
             reads=[b["sg"]], writes=[R.SG.bufs[tq]], tag="sig gd2")

    for cc in range(4):
        for st in range(3):
            load_w(st * 4 + cc, st)
        csl = slice(cc * 128, (cc + 1) * 128)
        for tq in range(4):
            tsl = slice(tq * 512, (tq + 1) * 512)
            jsl = slice(tq * 8, (tq + 1) * 8)
            cbufs = R.AR.bufs[tq * 8:(tq + 1) * 8]
            kbufs = R.BK.bufs[tq * 8:(tq + 1) * 8]
            proj_mix(cc, 0, 0, tq, mixed_to(h["rm"][:, :], [b["rm"]], cc))
            proj_mix(4 + cc, 1, 1, tq, mixed_to(h["km"][:, :], [b["km"]], 4 + cc))
            proj_mix(8 + cc, 2, 2, tq, mixed_to(VCh[:, cc, tsl], R.VC.bufs[tq * 8:(tq + 1) * 8], 8 + cc))
            bz, ba = bank_rot(), bank_rot()
            P.op("pe", lambda e, bz=bz, csl=csl, tsl=tsl: e.matmul(K.ps[bz][:, :], w2h[0:64, csl], LWh[0:64, tsl],
                                                                   start=True, stop=True),
                 reads=[R.w2a2.bufs[0], R.LW.bufs[tq]], writes=[K.psb[bz]], tag="w lora")
            P.op("pe", lambda e, ba=ba, csl=csl, tsl=tsl: e.matmul(K.ps[ba][:, :], w2h[64:128, csl], LWh[64:128, tsl],
                                                                   start=True, stop=True),
                 reads=[R.w2a2.bufs[0], R.LW.bufs[tq]], writes=[K.psb[ba]], tag="a lora")
            P.op("act", lambda e, bz=bz, cc=cc: e.activation(out=h["sg"][:, :], in_=K.ps[bz][:, :], func=AF.Sigmoid,
                                                            bias=pc[:, 15 + cc:16 + cc]),
                 reads=[K.psb[bz], R.pcol.bufs[0]], writes=[b["sg"]], tag="sig w")
            P.op("act", lambda e, ba=ba, cc=cc: e.activation(out=h["av"][:, :], in_=K.ps[ba][:, :], func=AF.Sigmoid,
                                                            bias=pc[:, 19 + cc:20 + cc]),
                 reads=[K.psb[ba], R.pcol.bufs[0]], writes=[b["av"]], tag="sig a")
            P.op("dve", lambda e, cc=cc: e.tensor_scalar(out=h["kk"][:, :], in0=h["km"][:, :],
                                                        scalar1=pc[:, 23 + cc:24 + cc], scalar2=None, op0=ALU.mult),
                 reads=[b["km"], R.pcol.bufs[0]], writes=[b["kk"]], tag="kkraw")
            P.op("act", lambda e: e.activation(out=sqh[:, :], in_=h["kk"][:, :], func=AF.Square),
                 reads=[b["kk"]], writes=sq.bufs, tag="kk sq")
            bn_ = bank_rot()
            P.op("pe", lambda e, bn_=bn_: e.matmul(K.ps[bn_][:, :], boh[:, :], sqh[:, :], start=True, stop=True),
                 reads=[R.bones.bufs[0], sq.bufs[0]], writes=[K.psb[bn_]], tag="ssq mm")
            P.op("act", lambda e, bn_=bn_: e.activation(out=h["nrm"][:, :], in_=K.ps[bn_][:, :], func=AF.Sqrt),
                 reads=[K.psb[bn_]], writes=[b["nrm"]], tag="nrm sqrt")
            P.op("dve", lambda e: e.tensor_scalar(out=h["nrm"][:, :], in0=h["nrm"][:, :], scalar1=1e-12, scalar2=None,
                                                  op0=ALU.max), reads=[], writes=[b["nrm"]], tag="nrm max")
            P.op("dve", lambda e: e.reciprocal(out=h["nrm"][:, :], in_=h["nrm"][:, :]), reads=[], writes=[b["nrm"]],
                 tag="nrm rcp")
            P.op("dve", lambda e: e.tensor_tensor(out=h["kk"][:, :], in0=h["kk"][:, :], in1=h["nrm"][:, :], op=ALU.mult),
                 reads=[b["nrm"]], writes=[b["kk"]], tag="kk")
            P.op("pool", lambda e: e.tensor_tensor(out=h["ka"][:, :], in0=h["kk"][:, :], in1=h["av"][:, :], op=ALU.mult),
                 reads=[b["kk"], b["av"]], writes=[b["ka"]], tag="ka")
            P.op("dve", lambda e, cc=cc: e.tensor_scalar(out=h["kp"][:, :], in0=h["av"][:, :], scalar1=-1.0,
                                                        scalar2=pc[:, 27 + cc:28 + cc], op0=ALU.add, op1=ALU.mult),
                 reads=[b["av"], R.pcol.bufs[0]], writes=[b["kp"]], tag="kp1")
            P.op("dve", lambda e: e.scalar_tensor_tensor(out=h["kp"][:, :], in0=h["kp"][:, :], scalar=1.0,
                                                         in1=h["km"][:, :], op0=ALU.add, op1=ALU.mult),
                 reads=[b["km"]], writes=[b["kp"]], tag="kp2")
            P.op("dve", lambda e, cc=cc: e.scalar_tensor_tensor(out=rbh[:, :], in0=h["rm"][:, :],
                                                               scalar=pc[:, 31 + cc:32 + cc], in1=h["kp"][:, :],
                                                               op0=ALU.mult, op1=ALU.mult),
                 reads=[b["rm"], b["kp"], R.pcol.bufs[0]], writes=rb.bufs, tag="rb")
            bb_ = bank_rot()

            def bon_mm(e, bb_=bb_):
                ins = None
                for jj in range(8):
                    ins = e.matmul(K.ps[bb_][0:64, jj * 2:jj * 2 + 2], rbh[:, jj * 64:(jj + 1) * 64], hsh[:, 0:2],
                                   start=(jj == 0), stop=True, skip_group_check=True)
                return ins
            P.op("pe", bon_mm, reads=[rb.bufs[0], R.hsel.bufs[0]], writes=[K.psb[bb_]], tag="bonus mm")
            P.op("dve", lambda e, bb_=bb_, jsl=jsl, cc=cc: e.tensor_copy(
                out=BONh[:, jsl, cc * 2:cc * 2 + 2], in_=K.ps[bb_][0:64, 0:16].rearrange("p (j c) -> p j c", c=2)),
                reads=[K.psb[bb_]], writes=[R.BON.bufs[tq]], tag="bonus evac")
            if tq == 0:
                P.op("dve", lambda e: e.tensor_tensor_scan(out=h["cum"][:, :], data0=onesh[:, :], data1=h["sg"][:, :],
                                                           initial=0.0, op0=ALU.mult, op1=ALU.add),
                     reads=[ones.bufs[0], b["sg"]], writes=[b["cum"]], tag="scan")
                P.op("pool", lambda e: e.memset(cbh[:, 0:1], 0.0), reads=[], writes=cb.bufs, tag="cb0")
            else:
                P.op("dve", lambda e: e.tensor_tensor_scan(out=h["cum"][:, :], data0=onesh[:, :], data1=h["sg"][:, :],
                                                           initial=cbh[:, 8:9], op0=ALU.mult, op1=ALU.add),
                     reads=[ones.bufs[0], b["sg"], cb.bufs[0]], writes=[b["cum"]], tag="scan")
                P.op("act", lambda e: e.activation(out=cbh[:, 0:1], in_=cbh[:, 8:9], func=AF.Copy),
                     reads=[], writes=cb.bufs, tag="cb carry")
            cum3 = h["cum"].rearrange("p (j t) -> p j t", t=CH)
            P.op("act", lambda e, cum3=cum3: e.activation(out=cbh[:, 1:9], in_=cum3[:, :, CH - 1], func=AF.Copy),
                 reads=[b["cum"]], writes=cb.bufs, tag="cb ends")
            P.op("dve", lambda e, cum3=cum3: e.tensor_tensor(out=cum3, in0=cum3,
                                                             in1=cbh[:, 0:8].unsqueeze(2).to_broadcast([128, 8, CH]),
                                                             op=ALU.subtract),
                 reads=[cb.bufs[0]], writes=[b["cum"]], tag="cumrel")
            P.op("pool", lambda e: e.tensor_tensor(out=h["cx"][:, :], in0=h["cum"][:, :], in1=h["sg"][:, :],
                                                   op=ALU.subtract), reads=[b["cum"], b["sg"]], writes=[b["cx"]],
                 tag="cumex")
            P.op("act", lambda e, cum3=cum3, cc=cc, jsl=jsl: e.activation(out=GCh[:, cc, jsl], in_=cum3[:, :, CH - 1],
                                                                         func=AF.Exp, scale=-C0),
                 reads=[b["cum"]], writes=[R.GC.bufs[cc]], tag="gammaC")
            P.op("act", lambda e: e.activation(out=h["av"][:, :], in_=h["cum"][:, :], func=AF.Exp, scale=-C0),
                 reads=[b["cum"]], writes=[b["av"]], tag="Eg")
            P.op("act", lambda e: e.activation(out=h["km"][:, :], in_=h["cx"][:, :], func=AF.Exp, scale=-C0),
                 reads=[b["cx"]], writes=[b["km"]], tag="Egx")
            P.op("act", lambda e: e.activation(out=h["sg"][:, :], in_=h["cum"][:, :], func=AF.Exp, scale=C0),
                 reads=[b["cum"]], writes=[b["sg"]], tag="Ei")

            def v3(x):
                return x.rearrange("p (j t) -> p j t", t=CH)
            P.op("dve", lambda e, cc=cc, jsl=jsl: e.scalar_tensor_tensor(
                out=ARh[:, cc, jsl, 0, :], in0=v3(h["kk"]), scalar=-1.0, in1=v3(h["km"]), op0=ALU.mult, op1=ALU.mult),
                reads=[b["kk"], b["km"]], writes=cbufs, tag="A~")
            P.op("pool", lambda e, cc=cc, jsl=jsl: e.tensor_tensor(
                out=ARh[:, cc, jsl, 1, :], in0=v3(h["rm"]), in1=v3(h["av"]), op=ALU.mult),
                reads=[b["rm"], b["av"]], writes=cbufs, tag="R~")
            P.op("dve", lambda e, cc=cc, jsl=jsl: e.tensor_tensor(
                out=BKh[:, cc, jsl, 0, :], in0=v3(h["ka"]), in1=v3(h["sg"]), op=ALU.mult),
                reads=[b["ka"], b["sg"]], writes=kbufs, tag="B~")
            P.op("pool", lambda e, cc=cc, jsl=jsl: e.tensor_tensor(
                out=BKh[:, cc, jsl, 1, :], in0=v3(h["kp"]), in1=v3(h["sg"]), op=ALU.mult),
                reads=[b["kp"], b["sg"]], writes=kbufs, tag="K~")
    for al in [wr, ones, pre, sq, rb, cb] + [T[n] for n in names]:
        P.free(al)


def rwkv_chunks(K, R, masks_d, gnb_d, o_dst, o_buf, base):
    P = K.P
    off = base
    mk = P.sbuf("masks", [64, 4, 128], F32, off); off += 4 * 128 * 4
    gnb = P.sbuf("gnb", [64, 2, 512], F32, off); off += 2 * 512 * 4
    M1 = P.sbuf("M1", [64, 2, 8, 128], BF16, off, nbufs=2); off += 2 * 8 * 128 * 2
    M2 = P.sbuf("M2", [64, 2, 8, 128], BF16, off, nbufs=2); off += 2 * 8 * 128 * 2
    M3 = P.sbuf("M3", [64, 2, 8, 64], BF16, off, nbufs=2); off += 2 * 8 * 64 * 2
    NL = P.sbuf("NL", [64, 2, 2, 8, 64], BF16, off, nbufs=4); off += 2 * 2 * 8 * 64 * 2
    PP = P.sbuf("PP", [64, 2, 8, 64], BF16, off, nbufs=2); off += 2 * 8 * 64 * 2
    BKh_ = P.sbuf("BKhat", [128, 2, 4, 2, CH], BF16, off, nbufs=2); off += 2 * 4 * 2 * CH * 2
    TOK = P.sbuf("TOK", [64, 2, 4, 512], BF16, off, nbufs=2); off += 2 * 4 * 512 * 2
    WTs = P.sbuf("WTs", [128, 2, 4, CH], BF16, off, nbufs=2); off += 2 * 4 * CH * 2
    MAK = P.sbuf("MAK", [64, 2, 8, 64], BF16, off, nbufs=2); off += 2 * 8 * 64 * 2
    Ub = P.sbuf("Ub", [64, 2, 8, 64], BF16, off, nbufs=2); off += 2 * 8 * 64 * 2
    Hf = P.sbuf("Hf", [128, 4, 64], F32, off); off += 4 * 64 * 4
    Hb = P.sbuf("Hb", [128, 4, 64], BF16, off); off += 4 * 64 * 2
    Ys = P.sbuf("Ys", [64, 2, 512], F32, off, nbufs=2); off += 2 * 512 * 4
    Yq = P.sbuf("Yq", [64, 512], F32, off); off += 512 * 4
    Gs = P.sbuf("Gs", [64, 512], F32, off); off += 512 * 4
    st = P.sbuf("gst", [64, 64], F32, off); off += 64 * 4
    Yo = P.sbuf("Yo", [64, 2, 512], BF16, off, nbufs=2); off += 2 * 512 * 2
    Yoh = Yo.handle
    mkh, gnh, M1h, M2h, M3h, NLh, PPh, BHh, TOKh, WTh, MAKh, Ubh, Hfh, Hbh, Ysh, Yqh, Gsh, sth = (
        mk.handle, gnb.handle, M1.handle, M2.handle, M3.handle, NL.handle, PP.handle, BKh_.handle, TOK.handle,
        WTs.handle, MAK.handle, Ub.handle, Hf.handle, Hb.handle, Ys.handle, Yq.handle, Gs.handle, st.handle)
    ARh, BKh, VCh, SGh, GCh, BONh, g2h = (R.AR.handle, R.BK.handle, R.VC.handle, R.SG.handle, R.GC.handle,
                                         R.BON.handle, R.g2.handle)
    ident = K.ident
    load_const(K, "masks", mk, mkh[:, :, :], masks_d[:, 0:4, :])
    load_const(K, "gnb", gnb, gnh[:, :, :], gnb_d[:, :, :])
    P.op("pool", lambda e: e.memset(Hfh[:, :, :], 0.0), reads=[], writes=Hf.bufs, tag="H0")
    P.op("pool", lambda e: e.memset(Hbh[:, :, :], 0.0), reads=[], writes=Hb.bufs, tag="H0b")

    rot = [0]

    def nb():
        rot[0] += 1
        return rot[0] % 8

    def pr(hh):
        return slice(64 * hh, 64 * hh + 64)

    def ps3(bk, n, w):
        return K.ps[bk][0:64, 0:n * w].rearrange("p (n w) -> p n w", w=w)

    for j in range(getattr(K, 'nchunk_override', NCHUNK)):
        s = j % 2
        ab, kb_, vb = R.AR.bufs[j], R.BK.bufs[j], R.VC.bufs[j]
        tq = j // 8
        for hh in range(2):
            b1, b2, b3 = nb(), nb(), nb()

            def mm1(e, b1=b1, hh=hh, j=j):
                ins = None
                for cc in range(4):
                    ins = e.matmul(K.ps[b1][0:64, cc * 128:(cc + 1) * 128], ARh[pr(hh), cc, j, 0, :],
                                   BKh[pr(hh), cc, j, :, :].rearrange("p a t -> p (a t)"), start=(cc == 0), stop=True,
                                   skip_group_check=True)
                return ins
            P.op("pe", mm1, reads=[ab, kb_], writes=[K.psb[b1]], tag="SA mm")
            P.op("dve", lambda e, b1=b1, hh=hh, s=s: e.tensor_tensor(
                out=M1h[:, s, hh * 4:(hh + 1) * 4, :], in0=ps3(b1, 4, 128),
                in1=mkh[:, 0:1, :].to_broadcast([64, 4, 128]), op=ALU.mult),
                reads=[K.psb[b1], mk.bufs[0]], writes=[M1.bufs[s]], tag="M1 evac")

            def mm2(e, b2=b2, hh=hh, j=j):
                ins = None
                for cc in range(4):
                    ins = e.matmul(K.ps[b2][0:64, cc * 128:(cc + 1) * 128], BKh[pr(hh), cc, j, 0, :],
                                   ARh[pr(hh), cc, j, :, :].rearrange("p a t -> p (a t)"), start=(cc == 0), stop=True,
                                   skip_group_check=True)
                return ins
            P.op("pe", mm2, reads=[ab, kb_], writes=[K.psb[b2]], tag="SB mm")
            P.op("dve", lambda e, b2=b2, hh=hh, s=s: e.tensor_tensor(
                out=M2h[:, s, hh * 4:(hh + 1) * 4, :], in0=ps3(b2, 4, 128),
                in1=mkh[:, 1:2, :].to_broadcast([64, 4, 128]), op=ALU.mult),
                reads=[K.psb[b2], mk.bufs[0]], writes=[M2.bufs[s]], tag="M2 evac")

            def mm3(e, b3=b3, hh=hh, j=j):
                ins = None
                for cc in range(4):
                    ins = e.matmul(K.ps[b3][0:64, cc * 64:(cc + 1) * 64], BKh[pr(hh), cc, j, 1, :],
                                   ARh[pr(hh), cc, j, 1, :], start=(cc == 0), stop=True, skip_group_check=True)
                return ins
            P.op("pe", mm3, reads=[ab, kb_], writes=[K.psb[b3]], tag="SK mm")
            P.op("dve", lambda e, b3=b3, hh=hh, s=s: e.tensor_tensor(
                out=M3h[:, s, hh * 4:(hh + 1) * 4, :], in0=ps3(b3, 4, 64),
                in1=mkh[:, 2:3, 0:64].to_broadcast([64, 4, 64]), op=ALU.mult),
                reads=[K.psb[b3], mk.bufs[0]], writes=[M3.bufs[s]], tag="M3 evac")

        if getattr(K, 'chunk_cut', 99) <= 1:
            continue
        P.op("pool", lambda e, s=s: e.tensor_tensor(out=PPh[:, 0, :, :], in0=M2h[:, s, :, 0:64],
                                                    in1=mkh[:, 3:4, 0:64].to_broadcast([64, 8, 64]), op=ALU.add),
             reads=[M2.bufs[s], mk.bufs[0]], writes=[PP.bufs[0]], tag="P0")
        Ncur = lambda hd, s=s: M2h[:, s, hd, 0:64]
        Lcur = lambda hd, s=s: M1h[:, s, hd, 0:64]
        ncur_buf, lcur_buf = M2.bufs[s], M1.bufs[s]
        pc_ = 0
        for lev in range(1, 6):
            sl = lev % 2
            bl = nb()
            bn2 = nb() if lev < 5 else None

            def mmL(e, bl=bl, Ncur=Ncur, Lcur=Lcur):
                ins = None
                for hd in range(8):
                    ins = e.matmul(K.ps[bl][0:64, hd * 64:(hd + 1) * 64], Ncur(hd), Lcur(hd), start=(hd == 0), stop=True,
                                   skip_group_check=True)
                return ins
            P.op("pe", mmL, reads=[ncur_buf, lcur_buf], writes=[K.psb[bl]], tag="L sq")
            P.op("act", lambda e, bl=bl, sl=sl: e.activation(out=NLh[:, sl, 1, :, :], in_=ps3(bl, 8, 64), func=AF.Copy),
                 reads=[K.psb[bl]], writes=[NL.bufs[sl * 2 + 1]], tag="L evac")
            if lev < 5:
                def mmN(e, bn2=bn2, Ncur=Ncur, Lcur=Lcur):
                    ins = None
                    for hd in range(8):
                        ins = e.matmul(K.ps[bn2][0:64, hd * 64:(hd + 1) * 64], Lcur(hd), Ncur(hd), start=(hd == 0),
                                       stop=True, skip_group_check=True)
                    return ins
                P.op("pe", mmN, reads=[ncur_buf, lcur_buf], writes=[K.psb[bn2]], tag="N sq")
                P.op("dve", lambda e, bn2=bn2, sl=sl: e.tensor_copy(out=NLh[:, sl, 0, :, :], in_=ps3(bn2, 8, 64)),
                     reads=[K.psb[bn2]], writes=[NL.bufs[sl * 2 + 0]], tag="N evac")
            bp = nb()

            def mmP(e, bp=bp, sl=sl, pc_=pc_):
                ins = None
                for hd in range(8):
                    ins = e.matmul(K.ps[bp][0:64, hd * 64:(hd + 1) * 64], NLh[:, sl, 1, hd, :], PPh[:, pc_, hd, :],
                                   start=(hd == 0), stop=True, skip_group_check=True)
                return ins
            P.op("pe", mmP, reads=[NL.bufs[sl * 2 + 1], PP.bufs[pc_]], writes=[K.psb[bp]], tag="P mm")
            pn = 1 - pc_
            P.op("dve", lambda e, bp=bp, pc_=pc_, pn=pn: e.tensor_tensor(out=PPh[:, pn, :, :], in0=ps3(bp, 8, 64),
                                                                        in1=PPh[:, pc_, :, :], op=ALU.add),
                 reads=[K.psb[bp], PP.bufs[pc_]], writes=[PP.bufs[pn]], tag="P add")
            pc_ = pn
            Ncur = lambda hd, sl=sl: NLh[:, sl, 0, hd, :]
            Lcur = lambda hd, sl=sl: NLh[:, sl, 1, hd, :]
            ncur_buf, lcur_buf = NL.bufs[sl * 2 + 0], NL.bufs[sl * 2 + 1]
        TT = lambda hd, pc_=pc_: PPh[:, pc_, hd, :]
        tt_buf = PP.bufs[pc_]

        if getattr(K, 'chunk_cut', 99) <= 2:
            continue
        P.op("pool", lambda e, s=s, j=j: e.tensor_tensor(
            out=BHh[:, s, :, :, :], in0=BKh[:, :, j, :, :],
            in1=GCh[:, :, j:j + 1].unsqueeze(3).to_broadcast([128, 4, 2, CH]), op=ALU.mult),
            reads=[kb_, R.GC.bufs[0], R.GC.bufs[1], R.GC.bufs[2], R.GC.bufs[3]], writes=[BKh_.bufs[s]], tag="BKhat")
        for g2_ in range(2):
            bt = nb()
            ptb = K.ps[bt].bitcast(BF16)

            def trs(e, ptb=ptb, g2_=g2_, s=s, j=j):
                ins = None
                for kk_ in range(2):
                    kind = g2_ * 2 + kk_
                    for cc in range(4):
                        if kind == 0:
                            src = ARh[:, cc, j, 0, :]
                        elif kind == 1:
                            src = BHh[:, s, cc, 0, :]
                        elif kind == 2:
                            src = BHh[:, s, cc, 1, :]
                        else:
                            src = VCh[:, cc, j * CH:(j + 1) * CH]
                        ins = e.transpose(ptb[0:64, kk_ * 512 + cc * 128: kk_ * 512 + (cc + 1) * 128], src, ident[:, :])
                return ins
            P.op("pe", trs, reads=[ab, BKh_.bufs[s], vb, K.ident_buf], writes=[K.psb[bt]], tag="tok transposes")
            eng = "act" if g2_ == 0 else "dve"

            def tev(e, ptb=ptb, g2_=g2_, s=s, eng=eng):
                o = TOKh[:, s, g2_ * 2:(g2_ + 1) * 2, :]
                i = ptb[0:64, :].rearrange("p (k c) -> p k c", k=2)
                if eng == "act":
                    return e.activation(out=o, in_=i, func=AF.Copy)
                return e.tensor_copy(out=o, in_=i)
            P.op(eng, tev, reads=[K.psb[bt]], writes=[TOK.bufs[s]], tag="tok evac")

        if getattr(K, 'chunk_cut', 99) <= 3:
            continue
        bw = nb()

        def mmW(e, bw=bw, s=s, TT=TT):
            ins = None
            for hp in range(8):
                cc = hp % 4
                ins = e.matmul(K.ps[bw][:, hp * 64:(hp + 1) * 64], TOKh[:, s, 0, cc * 128:(cc + 1) * 128], TT(hp),
                               start=(hp == 0), stop=True, skip_group_check=True)
            return ins
        P.op("pe", mmW, reads=[TOK.bufs[s], tt_buf], writes=[K.psb[bw]], tag="WT mm")
        for hh in range(2):
            eng = "act" if hh == 0 else "dve"

            def wev(e, bw=bw, hh=hh, s=s, eng=eng):
                o = WTh[pr(hh), s, :, :]
                i = K.ps[bw][pr(hh), hh * 256:(hh + 1) * 256].rearrange("p (c t) -> p c t", c=4)
                if eng == "act":
                    return e.activation(out=o, in_=i, func=AF.Copy)
                return e.tensor_copy(out=o, in_=i)
            P.op(eng, wev, reads=[K.psb[bw]], writes=[WTs.bufs[s]], tag="WT evac")
        bm_ = nb()

        def mmM(e, bm_=bm_, s=s, TT=TT):
            ins = None
            for hp in range(8):
                ins = e.matmul(K.ps[bm_][0:64, hp * 64:(hp + 1) * 64], M1h[:, s, hp, 64:128], TT(hp),
                               start=(hp == 0), stop=True, skip_group_check=True)
            return ins
        P.op("pe", mmM, reads=[M1.bufs[s], tt_buf], writes=[K.psb[bm_]], tag="MAK mm")
        P.op("act", lambda e, bm_=bm_, s=s: e.activation(out=MAKh[:, s, :, :], in_=ps3(bm_, 8, 64), func=AF.Copy),
             reads=[K.psb[bm_]], writes=[MAK.bufs[s]], tag="MAK evac")

        if getattr(K, 'chunk_cut', 99) <= 4:
            continue
        Vt = lambda hp, s=s: TOKh[:, s, 3, (hp % 4) * 128 + (hp // 4) * 64:(hp % 4) * 128 + (hp // 4) * 64 + 64]
        for hh in range(2):
            bu = nb()

            def mmU1(e, bu=bu, hh=hh, s=s, Vt=Vt):
                ins = None
                for cc in range(4):
                    hp = hh * 4 + cc
                    ins = e.matmul(K.ps[bu][0:64, cc * 64:(cc + 1) * 64], MAKh[:, s, hp, :], Vt(hp), start=(cc == 0),
                                   stop=False, skip_group_check=True)
                return ins
            P.op("pe", mmU1, reads=[MAK.bufs[s], TOK.bufs[s]], writes=[K.psb[bu]], tag="U mm1")

            def mmU2(e, bu=bu, hh=hh, s=s):
                ins = None
                for cc in range(4):
                    ins = e.matmul(K.ps[bu][0:64, cc * 64:(cc + 1) * 64], WTh[pr(hh), s, cc, :], Hbh[pr(hh), cc, :],
                                   start=False, stop=True, skip_group_check=True)
                return ins
            P.op("pe", mmU2, reads=[WTs.bufs[s], Hb.bufs[0]], writes=[K.psb[bu]], tag="U mm2", pe_sync=True)
            P.op("act", lambda e, bu=bu, hh=hh, s=s: e.activation(out=Ubh[:, s, hh * 4:(hh + 1) * 4, :],
                                                                 in_=ps3(bu, 4, 64), func=AF.Copy),
                 reads=[K.psb[bu]], writes=[Ub.bufs[s]], tag="U evac")
        for hh in range(2):
            by = nb()

            def mmY1(e, by=by, hh=hh, s=s, Vt=Vt):
                ins = None
                for cc in range(4):
                    hp = hh * 4 + cc
                    o = K.ps[by][0:64, cc * 64:(cc + 1) * 64]
                    e.matmul(o, M2h[:, s, hp, 64:128], Ubh[:, s, hp, :], start=(cc == 0), stop=False,
                             skip_group_check=True)
                    ins = e.matmul(o, M3h[:, s, hp, :], Vt(hp), start=False, stop=False, skip_group_check=True)
                return ins
            P.op("pe", mmY1, reads=[M2.bufs[s], Ub.bufs[s], M3.bufs[s], TOK.bufs[s]], writes=[K.psb[by]], tag="Y mm1")

            def mmY2(e, by=by, hh=hh, j=j):
                ins = None
                for cc in range(4):
                    ins = e.matmul(K.ps[by][0:64, cc * 64:(cc + 1) * 64], ARh[pr(hh), cc, j, 1, :], Hbh[pr(hh), cc, :],
                                   start=False, stop=True, skip_group_check=True)
                return ins
            P.op("pe", mmY2, reads=[ab, Hb.bufs[0]], writes=[K.psb[by]], tag="Y mm2", pe_sync=True)
            P.op("act", lambda e, by=by, hh=hh, s=s: e.activation(
                out=Ysh[:, s, :].rearrange("p (c h v) -> p c h v", c=4, h=2)[:, :, hh, :], in_=ps3(by, 4, 64),
                func=AF.Copy), reads=[K.psb[by]], writes=[Ys.bufs[s]], tag="Y evac")
        bh = nb()

        def mmH(e, bh=bh, s=s, Vt=Vt):
            ins = None
            for hp in range(8):
                cc = hp % 4
                o = K.ps[bh][:, hp * 64:(hp + 1) * 64]
                e.matmul(o, TOKh[:, s, 1, cc * 128:(cc + 1) * 128], Ubh[:, s, hp, :], start=(hp == 0), stop=False,
                         skip_group_check=True)
                ins = e.matmul(o, TOKh[:, s, 2, cc * 128:(cc + 1) * 128], Vt(hp), start=False, stop=True,
                               skip_group_check=True)
            return ins
        P.op("pe", mmH, reads=[TOK.bufs[s], Ub.bufs[s]], writes=[K.psb[bh]], tag="H mm")
        P.op("pool", lambda e, j=j: e.tensor_tensor(out=Hfh[:, :, :], in0=Hfh[:, :, :],
                                                    in1=GCh[:, :, j:j + 1].to_broadcast([128, 4, 64]), op=ALU.mult),
             reads=[R.GC.bufs[0], R.GC.bufs[1], R.GC.bufs[2], R.GC.bufs[3]], writes=Hf.bufs, tag="H decay")
        for hh in range(2):
            P.op("dve", lambda e, bh=bh, hh=hh: e.tensor_tensor(
                out=Hfh[pr(hh), :, :], in0=Hfh[pr(hh), :, :],
                in1=K.ps[bh][pr(hh), hh * 256:(hh + 1) * 256].rearrange("p (c v) -> p c v", c=4), op=ALU.add),
                reads=[K.psb[bh]], writes=Hf.bufs, tag="H add")
        P.op("act", lambda e: e.activation(out=Hbh[:, :, :], in_=Hfh[:, :, :], func=AF.Copy),
             reads=Hf.bufs, writes=Hb.bufs, tag="H bf16")

        if getattr(K, 'chunk_cut', 99) <= 5:
            continue
        bg = nb()

        def mmG(e, bg=bg, j=j):
            e.matmul(K.ps[bg][0:64, :], SGh[:, 0, j * CH:(j + 1) * CH], g2h[:, 0, :], start=True, stop=False)
            return e.matmul(K.ps[bg][0:64, :], SGh[0:32, 1, j * CH:(j + 1) * CH], g2h[0:32, 1, :], start=False, stop=True)
        P.op("pe", mmG, reads=[R.SG.bufs[tq], R.g2.bufs[0]], writes=[K.psb[bg]], tag="gate mm")
        P.op("act", lambda e, bg=bg: e.activation(out=Gsh[:, :], in_=K.ps[bg][0:64, :], func=AF.Copy),
             reads=[K.psb[bg]], writes=Gs.bufs, tag="gate evac")
        if getattr(K, 'chunk_cut', 99) <= 6:
            continue
        Y3 = Ysh[:, s, :].rearrange("p (h v) -> p h v", v=64)
        Q3 = Yqh.rearrange("p (h v) -> p h v", v=64)
        yb = Ys.bufs[s]
        P.op("dve", lambda e, Y3=Y3: e.tensor_reduce(out=sth[:, 0:8], in_=Y3, axis=AX.X, op=ALU.add),
             reads=[yb], writes=st.bufs, tag="gn sum")
        P.op("pool", lambda e, s=s: e.tensor_tensor(out=Yqh[:, :], in0=Ysh[:, s, :], in1=Ysh[:, s, :], op=ALU.mult),
             reads=[yb], writes=Yq.bufs, tag="gn sq")
        P.op("dve", lambda e, Q3=Q3: e.tensor_reduce(out=sth[:, 8:16], in_=Q3, axis=AX.X, op=ALU.add),
             reads=Yq.bufs, writes=st.bufs, tag="gn ssq")
        P.op("dve", lambda e: e.tensor_scalar(out=sth[:, 16:24], in0=sth[:, 0:8], scalar1=1.0 / 64, scalar2=None,
                                              op0=ALU.mult), reads=[], writes=st.bufs, tag="gn mean")
        P.op("dve", lambda e: e.tensor_tensor(out=sth[:, 24:32], in0=sth[:, 16:24], in1=sth[:, 16:24], op=ALU.mult),
             reads=[], writes=st.bufs, tag="gn m2")
        P.op("dve", lambda e: e.scalar_tensor_tensor(out=sth[:, 32:40], in0=sth[:, 8:16], scalar=1.0 / 64,
                                                     in1=sth[:, 24:32], op0=ALU.mult, op1=ALU.subtract),
             reads=[], writes=st.bufs, tag="gn var")
        P.op("dve", lambda e: e.tensor_scalar(out=sth[:, 32:40], in0=sth[:, 32:40], scalar1=GN_EPS, scalar2=None,
                                              op0=ALU.add), reads=[], writes=st.bufs, tag="gn var eps")
        P.op("act", lambda e: e.activation(out=sth[:, 40:48], in_=sth[:, 32:40], func=AF.Sqrt),
             reads=[], writes=st.bufs, tag="gn sqrt")
        P.op("dve", lambda e: e.reciprocal(out=sth[:, 48:56], in_=sth[:, 40:48]), reads=[], writes=st.bufs, tag="gn rstd")
        P.op("dve", lambda e, Y3=Y3: e.tensor_tensor(out=Y3, in0=Y3, in1=sth[:, 16:24].unsqueeze(2).to_broadcast([64, 8, 64]),
                                                     op=ALU.subtract), reads=[], writes=[yb] + st.bufs, tag="gn sub")
        P.op("pool", lambda e, Y3=Y3: e.tensor_tensor(out=Y3, in0=Y3, in1=sth[:, 48:56].unsqueeze(2).to_broadcast([64, 8, 64]),
                                                      op=ALU.mult), reads=st.bufs, writes=[yb], tag="gn mul")
        P.op("pool", lambda e, s=s: e.tensor_tensor(out=Ysh[:, s, :], in0=Ysh[:, s, :], in1=gnh[:, 0, :], op=ALU.mult),
             reads=gnb.bufs, writes=[yb], tag="gn g")
        P.op("pool", lambda e, s=s: e.tensor_tensor(out=Ysh[:, s, :], in0=Ysh[:, s, :], in1=gnh[:, 1, :], op=ALU.add),
             reads=gnb.bufs, writes=[yb], tag="gn b")
        if getattr(K, 'chunk_cut', 99) <= 7:
            continue
        P.op("dve", lambda e, Q3=Q3, s=s, j=j: e.tensor_tensor(
            out=Q3, in0=TOKh[:, s, 3, :].rearrange("p (h v) -> p h v", v=64),
            in1=BONh[:, j, :].unsqueeze(2).to_broadcast([64, 8, 64]), op=ALU.mult),
            reads=[TOK.bufs[s], R.BON.bufs[tq]], writes=Yq.bufs, tag="bonus v")
        P.op("pool", lambda e, s=s: e.tensor_tensor(out=Ysh[:, s, :], in0=Ysh[:, s, :], in1=Yqh[:, :], op=ALU.add),
             reads=Yq.bufs, writes=[yb], tag="y+bonus")
        P.op("dve", lambda e, s=s: e.tensor_tensor(out=Yoh[:, s, :], in0=Ysh[:, s, :], in1=Gsh[:, :], op=ALU.mult),
             reads=Gs.bufs + [yb], writes=[Yo.bufs[s]], tag="gate mul")
        if getattr(K, 'chunk_cut', 99) <= 8:
            continue
        P.op("sp", lambda e, s=s, j=j: e.dma_start(out=o_dst[j * CH:(j + 1) * CH, 512:1024], in_=Yoh[:, s, :]),
             reads=[Yo.bufs[s]], writes=[o_buf[j // 4] if isinstance(o_buf, list) else o_buf], dma=True,
             tag="o_rwkv out")
        if j % 4 == 3 and getattr(K, "after_oblock", None) is not None:
            K.after_oblock(j // 4)
    for al in (mk, gnb, M1, M2, M3, NL, PP, BKh_, TOK, WTs, MAK, Ub, Hf, Hb, Ys, Yq, Gs, st, Yo):
        P.free(al)


def host_rwkv_params(inp, half):
    c0, c1 = half * 512, (half + 1) * 512
    mu = inp["rwkv_mu"]
    pcol = np.zeros((128, 64), np.float32)
    for blk in range(3):
        for cc in range(4):
            pcol[:, blk * 4 + cc] = mu[blk * 1024 + c0 + cc * 128: blk * 1024 + c0 + (cc + 1) * 128]
    pcol[:, 12] = mu[3072:3072 + 128]
    pcol[:, 13] = mu[3072 + 128:3072 + 256]
    pcol[0:32, 14] = mu[3072 + 256:3072 + 288]
    rk = inp["rwkv_r_k"].reshape(-1)
    for cc in range(4):
        sl = slice(c0 + cc * 128, c0 + (cc + 1) * 128)
        pcol[:, 15 + cc] = inp["rwkv_w0"][sl]
        pcol[:, 19 + cc] = inp["rwkv_a0"][sl]
        pcol[:, 23 + cc] = inp["rwkv_k_k"][sl]
        pcol[:, 27 + cc] = inp["rwkv_k_a"][sl]
        pcol[:, 31 + cc] = rk[sl]
    w2a2 = np.concatenate([inp["rwkv_w2"][:, c0:c1], inp["rwkv_a2"][:, c0:c1]], 0).astype(np.float32)
    g2 = np.zeros((128, 2, 512), np.float32)
    g2[:, 0, :] = inp["rwkv_g2"][0:128, c0:c1]
    g2[0:32, 1, :] = inp["rwkv_g2"][128:160, c0:c1]
    gnb = np.zeros((64, 2, 512), np.float32)
    gnb[:, 0, :] = inp["rwkv_gn_g"][c0:c1][None]
    gnb[:, 1, :] = inp["rwkv_gn_b"][c0:c1][None]
    return dict(pcol=pcol, w2a2=np.ascontiguousarray(w2a2), g2=g2, gnb=gnb)


def host_rwkv_consts():
    bones = np.zeros((128, 128), np.float32)
    bones[0:64, 0:64] = 1.0
    bones[64:128, 64:128] = 1.0
    hsel = np.zeros((128, 16), np.float32)
    hsel[0:64, 0] = 1.0
    hsel[64:128, 1] = 1.0
    t = np.arange(64)
    sl = (t[None, :] < t[:, None]).astype(np.float32)
    su = (t[:, None] < t[None, :]).astype(np.float32)
    ui = (t[:, None] <= t[None, :]).astype(np.float32)
    masks = np.zeros((64, 6, 128), np.float32)
    masks[:, 0, 0:64] = sl
    masks[:, 0, 64:128] = sl
    masks[:, 1, 0:64] = su
    masks[:, 1, 64:128] = ui
    masks[:, 2, 0:64] = ui
    masks[:, 3, 0:64] = np.eye(64, dtype=np.float32)
    return dict(bones=bones, hsel=hsel, masks=masks)


def ln_tile(P, acch, ab, t, gbh, gb, lsth, st, s, outs):
    for j in range(4):
        P.op("dve", lambda e, t=t, j=j, s=s: e.bn_stats(out=lsth[:, s, j * 6:(j + 1) * 6],
                                                      in_=acch[:, t, j * 512:(j + 1) * 512]),
             reads=[ab[j]], writes=[st.bufs[s]], tag="bn_stats")
    P.op("dve", lambda e, s=s: e.bn_aggr(out=lsth[:, s, 24:26], in_=lsth[:, s, 0:24].rearrange("p (a b) -> p a b", b=6)),
         reads=[], writes=[st.bufs[s]], tag="bn_aggr")
    P.op("dve", lambda e, s=s: e.tensor_scalar(out=lsth[:, s, 28:29], in0=lsth[:, s, 25:26], scalar1=LN_EPS,
                                              scalar2=None, op0=ALU.add), reads=[], writes=[st.bufs[s]], tag="var+eps")
    P.op("act", lambda e, s=s: e.activation(out=lsth[:, s, 29:30], in_=lsth[:, s, 28:29], func=AF.Sqrt),
         reads=[], writes=[st.bufs[s]], tag="sqrt")
    P.op("dve", lambda e, s=s: e.reciprocal(out=lsth[:, s, 26:27], in_=lsth[:, s, 29:30]),
         reads=[], writes=[st.bufs[s]], tag="rstd")
    P.op("dve", lambda e, s=s: e.scalar_tensor_tensor(out=lsth[:, s, 27:28], in0=lsth[:, s, 24:25], scalar=-1.0,
                                                     in1=lsth[:, s, 26:27], op0=ALU.mult, op1=ALU.mult),
         reads=[], writes=[st.bufs[s]], tag="nmr")
    P.op("act", lambda e, t=t, s=s: e.activation(out=acch[:, t, :], in_=acch[:, t, :], func=AF.Identity,
                                                bias=lsth[:, s, 27:28], scale=lsth[:, s, 26:27]),
         reads=[st.bufs[s]], writes=ab, tag="ln norm")
    P.op("pool", lambda e, t=t: e.tensor_tensor(out=acch[:, t, :], in0=acch[:, t, :], in1=gbh[:, 0, :], op=ALU.mult),
         reads=[gb.bufs[0]], writes=ab, tag="ln g")
    P.op("pool", lambda e, t=t: e.tensor_tensor(out=acch[:, t, :], in0=acch[:, t, :], in1=gbh[:, 1, :], op=ALU.add),
         reads=[gb.bufs[0]], writes=ab, tag="ln b")
    for (dst, dbuf, dt) in outs:
        if isinstance(dbuf, list):
            dbuf = dbuf[t]
        if dt == F32:
            P.op("sp", lambda e, t=t, dst=dst: e.dma_start(out=dst[t * 128:(t + 1) * 128, :], in_=acch[:, t, :]),
                 reads=ab, writes=[dbuf], dma=True, tag="ln out")
        else:
            obf = P.lnbf
            P.op("act", lambda e, t=t, s=s, obf=obf: e.activation(out=obf.handle[:, s, :], in_=acch[:, t, :], func=AF.Copy),
                 reads=ab, writes=[obf.bufs[s]], tag="ln out cast")
            P.op("sp", lambda e, t=t, s=s, dst=dst, obf=obf: e.dma_start(out=dst[t * 128:(t + 1) * 128, :],
                                                                        in_=obf.handle[:, s, :]),
                 reads=[obf.bufs[s]], writes=[dbuf], dma=True, tag="ln out bf16")


def wout_stage(K, og, og_buf, x1f, x1f_buf, wout, hsc_d, lng, lnb, outs, base):
    P = K.P
    NT = NTOK // 128
    off = base
    acc = P.sbuf("acc3", [128, NT, D], F32, off, nbufs=NT * 4); off += NT * D * 4
    XT = P.sbuf("oT", [128, KC, NTOK], BF16, off, nbufs=NT); off += KC * NTOK * 2
    ws = P.sbuf("wouts", [128, KC, D], BF16, off, nbufs=4); off += KC * D * 2
    gb = P.sbuf("lngb3", [128, 2, D], F32, off); off += 2 * D * 4
    st = P.sbuf("lnst3", [128, 2, 32], F32, off, nbufs=2); off += 2 * 32 * 4
    hs = P.sbuf("hsc", [128, 8], F32, off); off += 32
    xa = P.sbuf("xa", [128, 2, 2, D], BF16, off, nbufs=2); off += 2 * 2 * D * 2
    xb = P.sbuf("xb3", [128, 2, D], BF16, off, nbufs=2); off += 2 * D * 2
    acch, XTh, wsh, gbh, lsth, hsh, xah, xbh = (acc.handle, XT.handle, ws.handle, gb.handle, st.handle, hs.handle,
                                                xa.handle, xb.handle)
    load_const(K, "lng3", gb, gbh[:, 0, :], lng[:, :])
    load_const(K, "lnb3", gb, gbh[:, 1, :], lnb[:, :])
    load_const(K, "hsc", hs, hsh[:, 0:2], hsc_d[:, :])
    wv = wout.rearrange("(kc p) d -> p kc d", p=128)
    for q in range(4):
        P.op("pool", lambda e, q=q: e.dma_start(out=wsh[:, q * 4:(q + 1) * 4, :], in_=wv[:, q * 4:(q + 1) * 4, :]),
             reads=[], writes=[ws.bufs[q]], dma=True, tag="wout load")
    for t in range(NT):
        P.op("sp", lambda e, t=t: e.dma_start(out=acch[:, t, :], in_=x1f[t * 128:(t + 1) * 128, :]),
             reads=[x1f_buf], writes=acc.bufs[t * 4:(t + 1) * 4], dma=True, tag="acc3 load")
        P.op("act", lambda e, t=t: e.activation(out=acch[:, t, :], in_=acch[:, t, :], func=AF.Copy, scale=ALPHA),
             reads=[], writes=acc.bufs[t * 4:(t + 1) * 4], tag="acc3 scale")
    banks = [4, 5, 6, 7]
    for t in range(NT):
        s = t % 2
        for cand in range(2):
            for r in range(2):
                if callable(og):
                    oap, obuf_ = og(cand, r, t)
                else:
                    row0 = r * SEQ + cand * NTOK + t * 128
                    oap, obuf_ = og[row0:row0 + 128, :], og_buf
                P.op("sp", lambda e, s=s, cand=cand, r=r, oap=oap: e.dma_start(
                    out=xah[:, s, cand, r * 1024:(r + 1) * 1024], in_=oap),
                    reads=[obuf_], writes=[xa.bufs[s]], dma=True, tag="og load")
        P.op("dve", lambda e, s=s: e.tensor_scalar(out=xbh[:, s, :], in0=xah[:, s, 0, :], scalar1=hsh[:, 0:1],
                                                  scalar2=None, op0=ALU.mult),
             reads=[xa.bufs[s], hs.bufs[0]], writes=[xb.bufs[s]], tag="blend0")
        P.op("dve", lambda e, s=s: e.scalar_tensor_tensor(out=xbh[:, s, :], in0=xah[:, s, 1, :], scalar=hsh[:, 1:2],
                                                         in1=xbh[:, s, :], op0=ALU.mult, op1=ALU.add),
             reads=[xa.bufs[s], hs.bufs[0]], writes=[xb.bufs[s]], tag="blend1")
        for half in range(2):
            bk = banks[(2 * t + half) % 4]
            pb = K.ps[bk].bitcast(BF16)

            def tr(e, s=s, half=half, pb=pb):
                ins = None
                for j in range(8):
                    kc = half * 8 + j
                    ins = e.transpose(pb[:, j * 128:(j + 1) * 128], xbh[:, s, kc * 128:(kc + 1) * 128], K.ident[:])
                return ins
            P.op("pe", tr, reads=[xb.bufs[s], K.ident_buf], writes=[K.psb[bk]], tag="oT transposes")
            eng = "act" if half == 0 else "dve"

            def ev(e, t=t, half=half, pb=pb, eng=eng):
                o = XTh[:, half * 8:(half + 1) * 8, t * 128:(t + 1) * 128]
                i = pb.rearrange("p (j c) -> p j c", j=8)
                if eng == "act":
                    return e.activation(out=o, in_=i, func=AF.Copy)
                return e.tensor_copy(out=o, in_=i)
            P.op(eng, ev, reads=[K.psb[bk]], writes=[XT.bufs[t]], tag="oT evac")
    cnt = 0
    for t in range(NT):
        for db in range(4):
            bk = cnt % 4
            cnt += 1

            def mm(e, bk=bk, t=t, db=db):
                ins = None
                for kc in range(KC):
                    ins = e.matmul(K.ps[bk][:, :], XTh[:, kc, t * 128:(t + 1) * 128], wsh[:, kc, db * 512:(db + 1) * 512],
                                   start=(kc == 0), stop=(kc == KC - 1))
                return ins
            P.op("pe", mm, reads=[XT.bufs[t]] + ws.bufs, writes=[K.psb[bk]], tag="wout mm")
            P.op("dve", lambda e, bk=bk, t=t, db=db: e.tensor_tensor(
                out=acch[:, t, db * 512:(db + 1) * 512], in0=K.ps[bk][:, :], in1=acch[:, t, db * 512:(db + 1) * 512],
                op=ALU.add), reads=[K.psb[bk]], writes=[acc.bufs[t * 4 + db]], tag="acc3 add")
        ln_tile(P, acch, acc.bufs[t * 4:(t + 1) * 4], t, gbh, gb, lsth, st, t % 2, outs)
    for a in (acc, XT, ws, gb, st, hs, xa, xb):
        P.free(a)


def build_program():
    nc = bass.Bass("TRN2", target_bir_lowering=False)
    K = mk_ctx(nc)
    P = K.P
    d = {}

    def din(name, shape):
        d[name] = dram_in(K, name, shape)
        return d[name]
    x = din("x", [NTOK, D])
    identd = din("ident", [128, 128])
    f1 = [din("f1_wg", [NG, 128, KC, FG]), din("f1_wu", [NG, 128, KC, FG]), din("f1_wd", [DFF, D]),
          din("ln1g", [128, D]), din("ln1b", [128, D])]
    f2 = [din("f2_wg", [NG, 128, KC, FG]), din("f2_wu", [NG, 128, KC, FG]), din("f2_wd", [DFF, D]),
          din("ln3g", [128, D]), din("ln3b", [128, D])]
    win = din("win", [27, 128, KC, 128])
    bm = din("bm", [4, 5, 128, 512]); ctab = din("ctab", [128, 64]); lamv = din("lamv", [128, 4, 64])
    ngt = din("ngt", [128, 512])
    pcol = din("pcol", [128, 64]); w2a2 = din("w2a2", [128, 512]); g2 = din("g2", [128, 2, 512])
    gnb = din("gnb", [64, 2, 512]); bones = din("bones", [128, 128]); hsel = din("hsel", [128, 16])
    masks = din("masks", [64, 6, 128])
    wout = din("wout", [D, D]); hsc = din("hsc", [128, 2]); ln2g = din("ln2g", [128, D]); ln2b = din("ln2b", [128, D])
    x1f = dram_tmp(K, "x1f", [NTOK, D], F32)
    x1b = dram_tmp(K, "x1b", [NTOK, D], BF16)
    x1g = [dram_tmp(K, f"x1g{i}", [256, D], BF16) for i in range(8)]
    oloc = dram_tmp(K, "oloc", [SEQ, 1024], BF16)
    og = [dram_tmp(K, f"og{i}", [512, 1024], BF16) for i in range(8)]
    x2s = dram_tmp(K, "x2s", [NTOK, D], F32)
    out = dram_tmp(K, "out", [NTOK, D], F32, kind="ExternalOutput")
    B = lambda n: K.dram[n][1]
    groups = [[0, 1], [2, 3], [4, 5], [6, 7]]

    setup_consts(K, identd)
    base = K.const_top
    x1b_bufs = [Buf(f"dram_x1b[{i}]") for i in range(8)]

    def gather_x1(i):
        P.op("pool", lambda e, i=i: e.collective_compute("AllGather", ALU.bypass, replica_groups=groups,
                                                         ins=[x1b.ap()[i * 128:(i + 1) * 128, :]],
                                                         outs=[x1g[i].ap()[:, :]]),
             reads=[x1b_bufs[i]], writes=[B(f"x1g{i}")], dma="cc", tag="allgather x1")
    ffn_stage(K, x.ap(), B("x"), f1[0].ap(), f1[1].ap(), f1[2].ap(), f1[3].ap(), f1[4].ap(),
              [(x1f.ap(), B("x1f"), F32), (x1b.ap(), x1b_bufs, BF16)], base=base, after_tile=gather_x1)

    def x1_src(t):
        r, i = t // 8, t % 8
        return x1g[i].ap()[r * 128:(r + 1) * 128, :], B(f"x1g{i}")
    X1T = P.sbuf("X1T", [128, KC, SEQ], BF16, base, nbufs=16)
    b2 = base + KC * SEQ * 2
    load_xT(K, x1_src, None, SEQ, X1T, b2, [4, 5, 6, 7], K.ident, False)
    ob_bufs = [Buf(f"dram_oloc[{i}]") for i in range(8)]

    def gather_o(i):
        P.op("pool", lambda e, i=i: e.collective_compute("AllGather", ALU.bypass, replica_groups=groups,
                                                         ins=[oloc.ap()[i * 256:(i + 1) * 256, :]],
                                                         outs=[og[i].ap()[:, :]]),
             reads=[ob_bufs[i]], writes=[B(f"og{i}")], dma="cc", tag="allgather o")
    K.after_oblock = gather_o
    attention_stage(K, X1T, win.ap(), bm.ap(), ctab.ap(), lamv.ap(), ngt.ap(), oloc.ap(), ob_bufs, b2)
    R = rwkv_alloc_persist(K, b2)
    rwkv_prep(K, R, X1T, win.ap(), pcol.ap(), w2a2.ap(), g2.ap(), bones.ap(), hsel.ap(), R.top)
    P.free(X1T)
    rwkv_chunks(K, R, masks.ap(), gnb.ap(), oloc.ap(), ob_bufs, base)
    for a in (R.AR, R.BK, R.VC, R.LW, R.SG, R.GC, R.BON, R.pcol, R.w2a2, R.g2, R.bones, R.hsel):
        P.free(a)
    def og_src(cand, r, t):
        tok = cand * NTOK + t * 128
        i, j = tok // 256, tok % 256
        return og[i].ap()[r * 256 + j:r * 256 + j + 128, :], B(f"og{i}")
    wout_stage(K, og_src, None, x1f.ap(), B("x1f"), wout.ap(), hsc.ap(), ln2g.ap(), ln2b.ap(),
               [(x2s.ap(), B("x2s"), F32)], base)
    ffn_stage(K, x2s.ap(), B("x2s"), f2[0].ap(), f2[1].ap(), f2[2].ap(), f2[3].ap(), f2[4].ap(),
              [(out.ap(), B("out"), F32)], base=base)
    finish(K, [B("out")])
    P.emit()
    return nc


def host_inputs(inp):
    l0 = {k: np.asarray(v[0], np.float32) for k, v in inp.items() if k != "x"}
    x = np.asarray(inp["x"], np.float32).reshape(8, NTOK, D)
    rep = lambda v, n=128: np.ascontiguousarray(np.broadcast_to(np.asarray(v, np.float32)[None], (n,) + v.shape))
    shared = dict(
        ident=np.eye(128, dtype=np.float32),
        f1_wg=host_tile_ffn_w(l0["ffn1_w_gate"]), f1_wu=host_tile_ffn_w(l0["ffn1_w_up"]), f1_wd=l0["ffn1_w_down"],
        ln1g=rep(l0["ln1_g"]), ln1b=rep(l0["ln1_b"]),
        f2_wg=host_tile_ffn_w(l0["ffn2_w_gate"]), f2_wu=host_tile_ffn_w(l0["ffn2_w_up"]), f2_wd=l0["ffn2_w_down"],
        ln3g=rep(l0["ln3_g"]), ln3b=rep(l0["ln3_b"]), ln2g=rep(l0["ln2_g"]), ln2b=rep(l0["ln2_b"]),
        lamv=rep(np.stack([l0["lambda_q1"], l0["lambda_k1"], l0["lambda_q2"], l0["lambda_k2"]])),
        ngt=rep(np.tile(l0["attn_norm_g"], 4)),
        wout=np.ascontiguousarray(l0["w_out"][np.concatenate([np.arange(0, 512), np.arange(1024, 1536),
                                                            np.arange(512, 1024), np.arange(1536, 2048)])]),
    )
    shared.update(host_rwkv_consts())
    per_half = []
    for h in range(2):
        bm, ctab = host_attn_consts(h)
        dd = dict(win=host_tile_win(l0["w_in"], h), bm=bm, ctab=ctab,
                  hsc=np.ascontiguousarray(np.broadcast_to(np.array([1.0 - h, float(h)], np.float32)[None], (128, 2))))
        dd.update(host_rwkv_params(l0, h))
        per_half.append(dd)
    maps = []
    for c in range(8):
        m = dict(shared)
        m.update(per_half[c % 2])
        m["x"] = np.ascontiguousarray(x[c])
        maps.append(m)
    return maps


def kernel(**inputs):
    nc = build_program()
    maps = host_inputs(inputs)
    res = run_bass_kernel_spmd(nc, maps, core_ids=list(range(8)))
    out = np.stack([np.asarray(res.results[c]["out"], np.float32) for c in range(8)], 0)
    return out.reshape(4, SEQ, D)
```
